# Optimizing a Trainium2 kernel written in Bass

```python
import jax
import jax.numpy as jnp
from jax import lax
import numpy as np

D_MODEL = 2048
BATCH = 4
SEQ = 4096
DEPTH = 4

N_MIXERS = 4
MEM_LEN = 256
FFN_DIM = 5632
NORM_EPS = 1e-6
NEG_INF = -1e30
ROPE_THETA = 500000.0
CONV_WIDTH = 3
DIL_PATTERNS = ((128, 1), (512, 4), (2048, 16))
DIL_GROUPS = len(DIL_PATTERNS)
DIL_HEADS = 8
DIL_HEAD_DIM = 128
DIL_BLOCK = 128
HGRN_EXPAND = 128
HGRN_HEADS = D_MODEL // HGRN_EXPAND
HGRN_V_DIM = D_MODEL // HGRN_HEADS
HGRN_CHUNK = 16
RWKV_HEAD_DIM = 64
RWKV_HEADS = D_MODEL // RWKV_HEAD_DIM
RWKV_DECAY_LORA = 96
RWKV_A_LORA = 96
RWKV_GATE_LORA = 256
RWKV_GN_EPS = 64e-5
XATTN_HEADS = 4
XATTN_HEAD_DIM = D_MODEL // XATTN_HEADS
N_CONV_LAYERS = (DEPTH + N_MIXERS - 1) // N_MIXERS
N_DIL_LAYERS = (DEPTH + N_MIXERS - 2) // N_MIXERS
N_HGRN_LAYERS = (DEPTH + N_MIXERS - 3) // N_MIXERS
N_RWKV_LAYERS = DEPTH // N_MIXERS

kernel_name = 'hybrid_interleaved_trunk'


def rms_norm(x, gain):
    xf = x.astype(jnp.float32)
    y = xf * lax.rsqrt(jnp.mean(xf * xf, axis=-1, keepdims=True) + NORM_EPS)
    return (y * gain.astype(jnp.float32)).astype(x.dtype)


def swiglu(h, w_gate, w_up, w_down):
    return (jax.nn.silu(h @ w_gate) * (h @ w_up)) @ w_down


def partial_rotary(x, positions):
    rot = x.shape[-1] // 4
    half = rot // 2
    inv_freq = ROPE_THETA ** (-jnp.arange(0, rot, 2, dtype=jnp.float32) / rot)
    ang = positions.astype(jnp.float32)[..., None] * inv_freq
    ang = ang.reshape(ang.shape[:2] + (1,) * (x.ndim - 3) + (half,))
    cos, sin = jnp.cos(ang), jnp.sin(ang)
    x1 = x[..., :half].astype(jnp.float32)
    x2 = x[..., half:rot].astype(jnp.float32)
    rotated = jnp.concatenate([x1 * cos - x2 * sin, x2 * cos + x1 * sin], axis=-1).astype(x.dtype)
    return jnp.concatenate([rotated, x[..., rot:]], axis=-1)


def short_conv_mixer(h, w_in, conv_w, w_out):
    b_gate, c_gate, u = jnp.split(h @ w_in, 3, axis=-1)
    y = lax.conv_general_dilated(c_gate * u, conv_w[:, None, :], window_strides=(1,),
                                 padding=[(CONV_WIDTH - 1, 0)],
                                 dimension_numbers=('NWC', 'WIO', 'NWC'),
                                 feature_group_count=h.shape[-1])
    return (b_gate * y) @ w_out


def dilated_group_attention(q, k, v, dilation, n_back):
    B, S, H, Dh = q.shape
    L = S // dilation
    nb = -(-L // DIL_BLOCK)
    Lp = nb * DIL_BLOCK

    def to_blocks(a):
        a = a.reshape(B, L, dilation, H, Dh)
        a = jnp.pad(a, ((0, 0), (0, Lp - L), (0, 0), (0, 0), (0, 0)))
        return a.reshape(B, nb, DIL_BLOCK, dilation, H, Dh)

    def with_prev(a):
        prev = jnp.pad(a, ((0, 0), (1, 0), (0, 0), (0, 0), (0, 0), (0, 0)))[:, :-1]
        return jnp.concatenate([prev, a], axis=2)

    qb = to_blocks(q)
    kb = with_prev(to_blocks(k))
    vb = with_prev(to_blocks(v))
    s = jnp.einsum('bnqrhd,bnkrhd->bnrhqk', qb, kb, preferred_element_type=jnp.float32) * (Dh ** -0.5)
    qi = jnp.arange(DIL_BLOCK)[:, None]
    kj = jnp.arange(2 * DIL_BLOCK)[None, :]
    dist = DIL_BLOCK + qi - kj
    band = (dist >= 0) & (dist <= n_back)
    has_prev = (jnp.arange(nb) > 0)[:, None, None] | (kj >= DIL_BLOCK)[None]
    mask = band[None] & has_prev
    s = jnp.where(mask[None, :, None, None], s, NEG_INF)
    lse = jax.nn.logsumexp(s, axis=-1)
    p = jnp.exp(s - lse[..., None]).astype(v.dtype)
    o = jnp.einsum('bnrhqk,bnkrhd->bnqrhd', p, vb, preferred_element_type=jnp.float32)
    o = o.reshape(B, Lp, dilation, H, Dh)[:, :L].reshape(B, S, H, Dh)
    lse = lse.transpose(0, 1, 4, 2, 3).reshape(B, Lp, dilation, H)[:, :L].reshape(B, S, H)
    return o, lse


def dilated_attention_mixer(h, positions, w_qkv, q_gain, k_gain, w_out):
    B, S, _ = h.shape
    qkv = (h @ w_qkv).reshape(B, S, 3, DIL_GROUPS, DIL_HEADS, DIL_HEAD_DIM)
    q = partial_rotary(rms_norm(qkv[:, :, 0], q_gain[:, None, :]), positions)
    k = partial_rotary(rms_norm(qkv[:, :, 1], k_gain[:, None, :]), positions)
    v = qkv[:, :, 2]
    outs, lses = [], []
    for g, (window, dilation) in enumerate(DIL_PATTERNS):
        o, lse = dilated_group_attention(q[:, :, g], k[:, :, g], v[:, :, g], dilation, window // dilation)
        outs.append(o)
        lses.append(lse)
    alpha = jax.nn.softmax(jnp.stack(lses, axis=0), axis=0)
    o = jnp.sum(alpha[..., None] * jnp.stack(outs, axis=0), axis=0)
    return o.reshape(B, S, DIL_HEADS * DIL_HEAD_DIM).astype(h.dtype) @ w_out


def hgrn2_chunked(q, k, v, log_f):
    B, S, H, Dk = q.shape
    Dv = v.shape[-1]
    C = HGRN_CHUNK
    nc = S // C

    def chunks(a):
        return a.reshape(B, nc, C, H, a.shape[-1]).transpose(0, 3, 1, 2, 4)

    q, k, v, g = chunks(q), chunks(k), chunks(v), chunks(log_f)
    A = jnp.cumsum(g, axis=3)
    A_last = A[:, :, :, -1:, :]
    q_dec = q * jnp.exp(A)
    k_in = k * jnp.exp(-A)
    k_end = k * jnp.exp(A_last - A)
    causal = jnp.tril(jnp.ones((C, C), dtype=bool))
    att = jnp.where(causal, jnp.einsum('bhncd,bhnsd->bhncs', q_dec, k_in), 0.0)
    o_intra = jnp.einsum('bhncs,bhnse->bhnce', att, v)
    chunk_decay = jnp.exp(A_last[:, :, :, 0, :])

    def step(state, inp):
        qd, ke, vc, dec = inp
        o = jnp.einsum('bhcd,bhde->bhce', qd, state)
        state = state * dec[..., None] + jnp.einsum('bhcd,bhce->bhde', ke, vc)
        return state, o

    xs = (jnp.moveaxis(q_dec, 2, 0), jnp.moveaxis(k_end, 2, 0), jnp.moveaxis(v, 2, 0),
          jnp.moveaxis(chunk_decay, 2, 0))
    _, o_inter = lax.scan(step, jnp.zeros((B, H, Dk, Dv), jnp.float32), xs)
    o = o_intra + jnp.moveaxis(o_inter, 0, 2)
    return o.transpose(0, 2, 3, 1, 4).reshape(B, S, H, Dv)


def hgrn2_mixer(h, w_in, lower_bound, norm_gain, w_out):
    B, S, D = h.shape
    f32 = jnp.float32
    q, f, i, gate = jnp.split(h @ w_in, 4, axis=-1)
    forget = lower_bound + (1.0 - lower_bound) * jax.nn.sigmoid(f.astype(f32))

    def heads(t):
        return t.astype(f32).reshape(B, S, HGRN_HEADS, HGRN_EXPAND)

    o = hgrn2_chunked(heads(q), heads(1.0 - forget),
                      i.astype(f32).reshape(B, S, HGRN_HEADS, HGRN_V_DIM), heads(jnp.log(forget)))
    o = rms_norm(o, norm_gain).reshape(B, S, D) * jax.nn.silu(gate.astype(f32))
    return o.astype(h.dtype) @ w_out


def rwkv7_mixer(h, mu, w_rkv, w0, w1, w2, a0, a1, a2, g1, g2, k_k, k_a, r_k, ln_w, ln_b, w_out):
    B, S, D = h.shape
    H, N = RWKV_HEADS, RWKV_HEAD_DIM
    f32 = jnp.float32
    h_prev = jnp.pad(h, ((0, 0), (1, 0), (0, 0)))[:, :-1]
    mixed = h[None] + (h_prev - h)[None] * mu[:, None, None, :]
    r, k, v = jnp.einsum('nbsd,nde->nbse', mixed[:3], w_rkv)
    xw, xa, xg = mixed[3], mixed[4], mixed[5]
    w_log = -jax.nn.softplus(-(w0 + jnp.tanh(xw @ w1) @ w2).astype(f32)) - 0.5
    decay = jnp.exp(-jnp.exp(w_log))
    a = jax.nn.sigmoid((a0 + (xa @ a1) @ a2).astype(f32))
    g = jax.nn.sigmoid(xg @ g1) @ g2

    def heads(t):
        return t.astype(f32).reshape(B, S, H, N)

    r, k, v, decay, a = heads(r), heads(k), heads(v), heads(decay), heads(a)
    kk = k * k_k.astype(f32).reshape(H, N)
    kk = kk * lax.rsqrt(jnp.maximum(jnp.sum(kk * kk, axis=-1, keepdims=True), 1e-24))
    k = k * (1.0 + (a - 1.0) * k_a.astype(f32).reshape(H, N))

    def step(state, inp):
        r_t, w_t, k_t, v_t, kk_t, b_t = inp
        sa = jnp.einsum('bhvk,bhk->bhv', state, -kk_t)
        state = (state * w_t[:, :, None, :] + sa[..., None] * b_t[:, :, None, :]
                 + v_t[..., None] * k_t[:, :, None, :])
        return state, jnp.einsum('bhvk,bhk->bhv', state, r_t)

    xs = (jnp.moveaxis(r, 1, 0), jnp.moveaxis(decay, 1, 0), jnp.moveaxis(k, 1, 0),
          jnp.moveaxis(v, 1, 0), jnp.moveaxis(kk, 1, 0), jnp.moveaxis(kk * a, 1, 0))
    _, y = lax.scan(step, jnp.zeros((B, H, N, N), f32), xs)
    y = jnp.moveaxis(y, 0, 1)
    mean = jnp.mean(y, axis=-1, keepdims=True)
    var = jnp.mean(jnp.square(y - mean), axis=-1, keepdims=True)
    y = ((y - mean) * lax.rsqrt(var + RWKV_GN_EPS)).reshape(B, S, D) * ln_w.astype(f32) + ln_b.astype(f32)
    bonus = jnp.sum(r * k * r_k.astype(f32), axis=-1, keepdims=True) * v
    y = (y + bonus.reshape(B, S, D)) * g.astype(f32)
    return y.astype(h.dtype) @ w_out


def memory_cross_attention(h, memn, wq, wkv, wo, q_gain, k_gain):
    B, S, D = h.shape
    M = memn.shape[1]
    q = rms_norm((h @ wq).reshape(B, S, XATTN_HEADS, XATTN_HEAD_DIM), q_gain)
    kv = (memn @ wkv).reshape(B, M, 2, XATTN_HEADS, XATTN_HEAD_DIM)
    k = rms_norm(kv[:, :, 0], k_gain)
    v = kv[:, :, 1]
    s = jnp.einsum('bshd,bmhd->bhsm', q, k, preferred_element_type=jnp.float32) * (XATTN_HEAD_DIM ** -0.5)
    p = jax.nn.softmax(s, axis=-1).astype(v.dtype)
    o = jnp.einsum('bhsm,bmhd->bshd', p, v).reshape(B, S, D)
    return o @ wo


def setup_inputs(seed: int = 0) -> dict:
    key = jax.random.key(seed)
    keys = jax.random.split(key, 64)
    counter = iter(range(64))
    f32 = jnp.float32
    D, F = D_MODEL, FFN_DIM

    def nk():
        return keys[next(counter)]

    def dense(shape, fan_in, scale=1.0):
        return jax.random.normal(nk(), shape, f32) * (scale * fan_in ** -0.5)

    def gain(shape):
        return 1.0 + 0.02 * jax.random.normal(nk(), shape, f32)

    def noise(shape, scale):
        return scale * jax.random.normal(nk(), shape, f32)

    x = jax.random.normal(nk(), (BATCH, SEQ, D), f32)
    mem = jax.random.normal(nk(), (BATCH, MEM_LEN, D), f32)
    positions = (jax.random.randint(nk(), (BATCH, 1), 0, 1024) + jnp.arange(SEQ)[None, :]).astype(jnp.int32)
    qkv_cols = 3 * DIL_GROUPS * DIL_HEADS * DIL_HEAD_DIM
    return {
        'x': x,
        'mem': mem,
        'positions': positions,
        'ffn_norm': gain((DEPTH, 2, D)),
        'ffn_w_gate': dense((DEPTH, 2, D, F), D),
        'ffn_w_up': dense((DEPTH, 2, D, F), D),
        'ffn_w_down': dense((DEPTH, 2, F, D), F, 0.5),
        'mix_norm': gain((DEPTH, D)),
        'xattn_norm': gain((DEPTH, D)),
        'mem_norm': gain((DEPTH, D)),
        'xattn_wq': dense((DEPTH, D, D), D),
        'xattn_wkv': dense((DEPTH, D, 2 * D), D),
        'xattn_wo': dense((DEPTH, D, D), D, 0.5),
        'xattn_q_gain': gain((DEPTH, XATTN_HEAD_DIM)),
        'xattn_k_gain': gain((DEPTH, XATTN_HEAD_DIM)),
        'conv_w_in': dense((N_CONV_LAYERS, D, 3 * D), D),
        'conv_w': dense((N_CONV_LAYERS, CONV_WIDTH, D), CONV_WIDTH),
        'conv_w_out': dense((N_CONV_LAYERS, D, D), D, 0.5),
        'dil_w_qkv': dense((N_DIL_LAYERS, D, qkv_cols), D),
        'dil_q_gain': gain((N_DIL_LAYERS, DIL_GROUPS, DIL_HEAD_DIM)),
        'dil_k_gain': gain((N_DIL_LAYERS, DIL_GROUPS, DIL_HEAD_DIM)),
        'dil_w_out': dense((N_DIL_LAYERS, DIL_HEADS * DIL_HEAD_DIM, D), DIL_HEADS * DIL_HEAD_DIM, 0.5),
        'hgrn_w_in': dense((N_HGRN_LAYERS, D, 4 * D), D),
        'hgrn_lb_logits': noise((DEPTH, D), 0.3),
        'hgrn_norm': gain((N_HGRN_LAYERS, HGRN_V_DIM)),
        'hgrn_w_out': dense((N_HGRN_LAYERS, D, D), D, 0.5),
        'rwkv_mu': jax.random.uniform(nk(), (N_RWKV_LAYERS, 6, D), f32),
        'rwkv_w_rkv': dense((N_RWKV_LAYERS, 3, D, D), D),
        'rwkv_w0': jax.random.uniform(nk(), (N_RWKV_LAYERS, D), f32, -3.0, 1.0),
        'rwkv_w1': dense((N_RWKV_LAYERS, D, RWKV_DECAY_LORA), D, 0.5),
        'rwkv_w2': dense((N_RWKV_LAYERS, RWKV_DECAY_LORA, D), RWKV_DECAY_LORA, 0.5),
        'rwkv_a0': noise((N_RWKV_LAYERS, D), 0.1),
        'rwkv_a1': dense((N_RWKV_LAYERS, D, RWKV_A_LORA), D, 0.5),
        'rwkv_a2': dense((N_RWKV_LAYERS, RWKV_A_LORA, D), RWKV_A_LORA, 0.5),
        'rwkv_g1': dense((N_RWKV_LAYERS, D, RWKV_GATE_LORA), D),
        'rwkv_g2': dense((N_RWKV_LAYERS, RWKV_GATE_LORA, D), RWKV_GATE_LORA),
        'rwkv_k_k': 0.85 + noise((N_RWKV_LAYERS, D), 0.02),
        'rwkv_k_a': gain((N_RWKV_LAYERS, D)),
        'rwkv_r_k': noise((N_RWKV_LAYERS, RWKV_HEADS, RWKV_HEAD_DIM), 0.1),
        'rwkv_ln_w': gain((N_RWKV_LAYERS, D)),
        'rwkv_ln_b': noise((N_RWKV_LAYERS, D), 0.02),
        'rwkv_w_out': dense((N_RWKV_LAYERS, D, D), D, 0.5),
    }


def reference(x, mem, positions, ffn_norm, ffn_w_gate, ffn_w_up, ffn_w_down, mix_norm,
              xattn_norm, mem_norm, xattn_wq, xattn_wkv, xattn_wo, xattn_q_gain, xattn_k_gain,
              conv_w_in, conv_w, conv_w_out,
              dil_w_qkv, dil_q_gain, dil_k_gain, dil_w_out,
              hgrn_w_in, hgrn_lb_logits, hgrn_norm, hgrn_w_out,
              rwkv_mu, rwkv_w_rkv, rwkv_w0, rwkv_w1, rwkv_w2, rwkv_a0, rwkv_a1, rwkv_a2,
              rwkv_g1, rwkv_g2, rwkv_k_k, rwkv_k_a, rwkv_r_k, rwkv_ln_w, rwkv_ln_b, rwkv_w_out):
    lb_p = jax.nn.softmax(hgrn_lb_logits.astype(jnp.float32), axis=0)
    lower_bounds = jnp.cumsum(lb_p, axis=0) - lb_p[0]
    for i in range(DEPTH):
        kind, j = i % N_MIXERS, i // N_MIXERS
        x = x + 0.5 * swiglu(rms_norm(x, ffn_norm[i, 0]), ffn_w_gate[i, 0], ffn_w_up[i, 0], ffn_w_down[i, 0])
        h = rms_norm(x, mix_norm[i])
        if kind == 0:
            y = short_conv_mixer(h, conv_w_in[j], conv_w[j], conv_w_out[j])
        elif kind == 1:
            y = dilated_attention_mixer(h, positions, dil_w_qkv[j], dil_q_gain[j], dil_k_gain[j], dil_w_out[j])
        elif kind == 2:
            y = hgrn2_mixer(h, hgrn_w_in[j], lower_bounds[i], hgrn_norm[j], hgrn_w_out[j])
        else:
            y = rwkv7_mixer(h, rwkv_mu[j], rwkv_w_rkv[j], rwkv_w0[j], rwkv_w1[j], rwkv_w2[j],
                            rwkv_a0[j], rwkv_a1[j], rwkv_a2[j], rwkv_g1[j], rwkv_g2[j],
                            rwkv_k_k[j], rwkv_k_a[j], rwkv_r_k[j], rwkv_ln_w[j], rwkv_ln_b[j], rwkv_w_out[j])
        x = x + y
        x = x + memory_cross_attention(rms_norm(x, xattn_norm[i]), rms_norm(mem, mem_norm[i]),
                                       xattn_wq[i], xattn_wkv[i], xattn_wo[i], xattn_q_gain[i], xattn_k_gain[i])
        x = x + 0.5 * swiglu(rms_norm(x, ffn_norm[i, 1]), ffn_w_gate[i, 1], ffn_w_up[i, 1], ffn_w_down[i, 1])
    return x
```

```python
import contextlib
import numpy as np
import concourse.bass as bass
import concourse.mybir as mybir
from concourse.bass_utils import run_bass_kernel_spmd

F32 = mybir.dt.float32
BF16 = mybir.dt.bfloat16
I32 = mybir.dt.int32
AF = mybir.ActivationFunctionType
ALU = mybir.AluOpType
AX = mybir.AxisListType

D = 2048
KC = D // 128
FF = 5632
FC = FF // 128
NCORES = 8
EPS = 1e-6


class Prog:
    ENGS = ("pe", "act", "dve", "pool", "sp")
    NRING = 6

    def __init__(self, nc, stack):
        self.nc = nc
        self.stack = stack
        self.streams = {e: [] for e in self.ENGS}
        self.count = {e: 0 for e in self.ENGS}
        self.sems = {}
        for e in ("pe", "act", "dve", "pool"):
            self.sems[e] = stack.enter_context(nc.semaphore("s_" + e))
        self.rings = {}
        self.dma_k = {}
        for q in ("sp", "pool", "act"):
            self.rings[q] = [stack.enter_context(nc.semaphore(f"r_{q}{i}")) for i in range(self.NRING)]
            self.dma_k[q] = 0
        self.seen = {e: {} for e in self.ENGS}
        self.res = {}
        self.final_events = []

    def _need(self, eng, ev, waits):
        if ev is None:
            return
        sem, val = ev
        key = id(sem)
        if self.seen[eng].get(key, 0) >= val:
            return
        self.seen[eng][key] = val
        waits.append((sem, val))

    def _deps(self, eng, reads, writes):
        waits = []
        for r in reads:
            st = self.res.get(r)
            if st is not None:
                self._need(eng, st["w"], waits)
        for w in writes:
            st = self.res.get(w)
            if st is not None:
                self._need(eng, st["w"], waits)
                for ev in st["r"].values():
                    self._need(eng, ev, waits)
        return waits

    def _commit(self, ev, reads, writes):
        for r in reads:
            st = self.res.setdefault(r, {"w": None, "r": {}})
            st["r"][id(ev[0])] = ev
        for w in writes:
            self.res[w] = {"w": ev, "r": {}}

    def op(self, eng, fn, reads=(), writes=()):
        waits = self._deps(eng, reads, writes)
        self.count[eng] += 1
        ev = (self.sems[eng], self.count[eng])
        self.streams[eng].append((waits, fn, (self.sems[eng], 1)))
        self._commit(ev, reads, writes)
        return ev

    def dma(self, q, out, in_, reads=(), writes=(), final=False):
        waits = self._deps(q, reads, writes)
        k = self.dma_k[q]
        self.dma_k[q] += 1
        sem = self.rings[q][k % self.NRING]
        gen = k // self.NRING
        if gen > 0:
            self._need(q, (sem, 16 * gen), waits)
        ev = (sem, 16 * (gen + 1))

        def fn(e, out=out, in_=in_):
            return e.dma_start(out=out, in_=in_, allow_slow_non_contiguous=True)

        self.streams[q].append((waits, fn, (sem, 16)))
        self._commit(ev, reads, writes)
        if final:
            self.final_events.append(ev)
        return ev

    def emit(self):
        nc = self.nc
        fw = []
        for ev in self.final_events:
            self._need("sp", ev, fw)
        self.streams["sp"].append((fw, None, None))
        with nc.Block() as block:
            def run(eng_obj, name):
                for waits, fn, inc in self.streams[name]:
                    for sem, val in waits:
                        eng_obj.wait_ge(sem, val)
                    if fn is not None:
                        ins = fn(eng_obj)
                        ins.then_inc(inc[0], inc[1])

            @block.tensor
            def _(e):
                run(e, "pe")

            @block.scalar
            def _(e):
                run(e, "act")

            @block.vector
            def _(e):
                run(e, "dve")

            @block.gpsimd
            def _(e):
                run(e, "pool")

            @block.sync
            def _(e):
                run(e, "sp")


def _sb(nc, stack, name, shape, dt):
    return stack.enter_context(nc.sbuf_tensor("t_" + name, list(shape), dt))


def _ps(nc, stack, name, shape, dt=F32):
    return stack.enter_context(nc.psum_tensor("t_" + name, list(shape), dt))


class Ctx:
    def __init__(self, nc, stack, P):
        self.nc, self.stack, self.P = nc, stack, P
        self.tiles = {}
        self.cnt = {}

    def sb(self, name, shape, dt):
        if name not in self.tiles:
            self.tiles[name] = _sb(self.nc, self.stack, name, shape, dt)
        return self.tiles[name]

    def ps(self, name, shape=(128, 512), dt=F32):
        if name not in self.tiles:
            self.tiles[name] = _ps(self.nc, self.stack, name, shape, dt)
        return self.tiles[name]

    def rot(self, name, n):
        k = self.cnt.get(name, 0)
        self.cnt[name] = k + 1
        return k % n

    def const_ones(self):
        if "ones32" not in self.tiles:
            t = self.sb("ones32", [128, 128], F32)
            self.P.op("pool", lambda e: e.memset(t[:], 1.0), writes=["ones32"])
            tb = self.sb("ones16", [128, 128], BF16)
            self.P.op("pool", lambda e: e.memset(tb[:], 1.0), writes=["ones16"])
        return self.tiles["ones32"], self.tiles["ones16"]


def emit_norm(c, src, ntok, gn, xn, xoff, xn_key, eps=EPS, scale_d=1.0 / D, kcs=KC, sumsq_ones=None, ones_key="ones32", gn_key="gn"):
    P = c.P
    ones32, _ = c.const_ones()
    if sumsq_ones is None:
        sumsq_ones = ones32
    TN = 256
    xin = c.sb("n_xin", [128, KC, TN], F32)
    sqs = [c.sb(f"n_sq{i}", [128, TN], F32) for i in range(2)]
    rstd = c.sb("n_rstd", [128, TN], F32)
    psn = c.ps("ps_n", [128, 512])
    for a in range(0, ntok, TN):
        n = min(TN, ntok - a)
        P.dma("sp", xin[:, 0:kcs, 0:n], src[:, :, a:a + n], writes=["n_xin"])
        for kc in range(kcs):
            si = c.rot("n_sq", 2)
            s = sqs[si]
            P.op("act", lambda e, s=s, kc=kc, n=n: e.activation(out=s[:, 0:n], in_=xin[:, kc, 0:n], func=AF.Square),
                 reads=["n_xin"], writes=[f"n_sq{si}"])
            P.op("pe", lambda e, s=s, kc=kc, n=n: e.matmul(psn[:, 0:n], sumsq_ones[:], s[:, 0:n], start=(kc == 0), stop=(kc == kcs - 1)),
                 reads=[f"n_sq{si}", ones_key], writes=["ps_n"])
        P.op("act", lambda e, n=n: e.activation(out=rstd[:, 0:n], in_=psn[:, 0:n], func=AF.Sqrt, bias=eps, scale=scale_d),
             reads=["ps_n"], writes=["n_rstd"])
        P.op("dve", lambda e, n=n: e.reciprocal(out=rstd[:, 0:n], in_=rstd[:, 0:n]), reads=["n_rstd"], writes=["n_rstd"])
        for kc in range(kcs):
            P.op("dve", lambda e, kc=kc, a=a, n=n: e.scalar_tensor_tensor(
                out=xn[:, kc, xoff + a:xoff + a + n], in0=xin[:, kc, 0:n], scalar=gn[:, kc:kc + 1],
                in1=rstd[:, 0:n], op0=ALU.mult, op1=ALU.mult),
                reads=["n_xin", "n_rstd", gn_key], writes=[xn_key])


def emit_proj(c, wv, col0, nchunks, xn, xn_keys, ntok, cb, kcs=KC, cw=128, toff=0):
    P = c.P
    wb = [c.sb(f"p_w{i}", [128, KC, 128], BF16) for i in range(2)]
    pss = [c.ps(f"ps_a{i}") for i in range(2)]
    for j in range(nchunks):
        b = c.rot("p_w", 2)
        P.dma("pool", wb[b][:, 0:kcs, 0:cw], wv[:, :, col0 + j * cw: col0 + (j + 1) * cw], writes=[f"p_w{b}"])
        for t0 in range(0, ntok, 512):
            n = min(512, ntok - t0)
            pi = c.rot("ps_a", 2)
            ps = pss[pi]

            def mm(e, b=b, ps=ps, t0=t0, n=n):
                ins = None
                for kc in range(kcs):
                    ins = e.matmul(ps[0:cw, 0:n], wb[b][:, kc, 0:cw], xn[:, kc, toff + t0:toff + t0 + n],
                                   start=(kc == 0), stop=(kc == kcs - 1))
                return ins
            P.op("pe", mm, reads=[f"p_w{b}"] + list(xn_keys), writes=[f"ps_a{pi}"])
            cb(j, t0, n, ps, f"ps_a{pi}")


def emit_outproj(c, wv, cc, src, src_keys, xTv, oTv, t0g, ntok, scale, final, soff=0):
    P = c.P
    wb = [c.sb(f"o_w{cc}_{i}", [128, cc, 128], BF16) for i in range(2)]
    pss = [c.ps(f"ps_c{i}") for i in range(2)]
    xres = [c.sb(f"o_xres{i}", [128, 512], F32) for i in range(2)]
    osb = [c.sb(f"o_osb{i}", [128, 512], F32) for i in range(2)]
    for nn in range(KC):
        b = c.rot("o_w", 2)
        P.dma("pool", wb[b][:, 0:cc, :], wv[:, :, nn * 128:(nn + 1) * 128], writes=[f"o_w{cc}_{b}"])
        for t0 in range(0, ntok, 512):
            n = min(512, ntok - t0)
            pb = c.rot("ps_c", 2)
            P.dma("sp", xres[pb][:, 0:n], xTv[:, nn, t0g + t0:t0g + t0 + n], writes=[f"o_xres{pb}"])

            def mm(e, b=b, pb=pb, t0=t0, n=n):
                ins = None
                for f in range(cc):
                    ins = e.matmul(pss[pb][:, 0:n], wb[b][:, f, :], src[:, f, soff + t0:soff + t0 + n],
                                   start=(f == 0), stop=(f == cc - 1))
                return ins
            P.op("pe", mm, reads=[f"o_w{cc}_{b}"] + list(src_keys), writes=[f"ps_c{pb}"])
            P.op("dve", lambda e, pb=pb, n=n: e.scalar_tensor_tensor(
                out=osb[pb][:, 0:n], in0=pss[pb][:, 0:n], scalar=float(scale), in1=xres[pb][:, 0:n],
                op0=ALU.mult, op1=ALU.add),
                reads=[f"ps_c{pb}", f"o_xres{pb}"], writes=[f"o_osb{pb}"])
            P.dma("sp", oTv[:, nn, t0g + t0:t0g + t0 + n], osb[pb][:, 0:n], reads=[f"o_osb{pb}"], final=final)


def _fm(ap):
    return ap.rearrange("(kc p) t -> p kc t", p=128)


def _new_nc():
    return bass.Bass("TRN2", target_bir_lowering=False)


def _din(nc, name, shape, dt=F32):
    return nc.dram_tensor(name, list(shape), dt, kind="ExternalInput").ap()


def _dout(nc, name, shape, dt=F32):
    return nc.dram_tensor(name, list(shape), dt, kind="ExternalOutput").ap()


def build_normproj(N, T=2048):
    nc = _new_nc()
    xT = _din(nc, "xT", [D, T]); gain = _din(nc, "gain", [128, KC]); w = _din(nc, "w", [D, N])
    yT = _dout(nc, "yT", [N, T])
    with contextlib.ExitStack() as stack:
        P = Prog(nc, stack); c = Ctx(nc, stack, P)
        gn = c.sb("gn", [128, KC], F32)
        P.dma("sp", gn[:], gain, writes=["gn"])
        xn = c.sb("xn", [128, KC, T], BF16)
        emit_norm(c, _fm(xT), T, gn, xn, 0, "xn")
        ysb = [c.sb(f"ysb{i}", [128, 512], F32) for i in range(2)]
        yv = _fm(yT)

        def cb(j, t0, n, ps, psk):
            i = c.rot("ysb", 2)
            P.op("act", lambda e: e.copy(out=ysb[i][:, 0:n], in_=ps[:, 0:n]), reads=[psk], writes=[f"ysb{i}"])
            P.dma("sp", yv[:, j, t0:t0 + n], ysb[i][:, 0:n], reads=[f"ysb{i}"], final=True)
        emit_proj(c, _fm(w), 0, N // 128, xn, ["xn"], T, cb)
        P.emit()
    return nc


def build_outproj(CC, T=2048):
    nc = _new_nc()
    xT = _din(nc, "xT", [D, T]); sT = _din(nc, "sT", [CC * 128, T]); w = _din(nc, "w", [CC * 128, D])
    oT = _dout(nc, "oT", [D, T])
    with contextlib.ExitStack() as stack:
        P = Prog(nc, stack); c = Ctx(nc, stack, P)
        src = c.sb("src", [128, CC, T], BF16)
        sv = sT.rearrange("(c p) t -> p c t", p=128)
        for cc in range(CC):
            P.dma("pool", src[:, cc, :], sv[:, cc, :], writes=["src"])
        emit_outproj(c, w.rearrange("(c p) n -> p c n", p=128), CC, src, ["src"], _fm(xT), _fm(oT), 0, T, 1.0, True)
        P.emit()
    return nc
def build_ffn(T=2048):
    nc = bass.Bass("TRN2", target_bir_lowering=False)
    xT = nc.dram_tensor("xT", [D, T], F32, kind="ExternalInput").ap()
    gain = nc.dram_tensor("gain", [128, KC], F32, kind="ExternalInput").ap()
    wg = nc.dram_tensor("wg", [D, FF], F32, kind="ExternalInput").ap()
    wu = nc.dram_tensor("wu", [D, FF], F32, kind="ExternalInput").ap()
    wd = nc.dram_tensor("wd", [FF, D], F32, kind="ExternalInput").ap()
    oT = nc.dram_tensor("oT", [D, T], F32, kind="ExternalOutput").ap()
    with contextlib.ExitStack() as stack:
        P = Prog(nc, stack)
        emit_ffn(nc, stack, P, xT, gain, wg, wu, wd, oT, T, "f")
        P.emit()
    return nc


def emit_ffn(nc, stack, P, xT, gain, wg, wu, wd, oT, T, pfx, final=True):
    TH = 1024
    NH = T // TH
    TN = 256
    TT = 512
    FG = 2
    xTv = xT.rearrange("(kc p) t -> p kc t", p=128)
    oTv = oT.rearrange("(kc p) t -> p kc t", p=128)
    wgv = wg.rearrange("(kc p) f -> p kc f", p=128)
    wuv = wu.rearrange("(kc p) f -> p kc f", p=128)
    wdv = wd.rearrange("(fc p) n -> p fc n", p=128)

    act = _sb(nc, stack, pfx + "act", [128, FC, TH], BF16)
    xn = _sb(nc, stack, pfx + "xn", [128, KC, TH], BF16)
    wgb = [_sb(nc, stack, pfx + f"wg{i}", [128, KC, FG * 128], BF16) for i in range(2)]
    wub = [_sb(nc, stack, pfx + f"wu{i}", [128, KC, FG * 128], BF16) for i in range(2)]
    wdb = [_sb(nc, stack, pfx + f"wd{i}", [128, FC, 128], BF16) for i in range(2)]
    xin = _sb(nc, stack, pfx + "xin", [128, KC, TN], F32)
    sq = [_sb(nc, stack, pfx + f"sq{i}", [128, TN], F32) for i in range(2)]
    rstd = _sb(nc, stack, pfx + "rstd", [128, TN], F32)
    ones = _sb(nc, stack, pfx + "ones", [128, 128], F32)
    gn = _sb(nc, stack, pfx + "gn", [128, KC], F32)
    sil = [_sb(nc, stack, pfx + f"sil{i}", [128, TT], F32) for i in range(2)]
    xres = [_sb(nc, stack, pfx + f"xres{i}", [128, TT], F32) for i in range(2)]
    osb = [_sb(nc, stack, pfx + f"osb{i}", [128, TT], F32) for i in range(2)]
    ps_n = _ps(nc, stack, pfx + "psn", [128, TN])
    ps_g = [_ps(nc, stack, pfx + f"psg{i}", [128, TT]) for i in range(2)]
    ps_u = [_ps(nc, stack, pfx + f"psu{i}", [128, TT]) for i in range(2)]
    ps_o = [_ps(nc, stack, pfx + f"pso{i}", [128, TT]) for i in range(2)]

    K = lambda *a: (pfx,) + a
    P.op("pool", lambda e: e.memset(ones[:], 1.0), writes=[K("ones")])
    P.dma("sp", gn[:], gain, writes=[K("gn")])

    gi = 0
    di = 0
    ei = 0
    si = 0
    for h in range(NH):
        t0 = h * TH
        for nt in range(TH // TN):
            ta = t0 + nt * TN
            P.dma("sp", xin[:], xTv[:, :, ta:ta + TN], writes=[K("xin")])
            for kc in range(KC):
                s = sq[si % 2]
                P.op("act", lambda e, s=s, kc=kc: e.activation(out=s[:], in_=xin[:, kc, :], func=AF.Square),
                     reads=[K("xin")], writes=[K("sq", si % 2)])
                P.op("pe", lambda e, s=s, kc=kc: e.matmul(ps_n[:], ones[:], s[:], start=(kc == 0), stop=(kc == KC - 1)),
                     reads=[K("sq", si % 2), K("ones")], writes=[K("psn")])
                si += 1
            P.op("act", lambda e: e.activation(out=rstd[:], in_=ps_n[:], func=AF.Sqrt, bias=EPS, scale=1.0 / D),
                 reads=[K("psn")], writes=[K("rstd")])
            P.op("dve", lambda e: e.reciprocal(out=rstd[:], in_=rstd[:]), reads=[K("rstd")], writes=[K("rstd")])
            for kc in range(KC):
                P.op("dve", lambda e, kc=kc, nt=nt: e.scalar_tensor_tensor(
                    out=xn[:, kc, nt * TN:(nt + 1) * TN], in0=xin[:, kc, :], scalar=gn[:, kc:kc + 1],
                    in1=rstd[:], op0=ALU.mult, op1=ALU.mult),
                    reads=[K("xin"), K("rstd"), K("gn")], writes=[K("xn", nt)])
        xn_keys = [K("xn", nt) for nt in range(TH // TN)]
        for fg in range(FC // FG):
            b = gi % 2
            f0 = fg * FG * 128
            P.dma("pool", wgb[b][:], wgv[:, :, f0:f0 + FG * 128], writes=[K("wg", b)])
            P.dma("pool", wub[b][:], wuv[:, :, f0:f0 + FG * 128], writes=[K("wu", b)])
            for fc in range(FG):
                f = fg * FG + fc
                for tt in range(TH // TT):
                    pb = ei % 2

                    def mm(e, wt, pt, fc=fc, tt=tt):
                        ins = None
                        for kc in range(KC):
                            ins = e.matmul(pt[:], wt[:, kc, fc * 128:(fc + 1) * 128],
                                           xn[:, kc, tt * TT:(tt + 1) * TT],
                                           start=(kc == 0), stop=(kc == KC - 1))
                        return ins
                    P.op("pe", lambda e, b=b, pb=pb, mm=mm: mm(e, wgb[b], ps_g[pb]),
                         reads=[K("wg", b)] + xn_keys, writes=[K("psg", pb)])
                    P.op("pe", lambda e, b=b, pb=pb, mm=mm: mm(e, wub[b], ps_u[pb]),
                         reads=[K("wu", b)] + xn_keys, writes=[K("psu", pb)])
                    P.op("act", lambda e, pb=pb: e.activation(out=sil[pb][:], in_=ps_g[pb][:], func=AF.Silu),
                         reads=[K("psg", pb)], writes=[K("sil", pb)])
                    P.op("dve", lambda e, pb=pb, f=f, tt=tt: e.tensor_tensor(
                        out=act[:, f, tt * TT:(tt + 1) * TT], in0=sil[pb][:], in1=ps_u[pb][:], op=ALU.mult),
                        reads=[K("sil", pb), K("psu", pb)], writes=[K("act", f)])
                    ei += 1
            gi += 1
        act_keys = [K("act", f) for f in range(FC)]
        for n in range(KC):
            b = di % 2
            P.dma("pool", wdb[b][:], wdv[:, :, n * 128:(n + 1) * 128], writes=[K("wd", b)])
            for tt in range(TH // TT):
                pb = ei % 2
                ta = t0 + tt * TT
                P.dma("sp", xres[pb][:], xTv[:, n, ta:ta + TT], writes=[K("xres", pb)])

                def mmd(e, b=b, pb=pb, tt=tt):
                    ins = None
                    for f in range(FC):
                        ins = e.matmul(ps_o[pb][:], wdb[b][:, f, :], act[:, f, tt * TT:(tt + 1) * TT],
                                       start=(f == 0), stop=(f == FC - 1))
                    return ins
                P.op("pe", mmd, reads=[K("wd", b)] + act_keys, writes=[K("pso", pb)])
                P.op("dve", lambda e, pb=pb: e.scalar_tensor_tensor(
                    out=osb[pb][:], in0=ps_o[pb][:], scalar=0.5, in1=xres[pb][:], op0=ALU.mult, op1=ALU.add),
                    reads=[K("pso", pb), K("xres", pb)], writes=[K("osb", pb)])
                P.dma("sp", oTv[:, n, ta:ta + TT], osb[pb][:], reads=[K("osb", pb)], final=final)
                ei += 1
            di += 1


XH, XD, ML = 4, 512, 256


def build_xattn(T=2048):
    nc = _new_nc()
    xT = _din(nc, "xT", [D, T]); memT = _din(nc, "memT", [D, ML])
    gx = _din(nc, "gx", [128, KC]); gm = _din(nc, "gm", [128, KC])
    gq = _din(nc, "gq", [128, 4]); gk = _din(nc, "gk", [128, 4])
    wq = _din(nc, "wq", [D, D]); wkv = _din(nc, "wkv", [D, 2 * D]); wo = _din(nc, "wo", [D, D])
    oT = _dout(nc, "oT", [D, T])
    with contextlib.ExitStack() as stack:
        P = Prog(nc, stack); c = Ctx(nc, stack, P)
        ones32, ones16 = c.const_ones()
        gxt = c.sb("gx", [128, KC], F32); gmt = c.sb("gm", [128, KC], F32)
        gqt = c.sb("gq", [128, 4], F32); gkt = c.sb("gk", [128, 4], F32)
        P.dma("sp", gxt[:], gx, writes=["gx"]); P.dma("sp", gmt[:], gm, writes=["gm"])
        P.dma("sp", gqt[:], gq, writes=["gq"]); P.dma("sp", gkt[:], gk, writes=["gk"])
        memn = c.sb("memn", [128, KC, ML], BF16)
        emit_norm(c, _fm(memT), ML, gmt, memn, 0, "memn", gn_key="gm")
        kT = c.sb("kT", [128, KC, ML], BF16)
        kraw = c.sb("kraw", [128, 4, 512], F32)
        sq = [c.sb(f"x_sq{i}", [128, 512], F32) for i in range(2)]
        rs = c.sb("x_rs", [128, 512], F32)
        psn = c.ps("ps_n")
        scale_h = 1.0 / XD

        def headnorm(raw, rawkey, n, gt, gkey, dst_fn, dstkey):
            for dc in range(4):
                si = c.rot("x_sq", 2)
                P.op("act", lambda e, si=si, dc=dc: e.activation(out=sq[si][:, 0:n], in_=raw[:, dc, 0:n], func=AF.Square),
                     reads=[rawkey], writes=[f"x_sq{si}"])
                P.op("pe", lambda e, si=si, dc=dc: e.matmul(psn[:, 0:n], ones32[:], sq[si][:, 0:n], start=(dc == 0), stop=(dc == 3)),
                     reads=[f"x_sq{si}", "ones32"], writes=["ps_n"])
            P.op("act", lambda e: e.activation(out=rs[:, 0:n], in_=psn[:, 0:n], func=AF.Sqrt, bias=EPS, scale=scale_h),
                 reads=["ps_n"], writes=["x_rs"])
            P.op("dve", lambda e: e.reciprocal(out=rs[:, 0:n], in_=rs[:, 0:n]), reads=["x_rs"], writes=["x_rs"])
            for dc in range(4):
                P.op("dve", lambda e, dc=dc: e.scalar_tensor_tensor(
                    out=dst_fn(dc), in0=raw[:, dc, 0:n], scalar=gt[:, dc:dc + 1], in1=rs[:, 0:n],
                    op0=ALU.mult, op1=ALU.mult), reads=[rawkey, "x_rs", gkey], writes=[dstkey])

        def cb_k(j, t0, n, ps, psk):
            dc = j % 4
            P.op("act", lambda e: e.copy(out=kraw[:, dc, 0:n], in_=ps[:, 0:n]), reads=[psk], writes=["kraw"])
            if dc == 3:
                h = j // 4
                headnorm(kraw, "kraw", ML, gkt, "gk", lambda dc2: kT[:, h * 4 + dc2, :], "kT")
        emit_proj(c, _fm(wkv), 0, KC, memn, ["memn"], ML, cb_k)
        v_sb = c.sb("v_sb", [128, 2, D], BF16)
        wvb = [c.sb(f"x_wv{i}", [128, KC, 256], BF16) for i in range(2)]
        wkvv = _fm(wkv)
        psb = [c.ps(f"ps_b{i}") for i in range(2)]
        for ct in range(8):
            b = c.rot("x_wv", 2)
            P.dma("pool", wvb[b][:], wkvv[:, :, D + ct * 256: D + (ct + 1) * 256], writes=[f"x_wv{b}"])
            for mc in range(2):
                pi = c.rot("ps_b", 2)

                def mm(e, b=b, pi=pi, mc=mc):
                    ins = None
                    for kc in range(KC):
                        ins = e.matmul(psb[pi][:, 0:256], memn[:, kc, mc * 128:(mc + 1) * 128], wvb[b][:, kc, :],
                                       start=(kc == 0), stop=(kc == KC - 1))
                    return ins
                P.op("pe", mm, reads=[f"x_wv{b}", "memn"], writes=[f"ps_b{pi}"])
                P.op("act", lambda e, pi=pi, mc=mc, ct=ct: e.copy(out=v_sb[:, mc, ct * 256:(ct + 1) * 256], in_=psb[pi][:, 0:256]),
                     reads=[f"ps_b{pi}"], writes=["v_sb"])
        TH = 1024
        xn = c.sb("xn", [128, KC, TH], BF16)
        oall = c.sb("oall", [128, KC, TH], BF16)
        qraw = c.sb("qraw", [128, 4, 512], F32)
        qn = c.sb("qn", [128, 4, 512], BF16)
        E = c.sb("E", [128, 2, 512], BF16)
        rden = c.sb("rden", [128, 512], F32)
        psc = [c.ps(f"ps_c{i}") for i in range(2)]
        sm_scale = float(XD) ** -0.5
        for hf in range(T // TH):
            tg = hf * TH
            emit_norm(c, _fm(xT)[:, :, tg:tg + TH], TH, gxt, xn, 0, "xn", gn_key="gx")

            def attn(h, t0, n):
                for mc in range(2):
                    pi = c.rot("ps_b", 2)

                    def mm(e, pi=pi, mc=mc):
                        ins = None
                        for dc in range(4):
                            ins = e.matmul(psb[pi][:, 0:n], kT[:, h * 4 + dc, mc * 128:(mc + 1) * 128], qn[:, dc, 0:n],
                                           start=(dc == 0), stop=(dc == 3))
                        return ins
                    P.op("pe", mm, reads=["kT", "qn"], writes=[f"ps_b{pi}"])
                    P.op("act", lambda e, pi=pi, mc=mc: e.activation(out=E[:, mc, 0:n], in_=psb[pi][:, 0:n], func=AF.Exp, scale=sm_scale),
                         reads=[f"ps_b{pi}"], writes=["E"])

                def mmz(e):
                    ins = None
                    for mc in range(2):
                        ins = e.matmul(psn[:, 0:n], ones16[:], E[:, mc, 0:n], start=(mc == 0), stop=(mc == 1))
                    return ins
                P.op("pe", mmz, reads=["E", "ones16"], writes=["ps_n"])
                P.op("dve", lambda e: e.reciprocal(out=rden[:, 0:n], in_=psn[:, 0:n]), reads=["ps_n"], writes=["rden"])
                for dc in range(4):
                    pi = c.rot("ps_c", 2)

                    def mmo(e, pi=pi, dc=dc):
                        ins = None
                        for mc in range(2):
                            ins = e.matmul(psc[pi][:, 0:n], v_sb[:, mc, h * XD + dc * 128: h * XD + (dc + 1) * 128], E[:, mc, 0:n],
                                           start=(mc == 0), stop=(mc == 1))
                        return ins
                    P.op("pe", mmo, reads=["E", "v_sb"], writes=[f"ps_c{pi}"])
                    P.op("dve", lambda e, pi=pi, dc=dc: e.tensor_tensor(
                        out=oall[:, h * 4 + dc, t0:t0 + n], in0=psc[pi][:, 0:n], in1=rden[:, 0:n], op=ALU.mult),
                        reads=[f"ps_c{pi}", "rden"], writes=["oall"])

            def cb_q(j, t0, n, ps, psk):
                dc = j % 4
                P.op("act", lambda e: e.copy(out=qraw[:, dc, 0:n], in_=ps[:, 0:n]), reads=[psk], writes=["qraw"])
                if dc == 3:
                    h = j // 4
                    headnorm(qraw, "qraw", n, gqt, "gq", lambda dc2: qn[:, dc2, 0:n], "qn")
                    attn(h, t0, n)
            for h in range(XH):
                for t0 in range(0, TH, 512):
                    def cb2(j, t0_, n, ps, psk, h=h, t0=t0):
                        cb_q(h * 4 + j, t0, n, ps, psk)
                    emit_proj(c, _fm(wq), h * XD, 4, xn, ["xn"], 512, cb2, toff=t0)
            emit_outproj(c, wo.rearrange("(c p) n -> p c n", p=128), KC, oall, ["oall"], _fm(xT), _fm(oT), tg, TH, 1.0, True)
        P.emit()
    return nc
def build_conv(T=2048):
    nc = _new_nc()
    xTe = _din(nc, "xTe", [D, 2 + T]); gain = _din(nc, "gain", [128, KC])
    w_in = _din(nc, "w_in", [D, 3 * D]); cwd = _din(nc, "cw", [128, KC * 3]); w_out = _din(nc, "w_out", [D, D])
    oT = _dout(nc, "oT", [D, T])
    with contextlib.ExitStack() as stack:
        P = Prog(nc, stack); c = Ctx(nc, stack, P)
        gn = c.sb("gn", [128, KC], F32); cw = c.sb("cwt", [128, KC * 3], F32)
        P.dma("sp", gn[:], gain, writes=["gn"]); P.dma("sp", cw[:], cwd, writes=["cwt"])
        TH = 1024
        NE = TH + 2
        xn = c.sb("xn", [128, KC, NE], BF16)
        gT = c.sb("gT", [128, KC, TH], BF16)
        cgs = c.sb("cgs", [128, NE], F32); zb = c.sb("zb", [128, NE], F32)
        bb = c.sb("bb", [128, NE], F32); yb = c.sb("yb", [128, TH], F32)
        xv = _fm(xTe)
        for hf in range(T // TH):
            tg = hf * TH
            emit_norm(c, xv[:, :, tg:tg + NE], NE, gn, xn, 0, "xn")
            for j in range(KC):
                def cb_cg(_, t0, n, ps, psk):
                    P.op("act", lambda e: e.copy(out=cgs[:, t0:t0 + n], in_=ps[:, 0:n]), reads=[psk], writes=["cgs"])

                def cb_u(_, t0, n, ps, psk):
                    P.op("dve", lambda e: e.tensor_tensor(out=zb[:, t0:t0 + n], in0=cgs[:, t0:t0 + n], in1=ps[:, 0:n], op=ALU.mult),
                         reads=[psk, "cgs"], writes=["zb"])

                def cb_b(_, t0, n, ps, psk):
                    P.op("act", lambda e: e.copy(out=bb[:, t0:t0 + n], in_=ps[:, 0:n]), reads=[psk], writes=["bb"])
                emit_proj(c, _fm(w_in), D + j * 128, 1, xn, ["xn"], NE, cb_cg)
                emit_proj(c, _fm(w_in), 2 * D + j * 128, 1, xn, ["xn"], NE, cb_u)
                emit_proj(c, _fm(w_in), j * 128, 1, xn, ["xn"], NE, cb_b)
                P.op("dve", lambda e, j=j: e.tensor_scalar(out=yb[:], in0=zb[:, 2:2 + TH], scalar1=cw[:, j * 3 + 2:j * 3 + 3], scalar2=None, op0=ALU.mult),
                     reads=["zb", "cwt"], writes=["yb"])
                P.op("dve", lambda e, j=j: e.scalar_tensor_tensor(out=yb[:], in0=zb[:, 1:1 + TH], scalar=cw[:, j * 3 + 1:j * 3 + 2], in1=yb[:], op0=ALU.mult, op1=ALU.add),
                     reads=["zb", "cwt", "yb"], writes=["yb"])
                P.op("dve", lambda e, j=j: e.scalar_tensor_tensor(out=yb[:], in0=zb[:, 0:TH], scalar=cw[:, j * 3:j * 3 + 1], in1=yb[:], op0=ALU.mult, op1=ALU.add),
                     reads=["zb", "cwt", "yb"], writes=["yb"])
                P.op("dve", lambda e, j=j: e.tensor_tensor(out=gT[:, j, :], in0=yb[:], in1=bb[:, 2:2 + TH], op=ALU.mult),
                     reads=["yb", "bb"], writes=["gT"])
            emit_outproj(c, w_out.rearrange("(c p) n -> p c n", p=128), KC, gT, ["gT"], xv[:, :, 2:2 + T], _fm(oT), tg, TH, 1.0, True)
        P.emit()
    return nc
DIL = ((128, 1), (512, 4), (2048, 16))
SEQ = 4096
PI = 3.14159265358979
MAGIC = 12582912.0


def dil_consts():
    invf = np.zeros((128, 1), np.float32)
    fr = (500000.0 ** (-np.arange(0, 32, 2, dtype=np.float32) / 32)).astype(np.float32)
    invf[0:16, 0] = fr; invf[16:32, 0] = fr
    rm = np.zeros((128, 128), np.float32)
    for m in range(16):
        rm[m + 16, m] = -1.0
        rm[m, m + 16] = 1.0
    p = np.arange(128)[:, None]; f = np.arange(128)[None, :]
    mask = np.concatenate([(p >= f), (p <= f)], axis=1).astype(np.float32)
    return invf, rm, mask


def build_dilcore(NH=4):
    nc = _new_nc()
    G = 3
    qT = _din(nc, "qT", [G * NH, 128, SEQ]); kT = _din(nc, "kT", [G * NH, 128, SEQ])
    vt = _din(nc, "vt", [G * NH, 128, 32 * 128])
    posb = _din(nc, "posb", [G, 128, SEQ], I32)
    invf_d = _din(nc, "invf", [128, 1]); rm_d = _din(nc, "rm", [128, 128]); mask_d = _din(nc, "mask", [128, 256])
    gq_d = _din(nc, "gq", [128, G]); gk_d = _din(nc, "gk", [128, G])
    oT = _dout(nc, "oT", [NH * 128, SEQ])
    with contextlib.ExitStack() as stack:
        P = Prog(nc, stack); c = Ctx(nc, stack, P)
        ones32, ones16 = c.const_ones()
        invf = c.sb("invf_t", [128, 1], F32); rm = c.sb("rm_t", [128, 128], F32); mask = c.sb("mask_t", [128, 256], F32)
        gq = c.sb("gq_t", [128, G], F32); gk = c.sb("gk_t", [128, G], F32)
        for t, d, k in ((invf, invf_d, "invf"), (rm, rm_d, "rm"), (mask, mask_d, "mask"), (gq, gq_d, "gq"), (gk, gk_d, "gk")):
            P.dma("sp", t[:], d, writes=[k])
        posi = c.sb("posi", [128, SEQ], I32)
        ang = c.sb("ang", [128, SEQ], F32); tmp = c.sb("tmpa", [128, SEQ], F32); kf = c.sb("kfa", [128, SEQ], F32)
        cosT = c.sb("cosT", [128, SEQ], F32); sinT = c.sb("sinT", [128, SEQ], F32)
        qf = c.sb("qf", [128, SEQ], F32)
        q16 = c.sb("q16", [128, SEQ], BF16); k16 = c.sb("k16", [128, SEQ], BF16)
        v16 = c.sb("v16", [128, 32 * 128], BF16)
        Uacc = c.sb("Uacc", [128, SEQ], F32); Zacc = c.sb("Zacc", [128, SEQ], F32)
        sq = c.sb("d_sq", [128, 512], F32); rs = c.sb("d_rs", [128, 512], F32); t1 = c.sb("d_t1", [128, 512], F32)
        t2 = c.sb("d_t2", [128, 512], F32)
        Ef = [c.sb(f"Ef{i}", [128, 256], F32) for i in range(2)]
        Em = [c.sb(f"Em{i}", [128, 256], BF16) for i in range(2)]
        psn = c.ps("ps_n"); psr = c.ps("ps_a0")
        pss = [c.ps(f"ps_b{i}") for i in range(2)]
        psU = [c.ps(f"ps_c{i}") for i in range(2)]
        psZ = [c.ps(f"ps_d{i}") for i in range(2)]
        C1 = 6.28125
        C2 = 2.0 * PI - C1
        sm_scale = 128.0 ** -0.5

        def table(dst, dkey, shift):
            P.op("dve", lambda e: e.tensor_scalar(out=tmp[:], in0=ang[:], scalar1=float(shift), scalar2=None, op0=ALU.add),
                 reads=["ang"], writes=["tmpa"])
            P.op("dve", lambda e: e.tensor_scalar(out=kf[:], in0=tmp[:], scalar1=1.0 / (2 * PI), scalar2=MAGIC, op0=ALU.mult, op1=ALU.add),
                 reads=["tmpa"], writes=["kfa"])
            P.op("dve", lambda e: e.tensor_scalar(out=kf[:], in0=kf[:], scalar1=-MAGIC, scalar2=None, op0=ALU.add),
                 reads=["kfa"], writes=["kfa"])
            P.op("dve", lambda e: e.scalar_tensor_tensor(out=tmp[:], in0=kf[:], scalar=-C1, in1=tmp[:], op0=ALU.mult, op1=ALU.add),
                 reads=["kfa", "tmpa"], writes=["tmpa"])
            P.op("dve", lambda e: e.scalar_tensor_tensor(out=tmp[:], in0=kf[:], scalar=-C2, in1=tmp[:], op0=ALU.mult, op1=ALU.add),
                 reads=["kfa", "tmpa"], writes=["tmpa"])
            P.op("dve", lambda e: e.tensor_scalar(out=tmp[:], in0=tmp[:], scalar1=3.1415925, scalar2=-3.1415925, op0=ALU.min, op1=ALU.max),
                 reads=["tmpa"], writes=["tmpa"])
            P.op("act", lambda e: e.activation(out=dst[:], in_=tmp[:], func=AF.Sin), reads=["tmpa"], writes=[dkey])

        def prep(src, g, gt, gkey, dst16, dkey):
            P.dma("sp", qf[:], src, writes=["qf"])
            for t0 in range(0, SEQ, 512):
                sl = slice(t0, t0 + 512)
                P.op("act", lambda e, sl=sl: e.activation(out=sq[:], in_=qf[:, sl], func=AF.Square), reads=["qf"], writes=["d_sq"])
                P.op("pe", lambda e: e.matmul(psn[:], ones32[:], sq[:], start=True, stop=True), reads=["d_sq", "ones32"], writes=["ps_n"])
                P.op("act", lambda e: e.activation(out=rs[:], in_=psn[:], func=AF.Sqrt, bias=EPS, scale=1.0 / 128), reads=["ps_n"], writes=["d_rs"])
                P.op("dve", lambda e: e.reciprocal(out=rs[:], in_=rs[:]), reads=["d_rs"], writes=["d_rs"])
                P.op("dve", lambda e, sl=sl: e.scalar_tensor_tensor(out=qf[:, sl], in0=qf[:, sl], scalar=gt[:, g:g + 1], in1=rs[:], op0=ALU.mult, op1=ALU.mult),
                     reads=["qf", "d_rs", gkey], writes=["qf"])
                P.op("pe", lambda e, sl=sl: e.matmul(psr[:], rm[:], qf[:, sl], start=True, stop=True), reads=["qf", "rm"], writes=["ps_a0"])
                P.op("dve", lambda e, sl=sl: e.tensor_tensor(out=t1[:], in0=qf[:, sl], in1=cosT[:, sl], op=ALU.mult), reads=["qf", "cosT"], writes=["d_t1"])
                P.op("dve", lambda e, sl=sl: e.tensor_tensor(out=t2[:], in0=psr[:], in1=sinT[:, sl], op=ALU.mult), reads=["ps_a0", "sinT"], writes=["d_t2"])
                P.op("dve", lambda e, sl=sl: e.tensor_tensor(out=dst16[:, sl], in0=t1[:], in1=t2[:], op=ALU.add), reads=["d_t1", "d_t2"], writes=[dkey])

        for hl in range(NH):
            for g, (window, dl) in enumerate(DIL):
                gh = g * NH + hl
                nb = SEQ // dl // 128
                P.dma("sp", posi[:], posb[g], writes=["posi"])
                P.op("dve", lambda e: e.tensor_copy(out=ang[:], in_=posi[:]), reads=["posi"], writes=["ang"])
                P.op("dve", lambda e: e.tensor_scalar(out=ang[:], in0=ang[:], scalar1=invf[:, 0:1], scalar2=None, op0=ALU.mult),
                     reads=["ang", "invf"], writes=["ang"])
                table(sinT, "sinT", 0.0)
                table(cosT, "cosT", PI / 2)
                prep(qT[gh], g, gq, "gq", q16, "q16")
                prep(kT[gh], g, gk, "gk", k16, "k16")
                P.dma("pool", v16[:], vt[gh], writes=["v16"])
                for qb in range(0, 32, 2):
                    r, b0 = qb // nb, qb % nb
                    ui = c.rot("psU", 2)
                    for s in range(2):
                        q = qb + s
                        bp = q % nb
                        ei = c.rot("Ef", 2)
                        lo = 0 if bp > 0 else 128
                        qs = slice(q * 128, (q + 1) * 128)

                        def mms(e, ei=ei, q=q, bp=bp, qs=qs):
                            ins = None
                            if bp > 0:
                                ins = e.matmul(pss[ei][:, 0:128], k16[:, (q - 1) * 128:q * 128], q16[:, qs], start=True, stop=True)
                            ins = e.matmul(pss[ei][:, 128:256], k16[:, qs], q16[:, qs], start=True, stop=True)
                            return ins
                        P.op("pe", mms, reads=["k16", "q16"], writes=[f"ps_b{ei}"])
                        P.op("act", lambda e, ei=ei, lo=lo: e.activation(out=Ef[ei][:, lo:256], in_=pss[ei][:, lo:256], func=AF.Exp, scale=sm_scale),
                             reads=[f"ps_b{ei}"], writes=[f"Ef{ei}"])
                        P.op("dve", lambda e, ei=ei, lo=lo: e.tensor_tensor(out=Em[ei][:, lo:256], in0=Ef[ei][:, lo:256], in1=mask[:, lo:256], op=ALU.mult),
                             reads=[f"Ef{ei}", "mask"], writes=[f"Em{ei}"])

                        def mmu(e, ei=ei, q=q, bp=bp, s=s, ui=ui):
                            o = psU[ui][:, s * 128:(s + 1) * 128]
                            if bp > 0:
                                e.matmul(o, v16[:, (q - 1) * 128:q * 128], Em[ei][:, 0:128], start=True, stop=False)
                            return e.matmul(o, v16[:, q * 128:(q + 1) * 128], Em[ei][:, 128:256], start=(bp == 0), stop=True)
                        P.op("pe", mmu, reads=[f"Em{ei}", "v16"], writes=[f"ps_c{ui}"])

                        def mmz(e, ei=ei, bp=bp, s=s, ui=ui):
                            o = psZ[ui][:, s * 128:(s + 1) * 128]
                            if bp > 0:
                                e.matmul(o, ones16[:], Em[ei][:, 0:128], start=True, stop=False)
                            return e.matmul(o, ones16[:], Em[ei][:, 128:256], start=(bp == 0), stop=True)
                        P.op("pe", mmz, reads=[f"Em{ei}", "ones16"], writes=[f"ps_d{ui}"])
                    ta = r + dl * b0 * 128
                    tsl = slice(ta, ta + dl * 255 + 1, dl) if dl > 1 else slice(ta, ta + 256)
                    if g == 0:
                        P.op("dve", lambda e, ui=ui, tsl=tsl: e.tensor_copy(out=Uacc[:, tsl], in_=psU[ui][:, 0:256]), reads=[f"ps_c{ui}"], writes=["Uacc"])
                        P.op("act", lambda e, ui=ui, tsl=tsl: e.copy(out=Zacc[:, tsl], in_=psZ[ui][:, 0:256]), reads=[f"ps_d{ui}"], writes=["Zacc"])
                    else:
                        P.op("dve", lambda e, ui=ui, tsl=tsl: e.tensor_tensor(out=Uacc[:, tsl], in0=Uacc[:, tsl], in1=psU[ui][:, 0:256], op=ALU.add),
                             reads=[f"ps_c{ui}", "Uacc"], writes=["Uacc"])
                        P.op("dve", lambda e, ui=ui, tsl=tsl: e.tensor_tensor(out=Zacc[:, tsl], in0=Zacc[:, tsl], in1=psZ[ui][:, 0:256], op=ALU.add),
                             reads=[f"ps_d{ui}", "Zacc"], writes=["Zacc"])
            P.op("dve", lambda e: e.reciprocal(out=Zacc[:], in_=Zacc[:]), reads=["Zacc"], writes=["Zacc"])
            P.op("dve", lambda e: e.tensor_tensor(out=Uacc[:], in0=Uacc[:], in1=Zacc[:], op=ALU.mult), reads=["Zacc", "Uacc"], writes=["Uacc"])
            P.dma("sp", oT[hl * 128:(hl + 1) * 128, :], Uacc[:], reads=["Uacc"], final=True)
        P.emit()
    return nc


def dil_perm(dl):
    L = SEQ // dl
    return (np.arange(L)[None, :] * dl + np.arange(dl)[:, None]).reshape(-1)
HC = 64
HNC = SEQ // HC


def hgrn_consts():
    cm = np.ones((128, SEQ), np.float32); cm[:, ::HC] = 0.0
    p = np.arange(HC)[:, None]; f = np.arange(HC)[None, :]
    tm = (p <= f).astype(np.float32)
    return cm, tm, np.eye(128, dtype=np.float32)


def build_hgrncore(NH=8, layer=2):
    nc = _new_nc()
    qT = _din(nc, "qT", [NH, 128, SEQ]); fT = _din(nc, "fT", [NH, 128, SEQ])
    vt = _din(nc, "vt", [NH, HC, HNC * 128]); gt = _din(nc, "gt", [NH, HC, HNC * 128])
    lbl = _din(nc, "lbl", [128, NH * 4]); ng_d = _din(nc, "ng", [HC, 128])
    cm_d = _din(nc, "cm", [128, SEQ]); tm_d = _din(nc, "tm", [HC, HC]); id_d = _din(nc, "ident", [128, 128])
    ot = _dout(nc, "ot", [NH, HC, HNC * 128])
    with contextlib.ExitStack() as stack:
        P = Prog(nc, stack); c = Ctx(nc, stack, P)
        cm = c.sb("cm_t", [128, SEQ], F32); tm = c.sb("tm_t", [HC, HC], F32); ident = c.sb("id_t", [128, 128], BF16)
        ng = c.sb("ng_t", [HC, 128], F32); lb4 = c.sb("lb4", [128, NH * 4], F32)
        P.dma("sp", cm[:], cm_d, writes=["cm"]); P.dma("sp", tm[:], tm_d, writes=["tm"])
        P.dma("pool", ident[:], id_d, writes=["ident"]); P.dma("sp", ng[:], ng_d, writes=["ng"])
        P.dma("sp", lb4[:], lbl, writes=["lb4"])
        lb = c.sb("lb", [128, NH], F32); oml = c.sb("oml", [128, NH], F32); den = c.sb("den", [128, NH], F32)
        P.op("act", lambda e: e.activation(out=lb4[:], in_=lb4[:], func=AF.Exp), reads=["lb4"], writes=["lb4"])
        l3 = lb4[:].rearrange("p (h l) -> p h l", l=4)
        P.op("dve", lambda e: e.tensor_reduce(out=den[:], in_=l3, axis=AX.X, op=ALU.add), reads=["lb4"], writes=["den"])
        P.op("dve", lambda e: e.reciprocal(out=den[:], in_=den[:]), reads=["den"], writes=["den"])
        P.op("dve", lambda e: e.tensor_copy(out=lb[:], in_=l3[:, :, 1]), reads=["lb4"], writes=["lb"])
        for l in range(2, layer + 1):
            P.op("dve", lambda e, l=l: e.tensor_tensor(out=lb[:], in0=lb[:], in1=l3[:, :, l], op=ALU.add), reads=["lb4", "lb"], writes=["lb"])
        P.op("dve", lambda e: e.tensor_tensor(out=lb[:], in0=lb[:], in1=den[:], op=ALU.mult), reads=["lb", "den"], writes=["lb"])
        P.op("dve", lambda e: e.tensor_scalar(out=oml[:], in0=lb[:], scalar1=-1.0, scalar2=1.0, op0=ALU.mult, op1=ALU.add), reads=["lb"], writes=["oml"])

        fb = c.sb("fb", [128, SEQ], F32); A = c.sb("A", [128, SEQ], F32); tmp = c.sb("htmp", [128, SEQ], F32)
        kk = c.sb("kk", [128, SEQ], F32); qf = c.sb("qf", [128, SEQ], F32)
        qd16 = c.sb("qd16", [128, SEQ], BF16); ki16 = c.sb("ki16", [128, SEQ], BF16); ke16 = c.sb("ke16", [128, SEQ], BF16)
        ketok = c.sb("ketok", [HC, HNC * 128], BF16); v16 = c.sb("v16", [HC, HNC * 128], BF16)
        dec = c.sb("dec", [128, HNC], F32)
        S32 = c.sb("S32", [128, 128], F32); S16 = c.sb("S16", [128, 128], BF16)
        att16 = [c.sb(f"att16_{i}", [HC, HC], BF16) for i in range(2)]
        gate = c.sb("gate", [HC, 8 * 128], F32); osb = c.sb("osb", [HC, 8 * 128], F32); sqb = c.sb("sqb", [HC, 8 * 128], F32)
        ss = c.sb("ss", [HC, 8], F32)
        psT = c.ps("ps_T", [128, 1024], BF16)
        psA = [c.ps(f"ps_a{i}") for i in range(2)]
        psO = [c.ps(f"ps_b{i}") for i in range(2)]
        psS = [c.ps(f"ps_c{i}") for i in range(2)]
        A3 = A[:].rearrange("p (n c) -> p n c", c=HC)
        tmp3 = tmp[:].rearrange("p (n c) -> p n c", c=HC)
        for hl in range(NH):
            P.dma("sp", fb[:], fT[hl], writes=["fb"])
            P.dma("sp", qf[:], qT[hl], writes=["qf"])
            P.dma("pool", v16[:], vt[hl], writes=["v16"])
            P.op("act", lambda e: e.activation(out=fb[:], in_=fb[:], func=AF.Sigmoid), reads=["fb"], writes=["fb"])
            P.op("dve", lambda e, hl=hl: e.tensor_scalar(out=fb[:], in0=fb[:], scalar1=oml[:, hl:hl + 1], scalar2=lb[:, hl:hl + 1], op0=ALU.mult, op1=ALU.add),
                 reads=["fb", "oml", "lb"], writes=["fb"])
            P.op("dve", lambda e: e.tensor_scalar(out=kk[:], in0=fb[:], scalar1=-1.0, scalar2=1.0, op0=ALU.mult, op1=ALU.add), reads=["fb"], writes=["kk"])
            P.op("act", lambda e: e.activation(out=fb[:], in_=fb[:], func=AF.Ln), reads=["fb"], writes=["fb"])
            P.op("dve", lambda e: e.tensor_tensor_scan(out=A[:], data0=cm[:], data1=fb[:], initial=0.0, op0=ALU.mult, op1=ALU.add),
                 reads=["cm", "fb"], writes=["A"])
            P.op("act", lambda e: e.activation(out=tmp[:], in_=A[:], func=AF.Exp), reads=["A"], writes=["htmp"])
            P.op("dve", lambda e: e.tensor_tensor(out=qd16[:], in0=qf[:], in1=tmp[:], op=ALU.mult), reads=["qf", "htmp"], writes=["qd16"])
            P.op("act", lambda e: e.copy(out=dec[:], in_=tmp3[:, :, HC - 1]), reads=["htmp"], writes=["dec"])
            P.op("act", lambda e: e.activation(out=tmp[:], in_=A[:], func=AF.Exp, scale=-1.0), reads=["A", "dec"], writes=["htmp"])
            P.op("dve", lambda e: e.tensor_tensor(out=ki16[:], in0=kk[:], in1=tmp[:], op=ALU.mult), reads=["kk", "htmp"], writes=["ki16"])
            P.op("dve", lambda e: e.tensor_tensor(out=tmp3, in0=A3[:, :, HC - 1:HC].broadcast_to([128, HNC, HC]), in1=A3, op=ALU.subtract),
                 reads=["A", "ki16"], writes=["htmp"])
            P.op("act", lambda e: e.activation(out=tmp[:], in_=tmp[:], func=AF.Exp), reads=["htmp"], writes=["htmp"])
            P.op("dve", lambda e: e.tensor_tensor(out=ke16[:], in0=kk[:], in1=tmp[:], op=ALU.mult), reads=["kk", "htmp"], writes=["ke16"])
            for n0 in range(0, HNC, 8):
                def tr(e, n0=n0):
                    ins = None
                    for i in range(8):
                        n = n0 + i
                        ins = e.transpose(psT[0:HC, i * 128:(i + 1) * 128], ke16[:, n * HC:(n + 1) * HC], ident[:])
                    return ins
                P.op("pe", tr, reads=["ke16", "ident"], writes=["ps_T"])
                P.op("act", lambda e, n0=n0: e.copy(out=ketok[:, n0 * 128:(n0 + 8) * 128], in_=psT[0:HC, :]), reads=["ps_T"], writes=["ketok"])
            P.op("pool", lambda e: e.memset(S32[:], 0.0), writes=["S32"])
            P.op("pool", lambda e: e.memset(S16[:], 0.0), writes=["S16"])
            for n in range(HNC):
                cs = slice(n * HC, (n + 1) * HC)
                vs = slice(n * 128, (n + 1) * 128)
                ai = c.rot("psA", 2)
                j = n % 8
                if j == 0:
                    P.dma("sp", gate[:], gt[hl][:, n * 128:(n + 8) * 128], writes=["gate"])
                P.op("pe", lambda e, ai=ai, cs=cs: e.matmul(psA[ai][0:HC, 0:HC], ki16[:, cs], qd16[:, cs], start=True, stop=True),
                     reads=["ki16", "qd16"], writes=[f"ps_a{ai}"])
                P.op("dve", lambda e, ai=ai: e.tensor_tensor(out=att16[ai][:], in0=psA[ai][0:HC, 0:HC], in1=tm[:], op=ALU.mult),
                     reads=[f"ps_a{ai}", "tm"], writes=[f"att16_{ai}"])

                def mmo(e, ai=ai, cs=cs, vs=vs):
                    e.matmul(psO[ai][0:HC, 0:128], qd16[:, cs], S16[:], start=True, stop=False)
                    return e.matmul(psO[ai][0:HC, 0:128], att16[ai][:], v16[:, vs], start=False, stop=True)
                P.op("pe", mmo, reads=["qd16", "S16", f"att16_{ai}", "v16"], writes=[f"ps_b{ai}"])
                P.op("act", lambda e, ai=ai, j=j: e.copy(out=osb[:, j * 128:(j + 1) * 128], in_=psO[ai][0:HC, 0:128]), reads=[f"ps_b{ai}"], writes=["osb"])
                P.op("pe", lambda e, ai=ai, vs=vs: e.matmul(psS[ai][:, 0:128], ketok[:, vs], v16[:, vs], start=True, stop=True),
                     reads=["ketok", "v16"], writes=[f"ps_c{ai}"])
                P.op("dve", lambda e, ai=ai, n=n: e.scalar_tensor_tensor(out=S32[:], in0=S32[:], scalar=dec[:, n:n + 1], in1=psS[ai][:, 0:128], op0=ALU.mult, op1=ALU.add),
                     reads=["S32", "dec", f"ps_c{ai}"], writes=["S32"])
                P.op("act", lambda e: e.copy(out=S16[:], in_=S32[:]), reads=["S32"], writes=["S16"])
                if j == 7:
                    o3 = osb[:].rearrange("p (j e) -> p j e", e=128)
                    P.op("dve", lambda e: e.tensor_tensor(out=sqb[:], in0=osb[:], in1=osb[:], op=ALU.mult), reads=["osb"], writes=["sqb"])
                    P.op("dve", lambda e: e.tensor_reduce(out=ss[:], in_=sqb[:].rearrange("p (j e) -> p j e", e=128), axis=AX.X, op=ALU.add),
                         reads=["sqb"], writes=["ss"])
                    P.op("act", lambda e: e.activation(out=ss[:], in_=ss[:], func=AF.Sqrt, bias=EPS, scale=1.0 / 128), reads=["ss"], writes=["ss"])
                    P.op("dve", lambda e: e.reciprocal(out=ss[:], in_=ss[:]), reads=["ss"], writes=["ss"])
                    P.op("dve", lambda e, o3=o3: e.tensor_tensor(out=o3, in0=o3, in1=ss[:].unsqueeze(2).broadcast_to([HC, 8, 128]), op=ALU.mult),
                         reads=["osb", "ss"], writes=["osb"])
                    P.op("dve", lambda e, o3=o3: e.tensor_tensor(out=o3, in0=o3, in1=ng[:].unsqueeze(1).broadcast_to([HC, 8, 128]), op=ALU.mult),
                         reads=["osb", "ng"], writes=["osb"])
                    P.op("act", lambda e: e.activation(out=gate[:], in_=gate[:], func=AF.Silu), reads=["gate"], writes=["gate"])
                    P.op("dve", lambda e: e.tensor_tensor(out=osb[:], in0=osb[:], in1=gate[:], op=ALU.mult), reads=["osb", "gate"], writes=["osb"])
                    P.dma("sp", ot[hl][:, (n - 7) * 128:(n + 1) * 128], osb[:], reads=["osb"], final=True)
        P.emit()
    return nc
RW_ROWS = 9


def rwkv_consts():
    bones = np.zeros((128, 128), np.float32)
    bones[:64, :64] = 1.0; bones[64:, 64:] = 1.0
    sel = np.zeros((32, 16 * 128), np.float32)
    for t in range(16):
        for j in range(2):
            sel[t * 2 + j, t * 128 + j * 64: t * 128 + (j + 1) * 64] = 1.0
    return bones, sel


def build_rwkvA(T=2048):
    nc = _new_nc()
    xTe = _din(nc, "xTe", [D, 1 + T]); gain = _din(nc, "gain", [128, KC]); mu_d = _din(nc, "mu", [128, 6 * KC])
    wrkv = _din(nc, "wrkv", [3, D, D])
    w0_d = _din(nc, "w0", [128, KC]); w1 = _din(nc, "w1", [D, 96]); w2 = _din(nc, "w2", [96, D])
    a0_d = _din(nc, "a0", [128, KC]); a1 = _din(nc, "a1", [D, 96]); a2 = _din(nc, "a2", [96, D])
    g1 = _din(nc, "g1", [D, 256]); g2 = _din(nc, "g2", [256, D])
    kk_d = _din(nc, "k_k", [128, KC]); ka_d = _din(nc, "k_a", [128, KC]); bones_d = _din(nc, "bones", [128, 128])
    yT = _dout(nc, "yT", [RW_ROWS * D, T])
    with contextlib.ExitStack() as stack:
        P = Prog(nc, stack); c = Ctx(nc, stack, P)
        gn = c.sb("gn", [128, KC], F32); mu = c.sb("mu_t", [128, 6 * KC], F32)
        w0 = c.sb("w0_t", [128, KC], F32); a0 = c.sb("a0_t", [128, KC], F32)
        k_k = c.sb("kk_t", [128, KC], F32); k_a = c.sb("ka_t", [128, KC], F32); omka = c.sb("omka", [128, KC], F32)
        bones = c.sb("bones_t", [128, 128], F32)
        for t, d, k in ((gn, gain, "gn"), (mu, mu_d, "mu"), (w0, w0_d, "w0"), (a0, a0_d, "a0"), (k_k, kk_d, "k_k"), (k_a, ka_d, "k_a"), (bones, bones_d, "bones")):
            P.dma("sp", t[:], d, writes=[k])
        P.op("dve", lambda e: e.tensor_scalar(out=omka[:], in0=k_a[:], scalar1=-1.0, scalar2=1.0, op0=ALU.mult, op1=ALU.add), reads=["k_a"], writes=["omka"])
        w2b = c.sb("w2b", [96, D], BF16); a2b = c.sb("a2b", [96, D], BF16); g2b = c.sb("g2b", [128, 2, D], BF16)
        P.dma("pool", w2b[:], w2, writes=["w2b"]); P.dma("pool", a2b[:], a2, writes=["a2b"])
        P.dma("pool", g2b[:], g2.rearrange("(c p) n -> p c n", p=128), writes=["g2b"])
        hfp = c.sb("hfp", [128, KC, 513], F32); diff = c.sb("diff", [128, KC, 512], F32)
        mx = [c.sb(f"mx{i}", [128, KC, 512], BF16) for i in range(2)]
        kbuf = c.sb("kbuf", [128, KC, 512], F32)
        t1 = c.sb("t1", [128, 2, 512], BF16)
        ysb = [c.sb(f"ysb{i}", [128, 512], F32) for i in range(2)]
        asb = c.sb("asb", [128, 512], F32); kkr = c.sb("kkr", [128, 512], F32); sq = c.sb("r_sq", [128, 512], F32)
        rn = c.sb("rn", [128, 512], F32)
        psb = [c.ps(f"ps_b{i}") for i in range(2)]
        psn2 = c.ps("ps_c0")
        yv = yT.rearrange("(r kc p) t -> r p kc t", p=128, kc=KC)
        xv = _fm(xTe)

        def store(row, j, tg, src_ap, key):
            P.dma("sp", yv[row][:, j, tg:tg + 512], src_ap, reads=[key], final=True)

        for tt in range(T // 512):
            tg = tt * 512
            emit_norm(c, xv[:, :, tg:tg + 513], 513, gn, hfp, 0, "hfp")
            P.op("dve", lambda e: e.tensor_tensor(out=diff[:], in0=hfp[:, :, 0:512], in1=hfp[:, :, 1:513], op=ALU.subtract), reads=["hfp"], writes=["diff"])
            for i in range(6):
                mi = c.rot("mx", 2)
                m = mx[mi]
                for kc in range(KC):
                    P.op("dve", lambda e, kc=kc, i=i, m=m: e.scalar_tensor_tensor(
                        out=m[:, kc, :], in0=diff[:, kc, :], scalar=mu[:, i * KC + kc:i * KC + kc + 1], in1=hfp[:, kc, 1:513],
                        op0=ALU.mult, op1=ALU.add), reads=["diff", "hfp", "mu"], writes=[f"mx{mi}"])
                mk = [f"mx{mi}"]
                if i < 3:
                    def cb(j, t0, n, ps, psk, i=i, tg=tg):
                        if i == 1:
                            P.op("act", lambda e: e.copy(out=kbuf[:, j, :], in_=ps[:, 0:512]), reads=[psk], writes=["kbuf"])
                            store(1, j, tg, kbuf[:, j, :], "kbuf")
                        else:
                            yi = c.rot("ysb", 2)
                            P.op("act", lambda e: e.copy(out=ysb[yi][:], in_=ps[:, 0:512]), reads=[psk], writes=[f"ysb{yi}"])
                            store(i, j, tg, ysb[yi][:], f"ysb{yi}")
                    emit_proj(c, wrkv[i].rearrange("(kc p) n -> p kc n", p=128), 0, KC, m, mk, 512, cb)
                elif i == 3 or i == 4:
                    wl = w1 if i == 3 else a1

                    def cb(j, t0, n, ps, psk, i=i):
                        if i == 3:
                            P.op("act", lambda e: e.activation(out=t1[0:96, 0, :], in_=ps[0:96, 0:512], func=AF.Tanh), reads=[psk], writes=["t1"])
                        else:
                            P.op("act", lambda e: e.copy(out=t1[0:96, 0, :], in_=ps[0:96, 0:512]), reads=[psk], writes=["t1"])
                    emit_proj(c, wl.rearrange("(kc p) n -> p kc n", p=128), 0, 1, m, mk, 512, cb, cw=96)
                    w2x, w2k, bias, bk = (w2b, "w2b", w0, "w0") if i == 3 else (a2b, "a2b", a0, "a0")
                    for j in range(KC):
                        pi = c.rot("ps_b", 2)
                        P.op("pe", lambda e, pi=pi, j=j, w2x=w2x: e.matmul(psb[pi][:], w2x[0:96, j * 128:(j + 1) * 128], t1[0:96, 0, :], start=True, stop=True),
                             reads=["t1", w2k], writes=[f"ps_b{pi}"])
                        if i == 3:
                            yi = c.rot("ysb", 2)
                            P.op("act", lambda e, pi=pi, j=j, yi=yi, bias=bias: e.activation(out=ysb[yi][:], in_=psb[pi][:], func=AF.Sigmoid, bias=bias[:, j:j + 1]),
                                 reads=[f"ps_b{pi}", bk], writes=[f"ysb{yi}"])
                            P.op("act", lambda e, yi=yi: e.activation(out=ysb[yi][:], in_=ysb[yi][:], func=AF.Exp, scale=-float(np.exp(-0.5))),
                                 reads=[f"ysb{yi}"], writes=[f"ysb{yi}"])
                            store(3, j, tg, ysb[yi][:], f"ysb{yi}")
                        else:
                            P.op("act", lambda e, pi=pi, j=j, bias=bias: e.activation(out=asb[:], in_=psb[pi][:], func=AF.Sigmoid, bias=bias[:, j:j + 1]),
                                 reads=[f"ps_b{pi}", bk], writes=["asb"])
                            store(4, j, tg, asb[:], "asb")
                            P.op("dve", lambda e, j=j: e.tensor_scalar(out=kkr[:], in0=kbuf[:, j, :], scalar1=k_k[:, j:j + 1], scalar2=None, op0=ALU.mult),
                                 reads=["kbuf", "k_k"], writes=["kkr"])
                            P.op("act", lambda e: e.activation(out=sq[:], in_=kkr[:], func=AF.Square), reads=["kkr"], writes=["r_sq"])
                            P.op("pe", lambda e: e.matmul(psn2[:], bones[:], sq[:], start=True, stop=True), reads=["r_sq", "bones"], writes=["ps_c0"])
                            P.op("dve", lambda e: e.tensor_scalar(out=rn[:], in0=psn2[:], scalar1=1e-24, scalar2=None, op0=ALU.max), reads=["ps_c0"], writes=["rn"])
                            P.op("act", lambda e: e.activation(out=rn[:], in_=rn[:], func=AF.Sqrt), reads=["rn"], writes=["rn"])
                            P.op("dve", lambda e: e.reciprocal(out=rn[:], in_=rn[:]), reads=["rn"], writes=["rn"])
                            P.op("dve", lambda e: e.tensor_tensor(out=kkr[:], in0=kkr[:], in1=rn[:], op=ALU.mult), reads=["kkr", "rn"], writes=["kkr"])
                            yi = c.rot("ysb", 2)
                            P.op("dve", lambda e, yi=yi: e.tensor_scalar(out=ysb[yi][:], in0=kkr[:], scalar1=-1.0, scalar2=None, op0=ALU.mult), reads=["kkr"], writes=[f"ysb{yi}"])
                            store(6, j, tg, ysb[yi][:], f"ysb{yi}")
                            yi = c.rot("ysb", 2)
                            P.op("dve", lambda e, yi=yi: e.tensor_tensor(out=ysb[yi][:], in0=kkr[:], in1=asb[:], op=ALU.mult), reads=["kkr", "asb"], writes=[f"ysb{yi}"])
                            store(7, j, tg, ysb[yi][:], f"ysb{yi}")
                            yi = c.rot("ysb", 2)
                            P.op("dve", lambda e, j=j: e.tensor_scalar(out=rn[:], in0=asb[:], scalar1=k_a[:, j:j + 1], scalar2=omka[:, j:j + 1], op0=ALU.mult, op1=ALU.add),
                                 reads=["asb", "k_a", "omka"], writes=["rn"])
                            P.op("dve", lambda e, yi=yi, j=j: e.tensor_tensor(out=ysb[yi][:], in0=kbuf[:, j, :], in1=rn[:], op=ALU.mult), reads=["kbuf", "rn"], writes=[f"ysb{yi}"])
                            store(8, j, tg, ysb[yi][:], f"ysb{yi}")
                else:
                    def cb(j, t0, n, ps, psk):
                        P.op("act", lambda e: e.activation(out=t1[:, j, :], in_=ps[:, 0:512], func=AF.Sigmoid), reads=[psk], writes=["t1"])
                    emit_proj(c, g1.rearrange("(kc p) n -> p kc n", p=128), 0, 2, m, mk, 512, cb)
                    for j in range(KC):
                        pi = c.rot("ps_b", 2)

                        def mm(e, pi=pi, j=j):
                            e.matmul(psb[pi][:], g2b[:, 0, j * 128:(j + 1) * 128], t1[:, 0, :], start=True, stop=False)
                            return e.matmul(psb[pi][:], g2b[:, 1, j * 128:(j + 1) * 128], t1[:, 1, :], start=False, stop=True)
                        P.op("pe", mm, reads=["t1", "g2b"], writes=[f"ps_b{pi}"])
                        yi = c.rot("ysb", 2)
                        P.op("act", lambda e, pi=pi, yi=yi: e.copy(out=ysb[yi][:], in_=psb[pi][:]), reads=[f"ps_b{pi}"], writes=[f"ysb{yi}"])
                        store(5, j, tg, ysb[yi][:], f"ysb{yi}")
        P.emit()
    return nc


def build_rwkvB(NSTEP=SEQ):
    nc = _new_nc()
    ops5 = _din(nc, "ops5", [5, NSTEP * 2, 512])
    vT = _din(nc, "vT", [128, 8, NSTEP]); sel_d = _din(nc, "sel", [32, 16 * 128])
    yo = _dout(nc, "yo", [128, 8, NSTEP])
    VB = 256
    with contextlib.ExitStack() as stack:
        P = Prog(nc, stack); c = Ctx(nc, stack, P)
        sel = c.sb("sel_t", [32, 16 * 128], F32)
        P.dma("sp", sel[:], sel_d, writes=["sel"])
        opb = [c.sb(f"opb{i}", [32, 5, 512], F32) for i in range(2)]
        vb = [c.sb(f"vb{i}", [128, 8, VB], F32) for i in range(2)]
        yb = [c.sb(f"yb{i}", [128, 8, VB], F32) for i in range(2)]
        S = c.sb("S", [128, 512], F32)
        tmp = c.sb("s_tmp", [128, 512], F32); tmp2 = c.sb("s_tmp2", [128, 512], F32); tmp3 = c.sb("s_tmp3", [128, 512], F32)
        tmp4 = c.sb("s_tmp4", [128, 512], F32); kc_ = c.sb("s_kc", [128, 512], F32)
        sa = c.sb("s_sa", [128, 8], F32)
        NPS = 7
        pss = [c.ps(f"ps_r{i}") for i in range(NPS)]
        P.op("pool", lambda e: e.memset(S[:], 0.0), writes=["S"])
        ov = ops5.rearrange("o (k r) f -> k r o f", r=32)
        r3 = lambda t: t[:].rearrange("p (i k) -> p i k", k=64)
        for t in range(NSTEP):
            bi = (t // 16) % 2
            if t % 16 == 0:
                P.dma("sp", opb[bi][:], ov[t // 16], writes=[f"opb{bi}"])
            vi = (t // VB) % 2
            if t % VB == 0:
                P.dma("sp", vb[vi][:], vT[:, :, t:t + VB], writes=[f"vb{vi}"])
            tl = t % 16
            pk = []
            for o in range(5):
                pi = c.rot("ps_r", NPS)
                P.op("pe", lambda e, pi=pi, o=o, bi=bi, tl=tl: e.matmul(pss[pi][:], sel[:, tl * 128:(tl + 1) * 128], opb[bi][:, o, :], start=True, stop=True),
                     reads=[f"opb{bi}", "sel"], writes=[f"ps_r{pi}"])
                pk.append(pi)
            pn, pw, pb, pkk, pr = pk
            tv = t % VB
            P.op("act", lambda e, pkk=pkk: e.copy(out=kc_[:], in_=pss[pkk][:]), reads=[f"ps_r{pkk}"], writes=["s_kc"])
            P.op("pool", lambda e, vi=vi, tv=tv: e.tensor_tensor(out=r3(tmp3), in0=r3(kc_), in1=vb[vi][:, :, tv:tv + 1].broadcast_to([128, 8, 64]), op=ALU.mult),
                 reads=["s_kc", f"vb{vi}"], writes=["s_tmp3"])
            P.op("dve", lambda e, pn=pn: e.tensor_tensor(out=tmp[:], in0=S[:], in1=pss[pn][:], op=ALU.mult), reads=["S", f"ps_r{pn}"], writes=["s_tmp"])
            P.op("dve", lambda e: e.tensor_reduce(out=sa[:], in_=r3(tmp), axis=AX.X, op=ALU.add), reads=["s_tmp"], writes=["s_sa"])
            P.op("dve", lambda e, pw=pw: e.tensor_tensor(out=S[:], in0=S[:], in1=pss[pw][:], op=ALU.mult), reads=["S", f"ps_r{pw}", "s_tmp"], writes=["S"])
            P.op("dve", lambda e, pb=pb: e.tensor_tensor(out=r3(tmp2), in0=pss[pb][:].rearrange("p (i k) -> p i k", k=64), in1=sa[:].unsqueeze(2).broadcast_to([128, 8, 64]), op=ALU.mult),
                 reads=["s_sa", f"ps_r{pb}"], writes=["s_tmp2"])
            P.op("dve", lambda e: e.tensor_tensor(out=S[:], in0=S[:], in1=tmp2[:], op=ALU.add), reads=["S", "s_tmp2"], writes=["S"])
            P.op("dve", lambda e: e.tensor_tensor(out=S[:], in0=S[:], in1=tmp3[:], op=ALU.add), reads=["S", "s_tmp3"], writes=["S"])
            P.op("dve", lambda e, pr=pr: e.tensor_tensor(out=tmp4[:], in0=S[:], in1=pss[pr][:], op=ALU.mult), reads=["S", f"ps_r{pr}"], writes=["s_tmp4"])
            P.op("dve", lambda e, vi=vi, tv=tv: e.tensor_reduce(out=yb[vi][:, :, tv], in_=r3(tmp4), axis=AX.X, op=ALU.add), reads=["s_tmp4"], writes=[f"yb{vi}"])
            if tv == VB - 1:
                P.dma("sp", yo[:, :, t - VB + 1:t + 1], yb[vi][:], reads=[f"yb{vi}"], final=True)
        P.emit()
    return nc


def build_rwkvC(T=2048):
    nc = _new_nc()
    xT = _din(nc, "xT", [D, T]); ysT = _din(nc, "ysT", [D, T]); yT = _din(nc, "yT", [RW_ROWS * D, T])
    lnw_d = _din(nc, "lnw", [128, KC]); lnb_d = _din(nc, "lnb", [128, KC]); rk_d = _din(nc, "r_k", [128, KC])
    bones_d = _din(nc, "bones", [128, 128]); w_out = _din(nc, "w_out", [D, D])
    oT = _dout(nc, "oT", [D, T])
    with contextlib.ExitStack() as stack:
        P = Prog(nc, stack); c = Ctx(nc, stack, P)
        lnw = c.sb("lnw_t", [128, KC], F32); lnb = c.sb("lnb_t", [128, KC], F32); rk = c.sb("rk_t", [128, KC], F32)
        bones = c.sb("bones_t", [128, 128], F32)
        for t, d, k in ((lnw, lnw_d, "lnw"), (lnb, lnb_d, "lnb"), (rk, rk_d, "rk"), (bones, bones_d, "bones")):
            P.dma("sp", t[:], d, writes=[k])
        TH = 1024
        zT = c.sb("zT", [128, KC, TH], BF16)
        names = ["cy", "cr", "ck", "cv", "cg"]
        tl = {nm: [c.sb(f"{nm}{i}", [128, 512], F32) for i in range(2)] for nm in names}
        yc = c.sb("c_yc", [128, 512], F32); sq = c.sb("c_sq", [128, 512], F32); rs = c.sb("c_rs", [128, 512], F32)
        rkk = c.sb("c_rkk", [128, 512], F32)
        ps1 = c.ps("ps_a0"); ps2 = c.ps("ps_a1"); ps3 = c.ps("ps_b0")
        yv = yT.rearrange("(r kc p) t -> r p kc t", p=128, kc=KC)
        ysv = _fm(ysT)
        for hf in range(T // TH):
            for j in range(KC):
                for t0 in range(0, TH, 512):
                    tg = hf * TH + t0
                    bi = c.rot("cbuf", 2)
                    srcs = {"cy": ysv[:, j, tg:tg + 512], "cr": yv[0][:, j, tg:tg + 512], "ck": yv[8][:, j, tg:tg + 512],
                            "cv": yv[2][:, j, tg:tg + 512], "cg": yv[5][:, j, tg:tg + 512]}
                    for nm in names:
                        P.dma("sp", tl[nm][bi][:], srcs[nm], writes=[f"{nm}{bi}"])
                    y, r, km, v, g = (tl[nm][bi] for nm in names)
                    ky, kr, kk_, kv, kg = (f"{nm}{bi}" for nm in names)
                    P.op("pe", lambda e, y=y: e.matmul(ps1[:], bones[:], y[:], start=True, stop=True), reads=[ky, "bones"], writes=["ps_a0"])
                    P.op("dve", lambda e, y=y: e.scalar_tensor_tensor(out=yc[:], in0=ps1[:], scalar=-1.0 / 64, in1=y[:], op0=ALU.mult, op1=ALU.add),
                         reads=["ps_a0", ky], writes=["c_yc"])
                    P.op("act", lambda e: e.activation(out=sq[:], in_=yc[:], func=AF.Square), reads=["c_yc"], writes=["c_sq"])
                    P.op("pe", lambda e: e.matmul(ps2[:], bones[:], sq[:], start=True, stop=True), reads=["c_sq", "bones"], writes=["ps_a1"])
                    P.op("act", lambda e: e.activation(out=rs[:], in_=ps2[:], func=AF.Sqrt, bias=64e-5, scale=1.0 / 64), reads=["ps_a1"], writes=["c_rs"])
                    P.op("dve", lambda e: e.reciprocal(out=rs[:], in_=rs[:]), reads=["c_rs"], writes=["c_rs"])
                    P.op("dve", lambda e: e.tensor_tensor(out=yc[:], in0=yc[:], in1=rs[:], op=ALU.mult), reads=["c_yc", "c_rs"], writes=["c_yc"])
                    P.op("dve", lambda e, j=j: e.tensor_scalar(out=yc[:], in0=yc[:], scalar1=lnw[:, j:j + 1], scalar2=lnb[:, j:j + 1], op0=ALU.mult, op1=ALU.add),
                         reads=["c_yc", "lnw", "lnb"], writes=["c_yc"])
                    P.op("dve", lambda e, j=j, r=r, km=km: e.scalar_tensor_tensor(out=rkk[:], in0=r[:], scalar=rk[:, j:j + 1], in1=km[:], op0=ALU.mult, op1=ALU.mult),
                         reads=[kr, kk_, "rk"], writes=["c_rkk"])
                    P.op("pe", lambda e: e.matmul(ps3[:], bones[:], rkk[:], start=True, stop=True), reads=["c_rkk", "bones"], writes=["ps_b0"])
                    P.op("dve", lambda e, v=v: e.tensor_tensor(out=rkk[:], in0=ps3[:], in1=v[:], op=ALU.mult), reads=["ps_b0", kv], writes=["c_rkk"])
                    P.op("dve", lambda e: e.tensor_tensor(out=yc[:], in0=yc[:], in1=rkk[:], op=ALU.add), reads=["c_yc", "c_rkk"], writes=["c_yc"])
                    P.op("dve", lambda e, j=j, t0=t0, g=g: e.tensor_tensor(out=zT[:, j, t0:t0 + 512], in0=yc[:], in1=g[:], op=ALU.mult), reads=["c_yc", kg], writes=["zT"])
            emit_outproj(c, w_out.rearrange("(c p) n -> p c n", p=128), KC, zT, ["zT"], _fm(xT), _fm(oT), hf * TH, TH, 1.0, True)
        P.emit()
    return nc


_NC_CACHE = {}


def _prog(key, fn):
    if key not in _NC_CACHE:
        _NC_CACHE[key] = fn()
    return _NC_CACHE[key]


def _run(nc, in_maps):
    res = run_bass_kernel_spmd(nc, in_maps, core_ids=list(range(len(in_maps))))
    return res.results


def _pc(v):
    return np.ascontiguousarray(np.asarray(v, np.float32).reshape(-1, 128).T)


def _c(a):
    return np.ascontiguousarray(a)


def g_ffn(xs, norm, wg, wu, wd):
    nc = _prog("ffn", lambda: build_ffn(2048))
    r = _run(nc, [{"xT": x, "gain": _pc(norm), "wg": wg, "wu": wu, "wd": wd} for x in xs])
    return [o["oT"] for o in r]


def g_xattn(xs, memTs, gx, gm, gq, gk, wq, wkv, wo):
    nc = _prog("xattn", lambda: build_xattn(2048))
    r = _run(nc, [{"xT": x, "memT": memTs[i // 2], "gx": _pc(gx), "gm": _pc(gm), "gq": _pc(gq), "gk": _pc(gk),
                   "wq": wq, "wkv": wkv, "wo": wo} for i, x in enumerate(xs)])
    return [o["oT"] for o in r]


def _halo(xs, i, n):
    if i % 2 == 1:
        return xs[i - 1][:, -n:]
    return np.zeros((D, n), np.float32)


def g_conv(xs, gain, w_in, cw, w_out):
    nc = _prog("conv", lambda: build_conv(2048))
    cwl = _c(np.asarray(cw).T.reshape(16, 128, 3).transpose(1, 0, 2).reshape(128, 48))
    r = _run(nc, [{"xTe": _c(np.concatenate([_halo(xs, i, 2), x], axis=1)), "gain": _pc(gain), "w_in": w_in, "cw": cwl,
                   "w_out": w_out} for i, x in enumerate(xs)])
    return [o["oT"] for o in r]


def g_normproj(xs, gain, w):
    N = w.shape[1]
    nc = _prog(("np", N), lambda: build_normproj(N, 2048))
    r = _run(nc, [{"xT": x, "gain": _pc(gain), "w": w} for x in xs])
    return [o["yT"] for o in r]


def g_outproj(xs, ss, w):
    CC = w.shape[0] // 128
    nc = _prog(("op", CC), lambda: build_outproj(CC, 2048))
    r = _run(nc, [{"xT": x, "sT": s, "w": w} for x, s in zip(xs, ss)])
    return [o["oT"] for o in r]


def _pairs(ys):
    return [np.concatenate([ys[2 * b], ys[2 * b + 1]], axis=1) for b in range(len(ys) // 2)]


def g_dil(xs, positions, gain, w_qkv, q_gain, k_gain, w_out):
    ys = g_normproj(xs, gain, w_qkv)
    full = _pairs(ys)
    invf, rm, mask = dil_consts()
    nc = _prog("dil", lambda: build_dilcore(4))
    perms = [dil_perm(dl) for _, dl in DIL]
    ims = []
    for i in range(len(xs)):
        b, hh = i // 2, i % 2
        f = full[b]
        qT = np.empty((12, 128, SEQ), np.float32); kT = np.empty_like(qT); vt = np.empty((12, 128, 32 * 128), np.float32)
        posb = np.empty((3, 128, SEQ), np.int32)
        for g in range(3):
            pm = perms[g]
            posb[g] = np.asarray(positions[b])[pm][None, :]
            for hl in range(4):
                h = hh * 4 + hl
                r0 = ((0 * 3 + g) * 8 + h) * 128
                r1 = ((1 * 3 + g) * 8 + h) * 128
                r2 = ((2 * 3 + g) * 8 + h) * 128
                qT[g * 4 + hl] = f[r0:r0 + 128][:, pm]
                kT[g * 4 + hl] = f[r1:r1 + 128][:, pm]
                vt[g * 4 + hl] = f[r2:r2 + 128][:, pm].T.reshape(32, 128, 128).transpose(1, 0, 2).reshape(128, -1)
        ims.append({"qT": qT, "kT": kT, "vt": vt, "posb": posb, "invf": invf, "rm": rm, "mask": mask,
                    "gq": _c(np.asarray(q_gain).T), "gk": _c(np.asarray(k_gain).T)})
    r = _run(nc, ims)
    ss = []
    for i in range(len(xs)):
        b, hf = i // 2, i % 2
        o_full = np.concatenate([r[2 * b]["oT"], r[2 * b + 1]["oT"]], axis=0)
        ss.append(_c(o_full[:, hf * 2048:(hf + 1) * 2048]))
    return g_outproj(xs, ss, w_out)


def g_hgrn(xs, gain, w_in, lb_logits, norm_gain, w_out, layer):
    ys = g_normproj(xs, gain, w_in)
    full = _pairs(ys)
    cm, tm, ident = hgrn_consts()
    nc = _prog(("hgrn", layer), lambda: build_hgrncore(8, layer))
    lg = np.asarray(lb_logits)
    ims = []

    def tok(a):
        return a.T.reshape(HNC, HC, 128).transpose(1, 0, 2).reshape(HC, HNC * 128)
    for i in range(len(xs)):
        b, hh = i // 2, i % 2
        f = full[b]
        qT = np.empty((8, 128, SEQ), np.float32); fT = np.empty_like(qT)
        vt = np.empty((8, HC, HNC * 128), np.float32); gt = np.empty_like(vt)
        for hl in range(8):
            h = hh * 8 + hl
            qT[hl] = f[h * 128:(h + 1) * 128]
            fT[hl] = f[2048 + h * 128:2048 + (h + 1) * 128]
            vt[hl] = tok(f[4096 + h * 128:4096 + (h + 1) * 128])
            gt[hl] = tok(f[6144 + h * 128:6144 + (h + 1) * 128])
        lbl = _c(lg[:, hh * 1024:(hh + 1) * 1024].reshape(4, 8, 128).transpose(2, 1, 0).reshape(128, 32))
        ims.append({"qT": qT, "fT": fT, "vt": vt, "gt": gt, "lbl": lbl, "ng": _c(np.tile(np.asarray(norm_gain)[None], (HC, 1))),
                    "cm": cm, "tm": tm, "ident": ident})
    r = _run(nc, ims)
    ss = []
    for i in range(len(xs)):
        b, hf = i // 2, i % 2
        rows = []
        for hh in range(2):
            ot = r[2 * b + hh]["ot"]
            for hl in range(8):
                rows.append(ot[hl].reshape(HC, HNC, 128).transpose(1, 0, 2).reshape(SEQ, 128).T)
        o_full = np.concatenate(rows, axis=0)
        ss.append(_c(o_full[:, hf * 2048:(hf + 1) * 2048]))
    return g_outproj(xs, ss, w_out)


def g_rwkv(xs, gain, p, nstep=SEQ):
    bones, sel = rwkv_consts()
    ncA = _prog("rwA", lambda: build_rwkvA(2048))
    mu = _c(np.concatenate([_pc(p["mu"][i]) for i in range(6)], axis=1))
    imsA = [{"xTe": _c(np.concatenate([_halo(xs, i, 1), x], axis=1)), "gain": _pc(gain), "mu": mu, "wrkv": p["w_rkv"],
             "w0": _pc(p["w0"]), "w1": p["w1"], "w2": p["w2"], "a0": _pc(p["a0"]), "a1": p["a1"], "a2": p["a2"],
             "g1": p["g1"], "g2": p["g2"], "k_k": _pc(p["k_k"]), "k_a": _pc(p["k_a"]), "bones": bones} for i, x in enumerate(xs)]
    ya = [o["yT"] for o in _run(ncA, imsA)]
    full = _pairs(ya)
    ncB = _prog(("rwB", nstep), lambda: build_rwkvB(nstep))
    imsB = []
    for i in range(len(xs)):
        b, hh = i // 2, i % 2
        f = full[b]
        ops5 = np.empty((5, nstep * 2, 512), np.float32)
        for oi, row in enumerate((6, 3, 7, 8, 0)):
            blk = f[row * 2048 + hh * 1024: row * 2048 + (hh + 1) * 1024, :nstep]
            ops5[oi] = blk.T.reshape(nstep * 2, 512)
        vb = f[2 * 2048 + hh * 1024: 2 * 2048 + (hh + 1) * 1024, :nstep]
        vT = _c(vb.reshape(2, 8, 64, nstep).transpose(0, 2, 1, 3).reshape(128, 8, nstep))
        imsB.append({"ops5": ops5, "vT": vT, "sel": sel})
    rb = _run(ncB, imsB)
    yss = []
    for b in range(len(xs) // 2):
        parts = [rb[2 * b + hh]["yo"].reshape(2, 64, 8, nstep).transpose(0, 2, 1, 3).reshape(1024, nstep) for hh in range(2)]
        yss.append(np.concatenate(parts, axis=0))
    if nstep < SEQ:
        return yss, full
    ncC = _prog("rwC", lambda: build_rwkvC(2048))
    imsC = []
    for i, x in enumerate(xs):
        b, hf = i // 2, i % 2
        imsC.append({"xT": x, "ysT": _c(yss[b][:, hf * 2048:(hf + 1) * 2048]), "yT": ya[i], "lnw": _pc(p["ln_w"]), "lnb": _pc(p["ln_b"]),
                     "r_k": _pc(np.asarray(p["r_k"]).reshape(-1)), "bones": bones, "w_out": p["w_out"]})
    return [o["oT"] for o in _run(ncC, imsC)]


def kernel(**inp):
    inp = {k: np.asarray(v) for k, v in inp.items()}
    x = inp["x"]
    B, S, _ = x.shape
    xs = []
    for b in range(B):
        for hf in range(2):
            xs.append(_c(x[b, hf * 2048:(hf + 1) * 2048].T))
    memTs = [_c(inp["mem"][b].T) for b in range(B)]
    for i in range(4):
        xs = g_ffn(xs, inp["ffn_norm"][i, 0], inp["ffn_w_gate"][i, 0], inp["ffn_w_up"][i, 0], inp["ffn_w_down"][i, 0])
        kind, j = i % 4, i // 4
        if kind == 0:
            xs = g_conv(xs, inp["mix_norm"][i], inp["conv_w_in"][j], inp["conv_w"][j], inp["conv_w_out"][j])
        elif kind == 1:
            xs = g_dil(xs, inp["positions"], inp["mix_norm"][i], inp["dil_w_qkv"][j], inp["dil_q_gain"][j], inp["dil_k_gain"][j], inp["dil_w_out"][j])
        elif kind == 2:
            xs = g_hgrn(xs, inp["mix_norm"][i], inp["hgrn_w_in"][j], inp["hgrn_lb_logits"], inp["hgrn_norm"][j], inp["hgrn_w_out"][j], i)
        else:
            p = {k[5:]: inp[k][j] for k in inp if k.startswith("rwkv_")}
            xs = g_rwkv(xs, inp["mix_norm"][i], p)
        xs = g_xattn(xs, memTs, inp["xattn_norm"][i], inp["mem_norm"][i], inp["xattn_q_gain"][i], inp["xattn_k_gain"][i],
                     inp["xattn_wq"][i], inp["xattn_wkv"][i], inp["xattn_wo"][i])
        xs = g_ffn(xs, inp["ffn_norm"][i, 1], inp["ffn_w_gate"][i, 1], inp["ffn_w_up"][i, 1], inp["ffn_w_down"][i, 1])
    out = np.empty((B, S, D), np.float32)
    for b in range(B):
        for hf in range(2):
            out[b, hf * 2048:(hf + 1) * 2048] = xs[2 * b + hf].T
    return out
```

```python
import contextlib
import numpy as np
import concourse.bass as bass
import concourse.mybir as mybir
from concourse.bass_utils import run_bass_kernel_spmd

F32 = mybir.dt.float32
BF16 = mybir.dt.bfloat16
I32 = mybir.dt.int32
AF = mybir.ActivationFunctionType
ALU = mybir.AluOpType
AX = mybir.AxisListType

D = 2048
KC = D // 128
FF = 5632
FC = FF // 128
NCORES = 8
EPS = 1e-6


class Prog:
    ENGS = ("pe", "act", "dve", "pool", "sp")
    NRING = 6

    def __init__(self, nc, stack):
        self.nc = nc
        self.stack = stack
        self.streams = {e: [] for e in self.ENGS}
        self.count = {e: 0 for e in self.ENGS}
        self.sems = {}
        for e in ("pe", "act", "dve", "pool"):
            self.sems[e] = stack.enter_context(nc.semaphore("s_" + e))
        self.rings = {}
        self.dma_k = {}
        for q in ("sp", "pool", "act"):
            self.rings[q] = [stack.enter_context(nc.semaphore(f"r_{q}{i}")) for i in range(self.NRING)]
            self.dma_k[q] = 0
        self.seen = {e: {} for e in self.ENGS}
        self.res = {}
        self.final_events = []

    def _need(self, eng, ev, waits):
        if ev is None:
            return
        sem, val = ev
        key = id(sem)
        if self.seen[eng].get(key, 0) >= val:
            return
        self.seen[eng][key] = val
        waits.append((sem, val))

    def _deps(self, eng, reads, writes):
        waits = []
        for r in reads:
            st = self.res.get(r)
            if st is not None:
                self._need(eng, st["w"], waits)
        for w in writes:
            st = self.res.get(w)
            if st is not None:
                self._need(eng, st["w"], waits)
                for ev in st["r"].values():
                    self._need(eng, ev, waits)
        return waits

    def _commit(self, ev, reads, writes):
        for r in reads:
            st = self.res.setdefault(r, {"w": None, "r": {}})
            st["r"][id(ev[0])] = ev
        for w in writes:
            self.res[w] = {"w": ev, "r": {}}

    def op(self, eng, fn, reads=(), writes=()):
        waits = self._deps(eng, reads, writes)
        self.count[eng] += 1
        ev = (self.sems[eng], self.count[eng])
        self.streams[eng].append((waits, fn, (self.sems[eng], 1)))
        self._commit(ev, reads, writes)
        return ev

    def dma(self, q, out, in_, reads=(), writes=(), final=False):
        waits = self._deps(q, reads, writes)
        k = self.dma_k[q]
        self.dma_k[q] += 1
        sem = self.rings[q][k % self.NRING]
        gen = k // self.NRING
        if gen > 0:
            self._need(q, (sem, 16 * gen), waits)
        ev = (sem, 16 * (gen + 1))

        def fn(e, out=out, in_=in_):
            return e.dma_start(out=out, in_=in_, allow_slow_non_contiguous=True)

        self.streams[q].append((waits, fn, (sem, 16)))
        self._commit(ev, reads, writes)
        if final:
            self.final_events.append(ev)
        return ev

    def barrier(self):
        evs = []
        for e in ("pe", "act", "dve", "pool"):
            if self.count[e] > 0:
                evs.append((self.sems[e], self.count[e]))
        for q in ("sp", "pool", "act"):
            k = self.dma_k[q]
            for i in range(self.NRING):
                n = (k - i + self.NRING - 1) // self.NRING if k > i else 0
                if n > 0:
                    evs.append((self.rings[q][i], 16 * n))
        for eng in self.ENGS:
            waits = []
            for ev in evs:
                self._need(eng, ev, waits)
            if waits:
                self.streams[eng].append((waits, None, None))
        self.res = {}

    def emit(self):
        fw = []
        for ev in self.final_events:
            self._need("sp", ev, fw)
        self.streams["sp"].append((fw, None, None))
        self.flush()

    def flush(self):
        nc = self.nc
        with nc.Block() as block:
            def run(eng_obj, name):
                for waits, fn, inc in self.streams[name]:
                    for sem, val in waits:
                        eng_obj.wait_ge(sem, val)
                    if fn is not None:
                        ins = fn(eng_obj)
                        ins.then_inc(inc[0], inc[1])

            @block.tensor
            def _(e):
                run(e, "pe")

            @block.scalar
            def _(e):
                run(e, "act")

            @block.vector
            def _(e):
                run(e, "dve")

            @block.gpsimd
            def _(e):
                run(e, "pool")

            @block.sync
            def _(e):
                run(e, "sp")
        self.streams = {e: [] for e in self.ENGS}


def _sb(nc, stack, name, shape, dt):
    return stack.enter_context(nc.sbuf_tensor("t_" + name, list(shape), dt))


def _ps(nc, stack, name, shape, dt=F32):
    return stack.enter_context(nc.psum_tensor("t_" + name, list(shape), dt))


class Ctx:
    def __init__(self, nc, stack, P, pfx=""):
        self.nc, self.stack, self.P, self.pfx = nc, stack, P, pfx
        self.tiles = {}
        self.cnt = {}

    def sb(self, name, shape, dt):
        if name not in self.tiles:
            self.tiles[name] = _sb(self.nc, self.stack, self.pfx + name, shape, dt)
        return self.tiles[name]

    def ps(self, name, shape=(128, 512), dt=F32):
        if name not in self.tiles:
            self.tiles[name] = _ps(self.nc, self.stack, self.pfx + name, shape, dt)
        return self.tiles[name]

    def rot(self, name, n):
        k = self.cnt.get(name, 0)
        self.cnt[name] = k + 1
        return k % n

    def const_ones(self):
        if "ones32" not in self.tiles:
            t = self.sb("ones32", [128, 128], F32)
            self.P.op("pool", lambda e: e.memset(t[:], 1.0), writes=["ones32"])
            tb = self.sb("ones16", [128, 128], BF16)
            self.P.op("pool", lambda e: e.memset(tb[:], 1.0), writes=["ones16"])
        return self.tiles["ones32"], self.tiles["ones16"]


def emit_norm(c, src, ntok, gn, xn, xoff, xn_key, eps=EPS, scale_d=1.0 / D, kcs=KC, sumsq_ones=None, ones_key="ones32", gn_key="gn"):
    P = c.P
    ones32, _ = c.const_ones()
    if sumsq_ones is None:
        sumsq_ones = ones32
    TN = 256
    xin = c.sb("n_xin", [128, KC, TN], F32)
    sqs = [c.sb(f"n_sq{i}", [128, TN], F32) for i in range(2)]
    rstd = c.sb("n_rstd", [128, TN], F32)
    psn = c.ps("ps_n", [128, 512])
    for a in range(0, ntok, TN):
        n = min(TN, ntok - a)
        P.dma("sp", xin[:, 0:kcs, 0:n], src[:, :, a:a + n], writes=["n_xin"])
        for kc in range(kcs):
            si = c.rot("n_sq", 2)
            s = sqs[si]
            P.op("act", lambda e, s=s, kc=kc, n=n: e.activation(out=s[:, 0:n], in_=xin[:, kc, 0:n], func=AF.Square),
                 reads=["n_xin"], writes=[f"n_sq{si}"])
            P.op("pe", lambda e, s=s, kc=kc, n=n: e.matmul(psn[:, 0:n], sumsq_ones[:], s[:, 0:n], start=(kc == 0), stop=(kc == kcs - 1)),
                 reads=[f"n_sq{si}", ones_key], writes=["ps_n"])
        P.op("act", lambda e, n=n: e.activation(out=rstd[:, 0:n], in_=psn[:, 0:n], func=AF.Ln, bias=eps, scale=scale_d),
             reads=["ps_n"], writes=["n_rstd"])
        P.op("act", lambda e, n=n: e.activation(out=rstd[:, 0:n], in_=rstd[:, 0:n], func=AF.Exp, scale=-0.5), reads=["n_rstd"], writes=["n_rstd"])
        for kc in range(kcs):
            P.op("dve", lambda e, kc=kc, a=a, n=n: e.scalar_tensor_tensor(
                out=xn[:, kc, xoff + a:xoff + a + n], in0=xin[:, kc, 0:n], scalar=gn[:, kc:kc + 1],
                in1=rstd[:, 0:n], op0=ALU.mult, op1=ALU.mult),
                reads=["n_xin", "n_rstd", gn_key], writes=[xn_key])


def emit_proj(c, wv, col0, nchunks, xn, xn_keys, ntok, cb, kcs=KC, cw=128, toff=0):
    P = c.P
    wb = [c.sb(f"p_w{i}", [128, KC, 128], BF16) for i in range(2)]
    pss = [c.ps(f"ps_a{i}") for i in range(2)]
    for j in range(nchunks):
        b = c.rot("p_w", 2)
        P.dma("pool", wb[b][:, 0:kcs, 0:cw], wv[:, :, col0 + j * cw: col0 + (j + 1) * cw], writes=[f"p_w{b}"])
        for t0 in range(0, ntok, 512):
            n = min(512, ntok - t0)
            pi = c.rot("ps_a", 2)
            ps = pss[pi]

            def mm(e, b=b, ps=ps, t0=t0, n=n):
                ins = None
                for kc in range(kcs):
                    ins = e.matmul(ps[0:cw, 0:n], wb[b][:, kc, 0:cw], xn[:, kc, toff + t0:toff + t0 + n],
                                   start=(kc == 0), stop=(kc == kcs - 1))
                return ins
            P.op("pe", mm, reads=[f"p_w{b}"] + list(xn_keys), writes=[f"ps_a{pi}"])
            cb(j, t0, n, ps, f"ps_a{pi}")


def emit_outproj(c, wv, cc, src, src_keys, xTv, oTv, t0g, ntok, scale, final, soff=0):
    P = c.P
    wb = [c.sb(f"o_w{cc}_{i}", [128, cc, 128], BF16) for i in range(2)]
    pss = [c.ps(f"ps_c{i}") for i in range(2)]
    xres = [c.sb(f"o_xres{i}", [128, 512], F32) for i in range(2)]
    osb = [c.sb(f"o_osb{i}", [128, 512], F32) for i in range(2)]
    for nn in range(KC):
        b = c.rot("o_w", 2)
        P.dma("pool", wb[b][:, 0:cc, :], wv[:, :, nn * 128:(nn + 1) * 128], writes=[f"o_w{cc}_{b}"])
        for t0 in range(0, ntok, 512):
            n = min(512, ntok - t0)
            pb = c.rot("ps_c", 2)
            P.dma("sp", xres[pb][:, 0:n], xTv[:, nn, t0g + t0:t0g + t0 + n], writes=[f"o_xres{pb}"])

            def mm(e, b=b, pb=pb, t0=t0, n=n):
                ins = None
                for f in range(cc):
                    ins = e.matmul(pss[pb][:, 0:n], wb[b][:, f, :], src[:, f, soff + t0:soff + t0 + n],
                                   start=(f == 0), stop=(f == cc - 1))
                return ins
            P.op("pe", mm, reads=[f"o_w{cc}_{b}"] + list(src_keys), writes=[f"ps_c{pb}"])
            P.op("dve", lambda e, pb=pb, n=n: e.scalar_tensor_tensor(
                out=osb[pb][:, 0:n], in0=pss[pb][:, 0:n], scalar=float(scale), in1=xres[pb][:, 0:n],
                op0=ALU.mult, op1=ALU.add),
                reads=[f"ps_c{pb}", f"o_xres{pb}"], writes=[f"o_osb{pb}"])
            P.dma("sp", oTv[:, nn, t0g + t0:t0g + t0 + n], osb[pb][:, 0:n], reads=[f"o_osb{pb}"], final=final)


def _fm(ap):
    return ap.rearrange("(kc p) t -> p kc t", p=128)


def _new_nc():
    return bass.Bass("TRN2", target_bir_lowering=False)


def _din(nc, name, shape, dt=F32):
    return nc.dram_tensor(name, list(shape), dt, kind="ExternalInput").ap()


def _dout(nc, name, shape, dt=F32):
    return nc.dram_tensor(name, list(shape), dt, kind="ExternalOutput").ap()


def emit_normproj(c, xT, gain, w, yT, N, T, final=False):
    P = c.P
    gn = c.sb("gn", [128, KC], F32)
    P.dma("sp", gn[:], gain, writes=["gn"])
    TB = min(T, 2048)
    xn = c.sb("xn", [128, KC, TB], BF16)
    ysb = [c.sb(f"ysb{i}", [128, 512], F32) for i in range(2)]
    yv = _fm(yT)
    for tb in range(0, T, TB):
        emit_norm(c, _fm(xT)[:, :, tb:tb + TB], TB, gn, xn, 0, "xn")

        def cb(j, t0, n, ps, psk, tb=tb):
            i = c.rot("ysb", 2)
            P.op("act", lambda e: e.copy(out=ysb[i][:, 0:n], in_=ps[:, 0:n]), reads=[psk], writes=[f"ysb{i}"])
            P.dma("sp", yv[:, j, tb + t0:tb + t0 + n], ysb[i][:, 0:n], reads=[f"ysb{i}"], final=final)
        emit_proj(c, _fm(w), 0, N // 128, xn, ["xn"], TB, cb)


def build_normproj(N, T=2048):
    nc = _new_nc()
    xT = _din(nc, "xT", [D, T]); gain = _din(nc, "gain", [128, KC]); w = _din(nc, "w", [D, N])
    yT = _dout(nc, "yT", [N, T])
    with contextlib.ExitStack() as stack:
        P = Prog(nc, stack); c = Ctx(nc, stack, P)
        emit_normproj(c, xT, gain, w, yT, N, T, final=True)
        P.emit()
    return nc


def emit_outproj_stage(c, xT, sT, w, oT, CC, T, final=False):
    P = c.P
    TB = min(T, 2048)
    src = c.sb("src", [128, CC, TB], BF16)
    sv = sT.rearrange("(c p) t -> p c t", p=128)
    for tb in range(0, T, TB):
        for cc in range(CC):
            P.dma("pool", src[:, cc, :], sv[:, cc, tb:tb + TB], writes=["src"])
        emit_outproj(c, w.rearrange("(c p) n -> p c n", p=128), CC, src, ["src"], _fm(xT), _fm(oT), tb, TB, 1.0, final)


def build_outproj(CC, T=2048):
    nc = _new_nc()
    xT = _din(nc, "xT", [D, T]); sT = _din(nc, "sT", [CC * 128, T]); w = _din(nc, "w", [CC * 128, D])
    oT = _dout(nc, "oT", [D, T])
    with contextlib.ExitStack() as stack:
        P = Prog(nc, stack); c = Ctx(nc, stack, P)
        emit_outproj_stage(c, xT, sT, w, oT, CC, T, final=True)
        P.emit()
    return nc
def build_ffn(T=2048):
    nc = bass.Bass("TRN2", target_bir_lowering=False)
    xT = nc.dram_tensor("xT", [D, T], F32, kind="ExternalInput").ap()
    gain = nc.dram_tensor("gain", [128, KC], F32, kind="ExternalInput").ap()
    wg = nc.dram_tensor("wg", [D, FF], F32, kind="ExternalInput").ap()
    wu = nc.dram_tensor("wu", [D, FF], F32, kind="ExternalInput").ap()
    wd = nc.dram_tensor("wd", [FF, D], F32, kind="ExternalInput").ap()
    oT = nc.dram_tensor("oT", [D, T], F32, kind="ExternalOutput").ap()
    with contextlib.ExitStack() as stack:
        P = Prog(nc, stack)
        emit_ffn(nc, stack, P, xT, gain, wg, wu, wd, oT, T, "f")
        P.emit()
    return nc


def emit_ffn(nc, stack, P, xT, gain, wg, wu, wd, oT, T, pfx, final=True):
    TH = 1024
    NH = T // TH
    TN = 256
    TT = 512
    FG = 2
    xTv = xT.rearrange("(kc p) t -> p kc t", p=128)
    oTv = oT.rearrange("(kc p) t -> p kc t", p=128)
    wgv = wg.rearrange("(kc p) f -> p kc f", p=128)
    wuv = wu.rearrange("(kc p) f -> p kc f", p=128)
    wdv = wd.rearrange("(fc p) n -> p fc n", p=128)

    act = _sb(nc, stack, pfx + "act", [128, FC, TH], BF16)
    xn = _sb(nc, stack, pfx + "xn", [128, KC, TH], BF16)
    wgb = [_sb(nc, stack, pfx + f"wg{i}", [128, KC, FG * 128], BF16) for i in range(2)]
    wub = [_sb(nc, stack, pfx + f"wu{i}", [128, KC, FG * 128], BF16) for i in range(2)]
    wdb = [_sb(nc, stack, pfx + f"wd{i}", [128, FC, 128], BF16) for i in range(2)]
    xin = _sb(nc, stack, pfx + "xin", [128, KC, TN], F32)
    sq = [_sb(nc, stack, pfx + f"sq{i}", [128, TN], F32) for i in range(2)]
    rstd = _sb(nc, stack, pfx + "rstd", [128, TN], F32)
    ones = _sb(nc, stack, pfx + "ones", [128, 128], F32)
    gn = _sb(nc, stack, pfx + "gn", [128, KC], F32)
    sil = [_sb(nc, stack, pfx + f"sil{i}", [128, TT], F32) for i in range(2)]
    xres = [_sb(nc, stack, pfx + f"xres{i}", [128, TT], F32) for i in range(2)]
    osb = [_sb(nc, stack, pfx + f"osb{i}", [128, TT], F32) for i in range(2)]
    ps_n = _ps(nc, stack, pfx + "psn", [128, TN])
    ps_g = [_ps(nc, stack, pfx + f"psg{i}", [128, TT]) for i in range(2)]
    ps_u = [_ps(nc, stack, pfx + f"psu{i}", [128, TT]) for i in range(2)]
    ps_o = [_ps(nc, stack, pfx + f"pso{i}", [128, TT]) for i in range(2)]

    K = lambda *a: (pfx,) + a
    P.op("pool", lambda e: e.memset(ones[:], 1.0), writes=[K("ones")])
    P.dma("sp", gn[:], gain, writes=[K("gn")])

    gi = 0
    di = 0
    ei = 0
    si = 0
    for h in range(NH):
        t0 = h * TH
        for nt in range(TH // TN):
            ta = t0 + nt * TN
            P.dma("sp", xin[:], xTv[:, :, ta:ta + TN], writes=[K("xin")])
            for kc in range(KC):
                s = sq[si % 2]
                P.op("act", lambda e, s=s, kc=kc: e.activation(out=s[:], in_=xin[:, kc, :], func=AF.Square),
                     reads=[K("xin")], writes=[K("sq", si % 2)])
                P.op("pe", lambda e, s=s, kc=kc: e.matmul(ps_n[:], ones[:], s[:], start=(kc == 0), stop=(kc == KC - 1)),
                     reads=[K("sq", si % 2), K("ones")], writes=[K("psn")])
                si += 1
            P.op("act", lambda e: e.activation(out=rstd[:], in_=ps_n[:], func=AF.Ln, bias=EPS, scale=1.0 / D),
                 reads=[K("psn")], writes=[K("rstd")])
            P.op("act", lambda e: e.activation(out=rstd[:], in_=rstd[:], func=AF.Exp, scale=-0.5), reads=[K("rstd")], writes=[K("rstd")])
            for kc in range(KC):
                P.op("dve", lambda e, kc=kc, nt=nt: e.scalar_tensor_tensor(
                    out=xn[:, kc, nt * TN:(nt + 1) * TN], in0=xin[:, kc, :], scalar=gn[:, kc:kc + 1],
                    in1=rstd[:], op0=ALU.mult, op1=ALU.mult),
                    reads=[K("xin"), K("rstd"), K("gn")], writes=[K("xn", nt)])
        xn_keys = [K("xn", nt) for nt in range(TH // TN)]
        for fg in range(FC // FG):
            b = gi % 2
            f0 = fg * FG * 128
            P.dma("pool", wgb[b][:], wgv[:, :, f0:f0 + FG * 128], writes=[K("wg", b)])
            P.dma("pool", wub[b][:], wuv[:, :, f0:f0 + FG * 128], writes=[K("wu", b)])
            for fc in range(FG):
                f = fg * FG + fc
                for tt in range(TH // TT):
                    pb = ei % 2

                    def mm(e, wt, pt, fc=fc, tt=tt):
                        ins = None
                        for kc in range(KC):
                            ins = e.matmul(pt[:], wt[:, kc, fc * 128:(fc + 1) * 128],
                                           xn[:, kc, tt * TT:(tt + 1) * TT],
                                           start=(kc == 0), stop=(kc == KC - 1))
                        return ins
                    P.op("pe", lambda e, b=b, pb=pb, mm=mm: mm(e, wgb[b], ps_g[pb]),
                         reads=[K("wg", b)] + xn_keys, writes=[K("psg", pb)])
                    P.op("pe", lambda e, b=b, pb=pb, mm=mm: mm(e, wub[b], ps_u[pb]),
                         reads=[K("wu", b)] + xn_keys, writes=[K("psu", pb)])
                    P.op("act", lambda e, pb=pb: e.activation(out=sil[pb][:], in_=ps_g[pb][:], func=AF.Silu),
                         reads=[K("psg", pb)], writes=[K("sil", pb)])
                    P.op("dve", lambda e, pb=pb, f=f, tt=tt: e.tensor_tensor(
                        out=act[:, f, tt * TT:(tt + 1) * TT], in0=sil[pb][:], in1=ps_u[pb][:], op=ALU.mult),
                        reads=[K("sil", pb), K("psu", pb)], writes=[K("act", f)])
                    ei += 1
            gi += 1
        act_keys = [K("act", f) for f in range(FC)]
        for n in range(KC):
            b = di % 2
            P.dma("pool", wdb[b][:], wdv[:, :, n * 128:(n + 1) * 128], writes=[K("wd", b)])
            for tt in range(TH // TT):
                pb = ei % 2
                ta = t0 + tt * TT
                P.dma("sp", xres[pb][:], xTv[:, n, ta:ta + TT], writes=[K("xres", pb)])

                def mmd(e, b=b, pb=pb, tt=tt):
                    ins = None
                    for f in range(FC):
                        ins = e.matmul(ps_o[pb][:], wdb[b][:, f, :], act[:, f, tt * TT:(tt + 1) * TT],
                                       start=(f == 0), stop=(f == FC - 1))
                    return ins
                P.op("pe", mmd, reads=[K("wd", b)] + act_keys, writes=[K("pso", pb)])
                P.op("dve", lambda e, pb=pb: e.scalar_tensor_tensor(
                    out=osb[pb][:], in0=ps_o[pb][:], scalar=0.5, in1=xres[pb][:], op0=ALU.mult, op1=ALU.add),
                    reads=[K("pso", pb), K("xres", pb)], writes=[K("osb", pb)])
                P.dma("sp", oTv[:, n, ta:ta + TT], osb[pb][:], reads=[K("osb", pb)], final=final)
                ei += 1
            di += 1


XH, XD, ML = 4, 512, 256


def build_xattn(T=2048):
    nc = _new_nc()
    xT = _din(nc, "xT", [D, T]); memT = _din(nc, "memT", [D, ML])
    gx = _din(nc, "gx", [128, KC]); gm = _din(nc, "gm", [128, KC])
    gq = _din(nc, "gq", [128, 4]); gk = _din(nc, "gk", [128, 4])
    wq = _din(nc, "wq", [D, D]); wkv = _din(nc, "wkv", [D, 2 * D]); wo = _din(nc, "wo", [D, D])
    oT = _dout(nc, "oT", [D, T])
    with contextlib.ExitStack() as stack:
        P = Prog(nc, stack); c = Ctx(nc, stack, P)
        emit_xattn(c, xT, memT, gx, gm, gq, gk, wq, wkv, wo, oT, T, True)
        P.emit()
    return nc


def emit_xattn(c, xT, memT, gx, gm, gq, gk, wq, wkv, wo, oT, T, final):
    P = c.P
    if True:
        ones32, ones16 = c.const_ones()
        gxt = c.sb("gx", [128, KC], F32); gmt = c.sb("gm", [128, KC], F32)
        gqt = c.sb("gq", [128, 4], F32); gkt = c.sb("gk", [128, 4], F32)
        P.dma("sp", gxt[:], gx, writes=["gx"]); P.dma("sp", gmt[:], gm, writes=["gm"])
        P.dma("sp", gqt[:], gq, writes=["gq"]); P.dma("sp", gkt[:], gk, writes=["gk"])
        memn = c.sb("memn", [128, KC, ML], BF16)
        emit_norm(c, _fm(memT), ML, gmt, memn, 0, "memn", gn_key="gm")
        kT = c.sb("kT", [128, KC, ML], BF16)
        kraw = c.sb("kraw", [128, 4, 512], F32)
        sq = [c.sb(f"x_sq{i}", [128, 512], F32) for i in range(2)]
        rs = c.sb("x_rs", [128, 512], F32)
        psn = c.ps("ps_n")
        scale_h = 1.0 / XD

        def headnorm(raw, rawkey, n, gt, gkey, dst_fn, dstkey):
            for dc in range(4):
                si = c.rot("x_sq", 2)
                P.op("act", lambda e, si=si, dc=dc: e.activation(out=sq[si][:, 0:n], in_=raw[:, dc, 0:n], func=AF.Square),
                     reads=[rawkey], writes=[f"x_sq{si}"])
                P.op("pe", lambda e, si=si, dc=dc: e.matmul(psn[:, 0:n], ones32[:], sq[si][:, 0:n], start=(dc == 0), stop=(dc == 3)),
                     reads=[f"x_sq{si}", "ones32"], writes=["ps_n"])
            P.op("act", lambda e: e.activation(out=rs[:, 0:n], in_=psn[:, 0:n], func=AF.Ln, bias=EPS, scale=scale_h),
                 reads=["ps_n"], writes=["x_rs"])
            P.op("act", lambda e: e.activation(out=rs[:, 0:n], in_=rs[:, 0:n], func=AF.Exp, scale=-0.5), reads=["x_rs"], writes=["x_rs"])
            for dc in range(4):
                P.op("dve", lambda e, dc=dc: e.scalar_tensor_tensor(
                    out=dst_fn(dc), in0=raw[:, dc, 0:n], scalar=gt[:, dc:dc + 1], in1=rs[:, 0:n],
                    op0=ALU.mult, op1=ALU.mult), reads=[rawkey, "x_rs", gkey], writes=[dstkey])

        def cb_k(j, t0, n, ps, psk):
            dc = j % 4
            P.op("act", lambda e: e.copy(out=kraw[:, dc, 0:n], in_=ps[:, 0:n]), reads=[psk], writes=["kraw"])
            if dc == 3:
                h = j // 4
                headnorm(kraw, "kraw", ML, gkt, "gk", lambda dc2: kT[:, h * 4 + dc2, :], "kT")
        emit_proj(c, _fm(wkv), 0, KC, memn, ["memn"], ML, cb_k)
        v_sb = c.sb("v_sb", [128, 2, D], BF16)
        wvb = [c.sb(f"x_wv{i}", [128, KC, 256], BF16) for i in range(2)]
        wkvv = _fm(wkv)
        psb = [c.ps(f"ps_b{i}") for i in range(2)]
        for ct in range(8):
            b = c.rot("x_wv", 2)
            P.dma("pool", wvb[b][:], wkvv[:, :, D + ct * 256: D + (ct + 1) * 256], writes=[f"x_wv{b}"])
            for mc in range(2):
                pi = c.rot("ps_b", 2)

                def mm(e, b=b, pi=pi, mc=mc):
                    ins = None
                    for kc in range(KC):
                        ins = e.matmul(psb[pi][:, 0:256], memn[:, kc, mc * 128:(mc + 1) * 128], wvb[b][:, kc, :],
                                       start=(kc == 0), stop=(kc == KC - 1))
                    return ins
                P.op("pe", mm, reads=[f"x_wv{b}", "memn"], writes=[f"ps_b{pi}"])
                P.op("act", lambda e, pi=pi, mc=mc, ct=ct: e.copy(out=v_sb[:, mc, ct * 256:(ct + 1) * 256], in_=psb[pi][:, 0:256]),
                     reads=[f"ps_b{pi}"], writes=["v_sb"])
        TH = 1024
        xn = c.sb("xn", [128, KC, TH], BF16)
        oall = c.sb("oall", [128, KC, TH], BF16)
        qraw = c.sb("qraw", [128, 4, 512], F32)
        qn = c.sb("qn", [128, 4, 512], BF16)
        E = c.sb("E", [128, 2, 512], BF16)
        rden = c.sb("rden", [128, 512], F32)
        psc = [c.ps(f"ps_c{i}") for i in range(2)]
        sm_scale = float(XD) ** -0.5
        for hf in range(T // TH):
            tg = hf * TH
            emit_norm(c, _fm(xT)[:, :, tg:tg + TH], TH, gxt, xn, 0, "xn", gn_key="gx")

            def attn(h, t0, n):
                for mc in range(2):
                    pi = c.rot("ps_b", 2)

                    def mm(e, pi=pi, mc=mc):
                        ins = None
                        for dc in range(4):
                            ins = e.matmul(psb[pi][:, 0:n], kT[:, h * 4 + dc, mc * 128:(mc + 1) * 128], qn[:, dc, 0:n],
                                           start=(dc == 0), stop=(dc == 3))
                        return ins
                    P.op("pe", mm, reads=["kT", "qn"], writes=[f"ps_b{pi}"])
                    P.op("act", lambda e, pi=pi, mc=mc: e.activation(out=E[:, mc, 0:n], in_=psb[pi][:, 0:n], func=AF.Exp, scale=sm_scale),
                         reads=[f"ps_b{pi}"], writes=["E"])

                def mmz(e):
                    ins = None
                    for mc in range(2):
                        ins = e.matmul(psn[:, 0:n], ones16[:], E[:, mc, 0:n], start=(mc == 0), stop=(mc == 1))
                    return ins
                P.op("pe", mmz, reads=["E", "ones16"], writes=["ps_n"])
                P.op("dve", lambda e: e.reciprocal(out=rden[:, 0:n], in_=psn[:, 0:n]), reads=["ps_n"], writes=["rden"])
                for dc in range(4):
                    pi = c.rot("ps_c", 2)

                    def mmo(e, pi=pi, dc=dc):
                        ins = None
                        for mc in range(2):
                            ins = e.matmul(psc[pi][:, 0:n], v_sb[:, mc, h * XD + dc * 128: h * XD + (dc + 1) * 128], E[:, mc, 0:n],
                                           start=(mc == 0), stop=(mc == 1))
                        return ins
                    P.op("pe", mmo, reads=["E", "v_sb"], writes=[f"ps_c{pi}"])
                    P.op("dve", lambda e, pi=pi, dc=dc: e.tensor_tensor(
                        out=oall[:, h * 4 + dc, t0:t0 + n], in0=psc[pi][:, 0:n], in1=rden[:, 0:n], op=ALU.mult),
                        reads=[f"ps_c{pi}", "rden"], writes=["oall"])

            def cb_q(j, t0, n, ps, psk):
                dc = j % 4
                P.op("act", lambda e: e.copy(out=qraw[:, dc, 0:n], in_=ps[:, 0:n]), reads=[psk], writes=["qraw"])
                if dc == 3:
                    h = j // 4
                    headnorm(qraw, "qraw", n, gqt, "gq", lambda dc2: qn[:, dc2, 0:n], "qn")
                    attn(h, t0, n)
            for h in range(XH):
                for t0 in range(0, TH, 512):
                    def cb2(j, t0_, n, ps, psk, h=h, t0=t0):
                        cb_q(h * 4 + j, t0, n, ps, psk)
                    emit_proj(c, _fm(wq), h * XD, 4, xn, ["xn"], 512, cb2, toff=t0)
            emit_outproj(c, wo.rearrange("(c p) n -> p c n", p=128), KC, oall, ["oall"], _fm(xT), _fm(oT), tg, TH, 1.0, final)
def build_conv(T=4096):
    nc = _new_nc()
    xT = _din(nc, "xT", [D, T]); gain = _din(nc, "gain", [128, KC])
    w_in = _din(nc, "w_in", [D, 3 * D]); cwd = _din(nc, "cw", [128, KC * 3]); w_out = _din(nc, "w_out", [D, D])
    oT = _dout(nc, "oT", [D, T])
    with contextlib.ExitStack() as stack:
        P = Prog(nc, stack); c = Ctx(nc, stack, P)
        emit_conv(c, xT, gain, w_in, cwd, w_out, oT, T, True)
        P.emit()
    return nc


def emit_conv(c, xT, gain, w_in, cwd, w_out, oT, T, final):
    P = c.P
    if True:
        gn = c.sb("gn", [128, KC], F32); cw = c.sb("cwt", [128, KC * 3], F32)
        P.dma("sp", gn[:], gain, writes=["gn"]); P.dma("sp", cw[:], cwd, writes=["cwt"])
        TH = 1024
        NE = TH + 2
        xn = c.sb("xn", [128, KC, NE], BF16)
        gT = c.sb("gT", [128, KC, TH], BF16)
        cgs = c.sb("cgs", [128, NE], F32); zb = c.sb("zb", [128, NE], F32)
        bb = c.sb("bb", [128, NE], F32); yb = c.sb("yb", [128, TH], F32)
        xv = _fm(xT)
        for hf in range(T // TH):
            tg = hf * TH
            if hf == 0:
                P.op("pool", lambda e: e.memset(xn[:, :, 0:2], 0.0), writes=["xn"])
                emit_norm(c, xv[:, :, 0:TH], TH, gn, xn, 2, "xn")
            else:
                emit_norm(c, xv[:, :, tg - 2:tg + TH], NE, gn, xn, 0, "xn")
            for j in range(KC):
                def cb_cg(_, t0, n, ps, psk):
                    P.op("act", lambda e: e.copy(out=cgs[:, t0:t0 + n], in_=ps[:, 0:n]), reads=[psk], writes=["cgs"])

                def cb_u(_, t0, n, ps, psk):
                    P.op("dve", lambda e: e.tensor_tensor(out=zb[:, t0:t0 + n], in0=cgs[:, t0:t0 + n], in1=ps[:, 0:n], op=ALU.mult),
                         reads=[psk, "cgs"], writes=["zb"])

                def cb_b(_, t0, n, ps, psk):
                    P.op("act", lambda e: e.copy(out=bb[:, t0:t0 + n], in_=ps[:, 0:n]), reads=[psk], writes=["bb"])
                emit_proj(c, _fm(w_in), D + j * 128, 1, xn, ["xn"], NE, cb_cg)
                emit_proj(c, _fm(w_in), 2 * D + j * 128, 1, xn, ["xn"], NE, cb_u)
                emit_proj(c, _fm(w_in), j * 128, 1, xn, ["xn"], NE, cb_b)
                P.op("dve", lambda e, j=j: e.tensor_scalar(out=yb[:], in0=zb[:, 2:2 + TH], scalar1=cw[:, j * 3 + 2:j * 3 + 3], scalar2=None, op0=ALU.mult),
                     reads=["zb", "cwt"], writes=["yb"])
                P.op("dve", lambda e, j=j: e.scalar_tensor_tensor(out=yb[:], in0=zb[:, 1:1 + TH], scalar=cw[:, j * 3 + 1:j * 3 + 2], in1=yb[:], op0=ALU.mult, op1=ALU.add),
                     reads=["zb", "cwt", "yb"], writes=["yb"])
                P.op("dve", lambda e, j=j: e.scalar_tensor_tensor(out=yb[:], in0=zb[:, 0:TH], scalar=cw[:, j * 3:j * 3 + 1], in1=yb[:], op0=ALU.mult, op1=ALU.add),
                     reads=["zb", "cwt", "yb"], writes=["yb"])
                P.op("dve", lambda e, j=j: e.tensor_tensor(out=gT[:, j, :], in0=yb[:], in1=bb[:, 2:2 + TH], op=ALU.mult),
                     reads=["yb", "bb"], writes=["gT"])
            emit_outproj(c, w_out.rearrange("(c p) n -> p c n", p=128), KC, gT, ["gT"], xv, _fm(oT), tg, TH, 1.0, final)
DIL = ((128, 1), (512, 4), (2048, 16))
SEQ = 4096
PI = 3.14159265358979
MAGIC = 12582912.0


def dil_consts():
    invf = np.zeros((128, 1), np.float32)
    fr = (500000.0 ** (-np.arange(0, 32, 2, dtype=np.float32) / 32)).astype(np.float32)
    invf[0:16, 0] = fr; invf[16:32, 0] = fr
    rm = np.zeros((128, 128), np.float32)
    for m in range(16):
        rm[m + 16, m] = -1.0
        rm[m, m + 16] = 1.0
    p = np.arange(128)[:, None]; f = np.arange(128)[None, :]
    mask = np.concatenate([(p >= f), (p <= f)], axis=1).astype(np.float32)
    return invf, rm, mask


def build_dilcore(NH=8):
    nc = _new_nc()
    yT = _din(nc, "yT", [9216, SEQ]); posb = _din(nc, "posb", [128, SEQ], I32)
    invf_d = _din(nc, "invf", [128, 1]); rm_d = _din(nc, "rm", [128, 128]); mask_d = _din(nc, "mask", [128, 256])
    id_d = _din(nc, "ident", [128, 128])
    gq_d = _din(nc, "gq", [128, 3]); gk_d = _din(nc, "gk", [128, 3])
    oT = _dout(nc, "oT", [NH * 128, SEQ])
    with contextlib.ExitStack() as stack:
        P = Prog(nc, stack); c = Ctx(nc, stack, P)
        emit_dilcore(c, yT, posb, invf_d, rm_d, mask_d, id_d, gq_d, gk_d, oT, NH, True)
        P.emit()
    return nc


def emit_dilcore(c, yT, posb, invf_d, rm_d, mask_d, id_d, gq_d, gk_d, oT, NH, final):
    P = c.P
    G = 3
    if True:
        ones32, ones16 = c.const_ones()
        invf = c.sb("invf_t", [128, 1], F32); rm = c.sb("rm_t", [128, 128], F32); mask = c.sb("mask_t", [128, 256], F32)
        ident = c.sb("ident_t", [128, 128], F32)
        gq = c.sb("gq_t", [128, G], F32); gk = c.sb("gk_t", [128, G], F32)
        for t, dd, k in ((invf, invf_d, "invf"), (rm, rm_d, "rm"), (mask, mask_d, "mask"), (gq, gq_d, "gq"), (gk, gk_d, "gk"), (ident, id_d, "ident")):
            P.dma("sp", t[:], dd, writes=[k])
        posi = c.sb("posi", [128, SEQ], I32)
        ang = c.sb("ang", [128, SEQ], F32); tmp = c.sb("tmpa", [128, SEQ], F32); kf = c.sb("kfa", [128, SEQ], F32)
        cosT = c.sb("cosT", [128, SEQ], F32); sinT = c.sb("sinT", [128, SEQ], F32)
        qf = kf
        q16 = c.sb("q16", [128, SEQ], BF16); k16 = c.sb("k16", [128, SEQ], BF16)
        v16 = c.sb("v16", [128, 32 * 128], BF16)
        Uacc = c.sb("Uacc", [128, SEQ], F32); Zacc = ang
        sq = c.sb("d_sq", [128, 512], F32); rs = c.sb("d_rs", [128, 512], F32); t1 = c.sb("d_t1", [128, 512], F32)
        t2 = c.sb("d_t2", [128, 512], F32)
        Ef = [c.sb(f"Ef{i}", [128, 256], F32) for i in range(2)]
        Em = [c.sb(f"Em{i}", [128, 256], BF16) for i in range(2)]
        psn = c.ps("ps_n"); psr = c.ps("ps_a0")
        pss = [c.ps(f"ps_b{i}") for i in range(2)]
        psU = [c.ps(f"ps_c{i}") for i in range(2)]
        psZ = [c.ps(f"ps_d{i}") for i in range(2)]
        C1 = 6.28125
        C2 = 2.0 * PI - C1
        sm_scale = 128.0 ** -0.5

        def table(dst, dkey, shift):
            P.op("dve", lambda e: e.tensor_scalar(out=tmp[:], in0=ang[:], scalar1=float(shift), scalar2=None, op0=ALU.add),
                 reads=["ang"], writes=["tmpa"])
            P.op("dve", lambda e: e.tensor_scalar(out=kf[:], in0=tmp[:], scalar1=1.0 / (2 * PI), scalar2=MAGIC, op0=ALU.mult, op1=ALU.add),
                 reads=["tmpa"], writes=["kfa"])
            P.op("dve", lambda e: e.tensor_scalar(out=kf[:], in0=kf[:], scalar1=-MAGIC, scalar2=None, op0=ALU.add),
                 reads=["kfa"], writes=["kfa"])
            P.op("dve", lambda e: e.scalar_tensor_tensor(out=tmp[:], in0=kf[:], scalar=-C1, in1=tmp[:], op0=ALU.mult, op1=ALU.add),
                 reads=["kfa", "tmpa"], writes=["tmpa"])
            P.op("dve", lambda e: e.scalar_tensor_tensor(out=tmp[:], in0=kf[:], scalar=-C2, in1=tmp[:], op0=ALU.mult, op1=ALU.add),
                 reads=["kfa", "tmpa"], writes=["tmpa"])
            P.op("dve", lambda e: e.tensor_scalar(out=tmp[:], in0=tmp[:], scalar1=3.1415925, scalar2=-3.1415925, op0=ALU.min, op1=ALU.max),
                 reads=["tmpa"], writes=["tmpa"])
            P.op("act", lambda e: e.activation(out=dst[:], in_=tmp[:], func=AF.Sin), reads=["tmpa"], writes=[dkey])

        P.dma("sp", posi[:], posb, writes=["posi"])
        P.op("dve", lambda e: e.tensor_copy(out=ang[:], in_=posi[:]), reads=["posi"], writes=["ang"])
        P.op("dve", lambda e: e.tensor_scalar(out=ang[:], in0=ang[:], scalar1=invf[:, 0:1], scalar2=None, op0=ALU.mult),
             reads=["ang", "invf"], writes=["ang"])
        table(sinT, "sinT", 0.0)
        table(cosT, "cosT", PI / 2)

        def prep(src, g, dl, gt, gkey, dst16, dkey):
            P.dma("sp", qf[:], src, reads=["kfa"], writes=["kfa"])
            dview = dst16[:].rearrange("p (r u) -> p u r", r=dl) if dl > 1 else None
            for t0 in range(0, SEQ, 512):
                sl = slice(t0, t0 + 512)
                P.op("act", lambda e, sl=sl: e.activation(out=sq[:], in_=qf[:, sl], func=AF.Square), reads=["kfa"], writes=["d_sq"])
                P.op("pe", lambda e: e.matmul(psn[:], ones32[:], sq[:], start=True, stop=True), reads=["d_sq", "ones32"], writes=["ps_n"])
                P.op("act", lambda e: e.activation(out=rs[:], in_=psn[:], func=AF.Ln, bias=EPS, scale=1.0 / 128), reads=["ps_n"], writes=["d_rs"])
                P.op("act", lambda e: e.activation(out=rs[:], in_=rs[:], func=AF.Exp, scale=-0.5), reads=["d_rs"], writes=["d_rs"])
                P.op("dve", lambda e, sl=sl: e.scalar_tensor_tensor(out=qf[:, sl], in0=qf[:, sl], scalar=gt[:, g:g + 1], in1=rs[:], op0=ALU.mult, op1=ALU.mult),
                     reads=["kfa", "d_rs", gkey], writes=["kfa"])
                P.op("pe", lambda e, sl=sl: e.matmul(psr[:], rm[:], qf[:, sl], start=True, stop=True), reads=["kfa", "rm"], writes=["ps_a0"])
                P.op("dve", lambda e, sl=sl: e.tensor_tensor(out=t1[:], in0=qf[:, sl], in1=cosT[:, sl], op=ALU.mult), reads=["kfa", "cosT"], writes=["d_t1"])
                P.op("dve", lambda e, sl=sl: e.tensor_tensor(out=t2[:], in0=psr[:], in1=sinT[:, sl], op=ALU.mult), reads=["ps_a0", "sinT"], writes=["d_t2"])
                if dl == 1:
                    P.op("dve", lambda e, sl=sl: e.tensor_tensor(out=dst16[:, sl], in0=t1[:], in1=t2[:], op=ALU.add), reads=["d_t1", "d_t2"], writes=[dkey])
                else:
                    u0 = t0 // dl
                    nu = 512 // dl
                    P.op("dve", lambda e, u0=u0, nu=nu: e.tensor_tensor(
                        out=dview[:, u0:u0 + nu, :], in0=t1[:].rearrange("p (u r) -> p u r", r=dl),
                        in1=t2[:].rearrange("p (u r) -> p u r", r=dl), op=ALU.add), reads=["d_t1", "d_t2"], writes=[dkey])

        def vprep(src, dl):
            nb = SEQ // dl // 128
            P.dma("sp", tmp[:], src, writes=["tmpa"])
            for q0 in range(0, 32, 4):
                pi = c.rot("vps", 2)

                def tr(e, q0=q0, pi=pi):
                    ins = None
                    for s in range(4):
                        q = q0 + s
                        r, bp = q // nb, q % nb
                        ta = bp * 128 * dl + r
                        sl = slice(ta, ta + dl * 127 + 1, dl) if dl > 1 else slice(ta, ta + 128)
                        ins = e.transpose(psU[pi][:, s * 128:(s + 1) * 128], tmp[:, sl], ident[:])
                    return ins
                P.op("pe", tr, reads=["tmpa", "ident"], writes=[f"ps_c{pi}"])
                P.op("act", lambda e, q0=q0, pi=pi: e.copy(out=v16[:, q0 * 128:(q0 + 4) * 128], in_=psU[pi][:]), reads=[f"ps_c{pi}"], writes=["v16"])

        for hl in range(NH):
            for g, (window, dl) in enumerate(DIL):
                nb = SEQ // dl // 128
                r0 = ((0 * 3 + g) * 8 + hl) * 128
                r1 = ((1 * 3 + g) * 8 + hl) * 128
                r2 = ((2 * 3 + g) * 8 + hl) * 128
                prep(yT[r0:r0 + 128, :], g, dl, gq, "gq", q16, "q16")
                prep(yT[r1:r1 + 128, :], g, dl, gk, "gk", k16, "k16")
                vprep(yT[r2:r2 + 128, :], dl)
                for qb in range(0, 32, 2):
                    r, b0 = qb // nb, qb % nb
                    ui = c.rot("psU", 2)
                    for s in range(2):
                        q = qb + s
                        bp = q % nb
                        ei = c.rot("Ef", 2)
                        lo = 0 if bp > 0 else 128
                        qs = slice(q * 128, (q + 1) * 128)

                        def mms(e, ei=ei, q=q, bp=bp, qs=qs):
                            ins = None
                            if bp > 0:
                                ins = e.matmul(pss[ei][:, 0:128], k16[:, (q - 1) * 128:q * 128], q16[:, qs], start=True, stop=True)
                            ins = e.matmul(pss[ei][:, 128:256], k16[:, qs], q16[:, qs], start=True, stop=True)
                            return ins
                        P.op("pe", mms, reads=["k16", "q16"], writes=[f"ps_b{ei}"])
                        P.op("act", lambda e, ei=ei, lo=lo: e.activation(out=Ef[ei][:, lo:256], in_=pss[ei][:, lo:256], func=AF.Exp, scale=sm_scale),
                             reads=[f"ps_b{ei}"], writes=[f"Ef{ei}"])
                        P.op("dve", lambda e, ei=ei, lo=lo: e.tensor_tensor(out=Em[ei][:, lo:256], in0=Ef[ei][:, lo:256], in1=mask[:, lo:256], op=ALU.mult),
                             reads=[f"Ef{ei}", "mask"], writes=[f"Em{ei}"])

                        def mmu(e, ei=ei, q=q, bp=bp, s=s, ui=ui):
                            o = psU[ui][:, s * 128:(s + 1) * 128]
                            if bp > 0:
                                e.matmul(o, v16[:, (q - 1) * 128:q * 128], Em[ei][:, 0:128], start=True, stop=False)
                            return e.matmul(o, v16[:, q * 128:(q + 1) * 128], Em[ei][:, 128:256], start=(bp == 0), stop=True)
                        P.op("pe", mmu, reads=[f"Em{ei}", "v16"], writes=[f"ps_c{ui}"])

                        def mmz(e, ei=ei, bp=bp, s=s, ui=ui):
                            o = psZ[ui][:, s * 128:(s + 1) * 128]
                            if bp > 0:
                                e.matmul(o, ones16[:], Em[ei][:, 0:128], start=True, stop=False)
                            return e.matmul(o, ones16[:], Em[ei][:, 128:256], start=(bp == 0), stop=True)
                        P.op("pe", mmz, reads=[f"Em{ei}", "ones16"], writes=[f"ps_d{ui}"])
                    ta = r + dl * b0 * 128
                    tsl = slice(ta, ta + dl * 255 + 1, dl) if dl > 1 else slice(ta, ta + 256)
                    if g == 0:
                        P.op("dve", lambda e, ui=ui, tsl=tsl: e.tensor_copy(out=Uacc[:, tsl], in_=psU[ui][:, 0:256]), reads=[f"ps_c{ui}"], writes=["Uacc"])
                        P.op("act", lambda e, ui=ui, tsl=tsl: e.copy(out=Zacc[:, tsl], in_=psZ[ui][:, 0:256]), reads=[f"ps_d{ui}"], writes=["ang"])
                    else:
                        P.op("dve", lambda e, ui=ui, tsl=tsl: e.tensor_tensor(out=Uacc[:, tsl], in0=Uacc[:, tsl], in1=psU[ui][:, 0:256], op=ALU.add),
                             reads=[f"ps_c{ui}", "Uacc"], writes=["Uacc"])
                        P.op("dve", lambda e, ui=ui, tsl=tsl: e.tensor_tensor(out=Zacc[:, tsl], in0=Zacc[:, tsl], in1=psZ[ui][:, 0:256], op=ALU.add),
                             reads=[f"ps_d{ui}", "ang"], writes=["ang"])
            P.op("dve", lambda e: e.reciprocal(out=Zacc[:], in_=Zacc[:]), reads=["ang"], writes=["ang"])
            P.op("dve", lambda e: e.tensor_tensor(out=Uacc[:], in0=Uacc[:], in1=Zacc[:], op=ALU.mult), reads=["ang", "Uacc"], writes=["Uacc"])
            P.dma("sp", oT[hl * 128:(hl + 1) * 128, :], Uacc[:], reads=["Uacc"], final=final)


def dil_perm(dl):
    L = SEQ // dl
    return (np.arange(L)[None, :] * dl + np.arange(dl)[:, None]).reshape(-1)
HC = 64
HNC = SEQ // HC


def hgrn_consts():
    cm = np.ones((128, SEQ), np.float32); cm[:, ::HC] = 0.0
    p = np.arange(HC)[:, None]; f = np.arange(HC)[None, :]
    tm = (p <= f).astype(np.float32)
    return cm, tm, np.eye(128, dtype=np.float32)


def build_hgrncore(NH=16, layer=2):
    nc = _new_nc()
    yT = _din(nc, "yT", [8192, SEQ])
    lbl = _din(nc, "lbl", [128, NH * 4]); ng_d = _din(nc, "ng", [HC, 128])
    cm_d = _din(nc, "cm", [128, SEQ]); tm_d = _din(nc, "tm", [HC, HC]); id_d = _din(nc, "ident", [128, 128])
    oT = _dout(nc, "oT", [NH * 128, SEQ])
    with contextlib.ExitStack() as stack:
        P = Prog(nc, stack); c = Ctx(nc, stack, P)
        emit_hgrncore(c, yT, lbl, ng_d, cm_d, tm_d, id_d, oT, NH, layer, True)
        P.emit()
    return nc


def emit_hgrncore(c, yT, lbl, ng_d, cm_d, tm_d, id_d, oT, NH, layer, final):
    P = c.P
    if True:
        cm = c.sb("cm_t", [128, SEQ], F32); tm = c.sb("tm_t", [HC, HC], F32); ident = c.sb("id_t", [128, 128], BF16)
        ident32 = c.sb("id32_t", [128, 128], F32)
        P.dma("sp", ident32[:], id_d, writes=["ident32"])
        gnat = c.sb("gnat", [128, SEQ], F32)
        ofm = c.sb("ofm", [128, 512], F32)
        psG = c.ps("ps_G", [128, 1024], F32)
        ng = c.sb("ng_t", [HC, 128], F32); lb4 = c.sb("lb4", [128, NH * 4], F32)
        P.dma("sp", cm[:], cm_d, writes=["cm"]); P.dma("sp", tm[:], tm_d, writes=["tm"])
        P.dma("pool", ident[:], id_d, writes=["ident"]); P.dma("sp", ng[:], ng_d, writes=["ng"])
        P.dma("sp", lb4[:], lbl, writes=["lb4"])
        lb = c.sb("lb", [128, NH], F32); oml = c.sb("oml", [128, NH], F32); den = c.sb("den", [128, NH], F32)
        P.op("act", lambda e: e.activation(out=lb4[:], in_=lb4[:], func=AF.Exp), reads=["lb4"], writes=["lb4"])
        l3 = lb4[:].rearrange("p (h l) -> p h l", l=4)
        P.op("dve", lambda e: e.tensor_reduce(out=den[:], in_=l3, axis=AX.X, op=ALU.add), reads=["lb4"], writes=["den"])
        P.op("dve", lambda e: e.reciprocal(out=den[:], in_=den[:]), reads=["den"], writes=["den"])
        P.op("dve", lambda e: e.tensor_copy(out=lb[:], in_=l3[:, :, 1]), reads=["lb4"], writes=["lb"])
        for l in range(2, layer + 1):
            P.op("dve", lambda e, l=l: e.tensor_tensor(out=lb[:], in0=lb[:], in1=l3[:, :, l], op=ALU.add), reads=["lb4", "lb"], writes=["lb"])
        P.op("dve", lambda e: e.tensor_tensor(out=lb[:], in0=lb[:], in1=den[:], op=ALU.mult), reads=["lb", "den"], writes=["lb"])
        P.op("dve", lambda e: e.tensor_scalar(out=oml[:], in0=lb[:], scalar1=-1.0, scalar2=1.0, op0=ALU.mult, op1=ALU.add), reads=["lb"], writes=["oml"])

        fb = c.sb("fb", [128, SEQ], F32); A = c.sb("A", [128, SEQ], F32); tmp = c.sb("htmp", [128, SEQ], F32)
        kk = c.sb("kk", [128, SEQ], F32); qf = c.sb("qf", [128, SEQ], F32)
        qd16 = c.sb("qd16", [128, SEQ], BF16); ki16 = c.sb("ki16", [128, SEQ], BF16); ke16 = c.sb("ke16", [128, SEQ], BF16)
        ketok = c.sb("ketok", [HC, HNC * 128], BF16); v16 = c.sb("v16", [HC, HNC * 128], BF16)
        dec = c.sb("dec", [128, HNC], F32)
        S32 = c.sb("S32", [128, 128], F32); S16 = c.sb("S16", [128, 128], BF16)
        att16 = [c.sb(f"att16_{i}", [HC, HC], BF16) for i in range(2)]
        gate = c.sb("gate", [HC, 8 * 128], F32); osb = c.sb("osb", [HC, 8 * 128], F32); sqb = c.sb("sqb", [HC, 8 * 128], F32)
        ss = c.sb("ss", [HC, 8], F32)
        psT = c.ps("ps_T", [128, 1024], BF16)
        psA = [c.ps(f"ps_a{i}") for i in range(2)]
        psO = [c.ps(f"ps_b{i}") for i in range(2)]
        psS = [c.ps("ps_c0"), c.ps("ps_c0")]
        A3 = A[:].rearrange("p (n c) -> p n c", c=HC)
        tmp3 = tmp[:].rearrange("p (n c) -> p n c", c=HC)
        for hl in range(NH):
            P.dma("sp", fb[:], yT[2048 + hl * 128:2048 + (hl + 1) * 128, :], writes=["fb"])
            P.dma("sp", qf[:], yT[hl * 128:(hl + 1) * 128, :], writes=["qf"])
            P.dma("sp", kk[:], yT[4096 + hl * 128:4096 + (hl + 1) * 128, :], writes=["kk"])
            P.dma("sp", gnat[:], yT[6144 + hl * 128:6144 + (hl + 1) * 128, :], writes=["gnat"])
            for n0 in range(0, HNC, 8):
                def trv(e, n0=n0):
                    ins = None
                    for i in range(8):
                        n = n0 + i
                        ins = e.transpose(psG[0:HC, i * 128:(i + 1) * 128], kk[:, n * HC:(n + 1) * HC], ident32[:])
                    return ins
                P.op("pe", trv, reads=["kk", "ident32"], writes=["ps_G"])
                P.op("act", lambda e, n0=n0: e.copy(out=v16[:, n0 * 128:(n0 + 8) * 128], in_=psG[0:HC, :]), reads=["ps_G"], writes=["v16"])
            P.op("act", lambda e: e.activation(out=fb[:], in_=fb[:], func=AF.Exp, scale=-1.0), reads=["fb"], writes=["fb"])
            P.op("dve", lambda e: e.tensor_scalar(out=fb[:], in0=fb[:], scalar1=1.0, scalar2=None, op0=ALU.add), reads=["fb"], writes=["fb"])
            P.op("dve", lambda e: e.reciprocal(out=fb[:], in_=fb[:]), reads=["fb"], writes=["fb"])
            P.op("dve", lambda e, hl=hl: e.tensor_scalar(out=fb[:], in0=fb[:], scalar1=oml[:, hl:hl + 1], scalar2=lb[:, hl:hl + 1], op0=ALU.mult, op1=ALU.add),
                 reads=["fb", "oml", "lb"], writes=["fb"])
            P.op("dve", lambda e: e.tensor_scalar(out=kk[:], in0=fb[:], scalar1=-1.0, scalar2=1.0, op0=ALU.mult, op1=ALU.add), reads=["fb"], writes=["kk"])
            P.op("act", lambda e: e.activation(out=fb[:], in_=fb[:], func=AF.Ln), reads=["fb"], writes=["fb"])
            P.op("dve", lambda e: e.tensor_tensor_scan(out=A[:], data0=cm[:], data1=fb[:], initial=0.0, op0=ALU.mult, op1=ALU.add),
                 reads=["cm", "fb"], writes=["A"])
            P.op("act", lambda e: e.activation(out=tmp[:], in_=A[:], func=AF.Exp), reads=["A"], writes=["htmp"])
            P.op("dve", lambda e: e.tensor_tensor(out=qd16[:], in0=qf[:], in1=tmp[:], op=ALU.mult), reads=["qf", "htmp"], writes=["qd16"])
            P.op("act", lambda e: e.copy(out=dec[:], in_=tmp3[:, :, HC - 1]), reads=["htmp"], writes=["dec"])
            P.op("act", lambda e: e.activation(out=tmp[:], in_=A[:], func=AF.Exp, scale=-1.0), reads=["A", "dec"], writes=["htmp"])
            P.op("dve", lambda e: e.tensor_tensor(out=ki16[:], in0=kk[:], in1=tmp[:], op=ALU.mult), reads=["kk", "htmp"], writes=["ki16"])
            P.op("dve", lambda e: e.tensor_tensor(out=tmp3, in0=A3[:, :, HC - 1:HC].broadcast_to([128, HNC, HC]), in1=A3, op=ALU.subtract),
                 reads=["A", "ki16"], writes=["htmp"])
            P.op("act", lambda e: e.activation(out=tmp[:], in_=tmp[:], func=AF.Exp), reads=["htmp"], writes=["htmp"])
            P.op("dve", lambda e: e.tensor_tensor(out=ke16[:], in0=kk[:], in1=tmp[:], op=ALU.mult), reads=["kk", "htmp"], writes=["ke16"])
            for n0 in range(0, HNC, 8):
                def tr(e, n0=n0):
                    ins = None
                    for i in range(8):
                        n = n0 + i
                        ins = e.transpose(psT[0:HC, i * 128:(i + 1) * 128], ke16[:, n * HC:(n + 1) * HC], ident[:])
                    return ins
                P.op("pe", tr, reads=["ke16", "ident"], writes=["ps_T"])
                P.op("act", lambda e, n0=n0: e.copy(out=ketok[:, n0 * 128:(n0 + 8) * 128], in_=psT[0:HC, :]), reads=["ps_T"], writes=["ketok"])
            P.op("pool", lambda e: e.memset(S32[:], 0.0), writes=["S32"])
            P.op("pool", lambda e: e.memset(S16[:], 0.0), writes=["S16"])
            for n in range(HNC):
                cs = slice(n * HC, (n + 1) * HC)
                vs = slice(n * 128, (n + 1) * 128)
                ai = c.rot("psA", 2)
                j = n % 8
                if j == 0:
                    def trg(e, n=n):
                        ins = None
                        for i in range(8):
                            ins = e.transpose(psG[0:HC, i * 128:(i + 1) * 128], gnat[:, (n + i) * HC:(n + i + 1) * HC], ident32[:])
                        return ins
                    P.op("pe", trg, reads=["gnat", "ident32"], writes=["ps_G"])
                    P.op("act", lambda e: e.activation(out=gate[:], in_=psG[0:HC, :], func=AF.Silu), reads=["ps_G"], writes=["gate"])
                P.op("pe", lambda e, ai=ai, cs=cs: e.matmul(psA[ai][0:HC, 0:HC], ki16[:, cs], qd16[:, cs], start=True, stop=True),
                     reads=["ki16", "qd16"], writes=[f"ps_a{ai}"])
                P.op("dve", lambda e, ai=ai: e.tensor_tensor(out=att16[ai][:], in0=psA[ai][0:HC, 0:HC], in1=tm[:], op=ALU.mult),
                     reads=[f"ps_a{ai}", "tm"], writes=[f"att16_{ai}"])

                def mmo(e, ai=ai, cs=cs, vs=vs):
                    e.matmul(psO[ai][0:HC, 0:128], qd16[:, cs], S16[:], start=True, stop=False)
                    return e.matmul(psO[ai][0:HC, 0:128], att16[ai][:], v16[:, vs], start=False, stop=True)
                P.op("pe", mmo, reads=["qd16", "S16", f"att16_{ai}", "v16"], writes=[f"ps_b{ai}"])
                P.op("act", lambda e, ai=ai, j=j: e.copy(out=osb[:, j * 128:(j + 1) * 128], in_=psO[ai][0:HC, 0:128]), reads=[f"ps_b{ai}"], writes=["osb"])
                P.op("pe", lambda e, ai=ai, vs=vs: e.matmul(psS[ai][:, 0:128], ketok[:, vs], v16[:, vs], start=True, stop=True),
                     reads=["ketok", "v16"], writes=["ps_c0"])
                P.op("dve", lambda e, ai=ai, n=n: e.scalar_tensor_tensor(out=S32[:], in0=S32[:], scalar=dec[:, n:n + 1], in1=psS[ai][:, 0:128], op0=ALU.mult, op1=ALU.add),
                     reads=["S32", "dec", "ps_c0"], writes=["S32"])
                P.op("act", lambda e: e.copy(out=S16[:], in_=S32[:]), reads=["S32"], writes=["S16"])
                if j == 7:
                    o3 = osb[:].rearrange("p (j e) -> p j e", e=128)
                    P.op("dve", lambda e: e.tensor_tensor(out=sqb[:], in0=osb[:], in1=osb[:], op=ALU.mult), reads=["osb"], writes=["sqb"])
                    P.op("dve", lambda e: e.tensor_reduce(out=ss[:], in_=sqb[:].rearrange("p (j e) -> p j e", e=128), axis=AX.X, op=ALU.add),
                         reads=["sqb"], writes=["ss"])
                    P.op("act", lambda e: e.activation(out=ss[:], in_=ss[:], func=AF.Ln, bias=EPS, scale=1.0 / 128), reads=["ss"], writes=["ss"])
                    P.op("act", lambda e: e.activation(out=ss[:], in_=ss[:], func=AF.Exp, scale=-0.5), reads=["ss"], writes=["ss"])
                    P.op("dve", lambda e, o3=o3: e.tensor_tensor(out=o3, in0=o3, in1=ss[:].unsqueeze(2).broadcast_to([HC, 8, 128]), op=ALU.mult),
                         reads=["osb", "ss"], writes=["osb"])
                    P.op("dve", lambda e, o3=o3: e.tensor_tensor(out=o3, in0=o3, in1=ng[:].unsqueeze(1).broadcast_to([HC, 8, 128]), op=ALU.mult),
                         reads=["osb", "ng"], writes=["osb"])
                    P.op("dve", lambda e: e.tensor_tensor(out=osb[:], in0=osb[:], in1=gate[:], op=ALU.mult), reads=["osb", "gate"], writes=["osb"])

                    def tro(e):
                        ins = None
                        for i in range(8):
                            ins = e.transpose(psG[:, i * HC:(i + 1) * HC], osb[:, i * 128:(i + 1) * 128], ident32[0:HC, 0:HC])
                        return ins
                    P.op("pe", tro, reads=["osb", "ident32"], writes=["ps_G"])
                    P.op("act", lambda e: e.copy(out=ofm[:], in_=psG[:, 0:512]), reads=["ps_G"], writes=["ofm"])
                    P.dma("sp", oT[hl * 128:(hl + 1) * 128, (n - 7) * HC:(n + 1) * HC], ofm[:], reads=["ofm"], final=final)
RW_ROWS = 7
RWM = {0: 0, 2: 1, 3: 2, 5: 3, 6: 4, 7: 5, 8: 6}


def rwkv_consts():
    bones = np.zeros((128, 128), np.float32)
    bones[:64, :64] = 1.0; bones[64:, 64:] = 1.0
    sel = np.zeros((32, 16 * 128), np.float32)
    for t in range(16):
        for j in range(2):
            sel[t * 2 + j, t * 128 + j * 64: t * 128 + (j + 1) * 64] = 1.0
    return bones, sel


def build_rwkvA(T=4096):
    nc = _new_nc()
    xT = _din(nc, "xT", [D, T]); gain = _din(nc, "gain", [128, KC]); mu_d = _din(nc, "mu", [128, 6 * KC])
    wrkv = _din(nc, "wrkv", [3, D, D])
    w0_d = _din(nc, "w0", [128, KC]); w1 = _din(nc, "w1", [D, 96]); w2 = _din(nc, "w2", [96, D])
    a0_d = _din(nc, "a0", [128, KC]); a1 = _din(nc, "a1", [D, 96]); a2 = _din(nc, "a2", [96, D])
    g1 = _din(nc, "g1", [D, 256]); g2 = _din(nc, "g2", [256, D])
    kk_d = _din(nc, "k_k", [128, KC]); ka_d = _din(nc, "k_a", [128, KC]); bones_d = _din(nc, "bones", [128, 128])
    yT = _dout(nc, "yT", [RW_ROWS * D, T])
    with contextlib.ExitStack() as stack:
        P = Prog(nc, stack); c = Ctx(nc, stack, P)
        emit_rwkvA(c, xT, gain, mu_d, wrkv, w0_d, w1, w2, a0_d, a1, a2, g1, g2, kk_d, ka_d, bones_d, yT, T, True)
        P.emit()
    return nc


def emit_rwkvA(c, xT, gain, mu_d, wrkv, w0_d, w1, w2, a0_d, a1, a2, g1, g2, kk_d, ka_d, bones_d, yT, T, final):
    P = c.P
    if True:
        gn = c.sb("gn", [128, KC], F32); mu = c.sb("mu_t", [128, 6 * KC], F32)
        w0 = c.sb("w0_t", [128, KC], F32); a0 = c.sb("a0_t", [128, KC], F32)
        k_k = c.sb("kk_t", [128, KC], F32); k_a = c.sb("ka_t", [128, KC], F32); omka = c.sb("omka", [128, KC], F32)
        bones = c.sb("bones_t", [128, 128], F32)
        for t, d, k in ((gn, gain, "gn"), (mu, mu_d, "mu"), (w0, w0_d, "w0"), (a0, a0_d, "a0"), (k_k, kk_d, "k_k"), (k_a, ka_d, "k_a"), (bones, bones_d, "bones")):
            P.dma("sp", t[:], d, writes=[k])
        P.op("dve", lambda e: e.tensor_scalar(out=omka[:], in0=k_a[:], scalar1=-1.0, scalar2=1.0, op0=ALU.mult, op1=ALU.add), reads=["k_a"], writes=["omka"])
        nw0 = c.sb("nw0", [128, KC], F32); na0 = c.sb("na0", [128, KC], F32); th = c.sb("th", [128, 512], F32)
        P.op("dve", lambda e: e.tensor_scalar(out=nw0[:], in0=w0[:], scalar1=-1.0, scalar2=None, op0=ALU.mult), reads=["w0"], writes=["nw0"])
        P.op("dve", lambda e: e.tensor_scalar(out=na0[:], in0=a0[:], scalar1=-1.0, scalar2=None, op0=ALU.mult), reads=["a0"], writes=["na0"])
        w2b = c.sb("w2b", [96, D], BF16); a2b = c.sb("a2b", [96, D], BF16); g2b = c.sb("g2b", [128, 2, D], BF16)
        P.dma("pool", w2b[:], w2, writes=["w2b"]); P.dma("pool", a2b[:], a2, writes=["a2b"])
        P.dma("pool", g2b[:], g2.rearrange("(c p) n -> p c n", p=128), writes=["g2b"])
        hfp = c.sb("hfp", [128, KC, 513], F32); diff = c.sb("diff", [128, KC, 512], F32)
        mx = [c.sb(f"mx{i}", [128, KC, 512], BF16) for i in range(2)]
        kbuf = c.sb("kbuf", [128, KC, 512], F32)
        t1 = c.sb("t1", [128, 2, 512], BF16)
        ysb = [c.sb(f"ysb{i}", [128, 512], F32) for i in range(2)]
        asb = c.sb("asb", [128, 512], F32); kkr = c.sb("kkr", [128, 512], F32); sq = c.sb("r_sq", [128, 512], F32)
        rn = c.sb("rn", [128, 512], F32)
        psb = [c.ps(f"ps_b{i}") for i in range(2)]
        psn2 = c.ps("ps_c0")
        yv = yT.rearrange("(r kc p) t -> r p kc t", p=128, kc=KC)
        xv = _fm(xT)

        def store(row, j, tg, src_ap, key):
            if row not in RWM:
                return
            P.dma("sp", yv[RWM[row]][:, j, tg:tg + 512], src_ap, reads=[key], final=final)

        for tt in range(T // 512):
            tg = tt * 512
            if tt == 0:
                P.op("pool", lambda e: e.memset(hfp[:, :, 0:1], 0.0), writes=["hfp"])
                emit_norm(c, xv[:, :, 0:512], 512, gn, hfp, 1, "hfp")
            else:
                emit_norm(c, xv[:, :, tg - 1:tg + 512], 513, gn, hfp, 0, "hfp")
            P.op("dve", lambda e: e.tensor_tensor(out=diff[:], in0=hfp[:, :, 0:512], in1=hfp[:, :, 1:513], op=ALU.subtract), reads=["hfp"], writes=["diff"])
            for i in range(6):
                mi = c.rot("mx", 2)
                m = mx[mi]
                for kc in range(KC):
                    P.op("dve", lambda e, kc=kc, i=i, m=m: e.scalar_tensor_tensor(
                        out=m[:, kc, :], in0=diff[:, kc, :], scalar=mu[:, i * KC + kc:i * KC + kc + 1], in1=hfp[:, kc, 1:513],
                        op0=ALU.mult, op1=ALU.add), reads=["diff", "hfp", "mu"], writes=[f"mx{mi}"])
                mk = [f"mx{mi}"]
                if i < 3:
                    def cb(j, t0, n, ps, psk, i=i, tg=tg):
                        if i == 1:
                            P.op("act", lambda e: e.copy(out=kbuf[:, j, :], in_=ps[:, 0:512]), reads=[psk], writes=["kbuf"])
                            store(1, j, tg, kbuf[:, j, :], "kbuf")
                        else:
                            yi = c.rot("ysb", 2)
                            P.op("act", lambda e: e.copy(out=ysb[yi][:], in_=ps[:, 0:512]), reads=[psk], writes=[f"ysb{yi}"])
                            store(i, j, tg, ysb[yi][:], f"ysb{yi}")
                    emit_proj(c, wrkv[i].rearrange("(kc p) n -> p kc n", p=128), 0, KC, m, mk, 512, cb)
                elif i == 3 or i == 4:
                    wl = w1 if i == 3 else a1

                    def cb(j, t0, n, ps, psk, i=i):
                        if i == 3:
                            P.op("act", lambda e: e.activation(out=th[0:96, :], in_=ps[0:96, 0:512], func=AF.Exp, scale=-2.0), reads=[psk], writes=["th"])
                            P.op("dve", lambda e: e.tensor_scalar(out=th[0:96, :], in0=th[0:96, :], scalar1=1.0, scalar2=None, op0=ALU.add), reads=["th"], writes=["th"])
                            P.op("dve", lambda e: e.reciprocal(out=th[0:96, :], in_=th[0:96, :]), reads=["th"], writes=["th"])
                            P.op("dve", lambda e: e.tensor_scalar(out=t1[0:96, 0, :], in0=th[0:96, :], scalar1=2.0, scalar2=-1.0, op0=ALU.mult, op1=ALU.add), reads=["th"], writes=["t1"])
                        else:
                            P.op("act", lambda e: e.copy(out=t1[0:96, 0, :], in_=ps[0:96, 0:512]), reads=[psk], writes=["t1"])
                    emit_proj(c, wl.rearrange("(kc p) n -> p kc n", p=128), 0, 1, m, mk, 512, cb, cw=96)
                    w2x, w2k, bias, bk = (w2b, "w2b", w0, "w0") if i == 3 else (a2b, "a2b", a0, "a0")
                    for j in range(KC):
                        pi = c.rot("ps_b", 2)
                        P.op("pe", lambda e, pi=pi, j=j, w2x=w2x: e.matmul(psb[pi][:], w2x[0:96, j * 128:(j + 1) * 128], t1[0:96, 0, :], start=True, stop=True),
                             reads=["t1", w2k], writes=[f"ps_b{pi}"])
                        if i == 3:
                            yi = c.rot("ysb", 2)
                            P.op("act", lambda e, pi=pi, j=j, yi=yi: e.activation(out=ysb[yi][:], in_=psb[pi][:], func=AF.Exp, bias=nw0[:, j:j + 1], scale=-1.0),
                                 reads=[f"ps_b{pi}", "nw0"], writes=[f"ysb{yi}"])
                            P.op("dve", lambda e, yi=yi: e.tensor_scalar(out=ysb[yi][:], in0=ysb[yi][:], scalar1=1.0, scalar2=None, op0=ALU.add), reads=[f"ysb{yi}"], writes=[f"ysb{yi}"])
                            P.op("dve", lambda e, yi=yi: e.reciprocal(out=ysb[yi][:], in_=ysb[yi][:]), reads=[f"ysb{yi}"], writes=[f"ysb{yi}"])
                            P.op("act", lambda e, yi=yi: e.activation(out=ysb[yi][:], in_=ysb[yi][:], func=AF.Exp, scale=-float(np.exp(-0.5))),
                                 reads=[f"ysb{yi}"], writes=[f"ysb{yi}"])
                            store(3, j, tg, ysb[yi][:], f"ysb{yi}")
                        else:
                            P.op("act", lambda e, pi=pi, j=j: e.activation(out=asb[:], in_=psb[pi][:], func=AF.Exp, bias=na0[:, j:j + 1], scale=-1.0),
                                 reads=[f"ps_b{pi}", "na0"], writes=["asb"])
                            P.op("dve", lambda e: e.tensor_scalar(out=asb[:], in0=asb[:], scalar1=1.0, scalar2=None, op0=ALU.add), reads=["asb"], writes=["asb"])
                            P.op("dve", lambda e: e.reciprocal(out=asb[:], in_=asb[:]), reads=["asb"], writes=["asb"])
                            store(4, j, tg, asb[:], "asb")
                            P.op("dve", lambda e, j=j: e.tensor_scalar(out=kkr[:], in0=kbuf[:, j, :], scalar1=k_k[:, j:j + 1], scalar2=None, op0=ALU.mult),
                                 reads=["kbuf", "k_k"], writes=["kkr"])
                            P.op("act", lambda e: e.activation(out=sq[:], in_=kkr[:], func=AF.Square), reads=["kkr"], writes=["r_sq"])
                            P.op("pe", lambda e: e.matmul(psn2[:], bones[:], sq[:], start=True, stop=True), reads=["r_sq", "bones"], writes=["ps_c0"])
                            P.op("dve", lambda e: e.tensor_scalar(out=rn[:], in0=psn2[:], scalar1=1e-24, scalar2=None, op0=ALU.max), reads=["ps_c0"], writes=["rn"])
                            P.op("act", lambda e: e.activation(out=rn[:], in_=rn[:], func=AF.Ln), reads=["rn"], writes=["rn"])
                            P.op("act", lambda e: e.activation(out=rn[:], in_=rn[:], func=AF.Exp, scale=-0.5), reads=["rn"], writes=["rn"])
                            P.op("dve", lambda e: e.tensor_tensor(out=kkr[:], in0=kkr[:], in1=rn[:], op=ALU.mult), reads=["kkr", "rn"], writes=["kkr"])
                            yi = c.rot("ysb", 2)
                            P.op("dve", lambda e, yi=yi: e.tensor_scalar(out=ysb[yi][:], in0=kkr[:], scalar1=-1.0, scalar2=None, op0=ALU.mult), reads=["kkr"], writes=[f"ysb{yi}"])
                            store(6, j, tg, ysb[yi][:], f"ysb{yi}")
                            yi = c.rot("ysb", 2)
                            P.op("dve", lambda e, yi=yi: e.tensor_tensor(out=ysb[yi][:], in0=kkr[:], in1=asb[:], op=ALU.mult), reads=["kkr", "asb"], writes=[f"ysb{yi}"])
                            store(7, j, tg, ysb[yi][:], f"ysb{yi}")
                            yi = c.rot("ysb", 2)
                            P.op("dve", lambda e, j=j: e.tensor_scalar(out=rn[:], in0=asb[:], scalar1=k_a[:, j:j + 1], scalar2=omka[:, j:j + 1], op0=ALU.mult, op1=ALU.add),
                                 reads=["asb", "k_a", "omka"], writes=["rn"])
                            P.op("dve", lambda e, yi=yi, j=j: e.tensor_tensor(out=ysb[yi][:], in0=kbuf[:, j, :], in1=rn[:], op=ALU.mult), reads=["kbuf", "rn"], writes=[f"ysb{yi}"])
                            store(8, j, tg, ysb[yi][:], f"ysb{yi}")
                else:
                    def cb(j, t0, n, ps, psk):
                        P.op("act", lambda e: e.activation(out=th[:], in_=ps[:, 0:512], func=AF.Exp, scale=-1.0), reads=[psk], writes=["th"])
                        P.op("dve", lambda e: e.tensor_scalar(out=th[:], in0=th[:], scalar1=1.0, scalar2=None, op0=ALU.add), reads=["th"], writes=["th"])
                        P.op("dve", lambda e: e.reciprocal(out=th[:], in_=th[:]), reads=["th"], writes=["th"])
                        P.op("dve", lambda e: e.tensor_copy(out=t1[:, j, :], in_=th[:]), reads=["th"], writes=["t1"])
                    emit_proj(c, g1.rearrange("(kc p) n -> p kc n", p=128), 0, 2, m, mk, 512, cb)
                    for j in range(KC):
                        pi = c.rot("ps_b", 2)

                        def mm(e, pi=pi, j=j):
                            e.matmul(psb[pi][:], g2b[:, 0, j * 128:(j + 1) * 128], t1[:, 0, :], start=True, stop=False)
                            return e.matmul(psb[pi][:], g2b[:, 1, j * 128:(j + 1) * 128], t1[:, 1, :], start=False, stop=True)
                        P.op("pe", mm, reads=["t1", "g2b"], writes=[f"ps_b{pi}"])
                        yi = c.rot("ysb", 2)
                        P.op("act", lambda e, pi=pi, yi=yi: e.copy(out=ysb[yi][:], in_=psb[pi][:]), reads=[f"ps_b{pi}"], writes=[f"ysb{yi}"])
                        store(5, j, tg, ysb[yi][:], f"ysb{yi}")


def build_rwkvB(NSTEP=SEQ, NPASS=2):
    nc = _new_nc()
    yT = _din(nc, "yT", [RW_ROWS * D, NSTEP]); sel_d = _din(nc, "sel", [32, 16 * 128]); id_d = _din(nc, "ident", [128, 128])
    ys = _dout(nc, "ys", [D, NSTEP])
    tm = nc.dram_tensor("rw_tm", [5, NSTEP, 1024], F32, kind="Internal").ap()
    with contextlib.ExitStack() as stack:
        P = Prog(nc, stack); c = Ctx(nc, stack, P)
        emit_rwkvB(c, yT, sel_d, id_d, tm, ys, NSTEP, NPASS, True)
        P.emit()
    return nc


def emit_rwkvB(c, yT, sel_d, id_d, tm, ys, NSTEP, NPASS, final):
    P = c.P
    VB = 256
    ROWS = (RWM[6], RWM[3], RWM[7], RWM[8], RWM[0])
    if True:
        sel = c.sb("sel_t", [32, 16 * 128], F32); ident = c.sb("ident_t", [128, 128], F32)
        P.dma("sp", sel[:], sel_d, writes=["sel"]); P.dma("sp", ident[:], id_d, writes=["ident"])
        opb = [c.sb(f"opb{i}", [32, 5, 512], F32) for i in range(2)]
        vb = [c.sb(f"vb{i}", [128, 8, VB], F32) for i in range(2)]
        yb = [c.sb(f"yb{i}", [128, 8, VB], F32) for i in range(2)]
        S = c.sb("S", [128, 512], F32)
        tmp = c.sb("s_tmp", [128, 512], F32); tmp2 = c.sb("s_tmp2", [128, 512], F32); tmp3 = c.sb("s_tmp3", [128, 512], F32)
        tmp4 = c.sb("s_tmp4", [128, 512], F32); kc_ = c.sb("s_kc", [128, 512], F32)
        sa = c.sb("s_sa", [128, 8], F32)
        fmb = [c.sb(f"fmb{i}", [128, 8, 128], F32) for i in range(2)]
        tmb = [c.sb(f"tmb{i}", [128, 1024], F32) for i in range(2)]
        NPS = 7
        pss = [c.ps(f"ps_r{i}") for i in range(NPS)]
        r3 = lambda t: t[:].rearrange("p (i k) -> p i k", k=64)
        for hh in range(NPASS):
            for oi, row in enumerate(ROWS):
                src = yT[row * D + hh * 1024: row * D + (hh + 1) * 1024, :].rearrange("(c p) t -> p c t", p=128)
                for tb in range(NSTEP // 128):
                    fi = c.rot("fmb", 2)
                    P.dma("sp", fmb[fi][:], src[:, :, tb * 128:(tb + 1) * 128], writes=[f"fmb{fi}"])
                    for half in range(2):
                        pi = c.rot("ps_r", NPS)

                        def tr(e, fi=fi, half=half, pi=pi):
                            ins = None
                            for q in range(4):
                                ins = e.transpose(pss[pi][:, q * 128:(q + 1) * 128], fmb[fi][:, half * 4 + q, :], ident[:])
                            return ins
                        P.op("pe", tr, reads=[f"fmb{fi}", "ident"], writes=[f"ps_r{pi}"])
                        P.op("act", lambda e, fi=fi, half=half, pi=pi: e.copy(out=tmb[fi][:, half * 512:(half + 1) * 512], in_=pss[pi][:]),
                             reads=[f"ps_r{pi}"], writes=[f"tmb{fi}"])
                    P.dma("sp", tm[oi, tb * 128:(tb + 1) * 128, :], tmb[fi][:], reads=[f"tmb{fi}"], writes=["tm_dram"])
            ov = tm.rearrange("o (k t) (j f) -> k (t j) o f", t=16, j=2)
            P.op("pool", lambda e: e.memset(S[:], 0.0), reads=["S"], writes=["S"])
            vbase = RWM[2] * D + hh * 1024
            for t in range(NSTEP):
                bi = (t // 16) % 2
                if t % 16 == 0:
                    P.dma("sp", opb[bi][:], ov[t // 16], reads=["tm_dram"], writes=[f"opb{bi}"])
                vi = (t // VB) % 2
                if t % VB == 0:
                    for j in range(2):
                        P.dma("sp", vb[vi][j * 64:(j + 1) * 64, :, :],
                              yT[vbase + j * 512: vbase + (j + 1) * 512, t:t + VB].rearrange("(i v) t -> v i t", v=64), writes=[f"vb{vi}"])
                tl = t % 16
                pk = []
                for o in range(5):
                    pi = c.rot("ps_r", NPS)
                    P.op("pe", lambda e, pi=pi, o=o, bi=bi, tl=tl: e.matmul(pss[pi][:], sel[:, tl * 128:(tl + 1) * 128], opb[bi][:, o, :], start=True, stop=True),
                         reads=[f"opb{bi}", "sel"], writes=[f"ps_r{pi}"])
                    pk.append(pi)
                pn, pw, pb, pkk, pr = pk
                tv = t % VB
                P.op("act", lambda e, pkk=pkk: e.copy(out=kc_[:], in_=pss[pkk][:]), reads=[f"ps_r{pkk}"], writes=["s_kc"])
                P.op("pool", lambda e, vi=vi, tv=tv: e.tensor_tensor(out=r3(tmp3), in0=r3(kc_), in1=vb[vi][:, :, tv:tv + 1].broadcast_to([128, 8, 64]), op=ALU.mult),
                     reads=["s_kc", f"vb{vi}"], writes=["s_tmp3"])
                P.op("dve", lambda e, pn=pn: e.tensor_tensor(out=tmp[:], in0=S[:], in1=pss[pn][:], op=ALU.mult), reads=["S", f"ps_r{pn}"], writes=["s_tmp"])
                P.op("dve", lambda e: e.tensor_reduce(out=sa[:], in_=r3(tmp), axis=AX.X, op=ALU.add), reads=["s_tmp"], writes=["s_sa"])
                P.op("dve", lambda e, pw=pw: e.tensor_tensor(out=S[:], in0=S[:], in1=pss[pw][:], op=ALU.mult), reads=["S", f"ps_r{pw}", "s_tmp"], writes=["S"])
                P.op("dve", lambda e, pb=pb: e.tensor_tensor(out=r3(tmp2), in0=pss[pb][:].rearrange("p (i k) -> p i k", k=64), in1=sa[:].unsqueeze(2).broadcast_to([128, 8, 64]), op=ALU.mult),
                     reads=["s_sa", f"ps_r{pb}"], writes=["s_tmp2"])
                P.op("dve", lambda e: e.tensor_tensor(out=S[:], in0=S[:], in1=tmp2[:], op=ALU.add), reads=["S", "s_tmp2"], writes=["S"])
                P.op("dve", lambda e: e.tensor_tensor(out=S[:], in0=S[:], in1=tmp3[:], op=ALU.add), reads=["S", "s_tmp3"], writes=["S"])
                P.op("dve", lambda e, pr=pr: e.tensor_tensor(out=tmp4[:], in0=S[:], in1=pss[pr][:], op=ALU.mult), reads=["S", f"ps_r{pr}"], writes=["s_tmp4"])
                P.op("dve", lambda e, vi=vi, tv=tv: e.tensor_reduce(out=yb[vi][:, :, tv], in_=r3(tmp4), axis=AX.X, op=ALU.add), reads=["s_tmp4"], writes=[f"yb{vi}"])
                if tv == VB - 1:
                    for j in range(2):
                        P.dma("sp", ys[hh * 1024 + j * 512: hh * 1024 + (j + 1) * 512, t - VB + 1:t + 1].rearrange("(i v) t -> v i t", v=64),
                              yb[vi][j * 64:(j + 1) * 64, :, :], reads=[f"yb{vi}"], final=final)
            P.op("pool", lambda e: e.memset(sa[:], 0.0), reads=[f"opb0", f"opb1"], writes=["s_sa", "tm_dram"])


def build_rwkvC(T=4096):
    nc = _new_nc()
    xT = _din(nc, "xT", [D, T]); ysT = _din(nc, "ysT", [D, T]); yT = _din(nc, "yT", [RW_ROWS * D, T])
    lnw_d = _din(nc, "lnw", [128, KC]); lnb_d = _din(nc, "lnb", [128, KC]); rk_d = _din(nc, "r_k", [128, KC])
    bones_d = _din(nc, "bones", [128, 128]); w_out = _din(nc, "w_out", [D, D])
    oT = _dout(nc, "oT", [D, T])
    with contextlib.ExitStack() as stack:
        P = Prog(nc, stack); c = Ctx(nc, stack, P)
        emit_rwkvC(c, xT, ysT, yT, lnw_d, lnb_d, rk_d, bones_d, w_out, oT, T, True)
        P.emit()
    return nc


def emit_rwkvC(c, xT, ysT, yT, lnw_d, lnb_d, rk_d, bones_d, w_out, oT, T, final):
    P = c.P
    if True:
        lnw = c.sb("lnw_t", [128, KC], F32); lnb = c.sb("lnb_t", [128, KC], F32); rk = c.sb("rk_t", [128, KC], F32)
        bones = c.sb("bones_t", [128, 128], F32)
        for t, d, k in ((lnw, lnw_d, "lnw"), (lnb, lnb_d, "lnb"), (rk, rk_d, "rk"), (bones, bones_d, "bones")):
            P.dma("sp", t[:], d, writes=[k])
        TH = 1024
        zT = c.sb("zT", [128, KC, TH], BF16)
        names = ["cy", "cr", "ck", "cv", "cg"]
        tl = {nm: [c.sb(f"{nm}{i}", [128, 512], F32) for i in range(2)] for nm in names}
        yc = c.sb("c_yc", [128, 512], F32); sq = c.sb("c_sq", [128, 512], F32); rs = c.sb("c_rs", [128, 512], F32)
        rkk = c.sb("c_rkk", [128, 512], F32)
        ps1 = c.ps("ps_a0"); ps2 = c.ps("ps_a1"); ps3 = c.ps("ps_b0")
        yv = yT.rearrange("(r kc p) t -> r p kc t", p=128, kc=KC)
        ysv = _fm(ysT)
        for hf in range(T // TH):
            for j in range(KC):
                for t0 in range(0, TH, 512):
                    tg = hf * TH + t0
                    bi = c.rot("cbuf", 2)
                    srcs = {"cy": ysv[:, j, tg:tg + 512], "cr": yv[RWM[0]][:, j, tg:tg + 512], "ck": yv[RWM[8]][:, j, tg:tg + 512],
                            "cv": yv[RWM[2]][:, j, tg:tg + 512], "cg": yv[RWM[5]][:, j, tg:tg + 512]}
                    for nm in names:
                        P.dma("sp", tl[nm][bi][:], srcs[nm], writes=[f"{nm}{bi}"])
                    y, r, km, v, g = (tl[nm][bi] for nm in names)
                    ky, kr, kk_, kv, kg = (f"{nm}{bi}" for nm in names)
                    P.op("pe", lambda e, y=y: e.matmul(ps1[:], bones[:], y[:], start=True, stop=True), reads=[ky, "bones"], writes=["ps_a0"])
                    P.op("dve", lambda e, y=y: e.scalar_tensor_tensor(out=yc[:], in0=ps1[:], scalar=-1.0 / 64, in1=y[:], op0=ALU.mult, op1=ALU.add),
                         reads=["ps_a0", ky], writes=["c_yc"])
                    P.op("act", lambda e: e.activation(out=sq[:], in_=yc[:], func=AF.Square), reads=["c_yc"], writes=["c_sq"])
                    P.op("pe", lambda e: e.matmul(ps2[:], bones[:], sq[:], start=True, stop=True), reads=["c_sq", "bones"], writes=["ps_a1"])
                    P.op("act", lambda e: e.activation(out=rs[:], in_=ps2[:], func=AF.Ln, bias=64e-5, scale=1.0 / 64), reads=["ps_a1"], writes=["c_rs"])
                    P.op("act", lambda e: e.activation(out=rs[:], in_=rs[:], func=AF.Exp, scale=-0.5), reads=["c_rs"], writes=["c_rs"])
                    P.op("dve", lambda e: e.tensor_tensor(out=yc[:], in0=yc[:], in1=rs[:], op=ALU.mult), reads=["c_yc", "c_rs"], writes=["c_yc"])
                    P.op("dve", lambda e, j=j: e.tensor_scalar(out=yc[:], in0=yc[:], scalar1=lnw[:, j:j + 1], scalar2=lnb[:, j:j + 1], op0=ALU.mult, op1=ALU.add),
                         reads=["c_yc", "lnw", "lnb"], writes=["c_yc"])
                    P.op("dve", lambda e, j=j, r=r, km=km: e.scalar_tensor_tensor(out=rkk[:], in0=r[:], scalar=rk[:, j:j + 1], in1=km[:], op0=ALU.mult, op1=ALU.mult),
                         reads=[kr, kk_, "rk"], writes=["c_rkk"])
                    P.op("pe", lambda e: e.matmul(ps3[:], bones[:], rkk[:], start=True, stop=True), reads=["c_rkk", "bones"], writes=["ps_b0"])
                    P.op("dve", lambda e, v=v: e.tensor_tensor(out=rkk[:], in0=ps3[:], in1=v[:], op=ALU.mult), reads=["ps_b0", kv], writes=["c_rkk"])
                    P.op("dve", lambda e: e.tensor_tensor(out=yc[:], in0=yc[:], in1=rkk[:], op=ALU.add), reads=["c_yc", "c_rkk"], writes=["c_yc"])
                    P.op("dve", lambda e, j=j, t0=t0, g=g: e.tensor_tensor(out=zT[:, j, t0:t0 + 512], in0=yc[:], in1=g[:], op=ALU.mult), reads=["c_yc", kg], writes=["zT"])
            emit_outproj(c, w_out.rearrange("(c p) n -> p c n", p=128), KC, zT, ["zT"], _fm(xT), _fm(oT), hf * TH, TH, 1.0, final)


TF = SEQ


def build_fused():
    nc = _new_nc()
    A = {}

    def din(name, shape, dt=F32):
        A[name] = _din(nc, name, shape, dt)
        return A[name]
    xT = din("xT", [D, TF]); memT = din("memT", [D, ML]); posb = din("posb", [128, TF], I32)
    din("ffn_gain", [4, 2, 128, KC]); din("ffn_w_gate", [4, 2, D, FF]); din("ffn_w_up", [4, 2, D, FF]); din("ffn_w_down", [4, 2, FF, D])
    din("mix_gain", [4, 128, KC]); din("xg", [4, 128, KC]); din("mg", [4, 128, KC]); din("xq", [4, 128, 4]); din("xk", [4, 128, 4])
    din("xattn_wq", [4, D, D]); din("xattn_wkv", [4, D, 2 * D]); din("xattn_wo", [4, D, D])
    din("conv_w_in", [1, D, 3 * D]); din("conv_cw", [128, KC * 3]); din("conv_w_out", [1, D, D])
    din("dil_w_qkv", [1, D, 9216]); din("dil_gq", [128, 3]); din("dil_gk", [128, 3]); din("dil_w_out", [1, 1024, D])
    din("hgrn_w_in", [1, D, 4 * D]); din("hgrn_lbl", [128, 64]); din("hgrn_ng", [HC, 128]); din("hgrn_w_out", [1, D, D])
    din("rwkv_mu", [128, 6 * KC]); din("rwkv_w_rkv", [1, 3, D, D])
    for nm in ("w0", "a0", "k_k", "k_a", "lnw", "lnb", "r_k"):
        din("rwkv_" + nm, [128, KC])
    din("rwkv_w1", [1, D, 96]); din("rwkv_w2", [1, 96, D]); din("rwkv_a1", [1, D, 96]); din("rwkv_a2", [1, 96, D])
    din("rwkv_g1", [1, D, 256]); din("rwkv_g2", [1, 256, D]); din("rwkv_w_out", [1, D, D])
    din("c_invf", [128, 1]); din("c_rm", [128, 128]); din("c_mask", [128, 256]); din("c_ident", [128, 128])
    din("c_cm", [128, SEQ]); din("c_tm", [HC, HC]); din("c_bones", [128, 128]); din("c_sel", [32, 16 * 128])
    oT = _dout(nc, "oT", [D, TF])

    def scratch(name, shape):
        return nc.dram_tensor(name, list(shape), F32, kind="Internal").ap()
    xa = scratch("scr_xa", [D, TF]); xb = scratch("scr_xb", [D, TF])
    yTs = scratch("scr_y", [RW_ROWS * D, TF]); yQs = scratch("scr_q", [9216, TF]); sTs = scratch("scr_s", [D, TF]); ysT = scratch("scr_ys", [D, TF])
    rw_tm = scratch("scr_tm", [5, TF, 1024])

    with contextlib.ExitStack() as stack:
        P = Prog(nc, stack)
        state = {"k": 0}

        def stage(fn, last=False):
            k = state["k"]
            state["k"] += 1
            with contextlib.ExitStack() as st:
                c = Ctx(nc, st, P, pfx=f"s{k}_")
                fn(c, st, f"s{k}_")
                P.barrier()
                if last:
                    P.emit()
                else:
                    P.flush()

        cur = xT
        bufs = [xa, xb]
        nb = 0

        def nxt():
            nonlocal nb
            b = bufs[nb % 2]
            nb += 1
            return b

        for i in range(4):
            dst = nxt()
            stage(lambda c, st, pf, cur=cur, dst=dst, i=i: emit_ffn(nc, st, P, cur, A["ffn_gain"][i, 0], A["ffn_w_gate"][i, 0], A["ffn_w_up"][i, 0],
                                                                      A["ffn_w_down"][i, 0], dst, TF, pf, final=False))
            cur = dst
            dst = nxt()
            mg = A["mix_gain"][i]
            if i == 0:
                stage(lambda c, st, pf, cur=cur, dst=dst: emit_conv(c, cur, mg, A["conv_w_in"][0], A["conv_cw"], A["conv_w_out"][0], dst, TF, False))
            elif i == 1:
                stage(lambda c, st, pf, cur=cur: emit_normproj(c, cur, mg, A["dil_w_qkv"][0], yQs, 9216, TF))
                stage(lambda c, st, pf: emit_dilcore(c, yQs, posb, A["c_invf"], A["c_rm"], A["c_mask"], A["c_ident"], A["dil_gq"], A["dil_gk"],
                                                     sTs[0:1024, :], 8, False))
                stage(lambda c, st, pf, cur=cur, dst=dst: emit_outproj_stage(c, cur, sTs[0:1024, :], A["dil_w_out"][0], dst, 8, TF))
            elif i == 2:
                stage(lambda c, st, pf, cur=cur: emit_normproj(c, cur, mg, A["hgrn_w_in"][0], yQs[0:8192, :], 8192, TF))
                stage(lambda c, st, pf: emit_hgrncore(c, yQs[0:8192, :], A["hgrn_lbl"], A["hgrn_ng"], A["c_cm"], A["c_tm"], A["c_ident"], sTs, 16, 2, False))
                stage(lambda c, st, pf, cur=cur, dst=dst: emit_outproj_stage(c, cur, sTs, A["hgrn_w_out"][0], dst, 16, TF))
            else:
                stage(lambda c, st, pf, cur=cur: emit_rwkvA(c, cur, mg, A["rwkv_mu"], A["rwkv_w_rkv"][0], A["rwkv_w0"], A["rwkv_w1"][0], A["rwkv_w2"][0],
                                                            A["rwkv_a0"], A["rwkv_a1"][0], A["rwkv_a2"][0], A["rwkv_g1"][0], A["rwkv_g2"][0],
                                                            A["rwkv_k_k"], A["rwkv_k_a"], A["c_bones"], yTs, TF, False))
                stage(lambda c, st, pf: emit_rwkvB(c, yTs, A["c_sel"], A["c_ident"], rw_tm, ysT, TF, 2, False))
                stage(lambda c, st, pf, cur=cur, dst=dst: emit_rwkvC(c, cur, ysT, yTs, A["rwkv_lnw"], A["rwkv_lnb"], A["rwkv_r_k"], A["c_bones"],
                                                                     A["rwkv_w_out"][0], dst, TF, False))
            cur = dst
            dst = nxt()
            stage(lambda c, st, pf, cur=cur, dst=dst, i=i: emit_xattn(c, cur, memT, A["xg"][i], A["mg"][i], A["xq"][i], A["xk"][i],
                                                                       A["xattn_wq"][i], A["xattn_wkv"][i], A["xattn_wo"][i], dst, TF, False))
            cur = dst
            last = (i == 3)
            dst = oT if last else nxt()
            stage(lambda c, st, pf, cur=cur, dst=dst, i=i, last=last: emit_ffn(nc, st, P, cur, A["ffn_gain"][i, 1], A["ffn_w_gate"][i, 1], A["ffn_w_up"][i, 1],
                                                                                A["ffn_w_down"][i, 1], dst, TF, pf, final=last), last=last)
            cur = dst
    return nc


_NC_CACHE = {}


def _pc(v):
    return np.ascontiguousarray(np.asarray(v, np.float32).reshape(-1, 128).T)


def _c(a):
    return np.ascontiguousarray(a)


def kernel(**inp):
    inp = {k: np.asarray(v) for k, v in inp.items()}
    x = inp["x"]
    B, S, _ = x.shape
    if "fused" not in _NC_CACHE:
        _NC_CACHE["fused"] = build_fused()
    nc = _NC_CACHE["fused"]
    invf, rm, mask = dil_consts()
    cm, tm, ident = hgrn_consts()
    bones, sel = rwkv_consts()
    shared = {
        "ffn_gain": _c(np.stack([np.stack([_pc(inp["ffn_norm"][i, j]) for j in range(2)]) for i in range(4)])),
        "ffn_w_gate": inp["ffn_w_gate"], "ffn_w_up": inp["ffn_w_up"], "ffn_w_down": inp["ffn_w_down"],
        "mix_gain": _c(np.stack([_pc(inp["mix_norm"][i]) for i in range(4)])),
        "xg": _c(np.stack([_pc(inp["xattn_norm"][i]) for i in range(4)])),
        "mg": _c(np.stack([_pc(inp["mem_norm"][i]) for i in range(4)])),
        "xq": _c(np.stack([_pc(inp["xattn_q_gain"][i]) for i in range(4)])),
        "xk": _c(np.stack([_pc(inp["xattn_k_gain"][i]) for i in range(4)])),
        "xattn_wq": inp["xattn_wq"], "xattn_wkv": inp["xattn_wkv"], "xattn_wo": inp["xattn_wo"],
        "conv_w_in": inp["conv_w_in"], "conv_w_out": inp["conv_w_out"],
        "conv_cw": _c(inp["conv_w"][0].T.reshape(16, 128, 3).transpose(1, 0, 2).reshape(128, 48)),
        "dil_w_qkv": inp["dil_w_qkv"], "dil_w_out": inp["dil_w_out"],
        "dil_gq": _c(inp["dil_q_gain"][0].T), "dil_gk": _c(inp["dil_k_gain"][0].T),
        "hgrn_w_in": inp["hgrn_w_in"], "hgrn_w_out": inp["hgrn_w_out"],
        "hgrn_lbl": _c(inp["hgrn_lb_logits"].reshape(4, 16, 128).transpose(2, 1, 0).reshape(128, 64)),
        "hgrn_ng": _c(np.tile(inp["hgrn_norm"][0][None], (HC, 1))),
        "rwkv_mu": _c(np.concatenate([_pc(inp["rwkv_mu"][0][i]) for i in range(6)], axis=1)),
        "rwkv_w_rkv": inp["rwkv_w_rkv"],
        "rwkv_w0": _pc(inp["rwkv_w0"][0]), "rwkv_a0": _pc(inp["rwkv_a0"][0]), "rwkv_k_k": _pc(inp["rwkv_k_k"][0]),
        "rwkv_k_a": _pc(inp["rwkv_k_a"][0]), "rwkv_lnw": _pc(inp["rwkv_ln_w"][0]), "rwkv_lnb": _pc(inp["rwkv_ln_b"][0]),
        "rwkv_r_k": _pc(inp["rwkv_r_k"][0].reshape(-1)),
        "rwkv_w1": inp["rwkv_w1"], "rwkv_w2": inp["rwkv_w2"], "rwkv_a1": inp["rwkv_a1"], "rwkv_a2": inp["rwkv_a2"],
        "rwkv_g1": inp["rwkv_g1"], "rwkv_g2": inp["rwkv_g2"], "rwkv_w_out": inp["rwkv_w_out"],
        "c_invf": invf, "c_rm": rm, "c_mask": mask, "c_ident": ident, "c_cm": cm, "c_tm": tm, "c_bones": bones, "c_sel": sel,
    }
    shared = {k: _c(np.asarray(v, np.float32)) for k, v in shared.items()}
    in_maps = []
    for b in range(B):
        m = dict(shared)
        m["xT"] = _c(x[b].T)
        m["memT"] = _c(inp["mem"][b].T)
        m["posb"] = _c(np.tile(inp["positions"][b][None].astype(np.int32), (128, 1)))
        in_maps.append(m)
    res = run_bass_kernel_spmd(nc, in_maps, core_ids=list(range(B)))
    out = np.empty((B, S, D), np.float32)
    for b in range(B):
        out[b] = res.results[b]["oT"].T
    return out
```

```python
import contextlib
import numpy as np
import concourse.bass as bass
import concourse.mybir as mybir
from concourse.bass_utils import run_bass_kernel_spmd

F32 = mybir.dt.float32
BF16 = mybir.dt.bfloat16
I32 = mybir.dt.int32
AF = mybir.ActivationFunctionType
ALU = mybir.AluOpType
AX = mybir.AxisListType

D = 2048
KC = D // 128
FF = 5632
FC = FF // 128
NCORES = 8
EPS = 1e-6


class Prog:
    ENGS = ("pe", "act", "dve", "pool", "sp")
    NRING = 6

    def __init__(self, nc, stack):
        self.nc = nc
        self.stack = stack
        self.streams = {e: [] for e in self.ENGS}
        self.count = {e: 0 for e in self.ENGS}
        self.sems = {}
        for e in ("pe", "act", "dve", "pool"):
            self.sems[e] = stack.enter_context(nc.semaphore("s_" + e))
        self.rings = {}
        self.dma_k = {}
        for q in ("sp", "pool", "act"):
            self.rings[q] = [stack.enter_context(nc.semaphore(f"r_{q}{i}")) for i in range(self.NRING)]
            self.dma_k[q] = 0
        self.seen = {e: {} for e in self.ENGS}
        self.nosync_self = set()
        self.res = {}
        self.final_events = []

    def _need(self, eng, ev, waits):
        if ev is None:
            return
        sem, val = ev
        if eng in self.nosync_self and eng in self.sems and sem is self.sems[eng]:
            return
        key = id(sem)
        if self.seen[eng].get(key, 0) >= val:
            return
        self.seen[eng][key] = val
        waits.append((sem, val))

    def _deps(self, eng, reads, writes):
        waits = []
        for r in reads:
            st = self.res.get(r)
            if st is not None:
                self._need(eng, st["w"], waits)
        for w in writes:
            st = self.res.get(w)
            if st is not None:
                self._need(eng, st["w"], waits)
                for ev in st["r"].values():
                    self._need(eng, ev, waits)
        return waits

    def _commit(self, ev, reads, writes):
        for r in reads:
            st = self.res.setdefault(r, {"w": None, "r": {}})
            st["r"][id(ev[0])] = ev
        for w in writes:
            self.res[w] = {"w": ev, "r": {}}

    def op(self, eng, fn, reads=(), writes=()):
        waits = self._deps(eng, reads, writes)
        self.count[eng] += 1
        ev = (self.sems[eng], self.count[eng])
        self.streams[eng].append((waits, fn, (self.sems[eng], 1)))
        self._commit(ev, reads, writes)
        return ev

    def dma(self, q, out, in_, reads=(), writes=(), final=False):
        waits = self._deps(q, reads, writes)
        k = self.dma_k[q]
        self.dma_k[q] += 1
        sem = self.rings[q][k % self.NRING]
        gen = k // self.NRING
        if gen > 0:
            self._need(q, (sem, 16 * gen), waits)
        ev = (sem, 16 * (gen + 1))

        def fn(e, out=out, in_=in_):
            return e.dma_start(out=out, in_=in_, allow_slow_non_contiguous=True)

        self.streams[q].append((waits, fn, (sem, 16)))
        self._commit(ev, reads, writes)
        if final:
            self.final_events.append(ev)
        return ev

    def barrier(self):
        evs = []
        for e in ("pe", "act", "dve", "pool"):
            if self.count[e] > 0:
                evs.append((self.sems[e], self.count[e]))
        for q in ("sp", "pool", "act"):
            k = self.dma_k[q]
            for i in range(self.NRING):
                n = (k - i + self.NRING - 1) // self.NRING if k > i else 0
                if n > 0:
                    evs.append((self.rings[q][i], 16 * n))
        for eng in self.ENGS:
            waits = []
            for ev in evs:
                self._need(eng, ev, waits)
            if waits:
                self.streams[eng].append((waits, None, None))
        self.res = {}

    def emit(self):
        fw = []
        for ev in self.final_events:
            self._need("sp", ev, fw)
        self.streams["sp"].append((fw, None, None))
        self.flush()

    def flush(self):
        nc = self.nc
        with nc.Block() as block:
            def run(eng_obj, name):
                for waits, fn, inc in self.streams[name]:
                    for sem, val in waits:
                        eng_obj.wait_ge(sem, val)
                    if fn is not None:
                        ins = fn(eng_obj)
                        ins.then_inc(inc[0], inc[1])

            @block.tensor
            def _(e):
                run(e, "pe")

            @block.scalar
            def _(e):
                run(e, "act")

            @block.vector
            def _(e):
                run(e, "dve")

            @block.gpsimd
            def _(e):
                run(e, "pool")

            @block.sync
            def _(e):
                run(e, "sp")
        self.streams = {e: [] for e in self.ENGS}


def _sb(nc, stack, name, shape, dt):
    return stack.enter_context(nc.sbuf_tensor("t_" + name, list(shape), dt))


def _ps(nc, stack, name, shape, dt=F32):
    return stack.enter_context(nc.psum_tensor("t_" + name, list(shape), dt))


class Ctx:
    def __init__(self, nc, stack, P, pfx=""):
        self.nc, self.stack, self.P, self.pfx = nc, stack, P, pfx
        self.tiles = {}
        self.cnt = {}

    def sb(self, name, shape, dt):
        if name not in self.tiles:
            self.tiles[name] = _sb(self.nc, self.stack, self.pfx + name, shape, dt)
        return self.tiles[name]

    def ps(self, name, shape=(128, 512), dt=F32):
        if name not in self.tiles:
            self.tiles[name] = _ps(self.nc, self.stack, self.pfx + name, shape, dt)
        return self.tiles[name]

    def rot(self, name, n):
        k = self.cnt.get(name, 0)
        self.cnt[name] = k + 1
        return k % n

    def const_ones(self):
        if "ones32" not in self.tiles:
            t = self.sb("ones32", [128, 128], F32)
            self.P.op("pool", lambda e: e.memset(t[:], 1.0), writes=["ones32"])
            tb = self.sb("ones16", [128, 128], BF16)
            self.P.op("pool", lambda e: e.memset(tb[:], 1.0), writes=["ones16"])
        return self.tiles["ones32"], self.tiles["ones16"]


def emit_norm(c, src, ntok, gn, xn, xoff, xn_key, eps=EPS, scale_d=1.0 / D, kcs=KC, sumsq_ones=None, ones_key="ones32", gn_key="gn"):
    P = c.P
    ones32, _ = c.const_ones()
    if sumsq_ones is None:
        sumsq_ones = ones32
    TN = 256
    xin = c.sb("n_xin", [128, KC, TN], F32)
    sqs = [c.sb(f"n_sq{i}", [128, TN], F32) for i in range(2)]
    rstd = c.sb("n_rstd", [128, TN], F32)
    psn = c.ps("ps_n", [128, 512])
    for a in range(0, ntok, TN):
        n = min(TN, ntok - a)
        P.dma("sp", xin[:, 0:kcs, 0:n], src[:, :, a:a + n], writes=["n_xin"])
        for kc in range(kcs):
            si = c.rot("n_sq", 2)
            s = sqs[si]
            P.op("act", lambda e, s=s, kc=kc, n=n: e.activation(out=s[:, 0:n], in_=xin[:, kc, 0:n], func=AF.Square),
                 reads=["n_xin"], writes=[f"n_sq{si}"])
            P.op("pe", lambda e, s=s, kc=kc, n=n: e.matmul(psn[:, 0:n], sumsq_ones[:], s[:, 0:n], start=(kc == 0), stop=(kc == kcs - 1)),
                 reads=[f"n_sq{si}", ones_key], writes=["ps_n"])
        P.op("act", lambda e, n=n: e.activation(out=rstd[:, 0:n], in_=psn[:, 0:n], func=AF.Ln, bias=eps, scale=scale_d),
             reads=["ps_n"], writes=["n_rstd"])
        P.op("act", lambda e, n=n: e.activation(out=rstd[:, 0:n], in_=rstd[:, 0:n], func=AF.Exp, scale=-0.5), reads=["n_rstd"], writes=["n_rstd"])
        for kc in range(kcs):
            P.op("dve", lambda e, kc=kc, a=a, n=n: e.scalar_tensor_tensor(
                out=xn[:, kc, xoff + a:xoff + a + n], in0=xin[:, kc, 0:n], scalar=gn[:, kc:kc + 1],
                in1=rstd[:, 0:n], op0=ALU.mult, op1=ALU.mult),
                reads=["n_xin", "n_rstd", gn_key], writes=[xn_key])


def emit_proj(c, wv, col0, nchunks, xn, xn_keys, ntok, cb, kcs=KC, cw=128, toff=0):
    P = c.P
    wb = [c.sb(f"p_w{i}", [128, KC, 128], BF16) for i in range(2)]
    pss = [c.ps(f"ps_a{i}") for i in range(2)]
    for j in range(nchunks):
        b = c.rot("p_w", 2)
        P.dma("pool", wb[b][:, 0:kcs, 0:cw], wv[:, :, col0 + j * cw: col0 + (j + 1) * cw], writes=[f"p_w{b}"])
        for t0 in range(0, ntok, 512):
            n = min(512, ntok - t0)
            pi = c.rot("ps_a", 2)
            ps = pss[pi]

            def mm(e, b=b, ps=ps, t0=t0, n=n):
                ins = None
                for kc in range(kcs):
                    ins = e.matmul(ps[0:cw, 0:n], wb[b][:, kc, 0:cw], xn[:, kc, toff + t0:toff + t0 + n],
                                   start=(kc == 0), stop=(kc == kcs - 1))
                return ins
            P.op("pe", mm, reads=[f"p_w{b}"] + list(xn_keys), writes=[f"ps_a{pi}"])
            cb(j, t0, n, ps, f"ps_a{pi}")


def emit_outproj(c, wv, cc, src, src_keys, xTv, oTv, t0g, ntok, scale, final, soff=0):
    P = c.P
    wb = [c.sb(f"o_w{cc}_{i}", [128, cc, 128], BF16) for i in range(2)]
    pss = [c.ps(f"ps_c{i}") for i in range(2)]
    xres = [c.sb(f"o_xres{i}", [128, 512], F32) for i in range(2)]
    osb = [c.sb(f"o_osb{i}", [128, 512], F32) for i in range(2)]
    for nn in range(KC):
        b = c.rot("o_w", 2)
        P.dma("pool", wb[b][:, 0:cc, :], wv[:, :, nn * 128:(nn + 1) * 128], writes=[f"o_w{cc}_{b}"])
        for t0 in range(0, ntok, 512):
            n = min(512, ntok - t0)
            pb = c.rot("ps_c", 2)
            P.dma("sp", xres[pb][:, 0:n], xTv[:, nn, t0g + t0:t0g + t0 + n], writes=[f"o_xres{pb}"])

            def mm(e, b=b, pb=pb, t0=t0, n=n):
                ins = None
                for f in range(cc):
                    ins = e.matmul(pss[pb][:, 0:n], wb[b][:, f, :], src[:, f, soff + t0:soff + t0 + n],
                                   start=(f == 0), stop=(f == cc - 1))
                return ins
            P.op("pe", mm, reads=[f"o_w{cc}_{b}"] + list(src_keys), writes=[f"ps_c{pb}"])
            P.op("dve", lambda e, pb=pb, n=n: e.scalar_tensor_tensor(
                out=osb[pb][:, 0:n], in0=pss[pb][:, 0:n], scalar=float(scale), in1=xres[pb][:, 0:n],
                op0=ALU.mult, op1=ALU.add),
                reads=[f"ps_c{pb}", f"o_xres{pb}"], writes=[f"o_osb{pb}"])
            P.dma("sp", oTv[:, nn, t0g + t0:t0g + t0 + n], osb[pb][:, 0:n], reads=[f"o_osb{pb}"], final=final)


def _fm(ap):
    return ap.rearrange("(kc p) t -> p kc t", p=128)


def _new_nc():
    return bass.Bass("TRN2", target_bir_lowering=False)


def _din(nc, name, shape, dt=F32):
    return nc.dram_tensor(name, list(shape), dt, kind="ExternalInput").ap()


def _dout(nc, name, shape, dt=F32):
    return nc.dram_tensor(name, list(shape), dt, kind="ExternalOutput").ap()


def emit_normproj(c, xT, gain, w, yT, N, T, final=False):
    P = c.P
    gn = c.sb("gn", [128, KC], F32)
    P.dma("sp", gn[:], gain, writes=["gn"])
    TB = min(T, 2048)
    xn = c.sb("xn", [128, KC, TB], BF16)
    ysb = [c.sb(f"ysb{i}", [128, 512], F32) for i in range(2)]
    yv = _fm(yT)
    for tb in range(0, T, TB):
        emit_norm(c, _fm(xT)[:, :, tb:tb + TB], TB, gn, xn, 0, "xn")

        def cb(j, t0, n, ps, psk, tb=tb):
            i = c.rot("ysb", 2)
            P.op("act", lambda e: e.copy(out=ysb[i][:, 0:n], in_=ps[:, 0:n]), reads=[psk], writes=[f"ysb{i}"])
            P.dma("sp", yv[:, j, tb + t0:tb + t0 + n], ysb[i][:, 0:n], reads=[f"ysb{i}"], final=final)
        emit_proj(c, _fm(w), 0, N // 128, xn, ["xn"], TB, cb)


def build_normproj(N, T=2048):
    nc = _new_nc()
    xT = _din(nc, "xT", [D, T]); gain = _din(nc, "gain", [128, KC]); w = _din(nc, "w", [D, N])
    yT = _dout(nc, "yT", [N, T])
    with contextlib.ExitStack() as stack:
        P = Prog(nc, stack); c = Ctx(nc, stack, P)
        emit_normproj(c, xT, gain, w, yT, N, T, final=True)
        P.emit()
    return nc


def emit_outproj_stage(c, xT, sT, w, oT, CC, T, final=False):
    P = c.P
    TB = min(T, 2048)
    src = c.sb("src", [128, CC, TB], BF16)
    sv = sT.rearrange("(c p) t -> p c t", p=128)
    for tb in range(0, T, TB):
        for cc in range(CC):
            P.dma("pool", src[:, cc, :], sv[:, cc, tb:tb + TB], writes=["src"])
        emit_outproj(c, w.rearrange("(c p) n -> p c n", p=128), CC, src, ["src"], _fm(xT), _fm(oT), tb, TB, 1.0, final)


def build_outproj(CC, T=2048):
    nc = _new_nc()
    xT = _din(nc, "xT", [D, T]); sT = _din(nc, "sT", [CC * 128, T]); w = _din(nc, "w", [CC * 128, D])
    oT = _dout(nc, "oT", [D, T])
    with contextlib.ExitStack() as stack:
        P = Prog(nc, stack); c = Ctx(nc, stack, P)
        emit_outproj_stage(c, xT, sT, w, oT, CC, T, final=True)
        P.emit()
    return nc
def build_ffn(T=2048):
    nc = bass.Bass("TRN2", target_bir_lowering=False)
    xT = nc.dram_tensor("xT", [D, T], F32, kind="ExternalInput").ap()
    gain = nc.dram_tensor("gain", [128, KC], F32, kind="ExternalInput").ap()
    wg = nc.dram_tensor("wg", [D, FF], F32, kind="ExternalInput").ap()
    wu = nc.dram_tensor("wu", [D, FF], F32, kind="ExternalInput").ap()
    wd = nc.dram_tensor("wd", [FF, D], F32, kind="ExternalInput").ap()
    oT = nc.dram_tensor("oT", [D, T], F32, kind="ExternalOutput").ap()
    with contextlib.ExitStack() as stack:
        P = Prog(nc, stack)
        emit_ffn(nc, stack, P, xT, gain, wg, wu, wd, oT, T, "f")
        P.emit()
    return nc


def emit_ffn(nc, stack, P, xT, gain, wg, wu, wd, oT, T, pfx, final=True):
    TH = 1024
    NH = T // TH
    TN = 256
    TT = 512
    FG = 2
    xTv = xT.rearrange("(kc p) t -> p kc t", p=128)
    oTv = oT.rearrange("(kc p) t -> p kc t", p=128)
    wgv = wg.rearrange("(kc p) f -> p kc f", p=128)
    wuv = wu.rearrange("(kc p) f -> p kc f", p=128)
    wdv = wd.rearrange("(fc p) n -> p fc n", p=128)

    act = _sb(nc, stack, pfx + "act", [128, FC, TH], BF16)
    xn = _sb(nc, stack, pfx + "xn", [128, KC, TH], BF16)
    wgb = [_sb(nc, stack, pfx + f"wg{i}", [128, KC, FG * 128], BF16) for i in range(2)]
    wub = [_sb(nc, stack, pfx + f"wu{i}", [128, KC, FG * 128], BF16) for i in range(2)]
    wdb = [_sb(nc, stack, pfx + f"wd{i}", [128, FC, 128], BF16) for i in range(2)]
    xin = _sb(nc, stack, pfx + "xin", [128, KC, TN], F32)
    sq = [_sb(nc, stack, pfx + f"sq{i}", [128, TN], F32) for i in range(2)]
    rstd = _sb(nc, stack, pfx + "rstd", [128, TN], F32)
    ones = _sb(nc, stack, pfx + "ones", [128, 128], F32)
    gn = _sb(nc, stack, pfx + "gn", [128, KC], F32)
    sil = [_sb(nc, stack, pfx + f"sil{i}", [128, TT], F32) for i in range(2)]
    xres = [_sb(nc, stack, pfx + f"xres{i}", [128, TT], F32) for i in range(2)]
    osb = [_sb(nc, stack, pfx + f"osb{i}", [128, TT], F32) for i in range(2)]
    ps_n = _ps(nc, stack, pfx + "psn", [128, TN])
    ps_g = [_ps(nc, stack, pfx + f"psg{i}", [128, TT]) for i in range(2)]
    ps_u = [_ps(nc, stack, pfx + f"psu{i}", [128, TT]) for i in range(2)]
    ps_o = [_ps(nc, stack, pfx + f"pso{i}", [128, TT]) for i in range(2)]

    K = lambda *a: (pfx,) + a
    P.op("pool", lambda e: e.memset(ones[:], 1.0), writes=[K("ones")])
    P.dma("sp", gn[:], gain, writes=[K("gn")])

    gi = 0
    di = 0
    ei = 0
    si = 0
    for h in range(NH):
        t0 = h * TH
        for nt in range(TH // TN):
            ta = t0 + nt * TN
            P.dma("sp", xin[:], xTv[:, :, ta:ta + TN], writes=[K("xin")])
            for kc in range(KC):
                s = sq[si % 2]
                P.op("act", lambda e, s=s, kc=kc: e.activation(out=s[:], in_=xin[:, kc, :], func=AF.Square),
                     reads=[K("xin")], writes=[K("sq", si % 2)])
                P.op("pe", lambda e, s=s, kc=kc: e.matmul(ps_n[:], ones[:], s[:], start=(kc == 0), stop=(kc == KC - 1)),
                     reads=[K("sq", si % 2), K("ones")], writes=[K("psn")])
                si += 1
            P.op("act", lambda e: e.activation(out=rstd[:], in_=ps_n[:], func=AF.Ln, bias=EPS, scale=1.0 / D),
                 reads=[K("psn")], writes=[K("rstd")])
            P.op("act", lambda e: e.activation(out=rstd[:], in_=rstd[:], func=AF.Exp, scale=-0.5), reads=[K("rstd")], writes=[K("rstd")])
            for kc in range(KC):
                P.op("dve", lambda e, kc=kc, nt=nt: e.scalar_tensor_tensor(
                    out=xn[:, kc, nt * TN:(nt + 1) * TN], in0=xin[:, kc, :], scalar=gn[:, kc:kc + 1],
                    in1=rstd[:], op0=ALU.mult, op1=ALU.mult),
                    reads=[K("xin"), K("rstd"), K("gn")], writes=[K("xn", nt)])
        xn_keys = [K("xn", nt) for nt in range(TH // TN)]
        for fg in range(FC // FG):
            b = gi % 2
            f0 = fg * FG * 128
            P.dma("pool", wgb[b][:], wgv[:, :, f0:f0 + FG * 128], writes=[K("wg", b)])
            P.dma("pool", wub[b][:], wuv[:, :, f0:f0 + FG * 128], writes=[K("wu", b)])
            for fc in range(FG):
                f = fg * FG + fc
                for tt in range(TH // TT):
                    pb = ei % 2

                    def mm(e, wt, pt, fc=fc, tt=tt):
                        ins = None
                        for kc in range(KC):
                            ins = e.matmul(pt[:], wt[:, kc, fc * 128:(fc + 1) * 128],
                                           xn[:, kc, tt * TT:(tt + 1) * TT],
                                           start=(kc == 0), stop=(kc == KC - 1))
                        return ins
                    P.op("pe", lambda e, b=b, pb=pb, mm=mm: mm(e, wgb[b], ps_g[pb]),
                         reads=[K("wg", b)] + xn_keys, writes=[K("psg", pb)])
                    P.op("pe", lambda e, b=b, pb=pb, mm=mm: mm(e, wub[b], ps_u[pb]),
                         reads=[K("wu", b)] + xn_keys, writes=[K("psu", pb)])
                    P.op("act", lambda e, pb=pb: e.activation(out=sil[pb][:], in_=ps_g[pb][:], func=AF.Silu),
                         reads=[K("psg", pb)], writes=[K("sil", pb)])
                    P.op("dve", lambda e, pb=pb, f=f, tt=tt: e.tensor_tensor(
                        out=act[:, f, tt * TT:(tt + 1) * TT], in0=sil[pb][:], in1=ps_u[pb][:], op=ALU.mult),
                        reads=[K("sil", pb), K("psu", pb)], writes=[K("act", f)])
                    ei += 1
            gi += 1
        act_keys = [K("act", f) for f in range(FC)]
        for n in range(KC):
            b = di % 2
            P.dma("pool", wdb[b][:], wdv[:, :, n * 128:(n + 1) * 128], writes=[K("wd", b)])
            for tt in range(TH // TT):
                pb = ei % 2
                ta = t0 + tt * TT
                P.dma("sp", xres[pb][:], xTv[:, n, ta:ta + TT], writes=[K("xres", pb)])

                def mmd(e, b=b, pb=pb, tt=tt):
                    ins = None
                    for f in range(FC):
                        ins = e.matmul(ps_o[pb][:], wdb[b][:, f, :], act[:, f, tt * TT:(tt + 1) * TT],
                                       start=(f == 0), stop=(f == FC - 1))
                    return ins
                P.op("pe", mmd, reads=[K("wd", b)] + act_keys, writes=[K("pso", pb)])
                P.op("dve", lambda e, pb=pb: e.scalar_tensor_tensor(
                    out=osb[pb][:], in0=ps_o[pb][:], scalar=0.5, in1=xres[pb][:], op0=ALU.mult, op1=ALU.add),
                    reads=[K("pso", pb), K("xres", pb)], writes=[K("osb", pb)])
                P.dma("sp", oTv[:, n, ta:ta + TT], osb[pb][:], reads=[K("osb", pb)], final=final)
                ei += 1
            di += 1


XH, XD, ML = 4, 512, 256


def build_xattn(T=2048):
    nc = _new_nc()
    xT = _din(nc, "xT", [D, T]); memT = _din(nc, "memT", [D, ML])
    gx = _din(nc, "gx", [128, KC]); gm = _din(nc, "gm", [128, KC])
    gq = _din(nc, "gq", [128, 4]); gk = _din(nc, "gk", [128, 4])
    wq = _din(nc, "wq", [D, D]); wkv = _din(nc, "wkv", [D, 2 * D]); wo = _din(nc, "wo", [D, D])
    oT = _dout(nc, "oT", [D, T])
    with contextlib.ExitStack() as stack:
        P = Prog(nc, stack); c = Ctx(nc, stack, P)
        emit_xattn(c, xT, memT, gx, gm, gq, gk, wq, wkv, wo, oT, T, True)
        P.emit()
    return nc


def emit_xattn(c, xT, memT, gx, gm, gq, gk, wq, wkv, wo, oT, T, final):
    P = c.P
    if True:
        ones32, ones16 = c.const_ones()
        gxt = c.sb("gx", [128, KC], F32); gmt = c.sb("gm", [128, KC], F32)
        gqt = c.sb("gq", [128, 4], F32); gkt = c.sb("gk", [128, 4], F32)
        P.dma("sp", gxt[:], gx, writes=["gx"]); P.dma("sp", gmt[:], gm, writes=["gm"])
        P.dma("sp", gqt[:], gq, writes=["gq"]); P.dma("sp", gkt[:], gk, writes=["gk"])
        memn = c.sb("memn", [128, KC, ML], BF16)
        emit_norm(c, _fm(memT), ML, gmt, memn, 0, "memn", gn_key="gm")
        kT = c.sb("kT", [128, KC, ML], BF16)
        kraw = c.sb("kraw", [128, 4, 512], F32)
        sq = [c.sb(f"x_sq{i}", [128, 512], F32) for i in range(2)]
        rs = c.sb("x_rs", [128, 512], F32)
        psn = c.ps("ps_n")
        scale_h = 1.0 / XD

        def headnorm(raw, rawkey, n, gt, gkey, dst_fn, dstkey):
            for dc in range(4):
                si = c.rot("x_sq", 2)
                P.op("act", lambda e, si=si, dc=dc: e.activation(out=sq[si][:, 0:n], in_=raw[:, dc, 0:n], func=AF.Square),
                     reads=[rawkey], writes=[f"x_sq{si}"])
                P.op("pe", lambda e, si=si, dc=dc: e.matmul(psn[:, 0:n], ones32[:], sq[si][:, 0:n], start=(dc == 0), stop=(dc == 3)),
                     reads=[f"x_sq{si}", "ones32"], writes=["ps_n"])
            P.op("act", lambda e: e.activation(out=rs[:, 0:n], in_=psn[:, 0:n], func=AF.Ln, bias=EPS, scale=scale_h),
                 reads=["ps_n"], writes=["x_rs"])
            P.op("act", lambda e: e.activation(out=rs[:, 0:n], in_=rs[:, 0:n], func=AF.Exp, scale=-0.5), reads=["x_rs"], writes=["x_rs"])
            for dc in range(4):
                P.op("dve", lambda e, dc=dc: e.scalar_tensor_tensor(
                    out=dst_fn(dc), in0=raw[:, dc, 0:n], scalar=gt[:, dc:dc + 1], in1=rs[:, 0:n],
                    op0=ALU.mult, op1=ALU.mult), reads=[rawkey, "x_rs", gkey], writes=[dstkey])

        def cb_k(j, t0, n, ps, psk):
            dc = j % 4
            P.op("act", lambda e: e.copy(out=kraw[:, dc, 0:n], in_=ps[:, 0:n]), reads=[psk], writes=["kraw"])
            if dc == 3:
                h = j // 4
                headnorm(kraw, "kraw", ML, gkt, "gk", lambda dc2: kT[:, h * 4 + dc2, :], "kT")
        emit_proj(c, _fm(wkv), 0, KC, memn, ["memn"], ML, cb_k)
        v_sb = c.sb("v_sb", [128, 2, D], BF16)
        wvb = [c.sb(f"x_wv{i}", [128, KC, 256], BF16) for i in range(2)]
        wkvv = _fm(wkv)
        psb = [c.ps(f"ps_b{i}") for i in range(2)]
        for ct in range(8):
            b = c.rot("x_wv", 2)
            P.dma("pool", wvb[b][:], wkvv[:, :, D + ct * 256: D + (ct + 1) * 256], writes=[f"x_wv{b}"])
            for mc in range(2):
                pi = c.rot("ps_b", 2)

                def mm(e, b=b, pi=pi, mc=mc):
                    ins = None
                    for kc in range(KC):
                        ins = e.matmul(psb[pi][:, 0:256], memn[:, kc, mc * 128:(mc + 1) * 128], wvb[b][:, kc, :],
                                       start=(kc == 0), stop=(kc == KC - 1))
                    return ins
                P.op("pe", mm, reads=[f"x_wv{b}", "memn"], writes=[f"ps_b{pi}"])
                P.op("act", lambda e, pi=pi, mc=mc, ct=ct: e.copy(out=v_sb[:, mc, ct * 256:(ct + 1) * 256], in_=psb[pi][:, 0:256]),
                     reads=[f"ps_b{pi}"], writes=["v_sb"])
        TH = 1024
        xn = c.sb("xn", [128, KC, TH], BF16)
        oall = c.sb("oall", [128, KC, TH], BF16)
        qraw = c.sb("qraw", [128, 4, 512], F32)
        qn = c.sb("qn", [128, 4, 512], BF16)
        E = c.sb("E", [128, 2, 512], BF16)
        rden = c.sb("rden", [128, 512], F32)
        psc = [c.ps(f"ps_c{i}") for i in range(2)]
        sm_scale = float(XD) ** -0.5
        for hf in range(T // TH):
            tg = hf * TH
            emit_norm(c, _fm(xT)[:, :, tg:tg + TH], TH, gxt, xn, 0, "xn", gn_key="gx")

            def attn(h, t0, n):
                for mc in range(2):
                    pi = c.rot("ps_b", 2)

                    def mm(e, pi=pi, mc=mc):
                        ins = None
                        for dc in range(4):
                            ins = e.matmul(psb[pi][:, 0:n], kT[:, h * 4 + dc, mc * 128:(mc + 1) * 128], qn[:, dc, 0:n],
                                           start=(dc == 0), stop=(dc == 3))
                        return ins
                    P.op("pe", mm, reads=["kT", "qn"], writes=[f"ps_b{pi}"])
                    P.op("act", lambda e, pi=pi, mc=mc: e.activation(out=E[:, mc, 0:n], in_=psb[pi][:, 0:n], func=AF.Exp, scale=sm_scale),
                         reads=[f"ps_b{pi}"], writes=["E"])

                def mmz(e):
                    ins = None
                    for mc in range(2):
                        ins = e.matmul(psn[:, 0:n], ones16[:], E[:, mc, 0:n], start=(mc == 0), stop=(mc == 1))
                    return ins
                P.op("pe", mmz, reads=["E", "ones16"], writes=["ps_n"])
                P.op("dve", lambda e: e.reciprocal(out=rden[:, 0:n], in_=psn[:, 0:n]), reads=["ps_n"], writes=["rden"])
                for dc in range(4):
                    pi = c.rot("ps_c", 2)

                    def mmo(e, pi=pi, dc=dc):
                        ins = None
                        for mc in range(2):
                            ins = e.matmul(psc[pi][:, 0:n], v_sb[:, mc, h * XD + dc * 128: h * XD + (dc + 1) * 128], E[:, mc, 0:n],
                                           start=(mc == 0), stop=(mc == 1))
                        return ins
                    P.op("pe", mmo, reads=["E", "v_sb"], writes=[f"ps_c{pi}"])
                    P.op("dve", lambda e, pi=pi, dc=dc: e.tensor_tensor(
                        out=oall[:, h * 4 + dc, t0:t0 + n], in0=psc[pi][:, 0:n], in1=rden[:, 0:n], op=ALU.mult),
                        reads=[f"ps_c{pi}", "rden"], writes=["oall"])

            def cb_q(j, t0, n, ps, psk):
                dc = j % 4
                P.op("act", lambda e: e.copy(out=qraw[:, dc, 0:n], in_=ps[:, 0:n]), reads=[psk], writes=["qraw"])
                if dc == 3:
                    h = j // 4
                    headnorm(qraw, "qraw", n, gqt, "gq", lambda dc2: qn[:, dc2, 0:n], "qn")
                    attn(h, t0, n)
            for h in range(XH):
                for t0 in range(0, TH, 512):
                    def cb2(j, t0_, n, ps, psk, h=h, t0=t0):
                        cb_q(h * 4 + j, t0, n, ps, psk)
                    emit_proj(c, _fm(wq), h * XD, 4, xn, ["xn"], 512, cb2, toff=t0)
            emit_outproj(c, wo.rearrange("(c p) n -> p c n", p=128), KC, oall, ["oall"], _fm(xT), _fm(oT), tg, TH, 1.0, final)
def build_conv(T=4096):
    nc = _new_nc()
    xT = _din(nc, "xT", [D, T]); gain = _din(nc, "gain", [128, KC])
    w_in = _din(nc, "w_in", [D, 3 * D]); cwd = _din(nc, "cw", [128, KC * 3]); w_out = _din(nc, "w_out", [D, D])
    oT = _dout(nc, "oT", [D, T])
    with contextlib.ExitStack() as stack:
        P = Prog(nc, stack); c = Ctx(nc, stack, P)
        emit_conv(c, xT, gain, w_in, cwd, w_out, oT, T, True)
        P.emit()
    return nc


def emit_conv(c, xT, gain, w_in, cwd, w_out, oT, T, final):
    P = c.P
    if True:
        gn = c.sb("gn", [128, KC], F32); cw = c.sb("cwt", [128, KC * 3], F32)
        P.dma("sp", gn[:], gain, writes=["gn"]); P.dma("sp", cw[:], cwd, writes=["cwt"])
        TH = 1024
        NE = TH + 2
        xn = c.sb("xn", [128, KC, NE], BF16)
        gT = c.sb("gT", [128, KC, TH], BF16)
        cgs = c.sb("cgs", [128, NE], F32); zb = c.sb("zb", [128, NE], F32)
        bb = c.sb("bb", [128, NE], F32); yb = c.sb("yb", [128, TH], F32)
        xv = _fm(xT)
        for hf in range(T // TH):
            tg = hf * TH
            if hf == 0:
                P.op("pool", lambda e: e.memset(xn[:, :, 0:2], 0.0), writes=["xn"])
                emit_norm(c, xv[:, :, 0:TH], TH, gn, xn, 2, "xn")
            else:
                emit_norm(c, xv[:, :, tg - 2:tg + TH], NE, gn, xn, 0, "xn")
            for j in range(KC):
                def cb_cg(_, t0, n, ps, psk):
                    P.op("act", lambda e: e.copy(out=cgs[:, t0:t0 + n], in_=ps[:, 0:n]), reads=[psk], writes=["cgs"])

                def cb_u(_, t0, n, ps, psk):
                    P.op("dve", lambda e: e.tensor_tensor(out=zb[:, t0:t0 + n], in0=cgs[:, t0:t0 + n], in1=ps[:, 0:n], op=ALU.mult),
                         reads=[psk, "cgs"], writes=["zb"])

                def cb_b(_, t0, n, ps, psk):
                    P.op("act", lambda e: e.copy(out=bb[:, t0:t0 + n], in_=ps[:, 0:n]), reads=[psk], writes=["bb"])
                emit_proj(c, _fm(w_in), D + j * 128, 1, xn, ["xn"], NE, cb_cg)
                emit_proj(c, _fm(w_in), 2 * D + j * 128, 1, xn, ["xn"], NE, cb_u)
                emit_proj(c, _fm(w_in), j * 128, 1, xn, ["xn"], NE, cb_b)
                P.op("dve", lambda e, j=j: e.tensor_scalar(out=yb[:], in0=zb[:, 2:2 + TH], scalar1=cw[:, j * 3 + 2:j * 3 + 3], scalar2=None, op0=ALU.mult),
                     reads=["zb", "cwt"], writes=["yb"])
                P.op("dve", lambda e, j=j: e.scalar_tensor_tensor(out=yb[:], in0=zb[:, 1:1 + TH], scalar=cw[:, j * 3 + 1:j * 3 + 2], in1=yb[:], op0=ALU.mult, op1=ALU.add),
                     reads=["zb", "cwt", "yb"], writes=["yb"])
                P.op("dve", lambda e, j=j: e.scalar_tensor_tensor(out=yb[:], in0=zb[:, 0:TH], scalar=cw[:, j * 3:j * 3 + 1], in1=yb[:], op0=ALU.mult, op1=ALU.add),
                     reads=["zb", "cwt", "yb"], writes=["yb"])
                P.op("dve", lambda e, j=j: e.tensor_tensor(out=gT[:, j, :], in0=yb[:], in1=bb[:, 2:2 + TH], op=ALU.mult),
                     reads=["yb", "bb"], writes=["gT"])
            emit_outproj(c, w_out.rearrange("(c p) n -> p c n", p=128), KC, gT, ["gT"], xv, _fm(oT), tg, TH, 1.0, final)
DIL = ((128, 1), (512, 4), (2048, 16))
SEQ = 4096
PI = 3.14159265358979
MAGIC = 12582912.0


def dil_consts():
    invf = np.zeros((128, 1), np.float32)
    fr = (500000.0 ** (-np.arange(0, 32, 2, dtype=np.float32) / 32)).astype(np.float32)
    invf[0:16, 0] = fr; invf[16:32, 0] = fr
    rm = np.zeros((128, 128), np.float32)
    for m in range(16):
        rm[m + 16, m] = -1.0
        rm[m, m + 16] = 1.0
    p = np.arange(128)[:, None]; f = np.arange(128)[None, :]
    mask = np.concatenate([(p >= f), (p <= f)], axis=1).astype(np.float32)
    return invf, rm, mask


def build_dilcore(NH=8):
    nc = _new_nc()
    yT = _din(nc, "yT", [9216, SEQ]); posb = _din(nc, "posb", [128, SEQ], I32)
    invf_d = _din(nc, "invf", [128, 1]); rm_d = _din(nc, "rm", [128, 128]); mask_d = _din(nc, "mask", [128, 256])
    id_d = _din(nc, "ident", [128, 128])
    gq_d = _din(nc, "gq", [128, 3]); gk_d = _din(nc, "gk", [128, 3])
    oT = _dout(nc, "oT", [NH * 128, SEQ])
    with contextlib.ExitStack() as stack:
        P = Prog(nc, stack); c = Ctx(nc, stack, P)
        emit_dilcore(c, yT, posb, invf_d, rm_d, mask_d, id_d, gq_d, gk_d, oT, NH, True)
        P.emit()
    return nc


def emit_dilcore(c, yT, posb, invf_d, rm_d, mask_d, id_d, gq_d, gk_d, oT, NH, final):
    P = c.P
    G = 3
    if True:
        ones32, ones16 = c.const_ones()
        invf = c.sb("invf_t", [128, 1], F32); rm = c.sb("rm_t", [128, 128], F32); mask = c.sb("mask_t", [128, 256], F32)
        ident = c.sb("ident_t", [128, 128], F32)
        gq = c.sb("gq_t", [128, G], F32); gk = c.sb("gk_t", [128, G], F32)
        for t, dd, k in ((invf, invf_d, "invf"), (rm, rm_d, "rm"), (mask, mask_d, "mask"), (gq, gq_d, "gq"), (gk, gk_d, "gk"), (ident, id_d, "ident")):
            P.dma("sp", t[:], dd, writes=[k])
        posi = c.sb("posi", [128, SEQ], I32)
        ang = c.sb("ang", [128, SEQ], F32); tmp = c.sb("tmpa", [128, SEQ], F32); kf = c.sb("kfa", [128, SEQ], F32)
        cosT = c.sb("cosT", [128, SEQ], F32); sinT = c.sb("sinT", [128, SEQ], F32)
        qf = kf
        q16 = c.sb("q16", [128, SEQ], BF16); k16 = c.sb("k16", [128, SEQ], BF16)
        v16 = c.sb("v16", [128, 32 * 128], BF16)
        Uacc = c.sb("Uacc", [128, SEQ], F32); Zacc = ang
        sq = c.sb("d_sq", [128, 512], F32); rs = c.sb("d_rs", [128, 512], F32); t1 = c.sb("d_t1", [128, 512], F32)
        t2 = c.sb("d_t2", [128, 512], F32)
        Ef = [c.sb(f"Ef{i}", [128, 256], F32) for i in range(2)]
        Em = [c.sb(f"Em{i}", [128, 256], BF16) for i in range(2)]
        psn = c.ps("ps_n"); psr = c.ps("ps_a0")
        pss = [c.ps(f"ps_b{i}") for i in range(2)]
        psU = [c.ps(f"ps_c{i}") for i in range(2)]
        psZ = [c.ps(f"ps_d{i}") for i in range(2)]
        C1 = 6.28125
        C2 = 2.0 * PI - C1
        sm_scale = 128.0 ** -0.5

        def table(dst, dkey, shift):
            P.op("dve", lambda e: e.tensor_scalar(out=tmp[:], in0=ang[:], scalar1=float(shift), scalar2=None, op0=ALU.add),
                 reads=["ang"], writes=["tmpa"])
            P.op("dve", lambda e: e.tensor_scalar(out=kf[:], in0=tmp[:], scalar1=1.0 / (2 * PI), scalar2=MAGIC, op0=ALU.mult, op1=ALU.add),
                 reads=["tmpa"], writes=["kfa"])
            P.op("dve", lambda e: e.tensor_scalar(out=kf[:], in0=kf[:], scalar1=-MAGIC, scalar2=None, op0=ALU.add),
                 reads=["kfa"], writes=["kfa"])
            P.op("dve", lambda e: e.scalar_tensor_tensor(out=tmp[:], in0=kf[:], scalar=-C1, in1=tmp[:], op0=ALU.mult, op1=ALU.add),
                 reads=["kfa", "tmpa"], writes=["tmpa"])
            P.op("dve", lambda e: e.scalar_tensor_tensor(out=tmp[:], in0=kf[:], scalar=-C2, in1=tmp[:], op0=ALU.mult, op1=ALU.add),
                 reads=["kfa", "tmpa"], writes=["tmpa"])
            P.op("dve", lambda e: e.tensor_scalar(out=tmp[:], in0=tmp[:], scalar1=3.1415925, scalar2=-3.1415925, op0=ALU.min, op1=ALU.max),
                 reads=["tmpa"], writes=["tmpa"])
            P.op("act", lambda e: e.activation(out=dst[:], in_=tmp[:], func=AF.Sin), reads=["tmpa"], writes=[dkey])

        P.dma("sp", posi[:], posb, writes=["posi"])
        P.op("dve", lambda e: e.tensor_copy(out=ang[:], in_=posi[:]), reads=["posi"], writes=["ang"])
        P.op("dve", lambda e: e.tensor_scalar(out=ang[:], in0=ang[:], scalar1=invf[:, 0:1], scalar2=None, op0=ALU.mult),
             reads=["ang", "invf"], writes=["ang"])
        table(sinT, "sinT", 0.0)
        table(cosT, "cosT", PI / 2)

        def prep(src, g, dl, gt, gkey, dst16, dkey):
            P.dma("sp", qf[:], src, reads=["kfa"], writes=["kfa"])
            dview = dst16[:].rearrange("p (r u) -> p u r", r=dl) if dl > 1 else None
            for t0 in range(0, SEQ, 512):
                sl = slice(t0, t0 + 512)
                P.op("act", lambda e, sl=sl: e.activation(out=sq[:], in_=qf[:, sl], func=AF.Square), reads=["kfa"], writes=["d_sq"])
                P.op("pe", lambda e: e.matmul(psn[:], ones32[:], sq[:], start=True, stop=True), reads=["d_sq", "ones32"], writes=["ps_n"])
                P.op("act", lambda e: e.activation(out=rs[:], in_=psn[:], func=AF.Ln, bias=EPS, scale=1.0 / 128), reads=["ps_n"], writes=["d_rs"])
                P.op("act", lambda e: e.activation(out=rs[:], in_=rs[:], func=AF.Exp, scale=-0.5), reads=["d_rs"], writes=["d_rs"])
                P.op("dve", lambda e, sl=sl: e.scalar_tensor_tensor(out=qf[:, sl], in0=qf[:, sl], scalar=gt[:, g:g + 1], in1=rs[:], op0=ALU.mult, op1=ALU.mult),
                     reads=["kfa", "d_rs", gkey], writes=["kfa"])
                P.op("pe", lambda e, sl=sl: e.matmul(psr[:], rm[:], qf[:, sl], start=True, stop=True), reads=["kfa", "rm"], writes=["ps_a0"])
                P.op("dve", lambda e, sl=sl: e.tensor_tensor(out=t1[:], in0=qf[:, sl], in1=cosT[:, sl], op=ALU.mult), reads=["kfa", "cosT"], writes=["d_t1"])
                P.op("dve", lambda e, sl=sl: e.tensor_tensor(out=t2[:], in0=psr[:], in1=sinT[:, sl], op=ALU.mult), reads=["ps_a0", "sinT"], writes=["d_t2"])
                if dl == 1:
                    P.op("dve", lambda e, sl=sl: e.tensor_tensor(out=dst16[:, sl], in0=t1[:], in1=t2[:], op=ALU.add), reads=["d_t1", "d_t2"], writes=[dkey])
                else:
                    u0 = t0 // dl
                    nu = 512 // dl
                    P.op("dve", lambda e, u0=u0, nu=nu: e.tensor_tensor(
                        out=dview[:, u0:u0 + nu, :], in0=t1[:].rearrange("p (u r) -> p u r", r=dl),
                        in1=t2[:].rearrange("p (u r) -> p u r", r=dl), op=ALU.add), reads=["d_t1", "d_t2"], writes=[dkey])

        def vprep(src, dl):
            nb = SEQ // dl // 128
            P.dma("sp", tmp[:], src, writes=["tmpa"])
            for q0 in range(0, 32, 4):
                pi = c.rot("vps", 2)

                def tr(e, q0=q0, pi=pi):
                    ins = None
                    for s in range(4):
                        q = q0 + s
                        r, bp = q // nb, q % nb
                        ta = bp * 128 * dl + r
                        sl = slice(ta, ta + dl * 127 + 1, dl) if dl > 1 else slice(ta, ta + 128)
                        ins = e.transpose(psU[pi][:, s * 128:(s + 1) * 128], tmp[:, sl], ident[:])
                    return ins
                P.op("pe", tr, reads=["tmpa", "ident"], writes=[f"ps_c{pi}"])
                P.op("act", lambda e, q0=q0, pi=pi: e.copy(out=v16[:, q0 * 128:(q0 + 4) * 128], in_=psU[pi][:]), reads=[f"ps_c{pi}"], writes=["v16"])

        for hl in range(NH):
            for g, (window, dl) in enumerate(DIL):
                nb = SEQ // dl // 128
                r0 = ((0 * 3 + g) * 8 + hl) * 128
                r1 = ((1 * 3 + g) * 8 + hl) * 128
                r2 = ((2 * 3 + g) * 8 + hl) * 128
                prep(yT[r0:r0 + 128, :], g, dl, gq, "gq", q16, "q16")
                prep(yT[r1:r1 + 128, :], g, dl, gk, "gk", k16, "k16")
                vprep(yT[r2:r2 + 128, :], dl)
                for qb in range(0, 32, 2):
                    r, b0 = qb // nb, qb % nb
                    ui = c.rot("psU", 2)
                    for s in range(2):
                        q = qb + s
                        bp = q % nb
                        ei = c.rot("Ef", 2)
                        lo = 0 if bp > 0 else 128
                        qs = slice(q * 128, (q + 1) * 128)

                        def mms(e, ei=ei, q=q, bp=bp, qs=qs):
                            ins = None
                            if bp > 0:
                                ins = e.matmul(pss[ei][:, 0:128], k16[:, (q - 1) * 128:q * 128], q16[:, qs], start=True, stop=True)
                            ins = e.matmul(pss[ei][:, 128:256], k16[:, qs], q16[:, qs], start=True, stop=True)
                            return ins
                        P.op("pe", mms, reads=["k16", "q16"], writes=[f"ps_b{ei}"])
                        P.op("act", lambda e, ei=ei, lo=lo: e.activation(out=Ef[ei][:, lo:256], in_=pss[ei][:, lo:256], func=AF.Exp, scale=sm_scale),
                             reads=[f"ps_b{ei}"], writes=[f"Ef{ei}"])
                        P.op("dve", lambda e, ei=ei, lo=lo: e.tensor_tensor(out=Em[ei][:, lo:256], in0=Ef[ei][:, lo:256], in1=mask[:, lo:256], op=ALU.mult),
                             reads=[f"Ef{ei}", "mask"], writes=[f"Em{ei}"])

                        def mmu(e, ei=ei, q=q, bp=bp, s=s, ui=ui):
                            o = psU[ui][:, s * 128:(s + 1) * 128]
                            if bp > 0:
                                e.matmul(o, v16[:, (q - 1) * 128:q * 128], Em[ei][:, 0:128], start=True, stop=False)
                            return e.matmul(o, v16[:, q * 128:(q + 1) * 128], Em[ei][:, 128:256], start=(bp == 0), stop=True)
                        P.op("pe", mmu, reads=[f"Em{ei}", "v16"], writes=[f"ps_c{ui}"])

                        def mmz(e, ei=ei, bp=bp, s=s, ui=ui):
                            o = psZ[ui][:, s * 128:(s + 1) * 128]
                            if bp > 0:
                                e.matmul(o, ones16[:], Em[ei][:, 0:128], start=True, stop=False)
                            return e.matmul(o, ones16[:], Em[ei][:, 128:256], start=(bp == 0), stop=True)
                        P.op("pe", mmz, reads=[f"Em{ei}", "ones16"], writes=[f"ps_d{ui}"])
                    ta = r + dl * b0 * 128
                    tsl = slice(ta, ta + dl * 255 + 1, dl) if dl > 1 else slice(ta, ta + 256)
                    if g == 0:
                        P.op("dve", lambda e, ui=ui, tsl=tsl: e.tensor_copy(out=Uacc[:, tsl], in_=psU[ui][:, 0:256]), reads=[f"ps_c{ui}"], writes=["Uacc"])
                        P.op("act", lambda e, ui=ui, tsl=tsl: e.copy(out=Zacc[:, tsl], in_=psZ[ui][:, 0:256]), reads=[f"ps_d{ui}"], writes=["ang"])
                    else:
                        P.op("dve", lambda e, ui=ui, tsl=tsl: e.tensor_tensor(out=Uacc[:, tsl], in0=Uacc[:, tsl], in1=psU[ui][:, 0:256], op=ALU.add),
                             reads=[f"ps_c{ui}", "Uacc"], writes=["Uacc"])
                        P.op("dve", lambda e, ui=ui, tsl=tsl: e.tensor_tensor(out=Zacc[:, tsl], in0=Zacc[:, tsl], in1=psZ[ui][:, 0:256], op=ALU.add),
                             reads=[f"ps_d{ui}", "ang"], writes=["ang"])
            P.op("dve", lambda e: e.reciprocal(out=Zacc[:], in_=Zacc[:]), reads=["ang"], writes=["ang"])
            P.op("dve", lambda e: e.tensor_tensor(out=Uacc[:], in0=Uacc[:], in1=Zacc[:], op=ALU.mult), reads=["ang", "Uacc"], writes=["Uacc"])
            P.dma("sp", oT[hl * 128:(hl + 1) * 128, :], Uacc[:], reads=["Uacc"], final=final)


def dil_perm(dl):
    L = SEQ // dl
    return (np.arange(L)[None, :] * dl + np.arange(dl)[:, None]).reshape(-1)
HC = 64
HNC = SEQ // HC


def hgrn_consts():
    cm = np.ones((128, SEQ), np.float32); cm[:, ::HC] = 0.0
    p = np.arange(HC)[:, None]; f = np.arange(HC)[None, :]
    tm = (p <= f).astype(np.float32)
    return cm, tm, np.eye(128, dtype=np.float32)


def build_hgrncore(NH=16, layer=2):
    nc = _new_nc()
    yT = _din(nc, "yT", [8192, SEQ])
    lbl = _din(nc, "lbl", [128, NH * 4]); ng_d = _din(nc, "ng", [HC, 128])
    cm_d = _din(nc, "cm", [128, SEQ]); tm_d = _din(nc, "tm", [HC, HC]); id_d = _din(nc, "ident", [128, 128])
    oT = _dout(nc, "oT", [NH * 128, SEQ])
    with contextlib.ExitStack() as stack:
        P = Prog(nc, stack); c = Ctx(nc, stack, P)
        emit_hgrncore(c, yT, lbl, ng_d, cm_d, tm_d, id_d, oT, NH, layer, True)
        P.emit()
    return nc


def emit_hgrncore(c, yT, lbl, ng_d, cm_d, tm_d, id_d, oT, NH, layer, final):
    P = c.P
    if True:
        cm = c.sb("cm_t", [128, SEQ], F32); tm = c.sb("tm_t", [HC, HC], F32); ident = c.sb("id_t", [128, 128], BF16)
        ident32 = c.sb("id32_t", [128, 128], F32)
        P.dma("sp", ident32[:], id_d, writes=["ident32"])
        gnat = c.sb("gnat", [128, SEQ], F32)
        ofm = c.sb("ofm", [128, 512], F32)
        psG = c.ps("ps_G", [128, 1024], F32)
        ng = c.sb("ng_t", [HC, 128], F32); lb4 = c.sb("lb4", [128, NH * 4], F32)
        P.dma("sp", cm[:], cm_d, writes=["cm"]); P.dma("sp", tm[:], tm_d, writes=["tm"])
        P.dma("pool", ident[:], id_d, writes=["ident"]); P.dma("sp", ng[:], ng_d, writes=["ng"])
        P.dma("sp", lb4[:], lbl, writes=["lb4"])
        lb = c.sb("lb", [128, NH], F32); oml = c.sb("oml", [128, NH], F32); den = c.sb("den", [128, NH], F32)
        P.op("act", lambda e: e.activation(out=lb4[:], in_=lb4[:], func=AF.Exp), reads=["lb4"], writes=["lb4"])
        l3 = lb4[:].rearrange("p (h l) -> p h l", l=4)
        P.op("dve", lambda e: e.tensor_reduce(out=den[:], in_=l3, axis=AX.X, op=ALU.add), reads=["lb4"], writes=["den"])
        P.op("dve", lambda e: e.reciprocal(out=den[:], in_=den[:]), reads=["den"], writes=["den"])
        P.op("dve", lambda e: e.tensor_copy(out=lb[:], in_=l3[:, :, 1]), reads=["lb4"], writes=["lb"])
        for l in range(2, layer + 1):
            P.op("dve", lambda e, l=l: e.tensor_tensor(out=lb[:], in0=lb[:], in1=l3[:, :, l], op=ALU.add), reads=["lb4", "lb"], writes=["lb"])
        P.op("dve", lambda e: e.tensor_tensor(out=lb[:], in0=lb[:], in1=den[:], op=ALU.mult), reads=["lb", "den"], writes=["lb"])
        P.op("dve", lambda e: e.tensor_scalar(out=oml[:], in0=lb[:], scalar1=-1.0, scalar2=1.0, op0=ALU.mult, op1=ALU.add), reads=["lb"], writes=["oml"])

        fb = c.sb("fb", [128, SEQ], F32); A = c.sb("A", [128, SEQ], F32); tmp = c.sb("htmp", [128, SEQ], F32)
        kk = c.sb("kk", [128, SEQ], F32); qf = c.sb("qf", [128, SEQ], F32)
        qd16 = c.sb("qd16", [128, SEQ], BF16); ki16 = c.sb("ki16", [128, SEQ], BF16); ke16 = c.sb("ke16", [128, SEQ], BF16)
        ketok = c.sb("ketok", [HC, HNC * 128], BF16); v16 = c.sb("v16", [HC, HNC * 128], BF16)
        dec = c.sb("dec", [128, HNC], F32)
        S32 = c.sb("S32", [128, 128], F32); S16 = c.sb("S16", [128, 128], BF16)
        att16 = [c.sb(f"att16_{i}", [HC, HC], BF16) for i in range(2)]
        gate = c.sb("gate", [HC, 8 * 128], F32); osb = c.sb("osb", [HC, 8 * 128], F32); sqb = c.sb("sqb", [HC, 8 * 128], F32)
        ss = c.sb("ss", [HC, 8], F32)
        psT = c.ps("ps_T", [128, 1024], BF16)
        psA = [c.ps(f"ps_a{i}") for i in range(2)]
        psO = [c.ps(f"ps_b{i}") for i in range(2)]
        psS = [c.ps("ps_c0"), c.ps("ps_c0")]
        A3 = A[:].rearrange("p (n c) -> p n c", c=HC)
        tmp3 = tmp[:].rearrange("p (n c) -> p n c", c=HC)
        for hl in range(NH):
            P.dma("sp", fb[:], yT[2048 + hl * 128:2048 + (hl + 1) * 128, :], writes=["fb"])
            P.dma("sp", qf[:], yT[hl * 128:(hl + 1) * 128, :], writes=["qf"])
            P.dma("sp", kk[:], yT[4096 + hl * 128:4096 + (hl + 1) * 128, :], writes=["kk"])
            P.dma("sp", gnat[:], yT[6144 + hl * 128:6144 + (hl + 1) * 128, :], writes=["gnat"])
            for n0 in range(0, HNC, 8):
                def trv(e, n0=n0):
                    ins = None
                    for i in range(8):
                        n = n0 + i
                        ins = e.transpose(psG[0:HC, i * 128:(i + 1) * 128], kk[:, n * HC:(n + 1) * HC], ident32[:])
                    return ins
                P.op("pe", trv, reads=["kk", "ident32"], writes=["ps_G"])
                P.op("act", lambda e, n0=n0: e.copy(out=v16[:, n0 * 128:(n0 + 8) * 128], in_=psG[0:HC, :]), reads=["ps_G"], writes=["v16"])
            P.op("act", lambda e: e.activation(out=fb[:], in_=fb[:], func=AF.Exp, scale=-1.0), reads=["fb"], writes=["fb"])
            P.op("dve", lambda e: e.tensor_scalar(out=fb[:], in0=fb[:], scalar1=1.0, scalar2=None, op0=ALU.add), reads=["fb"], writes=["fb"])
            P.op("dve", lambda e: e.reciprocal(out=fb[:], in_=fb[:]), reads=["fb"], writes=["fb"])
            P.op("dve", lambda e, hl=hl: e.tensor_scalar(out=fb[:], in0=fb[:], scalar1=oml[:, hl:hl + 1], scalar2=lb[:, hl:hl + 1], op0=ALU.mult, op1=ALU.add),
                 reads=["fb", "oml", "lb"], writes=["fb"])
            P.op("dve", lambda e: e.tensor_scalar(out=kk[:], in0=fb[:], scalar1=-1.0, scalar2=1.0, op0=ALU.mult, op1=ALU.add), reads=["fb"], writes=["kk"])
            P.op("act", lambda e: e.activation(out=fb[:], in_=fb[:], func=AF.Ln), reads=["fb"], writes=["fb"])
            P.op("dve", lambda e: e.tensor_tensor_scan(out=A[:], data0=cm[:], data1=fb[:], initial=0.0, op0=ALU.mult, op1=ALU.add),
                 reads=["cm", "fb"], writes=["A"])
            P.op("act", lambda e: e.activation(out=tmp[:], in_=A[:], func=AF.Exp), reads=["A"], writes=["htmp"])
            P.op("dve", lambda e: e.tensor_tensor(out=qd16[:], in0=qf[:], in1=tmp[:], op=ALU.mult), reads=["qf", "htmp"], writes=["qd16"])
            P.op("act", lambda e: e.copy(out=dec[:], in_=tmp3[:, :, HC - 1]), reads=["htmp"], writes=["dec"])
            P.op("act", lambda e: e.activation(out=tmp[:], in_=A[:], func=AF.Exp, scale=-1.0), reads=["A", "dec"], writes=["htmp"])
            P.op("dve", lambda e: e.tensor_tensor(out=ki16[:], in0=kk[:], in1=tmp[:], op=ALU.mult), reads=["kk", "htmp"], writes=["ki16"])
            P.op("dve", lambda e: e.tensor_tensor(out=tmp3, in0=A3[:, :, HC - 1:HC].broadcast_to([128, HNC, HC]), in1=A3, op=ALU.subtract),
                 reads=["A", "ki16"], writes=["htmp"])
            P.op("act", lambda e: e.activation(out=tmp[:], in_=tmp[:], func=AF.Exp), reads=["htmp"], writes=["htmp"])
            P.op("dve", lambda e: e.tensor_tensor(out=ke16[:], in0=kk[:], in1=tmp[:], op=ALU.mult), reads=["kk", "htmp"], writes=["ke16"])
            for n0 in range(0, HNC, 8):
                def tr(e, n0=n0):
                    ins = None
                    for i in range(8):
                        n = n0 + i
                        ins = e.transpose(psT[0:HC, i * 128:(i + 1) * 128], ke16[:, n * HC:(n + 1) * HC], ident[:])
                    return ins
                P.op("pe", tr, reads=["ke16", "ident"], writes=["ps_T"])
                P.op("act", lambda e, n0=n0: e.copy(out=ketok[:, n0 * 128:(n0 + 8) * 128], in_=psT[0:HC, :]), reads=["ps_T"], writes=["ketok"])
            P.op("pool", lambda e: e.memset(S32[:], 0.0), writes=["S32"])
            P.op("pool", lambda e: e.memset(S16[:], 0.0), writes=["S16"])
            for n in range(HNC):
                cs = slice(n * HC, (n + 1) * HC)
                vs = slice(n * 128, (n + 1) * 128)
                ai = c.rot("psA", 2)
                j = n % 8
                if j == 0:
                    def trg(e, n=n):
                        ins = None
                        for i in range(8):
                            ins = e.transpose(psG[0:HC, i * 128:(i + 1) * 128], gnat[:, (n + i) * HC:(n + i + 1) * HC], ident32[:])
                        return ins
                    P.op("pe", trg, reads=["gnat", "ident32"], writes=["ps_G"])
                    P.op("act", lambda e: e.activation(out=gate[:], in_=psG[0:HC, :], func=AF.Silu), reads=["ps_G"], writes=["gate"])
                P.op("pe", lambda e, ai=ai, cs=cs: e.matmul(psA[ai][0:HC, 0:HC], ki16[:, cs], qd16[:, cs], start=True, stop=True),
                     reads=["ki16", "qd16"], writes=[f"ps_a{ai}"])
                P.op("dve", lambda e, ai=ai: e.tensor_tensor(out=att16[ai][:], in0=psA[ai][0:HC, 0:HC], in1=tm[:], op=ALU.mult),
                     reads=[f"ps_a{ai}", "tm"], writes=[f"att16_{ai}"])

                def mmo(e, ai=ai, cs=cs, vs=vs):
                    e.matmul(psO[ai][0:HC, 0:128], qd16[:, cs], S16[:], start=True, stop=False)
                    return e.matmul(psO[ai][0:HC, 0:128], att16[ai][:], v16[:, vs], start=False, stop=True)
                P.op("pe", mmo, reads=["qd16", "S16", f"att16_{ai}", "v16"], writes=[f"ps_b{ai}"])
                P.op("act", lambda e, ai=ai, j=j: e.copy(out=osb[:, j * 128:(j + 1) * 128], in_=psO[ai][0:HC, 0:128]), reads=[f"ps_b{ai}"], writes=["osb"])
                P.op("pe", lambda e, ai=ai, vs=vs: e.matmul(psS[ai][:, 0:128], ketok[:, vs], v16[:, vs], start=True, stop=True),
                     reads=["ketok", "v16"], writes=["ps_c0"])
                P.op("dve", lambda e, ai=ai, n=n: e.scalar_tensor_tensor(out=S32[:], in0=S32[:], scalar=dec[:, n:n + 1], in1=psS[ai][:, 0:128], op0=ALU.mult, op1=ALU.add),
                     reads=["S32", "dec", "ps_c0"], writes=["S32"])
                P.op("act", lambda e: e.copy(out=S16[:], in_=S32[:]), reads=["S32"], writes=["S16"])
                if j == 7:
                    o3 = osb[:].rearrange("p (j e) -> p j e", e=128)
                    P.op("dve", lambda e: e.tensor_tensor(out=sqb[:], in0=osb[:], in1=osb[:], op=ALU.mult), reads=["osb"], writes=["sqb"])
                    P.op("dve", lambda e: e.tensor_reduce(out=ss[:], in_=sqb[:].rearrange("p (j e) -> p j e", e=128), axis=AX.X, op=ALU.add),
                         reads=["sqb"], writes=["ss"])
                    P.op("act", lambda e: e.activation(out=ss[:], in_=ss[:], func=AF.Ln, bias=EPS, scale=1.0 / 128), reads=["ss"], writes=["ss"])
                    P.op("act", lambda e: e.activation(out=ss[:], in_=ss[:], func=AF.Exp, scale=-0.5), reads=["ss"], writes=["ss"])
                    P.op("dve", lambda e, o3=o3: e.tensor_tensor(out=o3, in0=o3, in1=ss[:].unsqueeze(2).broadcast_to([HC, 8, 128]), op=ALU.mult),
                         reads=["osb", "ss"], writes=["osb"])
                    P.op("dve", lambda e, o3=o3: e.tensor_tensor(out=o3, in0=o3, in1=ng[:].unsqueeze(1).broadcast_to([HC, 8, 128]), op=ALU.mult),
                         reads=["osb", "ng"], writes=["osb"])
                    P.op("dve", lambda e: e.tensor_tensor(out=osb[:], in0=osb[:], in1=gate[:], op=ALU.mult), reads=["osb", "gate"], writes=["osb"])

                    def tro(e):
                        ins = None
                        for i in range(8):
                            ins = e.transpose(psG[:, i * HC:(i + 1) * HC], osb[:, i * 128:(i + 1) * 128], ident32[0:HC, 0:HC])
                        return ins
                    P.op("pe", tro, reads=["osb", "ident32"], writes=["ps_G"])
                    P.op("act", lambda e: e.copy(out=ofm[:], in_=psG[:, 0:512]), reads=["ps_G"], writes=["ofm"])
                    P.dma("sp", oT[hl * 128:(hl + 1) * 128, (n - 7) * HC:(n + 1) * HC], ofm[:], reads=["ofm"], final=final)
NOSYNC = ('dve', 'pool', 'act')
RW_ROWS = 7
RWM = {0: 0, 2: 1, 3: 2, 5: 3, 6: 4, 7: 5, 8: 6}


def rwkv_consts():
    bones = np.zeros((128, 128), np.float32)
    bones[:64, :64] = 1.0; bones[64:, 64:] = 1.0
    sel = np.zeros((32, 16 * 128), np.float32)
    for t in range(16):
        for j in range(2):
            sel[t * 2 + j, t * 128 + j * 64: t * 128 + (j + 1) * 64] = 1.0
    return bones, sel


def build_rwkvA(T=4096):
    nc = _new_nc()
    xT = _din(nc, "xT", [D, T]); gain = _din(nc, "gain", [128, KC]); mu_d = _din(nc, "mu", [128, 6 * KC])
    wrkv = _din(nc, "wrkv", [3, D, D])
    w0_d = _din(nc, "w0", [128, KC]); w1 = _din(nc, "w1", [D, 96]); w2 = _din(nc, "w2", [96, D])
    a0_d = _din(nc, "a0", [128, KC]); a1 = _din(nc, "a1", [D, 96]); a2 = _din(nc, "a2", [96, D])
    g1 = _din(nc, "g1", [D, 256]); g2 = _din(nc, "g2", [256, D])
    kk_d = _din(nc, "k_k", [128, KC]); ka_d = _din(nc, "k_a", [128, KC]); bones_d = _din(nc, "bones", [128, 128])
    yT = _dout(nc, "yT", [RW_ROWS * D, T])
    with contextlib.ExitStack() as stack:
        P = Prog(nc, stack); c = Ctx(nc, stack, P)
        emit_rwkvA(c, xT, gain, mu_d, wrkv, w0_d, w1, w2, a0_d, a1, a2, g1, g2, kk_d, ka_d, bones_d, yT, T, True)
        P.emit()
    return nc


def emit_rwkvA(c, xT, gain, mu_d, wrkv, w0_d, w1, w2, a0_d, a1, a2, g1, g2, kk_d, ka_d, bones_d, yT, T, final):
    P = c.P
    if True:
        gn = c.sb("gn", [128, KC], F32); mu = c.sb("mu_t", [128, 6 * KC], F32)
        w0 = c.sb("w0_t", [128, KC], F32); a0 = c.sb("a0_t", [128, KC], F32)
        k_k = c.sb("kk_t", [128, KC], F32); k_a = c.sb("ka_t", [128, KC], F32); omka = c.sb("omka", [128, KC], F32)
        bones = c.sb("bones_t", [128, 128], F32)
        for t, d, k in ((gn, gain, "gn"), (mu, mu_d, "mu"), (w0, w0_d, "w0"), (a0, a0_d, "a0"), (k_k, kk_d, "k_k"), (k_a, ka_d, "k_a"), (bones, bones_d, "bones")):
            P.dma("sp", t[:], d, writes=[k])
        P.op("dve", lambda e: e.tensor_scalar(out=omka[:], in0=k_a[:], scalar1=-1.0, scalar2=1.0, op0=ALU.mult, op1=ALU.add), reads=["k_a"], writes=["omka"])
        nw0 = c.sb("nw0", [128, KC], F32); na0 = c.sb("na0", [128, KC], F32); th = c.sb("th", [128, 512], F32)
        P.op("dve", lambda e: e.tensor_scalar(out=nw0[:], in0=w0[:], scalar1=-1.0, scalar2=None, op0=ALU.mult), reads=["w0"], writes=["nw0"])
        P.op("dve", lambda e: e.tensor_scalar(out=na0[:], in0=a0[:], scalar1=-1.0, scalar2=None, op0=ALU.mult), reads=["a0"], writes=["na0"])
        w2b = c.sb("w2b", [96, D], BF16); a2b = c.sb("a2b", [96, D], BF16); g2b = c.sb("g2b", [128, 2, D], BF16)
        P.dma("pool", w2b[:], w2, writes=["w2b"]); P.dma("pool", a2b[:], a2, writes=["a2b"])
        P.dma("pool", g2b[:], g2.rearrange("(c p) n -> p c n", p=128), writes=["g2b"])
        hfp = c.sb("hfp", [128, KC, 513], F32); diff = c.sb("diff", [128, KC, 512], F32)
        mx = [c.sb(f"mx{i}", [128, KC, 512], BF16) for i in range(2)]
        kbuf = c.sb("kbuf", [128, KC, 512], F32)
        t1 = c.sb("t1", [128, 2, 512], BF16)
        ysb = [c.sb(f"ysb{i}", [128, 512], F32) for i in range(2)]
        asb = c.sb("asb", [128, 512], F32); kkr = c.sb("kkr", [128, 512], F32); sq = c.sb("r_sq", [128, 512], F32)
        rn = c.sb("rn", [128, 512], F32)
        psb = [c.ps(f"ps_b{i}") for i in range(2)]
        psn2 = c.ps("ps_c0")
        yv = yT.rearrange("(r kc p) t -> r p kc t", p=128, kc=KC)
        xv = _fm(xT)

        def store(row, j, tg, src_ap, key):
            if row not in RWM:
                return
            P.dma("sp", yv[RWM[row]][:, j, tg:tg + 512], src_ap, reads=[key], final=final)

        for tt in range(T // 512):
            tg = tt * 512
            if tt == 0:
                P.op("pool", lambda e: e.memset(hfp[:, :, 0:1], 0.0), writes=["hfp"])
                emit_norm(c, xv[:, :, 0:512], 512, gn, hfp, 1, "hfp")
            else:
                emit_norm(c, xv[:, :, tg - 1:tg + 512], 513, gn, hfp, 0, "hfp")
            P.op("dve", lambda e: e.tensor_tensor(out=diff[:], in0=hfp[:, :, 0:512], in1=hfp[:, :, 1:513], op=ALU.subtract), reads=["hfp"], writes=["diff"])
            for i in range(6):
                mi = c.rot("mx", 2)
                m = mx[mi]
                for kc in range(KC):
                    P.op("dve", lambda e, kc=kc, i=i, m=m: e.scalar_tensor_tensor(
                        out=m[:, kc, :], in0=diff[:, kc, :], scalar=mu[:, i * KC + kc:i * KC + kc + 1], in1=hfp[:, kc, 1:513],
                        op0=ALU.mult, op1=ALU.add), reads=["diff", "hfp", "mu"], writes=[f"mx{mi}"])
                mk = [f"mx{mi}"]
                if i < 3:
                    def cb(j, t0, n, ps, psk, i=i, tg=tg):
                        if i == 1:
                            P.op("act", lambda e: e.copy(out=kbuf[:, j, :], in_=ps[:, 0:512]), reads=[psk], writes=["kbuf"])
                            store(1, j, tg, kbuf[:, j, :], "kbuf")
                        else:
                            yi = c.rot("ysb", 2)
                            P.op("act", lambda e: e.copy(out=ysb[yi][:], in_=ps[:, 0:512]), reads=[psk], writes=[f"ysb{yi}"])
                            store(i, j, tg, ysb[yi][:], f"ysb{yi}")
                    emit_proj(c, wrkv[i].rearrange("(kc p) n -> p kc n", p=128), 0, KC, m, mk, 512, cb)
                elif i == 3 or i == 4:
                    wl = w1 if i == 3 else a1

                    def cb(j, t0, n, ps, psk, i=i):
                        if i == 3:
                            P.op("act", lambda e: e.activation(out=th[0:96, :], in_=ps[0:96, 0:512], func=AF.Exp, scale=-2.0), reads=[psk], writes=["th"])
                            P.op("dve", lambda e: e.tensor_scalar(out=th[0:96, :], in0=th[0:96, :], scalar1=1.0, scalar2=None, op0=ALU.add), reads=["th"], writes=["th"])
                            P.op("dve", lambda e: e.reciprocal(out=th[0:96, :], in_=th[0:96, :]), reads=["th"], writes=["th"])
                            P.op("dve", lambda e: e.tensor_scalar(out=t1[0:96, 0, :], in0=th[0:96, :], scalar1=2.0, scalar2=-1.0, op0=ALU.mult, op1=ALU.add), reads=["th"], writes=["t1"])
                        else:
                            P.op("act", lambda e: e.copy(out=t1[0:96, 0, :], in_=ps[0:96, 0:512]), reads=[psk], writes=["t1"])
                    emit_proj(c, wl.rearrange("(kc p) n -> p kc n", p=128), 0, 1, m, mk, 512, cb, cw=96)
                    w2x, w2k, bias, bk = (w2b, "w2b", w0, "w0") if i == 3 else (a2b, "a2b", a0, "a0")
                    for j in range(KC):
                        pi = c.rot("ps_b", 2)
                        P.op("pe", lambda e, pi=pi, j=j, w2x=w2x: e.matmul(psb[pi][:], w2x[0:96, j * 128:(j + 1) * 128], t1[0:96, 0, :], start=True, stop=True),
                             reads=["t1", w2k], writes=[f"ps_b{pi}"])
                        if i == 3:
                            yi = c.rot("ysb", 2)
                            P.op("act", lambda e, pi=pi, j=j, yi=yi: e.activation(out=ysb[yi][:], in_=psb[pi][:], func=AF.Exp, bias=nw0[:, j:j + 1], scale=-1.0),
                                 reads=[f"ps_b{pi}", "nw0"], writes=[f"ysb{yi}"])
                            P.op("dve", lambda e, yi=yi: e.tensor_scalar(out=ysb[yi][:], in0=ysb[yi][:], scalar1=1.0, scalar2=None, op0=ALU.add), reads=[f"ysb{yi}"], writes=[f"ysb{yi}"])
                            P.op("dve", lambda e, yi=yi: e.reciprocal(out=ysb[yi][:], in_=ysb[yi][:]), reads=[f"ysb{yi}"], writes=[f"ysb{yi}"])
                            P.op("act", lambda e, yi=yi: e.activation(out=ysb[yi][:], in_=ysb[yi][:], func=AF.Exp, scale=-float(np.exp(-0.5))),
                                 reads=[f"ysb{yi}"], writes=[f"ysb{yi}"])
                            store(3, j, tg, ysb[yi][:], f"ysb{yi}")
                        else:
                            P.op("act", lambda e, pi=pi, j=j: e.activation(out=asb[:], in_=psb[pi][:], func=AF.Exp, bias=na0[:, j:j + 1], scale=-1.0),
                                 reads=[f"ps_b{pi}", "na0"], writes=["asb"])
                            P.op("dve", lambda e: e.tensor_scalar(out=asb[:], in0=asb[:], scalar1=1.0, scalar2=None, op0=ALU.add), reads=["asb"], writes=["asb"])
                            P.op("dve", lambda e: e.reciprocal(out=asb[:], in_=asb[:]), reads=["asb"], writes=["asb"])
                            store(4, j, tg, asb[:], "asb")
                            P.op("dve", lambda e, j=j: e.tensor_scalar(out=kkr[:], in0=kbuf[:, j, :], scalar1=k_k[:, j:j + 1], scalar2=None, op0=ALU.mult),
                                 reads=["kbuf", "k_k"], writes=["kkr"])
                            P.op("act", lambda e: e.activation(out=sq[:], in_=kkr[:], func=AF.Square), reads=["kkr"], writes=["r_sq"])
                            P.op("pe", lambda e: e.matmul(psn2[:], bones[:], sq[:], start=True, stop=True), reads=["r_sq", "bones"], writes=["ps_c0"])
                            P.op("dve", lambda e: e.tensor_scalar(out=rn[:], in0=psn2[:], scalar1=1e-24, scalar2=None, op0=ALU.max), reads=["ps_c0"], writes=["rn"])
                            P.op("act", lambda e: e.activation(out=rn[:], in_=rn[:], func=AF.Ln), reads=["rn"], writes=["rn"])
                            P.op("act", lambda e: e.activation(out=rn[:], in_=rn[:], func=AF.Exp, scale=-0.5), reads=["rn"], writes=["rn"])
                            P.op("dve", lambda e: e.tensor_tensor(out=kkr[:], in0=kkr[:], in1=rn[:], op=ALU.mult), reads=["kkr", "rn"], writes=["kkr"])
                            yi = c.rot("ysb", 2)
                            P.op("dve", lambda e, yi=yi: e.tensor_scalar(out=ysb[yi][:], in0=kkr[:], scalar1=-1.0, scalar2=None, op0=ALU.mult), reads=["kkr"], writes=[f"ysb{yi}"])
                            store(6, j, tg, ysb[yi][:], f"ysb{yi}")
                            yi = c.rot("ysb", 2)
                            P.op("dve", lambda e, yi=yi: e.tensor_tensor(out=ysb[yi][:], in0=kkr[:], in1=asb[:], op=ALU.mult), reads=["kkr", "asb"], writes=[f"ysb{yi}"])
                            store(7, j, tg, ysb[yi][:], f"ysb{yi}")
                            yi = c.rot("ysb", 2)
                            P.op("dve", lambda e, j=j: e.tensor_scalar(out=rn[:], in0=asb[:], scalar1=k_a[:, j:j + 1], scalar2=omka[:, j:j + 1], op0=ALU.mult, op1=ALU.add),
                                 reads=["asb", "k_a", "omka"], writes=["rn"])
                            P.op("dve", lambda e, yi=yi, j=j: e.tensor_tensor(out=ysb[yi][:], in0=kbuf[:, j, :], in1=rn[:], op=ALU.mult), reads=["kbuf", "rn"], writes=[f"ysb{yi}"])
                            store(8, j, tg, ysb[yi][:], f"ysb{yi}")
                else:
                    def cb(j, t0, n, ps, psk):
                        P.op("act", lambda e: e.activation(out=th[:], in_=ps[:, 0:512], func=AF.Exp, scale=-1.0), reads=[psk], writes=["th"])
                        P.op("dve", lambda e: e.tensor_scalar(out=th[:], in0=th[:], scalar1=1.0, scalar2=None, op0=ALU.add), reads=["th"], writes=["th"])
                        P.op("dve", lambda e: e.reciprocal(out=th[:], in_=th[:]), reads=["th"], writes=["th"])
                        P.op("dve", lambda e: e.tensor_copy(out=t1[:, j, :], in_=th[:]), reads=["th"], writes=["t1"])
                    emit_proj(c, g1.rearrange("(kc p) n -> p kc n", p=128), 0, 2, m, mk, 512, cb)
                    for j in range(KC):
                        pi = c.rot("ps_b", 2)

                        def mm(e, pi=pi, j=j):
                            e.matmul(psb[pi][:], g2b[:, 0, j * 128:(j + 1) * 128], t1[:, 0, :], start=True, stop=False)
                            return e.matmul(psb[pi][:], g2b[:, 1, j * 128:(j + 1) * 128], t1[:, 1, :], start=False, stop=True)
                        P.op("pe", mm, reads=["t1", "g2b"], writes=[f"ps_b{pi}"])
                        yi = c.rot("ysb", 2)
                        P.op("act", lambda e, pi=pi, yi=yi: e.copy(out=ysb[yi][:], in_=psb[pi][:]), reads=[f"ps_b{pi}"], writes=[f"ysb{yi}"])
                        store(5, j, tg, ysb[yi][:], f"ysb{yi}")


def build_rwkvB(NSTEP=SEQ, NPASS=2):
    nc = _new_nc()
    yT = _din(nc, "yT", [RW_ROWS * D, NSTEP]); sel_d = _din(nc, "sel", [32, 16 * 128]); id_d = _din(nc, "ident", [128, 128])
    ys = _dout(nc, "ys", [D, NSTEP])
    tm = nc.dram_tensor("rw_tm", [5, NSTEP, 1024], F32, kind="Internal").ap()
    with contextlib.ExitStack() as stack:
        P = Prog(nc, stack); c = Ctx(nc, stack, P)
        emit_rwkvB(c, yT, sel_d, id_d, tm, ys, NSTEP, NPASS, True)
        P.emit()
    return nc


def emit_rwkvB(c, yT, sel_d, id_d, tm, ys, NSTEP, NPASS, final):
    P = c.P
    VB = 256
    ROWS = (RWM[6], RWM[3], RWM[7], RWM[8], RWM[0])
    if True:
        sel = c.sb("sel_t", [32, 16 * 128], F32); ident = c.sb("ident_t", [128, 128], F32)
        P.dma("sp", sel[:], sel_d, writes=["sel"]); P.dma("sp", ident[:], id_d, writes=["ident"])
        opb = [c.sb(f"opb{i}", [32, 5, 512], F32) for i in range(2)]
        ob16 = [c.sb(f"ob16_{i}", [32, 6, 512], BF16) for i in range(2)]
        sel16 = c.sb("sel16_t", [32, 16 * 128], BF16)
        P.dma("pool", sel16[:], sel_d, writes=["sel16"])
        vb = [c.sb(f"vb{i}", [128, 8, VB], F32) for i in range(2)]
        yb = [c.sb(f"yb{i}", [128, 8, VB], F32) for i in range(2)]
        S = c.sb("S", [128, 512], F32)
        tmp = c.sb("s_tmp", [128, 512], F32); tmp2 = c.sb("s_tmp2", [128, 512], F32); tmp3 = c.sb("s_tmp3", [128, 512], F32)
        tmp4 = c.sb("s_tmp4", [128, 512], F32); kc_ = c.sb("s_kc", [128, 512], F32)
        sa = c.sb("s_sa", [128, 8], F32); X = c.sb("s_X", [128, 512], F32); wsb = c.sb("s_w", [128, 512], F32)
        fmb = [c.sb(f"fmb{i}", [128, 8, 128], F32) for i in range(2)]
        tmb = [c.sb(f"tmb{i}", [128, 1024], F32) for i in range(2)]
        NPS = 7
        pss = [c.ps(f"ps_r{i}") for i in range(NPS)]
        r3 = lambda t: t[:].rearrange("p (i k) -> p i k", k=64)
        for hh in range(NPASS):
            for oi, row in enumerate(ROWS):
                src = yT[row * D + hh * 1024: row * D + (hh + 1) * 1024, :].rearrange("(c p) t -> p c t", p=128)
                for tb in range(NSTEP // 128):
                    fi = c.rot("fmb", 2)
                    P.dma("sp", fmb[fi][:], src[:, :, tb * 128:(tb + 1) * 128], writes=[f"fmb{fi}"])
                    for half in range(2):
                        pi = c.rot("ps_r", NPS)

                        def tr(e, fi=fi, half=half, pi=pi):
                            ins = None
                            for q in range(4):
                                ins = e.transpose(pss[pi][:, q * 128:(q + 1) * 128], fmb[fi][:, half * 4 + q, :], ident[:])
                            return ins
                        P.op("pe", tr, reads=[f"fmb{fi}", "ident"], writes=[f"ps_r{pi}"])
                        P.op("act", lambda e, fi=fi, half=half, pi=pi: e.copy(out=tmb[fi][:, half * 512:(half + 1) * 512], in_=pss[pi][:]),
                             reads=[f"ps_r{pi}"], writes=[f"tmb{fi}"])
                    P.dma("sp", tm[oi, tb * 128:(tb + 1) * 128, :], tmb[fi][:], reads=[f"tmb{fi}"], writes=["tm_dram"])
            ov = tm.rearrange("o (k t) (j f) -> k (t j) o f", t=16, j=2)
            P.op("pool", lambda e: e.memset(S[:], 0.0), reads=["S"], writes=["S"])
            P.nosync_self = set(NOSYNC)
            vbase = RWM[2] * D + hh * 1024
            for t in range(NSTEP):
                bi = (t // 16) % 2
                if t % 16 == 0:
                    P.dma("sp", opb[bi][:], ov[t // 16], reads=["tm_dram"], writes=[f"opb{bi}"])
                vi = (t // VB) % 2
                if t % VB == 0:
                    for j in range(2):
                        P.dma("sp", vb[vi][j * 64:(j + 1) * 64, :, :],
                              yT[vbase + j * 512: vbase + (j + 1) * 512, t:t + VB].rearrange("(i v) t -> v i t", v=64), writes=[f"vb{vi}"])
                tl = t % 16
                if tl == 0:
                    P.op("act", lambda e, bi=bi: e.copy(out=ob16[bi][:, 0:5, :], in_=opb[bi][:]), reads=[f"opb{bi}"], writes=[f"ob16_{bi}"])
                    P.op("pool", lambda e, bi=bi: e.tensor_tensor(out=ob16[bi][:, 5, :], in0=opb[bi][:, 1, :], in1=ob16[bi][:, 1, :], op=ALU.subtract),
                         reads=[f"opb{bi}", f"ob16_{bi}"], writes=[f"ob16_{bi}"])
                pk = []
                for o in range(5):
                    pi = c.rot("ps_r", NPS)

                    def bm(e, pi=pi, o=o, bi=bi, tl=tl):
                        if o == 1:
                            e.matmul(pss[pi][:], sel16[:, tl * 128:(tl + 1) * 128], ob16[bi][:, 1, :], start=True, stop=False)
                            return e.matmul(pss[pi][:], sel16[:, tl * 128:(tl + 1) * 128], ob16[bi][:, 5, :], start=False, stop=True)
                        return e.matmul(pss[pi][:], sel16[:, tl * 128:(tl + 1) * 128], ob16[bi][:, o, :], start=True, stop=True)
                    P.op("pe", bm, reads=[f"ob16_{bi}", "sel16"], writes=[f"ps_r{pi}"])
                    pk.append(pi)
                pn, pw, pb, pkk, pr = pk
                tv = t % VB
                P.op("act", lambda e, pkk=pkk: e.copy(out=kc_[:], in_=pss[pkk][:]), reads=[f"ps_r{pkk}"], writes=["s_kc"])
                P.op("act", lambda e, pw=pw: e.copy(out=wsb[:], in_=pss[pw][:]), reads=[f"ps_r{pw}"], writes=["s_w"])
                P.op("pool", lambda e, vi=vi, tv=tv: e.tensor_tensor(out=r3(tmp3), in0=r3(kc_), in1=vb[vi][:, :, tv:tv + 1].broadcast_to([128, 8, 64]), op=ALU.mult),
                     reads=["s_kc", f"vb{vi}"], writes=["s_tmp3"])
                P.op("pool", lambda e: e.tensor_tensor(out=X[:], in0=S[:], in1=wsb[:], op=ALU.mult), reads=["S", "s_w"], writes=["s_X"])
                P.op("pool", lambda e: e.tensor_tensor(out=X[:], in0=X[:], in1=tmp3[:], op=ALU.add), reads=["s_X", "s_tmp3"], writes=["s_X"])
                P.op("dve", lambda e, pn=pn: e.tensor_tensor(out=tmp[:], in0=S[:], in1=pss[pn][:], op=ALU.mult), reads=["S", f"ps_r{pn}"], writes=["s_tmp"])
                P.op("dve", lambda e: e.tensor_reduce(out=sa[:], in_=r3(tmp), axis=AX.X, op=ALU.add), reads=["s_tmp"], writes=["s_sa"])
                P.op("dve", lambda e, pb=pb: e.tensor_tensor(out=r3(tmp2), in0=pss[pb][:].rearrange("p (i k) -> p i k", k=64), in1=sa[:].unsqueeze(2).broadcast_to([128, 8, 64]), op=ALU.mult),
                     reads=["s_sa", f"ps_r{pb}"], writes=["s_tmp2"])
                P.op("dve", lambda e: e.tensor_tensor(out=S[:], in0=X[:], in1=tmp2[:], op=ALU.add), reads=["s_X", "s_tmp2", "s_tmp"], writes=["S"])
                P.op("dve", lambda e, pr=pr: e.tensor_tensor(out=tmp4[:], in0=S[:], in1=pss[pr][:], op=ALU.mult), reads=["S", f"ps_r{pr}"], writes=["s_tmp4"])
                P.op("dve", lambda e, vi=vi, tv=tv: e.tensor_reduce(out=yb[vi][:, :, tv], in_=r3(tmp4), axis=AX.X, op=ALU.add), reads=["s_tmp4"], writes=[f"yb{vi}"])
                if tv == VB - 1:
                    for j in range(2):
                        P.dma("sp", ys[hh * 1024 + j * 512: hh * 1024 + (j + 1) * 512, t - VB + 1:t + 1].rearrange("(i v) t -> v i t", v=64),
                              yb[vi][j * 64:(j + 1) * 64, :, :], reads=[f"yb{vi}"], final=final)
            P.nosync_self = set()
            P.op("pool", lambda e: e.memset(sa[:], 0.0), reads=[f"opb0", f"opb1"], writes=["s_sa", "tm_dram"])


def build_rwkvC(T=4096):
    nc = _new_nc()
    xT = _din(nc, "xT", [D, T]); ysT = _din(nc, "ysT", [D, T]); yT = _din(nc, "yT", [RW_ROWS * D, T])
    lnw_d = _din(nc, "lnw", [128, KC]); lnb_d = _din(nc, "lnb", [128, KC]); rk_d = _din(nc, "r_k", [128, KC])
    bones_d = _din(nc, "bones", [128, 128]); w_out = _din(nc, "w_out", [D, D])
    oT = _dout(nc, "oT", [D, T])
    with contextlib.ExitStack() as stack:
        P = Prog(nc, stack); c = Ctx(nc, stack, P)
        emit_rwkvC(c, xT, ysT, yT, lnw_d, lnb_d, rk_d, bones_d, w_out, oT, T, True)
        P.emit()
    return nc


def emit_rwkvC(c, xT, ysT, yT, lnw_d, lnb_d, rk_d, bones_d, w_out, oT, T, final):
    P = c.P
    if True:
        lnw = c.sb("lnw_t", [128, KC], F32); lnb = c.sb("lnb_t", [128, KC], F32); rk = c.sb("rk_t", [128, KC], F32)
        bones = c.sb("bones_t", [128, 128], F32)
        for t, d, k in ((lnw, lnw_d, "lnw"), (lnb, lnb_d, "lnb"), (rk, rk_d, "rk"), (bones, bones_d, "bones")):
            P.dma("sp", t[:], d, writes=[k])
        TH = 1024
        zT = c.sb("zT", [128, KC, TH], BF16)
        names = ["cy", "cr", "ck", "cv", "cg"]
        tl = {nm: [c.sb(f"{nm}{i}", [128, 512], F32) for i in range(2)] for nm in names}
        yc = c.sb("c_yc", [128, 512], F32); sq = c.sb("c_sq", [128, 512], F32); rs = c.sb("c_rs", [128, 512], F32)
        rkk = c.sb("c_rkk", [128, 512], F32)
        ps1 = c.ps("ps_a0"); ps2 = c.ps("ps_a1"); ps3 = c.ps("ps_b0")
        yv = yT.rearrange("(r kc p) t -> r p kc t", p=128, kc=KC)
        ysv = _fm(ysT)
        for hf in range(T // TH):
            for j in range(KC):
                for t0 in range(0, TH, 512):
                    tg = hf * TH + t0
                    bi = c.rot("cbuf", 2)
                    srcs = {"cy": ysv[:, j, tg:tg + 512], "cr": yv[RWM[0]][:, j, tg:tg + 512], "ck": yv[RWM[8]][:, j, tg:tg + 512],
                            "cv": yv[RWM[2]][:, j, tg:tg + 512], "cg": yv[RWM[5]][:, j, tg:tg + 512]}
                    for nm in names:
                        P.dma("sp", tl[nm][bi][:], srcs[nm], writes=[f"{nm}{bi}"])
                    y, r, km, v, g = (tl[nm][bi] for nm in names)
                    ky, kr, kk_, kv, kg = (f"{nm}{bi}" for nm in names)
                    P.op("pe", lambda e, y=y: e.matmul(ps1[:], bones[:], y[:], start=True, stop=True), reads=[ky, "bones"], writes=["ps_a0"])
                    P.op("dve", lambda e, y=y: e.scalar_tensor_tensor(out=yc[:], in0=ps1[:], scalar=-1.0 / 64, in1=y[:], op0=ALU.mult, op1=ALU.add),
                         reads=["ps_a0", ky], writes=["c_yc"])
                    P.op("act", lambda e: e.activation(out=sq[:], in_=yc[:], func=AF.Square), reads=["c_yc"], writes=["c_sq"])
                    P.op("pe", lambda e: e.matmul(ps2[:], bones[:], sq[:], start=True, stop=True), reads=["c_sq", "bones"], writes=["ps_a1"])
                    P.op("act", lambda e: e.activation(out=rs[:], in_=ps2[:], func=AF.Ln, bias=64e-5, scale=1.0 / 64), reads=["ps_a1"], writes=["c_rs"])
                    P.op("act", lambda e: e.activation(out=rs[:], in_=rs[:], func=AF.Exp, scale=-0.5), reads=["c_rs"], writes=["c_rs"])
                    P.op("dve", lambda e: e.tensor_tensor(out=yc[:], in0=yc[:], in1=rs[:], op=ALU.mult), reads=["c_yc", "c_rs"], writes=["c_yc"])
                    P.op("dve", lambda e, j=j: e.tensor_scalar(out=yc[:], in0=yc[:], scalar1=lnw[:, j:j + 1], scalar2=lnb[:, j:j + 1], op0=ALU.mult, op1=ALU.add),
                         reads=["c_yc", "lnw", "lnb"], writes=["c_yc"])
                    P.op("dve", lambda e, j=j, r=r, km=km: e.scalar_tensor_tensor(out=rkk[:], in0=r[:], scalar=rk[:, j:j + 1], in1=km[:], op0=ALU.mult, op1=ALU.mult),
                         reads=[kr, kk_, "rk"], writes=["c_rkk"])
                    P.op("pe", lambda e: e.matmul(ps3[:], bones[:], rkk[:], start=True, stop=True), reads=["c_rkk", "bones"], writes=["ps_b0"])
                    P.op("dve", lambda e, v=v: e.tensor_tensor(out=rkk[:], in0=ps3[:], in1=v[:], op=ALU.mult), reads=["ps_b0", kv], writes=["c_rkk"])
                    P.op("dve", lambda e: e.tensor_tensor(out=yc[:], in0=yc[:], in1=rkk[:], op=ALU.add), reads=["c_yc", "c_rkk"], writes=["c_yc"])
                    P.op("dve", lambda e, j=j, t0=t0, g=g: e.tensor_tensor(out=zT[:, j, t0:t0 + 512], in0=yc[:], in1=g[:], op=ALU.mult), reads=["c_yc", kg], writes=["zT"])
            emit_outproj(c, w_out.rearrange("(c p) n -> p c n", p=128), KC, zT, ["zT"], _fm(xT), _fm(oT), hf * TH, TH, 1.0, final)


TF = SEQ


def build_fused():
    nc = _new_nc()
    A = {}

    def din(name, shape, dt=F32):
        A[name] = _din(nc, name, shape, dt)
        return A[name]
    xT = din("xT", [D, TF]); memT = din("memT", [D, ML]); posb = din("posb", [128, TF], I32)
    din("ffn_gain", [4, 2, 128, KC]); din("ffn_w_gate", [4, 2, D, FF]); din("ffn_w_up", [4, 2, D, FF]); din("ffn_w_down", [4, 2, FF, D])
    din("mix_gain", [4, 128, KC]); din("xg", [4, 128, KC]); din("mg", [4, 128, KC]); din("xq", [4, 128, 4]); din("xk", [4, 128, 4])
    din("xattn_wq", [4, D, D]); din("xattn_wkv", [4, D, 2 * D]); din("xattn_wo", [4, D, D])
    din("conv_w_in", [1, D, 3 * D]); din("conv_cw", [128, KC * 3]); din("conv_w_out", [1, D, D])
    din("dil_w_qkv", [1, D, 9216]); din("dil_gq", [128, 3]); din("dil_gk", [128, 3]); din("dil_w_out", [1, 1024, D])
    din("hgrn_w_in", [1, D, 4 * D]); din("hgrn_lbl", [128, 64]); din("hgrn_ng", [HC, 128]); din("hgrn_w_out", [1, D, D])
    din("rwkv_mu", [128, 6 * KC]); din("rwkv_w_rkv", [1, 3, D, D])
    for nm in ("w0", "a0", "k_k", "k_a", "lnw", "lnb", "r_k"):
        din("rwkv_" + nm, [128, KC])
    din("rwkv_w1", [1, D, 96]); din("rwkv_w2", [1, 96, D]); din("rwkv_a1", [1, D, 96]); din("rwkv_a2", [1, 96, D])
    din("rwkv_g1", [1, D, 256]); din("rwkv_g2", [1, 256, D]); din("rwkv_w_out", [1, D, D])
    din("c_invf", [128, 1]); din("c_rm", [128, 128]); din("c_mask", [128, 256]); din("c_ident", [128, 128])
    din("c_cm", [128, SEQ]); din("c_tm", [HC, HC]); din("c_bones", [128, 128]); din("c_sel", [32, 16 * 128])
    oT = _dout(nc, "oT", [D, TF])

    def scratch(name, shape):
        return nc.dram_tensor(name, list(shape), F32, kind="Internal").ap()
    xa = scratch("scr_xa", [D, TF]); xb = scratch("scr_xb", [D, TF])
    yTs = scratch("scr_y", [RW_ROWS * D, TF]); yQs = scratch("scr_q", [9216, TF]); sTs = scratch("scr_s", [D, TF]); ysT = scratch("scr_ys", [D, TF])
    rw_tm = scratch("scr_tm", [5, TF, 1024])

    with contextlib.ExitStack() as stack:
        P = Prog(nc, stack)
        state = {"k": 0}

        def stage(fn, last=False):
            k = state["k"]
            state["k"] += 1
            with contextlib.ExitStack() as st:
                c = Ctx(nc, st, P, pfx=f"s{k}_")
                fn(c, st, f"s{k}_")
                P.barrier()
                if last:
                    P.emit()
                else:
                    P.flush()

        cur = xT
        bufs = [xa, xb]
        nb = 0

        def nxt():
            nonlocal nb
            b = bufs[nb % 2]
            nb += 1
            return b

        for i in range(4):
            dst = nxt()
            stage(lambda c, st, pf, cur=cur, dst=dst, i=i: emit_ffn(nc, st, P, cur, A["ffn_gain"][i, 0], A["ffn_w_gate"][i, 0], A["ffn_w_up"][i, 0],
                                                                      A["ffn_w_down"][i, 0], dst, TF, pf, final=False))
            cur = dst
            dst = nxt()
            mg = A["mix_gain"][i]
            if i == 0:
                stage(lambda c, st, pf, cur=cur, dst=dst: emit_conv(c, cur, mg, A["conv_w_in"][0], A["conv_cw"], A["conv_w_out"][0], dst, TF, False))
            elif i == 1:
                stage(lambda c, st, pf, cur=cur: emit_normproj(c, cur, mg, A["dil_w_qkv"][0], yQs, 9216, TF))
                stage(lambda c, st, pf: emit_dilcore(c, yQs, posb, A["c_invf"], A["c_rm"], A["c_mask"], A["c_ident"], A["dil_gq"], A["dil_gk"],
                                                     sTs[0:1024, :], 8, False))
                stage(lambda c, st, pf, cur=cur, dst=dst: emit_outproj_stage(c, cur, sTs[0:1024, :], A["dil_w_out"][0], dst, 8, TF))
            elif i == 2:
                stage(lambda c, st, pf, cur=cur: emit_normproj(c, cur, mg, A["hgrn_w_in"][0], yQs[0:8192, :], 8192, TF))
                stage(lambda c, st, pf: emit_hgrncore(c, yQs[0:8192, :], A["hgrn_lbl"], A["hgrn_ng"], A["c_cm"], A["c_tm"], A["c_ident"], sTs, 16, 2, False))
                stage(lambda c, st, pf, cur=cur, dst=dst: emit_outproj_stage(c, cur, sTs, A["hgrn_w_out"][0], dst, 16, TF))
            else:
                stage(lambda c, st, pf, cur=cur: emit_rwkvA(c, cur, mg, A["rwkv_mu"], A["rwkv_w_rkv"][0], A["rwkv_w0"], A["rwkv_w1"][0], A["rwkv_w2"][0],
                                                            A["rwkv_a0"], A["rwkv_a1"][0], A["rwkv_a2"][0], A["rwkv_g1"][0], A["rwkv_g2"][0],
                                                            A["rwkv_k_k"], A["rwkv_k_a"], A["c_bones"], yTs, TF, False))
                stage(lambda c, st, pf: emit_rwkvB(c, yTs, A["c_sel"], A["c_ident"], rw_tm, ysT, TF, 2, False))
                stage(lambda c, st, pf, cur=cur, dst=dst: emit_rwkvC(c, cur, ysT, yTs, A["rwkv_lnw"], A["rwkv_lnb"], A["rwkv_r_k"], A["c_bones"],
                                                                     A["rwkv_w_out"][0], dst, TF, False))
            cur = dst
            dst = nxt()
            stage(lambda c, st, pf, cur=cur, dst=dst, i=i: emit_xattn(c, cur, memT, A["xg"][i], A["mg"][i], A["xq"][i], A["xk"][i],
                                                                       A["xattn_wq"][i], A["xattn_wkv"][i], A["xattn_wo"][i], dst, TF, False))
            cur = dst
            last = (i == 3)
            dst = oT if last else nxt()
            stage(lambda c, st, pf, cur=cur, dst=dst, i=i, last=last: emit_ffn(nc, st, P, cur, A["ffn_gain"][i, 1], A["ffn_w_gate"][i, 1], A["ffn_w_up"][i, 1],
                                                                                A["ffn_w_down"][i, 1], dst, TF, pf, final=last), last=last)
            cur = dst
    return nc


_NC_CACHE = {}


def _pc(v):
    return np.ascontiguousarray(np.asarray(v, np.float32).reshape(-1, 128).T)


def _c(a):
    return np.ascontiguousarray(a)


def kernel(**inp):
    inp = {k: np.asarray(v) for k, v in inp.items()}
    x = inp["x"]
    B, S, _ = x.shape
    if "fused" not in _NC_CACHE:
        _NC_CACHE["fused"] = build_fused()
    nc = _NC_CACHE["fused"]
    invf, rm, mask = dil_consts()
    cm, tm, ident = hgrn_consts()
    bones, sel = rwkv_consts()
    shared = {
        "ffn_gain": _c(np.stack([np.stack([_pc(inp["ffn_norm"][i, j]) for j in range(2)]) for i in range(4)])),
        "ffn_w_gate": inp["ffn_w_gate"], "ffn_w_up": inp["ffn_w_up"], "ffn_w_down": inp["ffn_w_down"],
        "mix_gain": _c(np.stack([_pc(inp["mix_norm"][i]) for i in range(4)])),
        "xg": _c(np.stack([_pc(inp["xattn_norm"][i]) for i in range(4)])),
        "mg": _c(np.stack([_pc(inp["mem_norm"][i]) for i in range(4)])),
        "xq": _c(np.stack([_pc(inp["xattn_q_gain"][i]) for i in range(4)])),
        "xk": _c(np.stack([_pc(inp["xattn_k_gain"][i]) for i in range(4)])),
        "xattn_wq": inp["xattn_wq"], "xattn_wkv": inp["xattn_wkv"], "xattn_wo": inp["xattn_wo"],
        "conv_w_in": inp["conv_w_in"], "conv_w_out": inp["conv_w_out"],
        "conv_cw": _c(inp["conv_w"][0].T.reshape(16, 128, 3).transpose(1, 0, 2).reshape(128, 48)),
        "dil_w_qkv": inp["dil_w_qkv"], "dil_w_out": inp["dil_w_out"],
        "dil_gq": _c(inp["dil_q_gain"][0].T), "dil_gk": _c(inp["dil_k_gain"][0].T),
        "hgrn_w_in": inp["hgrn_w_in"], "hgrn_w_out": inp["hgrn_w_out"],
        "hgrn_lbl": _c(inp["hgrn_lb_logits"].reshape(4, 16, 128).transpose(2, 1, 0).reshape(128, 64)),
        "hgrn_ng": _c(np.tile(inp["hgrn_norm"][0][None], (HC, 1))),
        "rwkv_mu": _c(np.concatenate([_pc(inp["rwkv_mu"][0][i]) for i in range(6)], axis=1)),
        "rwkv_w_rkv": inp["rwkv_w_rkv"],
        "rwkv_w0": _pc(inp["rwkv_w0"][0]), "rwkv_a0": _pc(inp["rwkv_a0"][0]), "rwkv_k_k": _pc(inp["rwkv_k_k"][0]),
        "rwkv_k_a": _pc(inp["rwkv_k_a"][0]), "rwkv_lnw": _pc(inp["rwkv_ln_w"][0]), "rwkv_lnb": _pc(inp["rwkv_ln_b"][0]),
        "rwkv_r_k": _pc(inp["rwkv_r_k"][0].reshape(-1)),
        "rwkv_w1": inp["rwkv_w1"], "rwkv_w2": inp["rwkv_w2"], "rwkv_a1": inp["rwkv_a1"], "rwkv_a2": inp["rwkv_a2"],
        "rwkv_g1": inp["rwkv_g1"], "rwkv_g2": inp["rwkv_g2"], "rwkv_w_out": inp["rwkv_w_out"],
        "c_invf": invf, "c_rm": rm, "c_mask": mask, "c_ident": ident, "c_cm": cm, "c_tm": tm, "c_bones": bones, "c_sel": sel,
    }
    shared = {k: _c(np.asarray(v, np.float32)) for k, v in shared.items()}
    in_maps = []
    for b in range(B):
        m = dict(shared)
        m["xT"] = _c(x[b].T)
        m["memT"] = _c(inp["mem"][b].T)
        m["posb"] = _c(np.tile(inp["positions"][b][None].astype(np.int32), (128, 1)))
        in_maps.append(m)
    res = run_bass_kernel_spmd(nc, in_maps, core_ids=list(range(B)))
    out = np.empty((B, S, D), np.float32)
    for b in range(B):
        out[b] = res.results[b]["oT"].T
    return out
```

```python
import contextlib
import numpy as np
import concourse.bass as bass
import concourse.mybir as mybir
from concourse.bass_utils import run_bass_kernel_spmd

F32 = mybir.dt.float32
BF16 = mybir.dt.bfloat16
I32 = mybir.dt.int32
AF = mybir.ActivationFunctionType
ALU = mybir.AluOpType
AX = mybir.AxisListType

D = 2048
KC = D // 128
FF = 5632
FC = FF // 128
NCORES = 8
EPS = 1e-6


DEFAULT_NOSYNC = ()


class Prog:
    ENGS = ("pe", "act", "dve", "pool", "sp")
    NRING = 6

    def __init__(self, nc, stack):
        self.nc = nc
        self.stack = stack
        self.streams = {e: [] for e in self.ENGS}
        self.count = {e: 0 for e in self.ENGS}
        self.sems = {}
        for e in ("pe", "act", "dve", "pool"):
            self.sems[e] = stack.enter_context(nc.semaphore("s_" + e))
        self.rings = {}
        self.dma_k = {}
        for q in ("sp", "pool", "act"):
            self.rings[q] = [stack.enter_context(nc.semaphore(f"r_{q}{i}")) for i in range(self.NRING)]
            self.dma_k[q] = 0
        self.seen = {e: {} for e in self.ENGS}
        self.nosync_self = set(DEFAULT_NOSYNC)
        self.res = {}
        self.final_events = []

    def _need(self, eng, ev, waits):
        if ev is None:
            return
        sem, val = ev
        if eng in self.nosync_self and eng in self.sems and sem is self.sems[eng]:
            return
        key = id(sem)
        if self.seen[eng].get(key, 0) >= val:
            return
        self.seen[eng][key] = val
        waits.append((sem, val))

    def _deps(self, eng, reads, writes):
        waits = []
        for r in reads:
            st = self.res.get(r)
            if st is not None:
                self._need(eng, st["w"], waits)
        for w in writes:
            st = self.res.get(w)
            if st is not None:
                self._need(eng, st["w"], waits)
                for ev in st["r"].values():
                    self._need(eng, ev, waits)
        return waits

    def _commit(self, ev, reads, writes):
        for r in reads:
            st = self.res.setdefault(r, {"w": None, "r": {}})
            st["r"][id(ev[0])] = ev
        for w in writes:
            self.res[w] = {"w": ev, "r": {}}

    def op(self, eng, fn, reads=(), writes=()):
        waits = self._deps(eng, reads, writes)
        self.count[eng] += 1
        ev = (self.sems[eng], self.count[eng])
        self.streams[eng].append((waits, fn, (self.sems[eng], 1)))
        self._commit(ev, reads, writes)
        return ev

    def dma(self, q, out, in_, reads=(), writes=(), final=False):
        waits = self._deps(q, reads, writes)
        k = self.dma_k[q]
        self.dma_k[q] += 1
        sem = self.rings[q][k % self.NRING]
        gen = k // self.NRING
        if gen > 0:
            self._need(q, (sem, 16 * gen), waits)
        ev = (sem, 16 * (gen + 1))

        def fn(e, out=out, in_=in_):
            return e.dma_start(out=out, in_=in_, allow_slow_non_contiguous=True)

        self.streams[q].append((waits, fn, (sem, 16)))
        self._commit(ev, reads, writes)
        if final:
            self.final_events.append(ev)
        return ev

    def barrier(self):
        evs = []
        for e in ("pe", "act", "dve", "pool"):
            if self.count[e] > 0:
                evs.append((self.sems[e], self.count[e]))
        for q in ("sp", "pool", "act"):
            k = self.dma_k[q]
            for i in range(self.NRING):
                n = (k - i + self.NRING - 1) // self.NRING if k > i else 0
                if n > 0:
                    evs.append((self.rings[q][i], 16 * n))
        for eng in self.ENGS:
            waits = []
            for ev in evs:
                self._need(eng, ev, waits)
            if waits:
                self.streams[eng].append((waits, None, None))
        self.res = {}

    def emit(self):
        fw = []
        for ev in self.final_events:
            self._need("sp", ev, fw)
        self.streams["sp"].append((fw, None, None))
        self.flush()

    def flush(self):
        nc = self.nc
        with nc.Block() as block:
            def run(eng_obj, name):
                for waits, fn, inc in self.streams[name]:
                    for sem, val in waits:
                        eng_obj.wait_ge(sem, val)
                    if fn is not None:
                        ins = fn(eng_obj)
                        ins.then_inc(inc[0], inc[1])

            @block.tensor
            def _(e):
                run(e, "pe")

            @block.scalar
            def _(e):
                run(e, "act")

            @block.vector
            def _(e):
                run(e, "dve")

            @block.gpsimd
            def _(e):
                run(e, "pool")

            @block.sync
            def _(e):
                run(e, "sp")
        self.streams = {e: [] for e in self.ENGS}


def _sb(nc, stack, name, shape, dt):
    return stack.enter_context(nc.sbuf_tensor("t_" + name, list(shape), dt))


def _ps(nc, stack, name, shape, dt=F32):
    return stack.enter_context(nc.psum_tensor("t_" + name, list(shape), dt))


class Ctx:
    def __init__(self, nc, stack, P, pfx=""):
        self.nc, self.stack, self.P, self.pfx = nc, stack, P, pfx
        self.tiles = {}
        self.cnt = {}
        self.nxb = 1
        self.pwg = 1

    def sb(self, name, shape, dt):
        if name not in self.tiles:
            self.tiles[name] = _sb(self.nc, self.stack, self.pfx + name, shape, dt)
        return self.tiles[name]

    def ps(self, name, shape=(128, 512), dt=F32):
        if name not in self.tiles:
            self.tiles[name] = _ps(self.nc, self.stack, self.pfx + name, shape, dt)
        return self.tiles[name]

    def rot(self, name, n):
        k = self.cnt.get(name, 0)
        self.cnt[name] = k + 1
        return k % n

    def const_ones(self):
        if "ones32" not in self.tiles:
            t = self.sb("ones32", [128, 128], F32)
            self.P.op("pool", lambda e: e.memset(t[:], 1.0), writes=["ones32"])
            tb = self.sb("ones16", [128, 128], BF16)
            self.P.op("pool", lambda e: e.memset(tb[:], 1.0), writes=["ones16"])
        return self.tiles["ones32"], self.tiles["ones16"]


def emit_norm(c, src, ntok, gn, xn, xoff, xn_key, eps=EPS, scale_d=1.0 / D, kcs=KC, sumsq_ones=None, ones_key="ones32", gn_key="gn"):
    P = c.P
    ones32, _ = c.const_ones()
    if sumsq_ones is None:
        sumsq_ones = ones32
    TN = 256
    xins = [c.sb(f"n_xin{i}", [128, KC, TN], F32) for i in range(c.nxb)]
    sqs = [c.sb(f"n_sq{i}", [128, TN], F32) for i in range(2)]
    rstd = c.sb("n_rstd", [128, TN], F32)
    psn = c.ps("ps_n", [128, 512])
    for a in range(0, ntok, TN):
        n = min(TN, ntok - a)
        xb = c.rot("n_xin", c.nxb)
        xin = xins[xb]
        xkey = f"n_xin{xb}"
        P.dma("sp", xin[:, 0:kcs, 0:n], src[:, :, a:a + n], writes=[xkey])
        for kc in range(kcs):
            si = c.rot("n_sq", 2)
            s = sqs[si]
            P.op("act", lambda e, s=s, kc=kc, n=n, xin=xin: e.activation(out=s[:, 0:n], in_=xin[:, kc, 0:n], func=AF.Square),
                 reads=[xkey], writes=[f"n_sq{si}"])
            P.op("pe", lambda e, s=s, kc=kc, n=n: e.matmul(psn[:, 0:n], sumsq_ones[:], s[:, 0:n], start=(kc == 0), stop=(kc == kcs - 1)),
                 reads=[f"n_sq{si}", ones_key], writes=["ps_n"])
        P.op("act", lambda e, n=n: e.activation(out=rstd[:, 0:n], in_=psn[:, 0:n], func=AF.Ln, bias=eps, scale=scale_d),
             reads=["ps_n"], writes=["n_rstd"])
        P.op("act", lambda e, n=n: e.activation(out=rstd[:, 0:n], in_=rstd[:, 0:n], func=AF.Exp, scale=-0.5), reads=["n_rstd"], writes=["n_rstd"])
        for kc in range(kcs):
            P.op("dve", lambda e, kc=kc, a=a, n=n, xin=xin: e.scalar_tensor_tensor(
                out=xn[:, kc, xoff + a:xoff + a + n], in0=xin[:, kc, 0:n], scalar=gn[:, kc:kc + 1],
                in1=rstd[:, 0:n], op0=ALU.mult, op1=ALU.mult),
                reads=[xkey, "n_rstd", gn_key], writes=[xn_key])


def emit_proj(c, wv, col0, nchunks, xn, xn_keys, ntok, cb, kcs=KC, cw=128, toff=0):
    P = c.P
    G = c.pwg if (cw == 128 and nchunks % c.pwg == 0) else 1
    wb = [c.sb(f"p_w{i}", [128, KC, 128 * c.pwg], BF16) for i in range(2)]
    pss = [c.ps(f"ps_a{i}") for i in range(2)]
    for j0 in range(0, nchunks, G):
        b = c.rot("p_w", 2)
        P.dma("pool", wb[b][:, 0:kcs, 0:cw * G], wv[:, :, col0 + j0 * cw: col0 + (j0 + G) * cw], writes=[f"p_w{b}"])
        for jj in range(G):
            j = j0 + jj
            for t0 in range(0, ntok, 512):
                n = min(512, ntok - t0)
                pi = c.rot("ps_a", 2)
                ps = pss[pi]

                def mm(e, b=b, ps=ps, t0=t0, n=n, jj=jj):
                    ins = None
                    for kc in range(kcs):
                        ins = e.matmul(ps[0:cw, 0:n], wb[b][:, kc, jj * cw:(jj + 1) * cw], xn[:, kc, toff + t0:toff + t0 + n],
                                       start=(kc == 0), stop=(kc == kcs - 1))
                    return ins
                P.op("pe", mm, reads=[f"p_w{b}"] + list(xn_keys), writes=[f"ps_a{pi}"])
                cb(j, t0, n, ps, f"ps_a{pi}")


def emit_outproj(c, wv, cc, src, src_keys, xTv, oTv, t0g, ntok, scale, final, soff=0):
    P = c.P
    wb = [c.sb(f"o_w{cc}_{i}", [128, cc, 128], BF16) for i in range(2)]
    pss = [c.ps(f"ps_c{i}") for i in range(2)]
    xres = [c.sb(f"o_xres{i}", [128, 512], F32) for i in range(2)]
    osb = [c.sb(f"o_osb{i}", [128, 512], F32) for i in range(2)]
    for nn in range(KC):
        b = c.rot("o_w", 2)
        P.dma("pool", wb[b][:, 0:cc, :], wv[:, :, nn * 128:(nn + 1) * 128], writes=[f"o_w{cc}_{b}"])
        for t0 in range(0, ntok, 512):
            n = min(512, ntok - t0)
            pb = c.rot("ps_c", 2)
            P.dma("sp", xres[pb][:, 0:n], xTv[:, nn, t0g + t0:t0g + t0 + n], writes=[f"o_xres{pb}"])

            def mm(e, b=b, pb=pb, t0=t0, n=n):
                ins = None
                for f in range(cc):
                    ins = e.matmul(pss[pb][:, 0:n], wb[b][:, f, :], src[:, f, soff + t0:soff + t0 + n],
                                   start=(f == 0), stop=(f == cc - 1))
                return ins
            P.op("pe", mm, reads=[f"o_w{cc}_{b}"] + list(src_keys), writes=[f"ps_c{pb}"])
            P.op("dve", lambda e, pb=pb, n=n: e.scalar_tensor_tensor(
                out=osb[pb][:, 0:n], in0=pss[pb][:, 0:n], scalar=float(scale), in1=xres[pb][:, 0:n],
                op0=ALU.mult, op1=ALU.add),
                reads=[f"ps_c{pb}", f"o_xres{pb}"], writes=[f"o_osb{pb}"])
            P.dma("sp", oTv[:, nn, t0g + t0:t0g + t0 + n], osb[pb][:, 0:n], reads=[f"o_osb{pb}"], final=final)


def _fm(ap):
    return ap.rearrange("(kc p) t -> p kc t", p=128)


def _new_nc():
    return bass.Bass("TRN2", target_bir_lowering=False)


def _din(nc, name, shape, dt=F32):
    return nc.dram_tensor(name, list(shape), dt, kind="ExternalInput").ap()


def _dout(nc, name, shape, dt=F32):
    return nc.dram_tensor(name, list(shape), dt, kind="ExternalOutput").ap()


def emit_normproj(c, xT, gain, w, yT, N, T, final=False):
    P = c.P
    c.nxb, c.pwg = 2, 2
    gn = c.sb("gn", [128, KC], F32)
    P.dma("sp", gn[:], gain, writes=["gn"])
    TB = min(T, 2048)
    xn = c.sb("xn", [128, KC, TB], BF16)
    ysb = [c.sb(f"ysb{i}", [128, 512], F32) for i in range(2)]
    yv = _fm(yT)
    for tb in range(0, T, TB):
        emit_norm(c, _fm(xT)[:, :, tb:tb + TB], TB, gn, xn, 0, "xn")

        def cb(j, t0, n, ps, psk, tb=tb):
            i = c.rot("ysb", 2)
            P.op("act", lambda e: e.copy(out=ysb[i][:, 0:n], in_=ps[:, 0:n]), reads=[psk], writes=[f"ysb{i}"])
            P.dma("sp", yv[:, j, tb + t0:tb + t0 + n], ysb[i][:, 0:n], reads=[f"ysb{i}"], final=final)
        emit_proj(c, _fm(w), 0, N // 128, xn, ["xn"], TB, cb)


def build_normproj(N, T=2048):
    nc = _new_nc()
    xT = _din(nc, "xT", [D, T]); gain = _din(nc, "gain", [128, KC]); w = _din(nc, "w", [D, N])
    yT = _dout(nc, "yT", [N, T])
    with contextlib.ExitStack() as stack:
        P = Prog(nc, stack); c = Ctx(nc, stack, P)
        emit_normproj(c, xT, gain, w, yT, N, T, final=True)
        P.emit()
    return nc


def emit_outproj_stage(c, xT, sT, w, oT, CC, T, final=False):
    P = c.P
    TB = min(T, 2048)
    src = c.sb("src", [128, CC, TB], BF16)
    sv = sT.rearrange("(c p) t -> p c t", p=128)
    for tb in range(0, T, TB):
        for cc in range(CC):
            P.dma("pool", src[:, cc, :], sv[:, cc, tb:tb + TB], writes=["src"])
        emit_outproj(c, w.rearrange("(c p) n -> p c n", p=128), CC, src, ["src"], _fm(xT), _fm(oT), tb, TB, 1.0, final)


def build_outproj(CC, T=2048):
    nc = _new_nc()
    xT = _din(nc, "xT", [D, T]); sT = _din(nc, "sT", [CC * 128, T]); w = _din(nc, "w", [CC * 128, D])
    oT = _dout(nc, "oT", [D, T])
    with contextlib.ExitStack() as stack:
        P = Prog(nc, stack); c = Ctx(nc, stack, P)
        emit_outproj_stage(c, xT, sT, w, oT, CC, T, final=True)
        P.emit()
    return nc
def build_ffn(T=2048):
    nc = bass.Bass("TRN2", target_bir_lowering=False)
    xT = nc.dram_tensor("xT", [D, T], F32, kind="ExternalInput").ap()
    gain = nc.dram_tensor("gain", [128, KC], F32, kind="ExternalInput").ap()
    wg = nc.dram_tensor("wg", [D, FF], F32, kind="ExternalInput").ap()
    wu = nc.dram_tensor("wu", [D, FF], F32, kind="ExternalInput").ap()
    wd = nc.dram_tensor("wd", [FF, D], F32, kind="ExternalInput").ap()
    oT = nc.dram_tensor("oT", [D, T], F32, kind="ExternalOutput").ap()
    with contextlib.ExitStack() as stack:
        P = Prog(nc, stack)
        emit_ffn(nc, stack, P, xT, gain, wg, wu, wd, oT, T, "f")
        P.emit()
    return nc


def emit_ffn(nc, stack, P, xT, gain, wg, wu, wd, oT, T, pfx, final=True):
    TH = 1024
    NH = T // TH
    TN = 256
    TT = 512
    FG = 2
    xTv = xT.rearrange("(kc p) t -> p kc t", p=128)
    oTv = oT.rearrange("(kc p) t -> p kc t", p=128)
    wgv = wg.rearrange("(kc p) f -> p kc f", p=128)
    wuv = wu.rearrange("(kc p) f -> p kc f", p=128)
    wdv = wd.rearrange("(fc p) n -> p fc n", p=128)

    act = _sb(nc, stack, pfx + "act", [128, FC, TH], BF16)
    xn = _sb(nc, stack, pfx + "xn", [128, KC, TH], BF16)
    wgb = [_sb(nc, stack, pfx + f"wg{i}", [128, KC, FG * 128], BF16) for i in range(2)]
    wub = [_sb(nc, stack, pfx + f"wu{i}", [128, KC, FG * 128], BF16) for i in range(2)]
    wdb = [_sb(nc, stack, pfx + f"wd{i}", [128, FC, 128], BF16) for i in range(2)]
    xin = _sb(nc, stack, pfx + "xin", [128, KC, TN], F32)
    sq = [_sb(nc, stack, pfx + f"sq{i}", [128, TN], F32) for i in range(2)]
    rstd = _sb(nc, stack, pfx + "rstd", [128, TN], F32)
    ones = _sb(nc, stack, pfx + "ones", [128, 128], F32)
    gn = _sb(nc, stack, pfx + "gn", [128, KC], F32)
    sil = [_sb(nc, stack, pfx + f"sil{i}", [128, TT], F32) for i in range(2)]
    xres = [_sb(nc, stack, pfx + f"xres{i}", [128, TT], F32) for i in range(2)]
    osb = [_sb(nc, stack, pfx + f"osb{i}", [128, TT], F32) for i in range(2)]
    ps_n = _ps(nc, stack, pfx + "psn", [128, TN])
    ps_g = [_ps(nc, stack, pfx + f"psg{i}", [128, TT]) for i in range(2)]
    ps_u = [_ps(nc, stack, pfx + f"psu{i}", [128, TT]) for i in range(2)]
    ps_o = [_ps(nc, stack, pfx + f"pso{i}", [128, TT]) for i in range(2)]

    K = lambda *a: (pfx,) + a
    P.op("pool", lambda e: e.memset(ones[:], 1.0), writes=[K("ones")])
    P.dma("sp", gn[:], gain, writes=[K("gn")])

    cnt = {"gi": 0, "di": 0, "ei": 0, "si": 0}
    NTT = TH // TN
    xn_keys = [K("xn", nt) for nt in range(NTT)]
    act_keys = [K("act", f) for f in range(FC)]

    def norm_tile(h, nt):
        ta = h * TH + nt * TN
        P.dma("sp", xin[:], xTv[:, :, ta:ta + TN], writes=[K("xin")])
        for kc in range(KC):
            si = cnt["si"]
            s = sq[si % 2]
            P.op("act", lambda e, s=s, kc=kc: e.activation(out=s[:], in_=xin[:, kc, :], func=AF.Square),
                 reads=[K("xin")], writes=[K("sq", si % 2)])
            P.op("pe", lambda e, s=s, kc=kc: e.matmul(ps_n[:], ones[:], s[:], start=(kc == 0), stop=(kc == KC - 1)),
                 reads=[K("sq", si % 2), K("ones")], writes=[K("psn")])
            cnt["si"] += 1
        P.op("act", lambda e: e.activation(out=rstd[:], in_=ps_n[:], func=AF.Ln, bias=EPS, scale=1.0 / D),
             reads=[K("psn")], writes=[K("rstd")])
        P.op("act", lambda e: e.activation(out=rstd[:], in_=rstd[:], func=AF.Exp, scale=-0.5), reads=[K("rstd")], writes=[K("rstd")])
        for kc in range(KC):
            P.op("dve", lambda e, kc=kc, nt=nt: e.scalar_tensor_tensor(
                out=xn[:, kc, nt * TN:(nt + 1) * TN], in0=xin[:, kc, :], scalar=gn[:, kc:kc + 1],
                in1=rstd[:], op0=ALU.mult, op1=ALU.mult),
                reads=[K("xin"), K("rstd"), K("gn")], writes=[K("xn", nt)])

    def gate_up(h):
        for fg in range(FC // FG):
            b = cnt["gi"] % 2
            f0 = fg * FG * 128
            P.dma("pool", wgb[b][:], wgv[:, :, f0:f0 + FG * 128], writes=[K("wg", b)])
            P.dma("pool", wub[b][:], wuv[:, :, f0:f0 + FG * 128], writes=[K("wu", b)])
            for fc in range(FG):
                f = fg * FG + fc
                for tt in range(TH // TT):
                    pb = cnt["ei"] % 2

                    def mm(e, wt, pt, fc=fc, tt=tt):
                        ins = None
                        for kc in range(KC):
                            ins = e.matmul(pt[:], wt[:, kc, fc * 128:(fc + 1) * 128],
                                           xn[:, kc, tt * TT:(tt + 1) * TT],
                                           start=(kc == 0), stop=(kc == KC - 1))
                        return ins
                    P.op("pe", lambda e, b=b, pb=pb, mm=mm: mm(e, wgb[b], ps_g[pb]),
                         reads=[K("wg", b)] + xn_keys, writes=[K("psg", pb)])
                    P.op("pe", lambda e, b=b, pb=pb, mm=mm: mm(e, wub[b], ps_u[pb]),
                         reads=[K("wu", b)] + xn_keys, writes=[K("psu", pb)])
                    P.op("act", lambda e, pb=pb: e.activation(out=sil[pb][:], in_=ps_g[pb][:], func=AF.Silu),
                         reads=[K("psg", pb)], writes=[K("sil", pb)])
                    P.op("dve", lambda e, pb=pb, f=f, tt=tt: e.tensor_tensor(
                        out=act[:, f, tt * TT:(tt + 1) * TT], in0=sil[pb][:], in1=ps_u[pb][:], op=ALU.mult),
                        reads=[K("sil", pb), K("psu", pb)], writes=[K("act", f)])
                    cnt["ei"] += 1
            cnt["gi"] += 1

    def down_chunk(h, n):
        t0 = h * TH
        b = cnt["di"] % 2
        P.dma("pool", wdb[b][:], wdv[:, :, n * 128:(n + 1) * 128], writes=[K("wd", b)])
        for tt in range(TH // TT):
            pb = cnt["ei"] % 2
            ta = t0 + tt * TT
            P.dma("sp", xres[pb][:], xTv[:, n, ta:ta + TT], writes=[K("xres", pb)])

            def mmd(e, b=b, pb=pb, tt=tt):
                ins = None
                for f in range(FC):
                    ins = e.matmul(ps_o[pb][:], wdb[b][:, f, :], act[:, f, tt * TT:(tt + 1) * TT],
                                   start=(f == 0), stop=(f == FC - 1))
                return ins
            P.op("pe", mmd, reads=[K("wd", b)] + act_keys, writes=[K("pso", pb)])
            P.op("dve", lambda e, pb=pb: e.scalar_tensor_tensor(
                out=osb[pb][:], in0=ps_o[pb][:], scalar=0.5, in1=xres[pb][:], op0=ALU.mult, op1=ALU.add),
                reads=[K("pso", pb), K("xres", pb)], writes=[K("osb", pb)])
            P.dma("sp", oTv[:, n, ta:ta + TT], osb[pb][:], reads=[K("osb", pb)], final=final)
            cnt["ei"] += 1
        cnt["di"] += 1

    for nt in range(NTT):
        norm_tile(0, nt)
    for h in range(NH):
        gate_up(h)
        per = KC // NTT
        for n in range(KC):
            down_chunk(h, n)
            if h + 1 < NH and n % per == per - 1:
                norm_tile(h + 1, n // per)
XH, XD, ML = 4, 512, 256


def build_xattn(T=2048):
    nc = _new_nc()
    xT = _din(nc, "xT", [D, T]); memT = _din(nc, "memT", [D, ML])
    gx = _din(nc, "gx", [128, KC]); gm = _din(nc, "gm", [128, KC])
    gq = _din(nc, "gq", [128, 4]); gk = _din(nc, "gk", [128, 4])
    wq = _din(nc, "wq", [D, D]); wkv = _din(nc, "wkv", [D, 2 * D]); wo = _din(nc, "wo", [D, D])
    oT = _dout(nc, "oT", [D, T])
    with contextlib.ExitStack() as stack:
        P = Prog(nc, stack); c = Ctx(nc, stack, P)
        emit_xattn(c, xT, memT, gx, gm, gq, gk, wq, wkv, wo, oT, T, True)
        P.emit()
    return nc


def emit_xattn(c, xT, memT, gx, gm, gq, gk, wq, wkv, wo, oT, T, final):
    P = c.P
    c.pwg = 2
    if True:
        ones32, ones16 = c.const_ones()
        gxt = c.sb("gx", [128, KC], F32); gmt = c.sb("gm", [128, KC], F32)
        gqt = c.sb("gq", [128, 4], F32); gkt = c.sb("gk", [128, 4], F32)
        P.dma("sp", gxt[:], gx, writes=["gx"]); P.dma("sp", gmt[:], gm, writes=["gm"])
        P.dma("sp", gqt[:], gq, writes=["gq"]); P.dma("sp", gkt[:], gk, writes=["gk"])
        memn = c.sb("memn", [128, KC, ML], BF16)
        emit_norm(c, _fm(memT), ML, gmt, memn, 0, "memn", gn_key="gm")
        kT = c.sb("kT", [128, KC, ML], BF16)
        kraw = c.sb("kraw", [128, 4, 512], F32)
        sq = [c.sb(f"x_sq{i}", [128, 512], F32) for i in range(2)]
        rs = c.sb("x_rs", [128, 512], F32)
        psn = c.ps("ps_n")
        scale_h = 1.0 / XD

        def headnorm(raw, rawkey, n, gt, gkey, dst_fn, dstkey):
            for dc in range(4):
                si = c.rot("x_sq", 2)
                P.op("act", lambda e, si=si, dc=dc: e.activation(out=sq[si][:, 0:n], in_=raw[:, dc, 0:n], func=AF.Square),
                     reads=[rawkey], writes=[f"x_sq{si}"])
                P.op("pe", lambda e, si=si, dc=dc: e.matmul(psn[:, 0:n], ones32[:], sq[si][:, 0:n], start=(dc == 0), stop=(dc == 3)),
                     reads=[f"x_sq{si}", "ones32"], writes=["ps_n"])
            P.op("act", lambda e: e.activation(out=rs[:, 0:n], in_=psn[:, 0:n], func=AF.Ln, bias=EPS, scale=scale_h),
                 reads=["ps_n"], writes=["x_rs"])
            P.op("act", lambda e: e.activation(out=rs[:, 0:n], in_=rs[:, 0:n], func=AF.Exp, scale=-0.5), reads=["x_rs"], writes=["x_rs"])
            for dc in range(4):
                P.op("dve", lambda e, dc=dc: e.scalar_tensor_tensor(
                    out=dst_fn(dc), in0=raw[:, dc, 0:n], scalar=gt[:, dc:dc + 1], in1=rs[:, 0:n],
                    op0=ALU.mult, op1=ALU.mult), reads=[rawkey, "x_rs", gkey], writes=[dstkey])

        def cb_k(j, t0, n, ps, psk):
            dc = j % 4
            P.op("act", lambda e: e.copy(out=kraw[:, dc, 0:n], in_=ps[:, 0:n]), reads=[psk], writes=["kraw"])
            if dc == 3:
                h = j // 4
                headnorm(kraw, "kraw", ML, gkt, "gk", lambda dc2: kT[:, h * 4 + dc2, :], "kT")
        emit_proj(c, _fm(wkv), 0, KC, memn, ["memn"], ML, cb_k)
        v_sb = c.sb("v_sb", [128, 2, D], BF16)
        wvb = [c.sb(f"x_wv{i}", [128, KC, 256], BF16) for i in range(2)]
        wkvv = _fm(wkv)
        psb = [c.ps(f"ps_b{i}") for i in range(2)]
        for ct in range(8):
            b = c.rot("x_wv", 2)
            P.dma("pool", wvb[b][:], wkvv[:, :, D + ct * 256: D + (ct + 1) * 256], writes=[f"x_wv{b}"])
            for mc in range(2):
                pi = c.rot("ps_b", 2)

                def mm(e, b=b, pi=pi, mc=mc):
                    ins = None
                    for kc in range(KC):
                        ins = e.matmul(psb[pi][:, 0:256], memn[:, kc, mc * 128:(mc + 1) * 128], wvb[b][:, kc, :],
                                       start=(kc == 0), stop=(kc == KC - 1))
                    return ins
                P.op("pe", mm, reads=[f"x_wv{b}", "memn"], writes=[f"ps_b{pi}"])
                P.op("act", lambda e, pi=pi, mc=mc, ct=ct: e.copy(out=v_sb[:, mc, ct * 256:(ct + 1) * 256], in_=psb[pi][:, 0:256]),
                     reads=[f"ps_b{pi}"], writes=["v_sb"])
        TH = 1024
        xn = c.sb("xn", [128, KC, TH], BF16)
        oall = c.sb("oall", [128, KC, TH], BF16)
        qraw = c.sb("qraw", [128, 4, 1024], F32)
        qns = [c.sb(f"qn{i}", [128, 4, 512], BF16) for i in range(2)]
        Es = [c.sb(f"E{i}", [128, 2, 512], BF16) for i in range(2)]
        rdens = [c.sb(f"rden{i}", [128, 512], F32) for i in range(2)]
        psc = [c.ps(f"ps_c{i}") for i in range(2)]
        sm_scale = float(XD) ** -0.5
        for hf in range(T // TH):
            tg = hf * TH
            emit_norm(c, _fm(xT)[:, :, tg:tg + TH], TH, gxt, xn, 0, "xn", gn_key="gx")

            def attn(h, t0, n):
                bq = (t0 // 512) % 2
                qn, E, rden = qns[bq], Es[bq], rdens[bq]
                kq, kE, kr = f"qn{bq}", f"E{bq}", f"rden{bq}"
                for mc in range(2):
                    pi = c.rot("ps_b", 2)

                    def mm(e, pi=pi, mc=mc):
                        ins = None
                        for dc in range(4):
                            ins = e.matmul(psb[pi][:, 0:n], kT[:, h * 4 + dc, mc * 128:(mc + 1) * 128], qn[:, dc, 0:n],
                                           start=(dc == 0), stop=(dc == 3))
                        return ins
                    P.op("pe", mm, reads=["kT", kq], writes=[f"ps_b{pi}"])
                    P.op("act", lambda e, pi=pi, mc=mc: e.activation(out=E[:, mc, 0:n], in_=psb[pi][:, 0:n], func=AF.Exp, scale=sm_scale),
                         reads=[f"ps_b{pi}"], writes=[kE])

                def mmz(e):
                    ins = None
                    for mc in range(2):
                        ins = e.matmul(psn[:, 0:n], ones16[:], E[:, mc, 0:n], start=(mc == 0), stop=(mc == 1))
                    return ins
                P.op("pe", mmz, reads=[kE, "ones16"], writes=["ps_n"])
                P.op("dve", lambda e: e.reciprocal(out=rden[:, 0:n], in_=psn[:, 0:n]), reads=["ps_n"], writes=[kr])
                for dc in range(4):
                    pi = c.rot("ps_c", 2)

                    def mmo(e, pi=pi, dc=dc):
                        ins = None
                        for mc in range(2):
                            ins = e.matmul(psc[pi][:, 0:n], v_sb[:, mc, h * XD + dc * 128: h * XD + (dc + 1) * 128], E[:, mc, 0:n],
                                           start=(mc == 0), stop=(mc == 1))
                        return ins
                    P.op("pe", mmo, reads=[kE, "v_sb"], writes=[f"ps_c{pi}"])
                    P.op("dve", lambda e, pi=pi, dc=dc: e.tensor_tensor(
                        out=oall[:, h * 4 + dc, t0:t0 + n], in0=psc[pi][:, 0:n], in1=rden[:, 0:n], op=ALU.mult),
                        reads=[f"ps_c{pi}", kr], writes=["oall"])

            def cb_q(h, dc, t0, n, ps, psk):
                P.op("act", lambda e: e.copy(out=qraw[:, dc, t0:t0 + n], in_=ps[:, 0:n]), reads=[psk], writes=[f"qraw{t0}"])
            for h in range(XH):
                def cb2(j, t0, n, ps, psk, h=h):
                    cb_q(h, j, t0, n, ps, psk)
                emit_proj(c, _fm(wq), h * XD, 4, xn, ["xn"], TH, cb2)
                for t0 in range(0, TH, 512):
                    bq = (t0 // 512) % 2
                    headnorm(qraw[:, :, t0:t0 + 512], f"qraw{t0}", 512, gqt, "gq", lambda dc2, bq=bq: qns[bq][:, dc2, 0:512], f"qn{bq}")
                for t0 in range(0, TH, 512):
                    attn(h, t0, 512)
            emit_outproj(c, wo.rearrange("(c p) n -> p c n", p=128), KC, oall, ["oall"], _fm(xT), _fm(oT), tg, TH, 1.0, final)
def build_conv(T=4096):
    nc = _new_nc()
    xT = _din(nc, "xT", [D, T]); gain = _din(nc, "gain", [128, KC])
    w_in = _din(nc, "w_in", [D, 3 * D]); cwd = _din(nc, "cw", [128, KC * 3]); w_out = _din(nc, "w_out", [D, D])
    oT = _dout(nc, "oT", [D, T])
    with contextlib.ExitStack() as stack:
        P = Prog(nc, stack); c = Ctx(nc, stack, P)
        emit_conv(c, xT, gain, w_in, cwd, w_out, oT, T, True)
        P.emit()
    return nc


def emit_conv(c, xT, gain, w_in, cwd, w_out, oT, T, final):
    P = c.P
    c.nxb, c.pwg = 2, 1
    if True:
        gn = c.sb("gn", [128, KC], F32); cw = c.sb("cwt", [128, KC * 3], F32)
        P.dma("sp", gn[:], gain, writes=["gn"]); P.dma("sp", cw[:], cwd, writes=["cwt"])
        TH = 1024
        NE = TH + 2
        xn = c.sb("xn", [128, KC, NE], BF16)
        gT = c.sb("gT", [128, KC, TH], BF16)
        cgs = c.sb("cgs", [128, NE], F32); zb = c.sb("zb", [128, NE], F32)
        bb = c.sb("bb", [128, NE], F32); yb = c.sb("yb", [128, TH], F32)
        xv = _fm(xT)
        for hf in range(T // TH):
            tg = hf * TH
            if hf == 0:
                P.op("pool", lambda e: e.memset(xn[:, :, 0:2], 0.0), writes=["xn"])
                emit_norm(c, xv[:, :, 0:TH], TH, gn, xn, 2, "xn")
            else:
                emit_norm(c, xv[:, :, tg - 2:tg + TH], NE, gn, xn, 0, "xn")
            for j in range(KC):
                def cb_cg(_, t0, n, ps, psk):
                    P.op("act", lambda e: e.copy(out=cgs[:, t0:t0 + n], in_=ps[:, 0:n]), reads=[psk], writes=["cgs"])

                def cb_u(_, t0, n, ps, psk):
                    P.op("dve", lambda e: e.tensor_tensor(out=zb[:, t0:t0 + n], in0=cgs[:, t0:t0 + n], in1=ps[:, 0:n], op=ALU.mult),
                         reads=[psk, "cgs"], writes=["zb"])

                def cb_b(_, t0, n, ps, psk):
                    P.op("act", lambda e: e.copy(out=bb[:, t0:t0 + n], in_=ps[:, 0:n]), reads=[psk], writes=["bb"])
                emit_proj(c, _fm(w_in), D + j * 128, 1, xn, ["xn"], NE, cb_cg)
                emit_proj(c, _fm(w_in), 2 * D + j * 128, 1, xn, ["xn"], NE, cb_u)
                emit_proj(c, _fm(w_in), j * 128, 1, xn, ["xn"], NE, cb_b)
                P.op("dve", lambda e, j=j: e.tensor_scalar(out=yb[:], in0=zb[:, 2:2 + TH], scalar1=cw[:, j * 3 + 2:j * 3 + 3], scalar2=None, op0=ALU.mult),
                     reads=["zb", "cwt"], writes=["yb"])
                P.op("dve", lambda e, j=j: e.scalar_tensor_tensor(out=yb[:], in0=zb[:, 1:1 + TH], scalar=cw[:, j * 3 + 1:j * 3 + 2], in1=yb[:], op0=ALU.mult, op1=ALU.add),
                     reads=["zb", "cwt", "yb"], writes=["yb"])
                P.op("dve", lambda e, j=j: e.scalar_tensor_tensor(out=yb[:], in0=zb[:, 0:TH], scalar=cw[:, j * 3:j * 3 + 1], in1=yb[:], op0=ALU.mult, op1=ALU.add),
                     reads=["zb", "cwt", "yb"], writes=["yb"])
                P.op("dve", lambda e, j=j: e.tensor_tensor(out=gT[:, j, :], in0=yb[:], in1=bb[:, 2:2 + TH], op=ALU.mult),
                     reads=["yb", "bb"], writes=["gT"])
            emit_outproj(c, w_out.rearrange("(c p) n -> p c n", p=128), KC, gT, ["gT"], xv, _fm(oT), tg, TH, 1.0, final)
DIL = ((128, 1), (512, 4), (2048, 16))
SEQ = 4096
PI = 3.14159265358979
MAGIC = 12582912.0


def dil_consts():
    invf = np.zeros((128, 1), np.float32)
    fr = (500000.0 ** (-np.arange(0, 32, 2, dtype=np.float32) / 32)).astype(np.float32)
    invf[0:16, 0] = fr; invf[16:32, 0] = fr
    rm = np.zeros((128, 128), np.float32)
    for m in range(16):
        rm[m + 16, m] = -1.0
        rm[m, m + 16] = 1.0
    p = np.arange(128)[:, None]; f = np.arange(128)[None, :]
    mask = np.concatenate([(p >= f), (p <= f)], axis=1).astype(np.float32)
    return invf, rm, mask


def build_dilcore(NH=8):
    nc = _new_nc()
    yT = _din(nc, "yT", [9216, SEQ]); posb = _din(nc, "posb", [128, SEQ], I32)
    invf_d = _din(nc, "invf", [128, 1]); rm_d = _din(nc, "rm", [128, 128]); mask_d = _din(nc, "mask", [128, 256])
    id_d = _din(nc, "ident", [128, 128])
    gq_d = _din(nc, "gq", [128, 3]); gk_d = _din(nc, "gk", [128, 3])
    oT = _dout(nc, "oT", [NH * 128, SEQ])
    with contextlib.ExitStack() as stack:
        P = Prog(nc, stack); c = Ctx(nc, stack, P)
        emit_dilcore(c, yT, posb, invf_d, rm_d, mask_d, id_d, gq_d, gk_d, oT, NH, True)
        P.emit()
    return nc


def emit_dilcore(c, yT, posb, invf_d, rm_d, mask_d, id_d, gq_d, gk_d, oT, NH, final):
    P = c.P
    G = 3
    if True:
        ones32, ones16 = c.const_ones()
        invf = c.sb("invf_t", [128, 1], F32); rm = c.sb("rm_t", [128, 128], F32); mask = c.sb("mask_t", [128, 256], F32)
        ident = c.sb("ident_t", [128, 128], F32)
        gq = c.sb("gq_t", [128, G], F32); gk = c.sb("gk_t", [128, G], F32)
        for t, dd, k in ((invf, invf_d, "invf"), (rm, rm_d, "rm"), (mask, mask_d, "mask"), (gq, gq_d, "gq"), (gk, gk_d, "gk"), (ident, id_d, "ident")):
            P.dma("sp", t[:], dd, writes=[k])
        posi = c.sb("posi", [128, SEQ], I32)
        ang = c.sb("ang", [128, SEQ], F32); tmp = c.sb("tmpa", [128, SEQ], F32); kf = c.sb("kfa", [128, SEQ], F32)
        cosT = c.sb("cosT", [128, SEQ], F32); sinT = c.sb("sinT", [128, SEQ], F32)
        qf = kf
        q16 = c.sb("q16", [128, SEQ], BF16); k16 = c.sb("k16", [128, SEQ], BF16)
        v16 = c.sb("v16", [128, 32 * 128], BF16)
        Uacc = c.sb("Uacc", [128, SEQ], F32); Zacc = ang
        sq = c.sb("d_sq", [128, 512], F32); rs = c.sb("d_rs", [128, 512], F32); t1 = c.sb("d_t1", [128, 512], F32)
        t2 = c.sb("d_t2", [128, 512], F32)
        Ef = [c.sb(f"Ef{i}", [128, 256], F32) for i in range(2)]
        Em = [c.sb(f"Em{i}", [128, 256], BF16) for i in range(2)]
        psn = c.ps("ps_n"); psr = c.ps("ps_a0")
        pss = [c.ps(f"ps_b{i}") for i in range(2)]
        psU = [c.ps(f"ps_c{i}") for i in range(2)]
        psZ = [c.ps(f"ps_d{i}") for i in range(2)]
        C1 = 6.28125
        C2 = 2.0 * PI - C1
        sm_scale = 128.0 ** -0.5

        def table(dst, dkey, shift):
            P.op("dve", lambda e: e.tensor_scalar(out=tmp[:], in0=ang[:], scalar1=float(shift), scalar2=None, op0=ALU.add),
                 reads=["ang"], writes=["tmpa"])
            P.op("dve", lambda e: e.tensor_scalar(out=kf[:], in0=tmp[:], scalar1=1.0 / (2 * PI), scalar2=MAGIC, op0=ALU.mult, op1=ALU.add),
                 reads=["tmpa"], writes=["kfa"])
            P.op("dve", lambda e: e.tensor_scalar(out=kf[:], in0=kf[:], scalar1=-MAGIC, scalar2=None, op0=ALU.add),
                 reads=["kfa"], writes=["kfa"])
            P.op("dve", lambda e: e.scalar_tensor_tensor(out=tmp[:], in0=kf[:], scalar=-C1, in1=tmp[:], op0=ALU.mult, op1=ALU.add),
                 reads=["kfa", "tmpa"], writes=["tmpa"])
            P.op("dve", lambda e: e.scalar_tensor_tensor(out=tmp[:], in0=kf[:], scalar=-C2, in1=tmp[:], op0=ALU.mult, op1=ALU.add),
                 reads=["kfa", "tmpa"], writes=["tmpa"])
            P.op("dve", lambda e: e.tensor_scalar(out=tmp[:], in0=tmp[:], scalar1=3.1415925, scalar2=-3.1415925, op0=ALU.min, op1=ALU.max),
                 reads=["tmpa"], writes=["tmpa"])
            P.op("act", lambda e: e.activation(out=dst[:], in_=tmp[:], func=AF.Sin), reads=["tmpa"], writes=[dkey])

        P.dma("sp", posi[:], posb, writes=["posi"])
        P.op("dve", lambda e: e.tensor_copy(out=ang[:], in_=posi[:]), reads=["posi"], writes=["ang"])
        P.op("dve", lambda e: e.tensor_scalar(out=ang[:], in0=ang[:], scalar1=invf[:, 0:1], scalar2=None, op0=ALU.mult),
             reads=["ang", "invf"], writes=["ang"])
        table(sinT, "sinT", 0.0)
        table(cosT, "cosT", PI / 2)

        def prep(src, g, dl, gt, gkey, dst16, dkey):
            P.dma("sp", qf[:], src, reads=["kfa"], writes=["kfa"])
            dview = dst16[:].rearrange("p (r u) -> p u r", r=dl) if dl > 1 else None
            for t0 in range(0, SEQ, 512):
                sl = slice(t0, t0 + 512)
                P.op("act", lambda e, sl=sl: e.activation(out=sq[:], in_=qf[:, sl], func=AF.Square), reads=["kfa"], writes=["d_sq"])
                P.op("pe", lambda e: e.matmul(psn[:], ones32[:], sq[:], start=True, stop=True), reads=["d_sq", "ones32"], writes=["ps_n"])
                P.op("act", lambda e: e.activation(out=rs[:], in_=psn[:], func=AF.Ln, bias=EPS, scale=1.0 / 128), reads=["ps_n"], writes=["d_rs"])
                P.op("act", lambda e: e.activation(out=rs[:], in_=rs[:], func=AF.Exp, scale=-0.5), reads=["d_rs"], writes=["d_rs"])
                P.op("dve", lambda e, sl=sl: e.scalar_tensor_tensor(out=qf[:, sl], in0=qf[:, sl], scalar=gt[:, g:g + 1], in1=rs[:], op0=ALU.mult, op1=ALU.mult),
                     reads=["kfa", "d_rs", gkey], writes=["kfa"])
                P.op("pe", lambda e, sl=sl: e.matmul(psr[:], rm[:], qf[:, sl], start=True, stop=True), reads=["kfa", "rm"], writes=["ps_a0"])
                P.op("dve", lambda e, sl=sl: e.tensor_tensor(out=t1[:], in0=qf[:, sl], in1=cosT[:, sl], op=ALU.mult), reads=["kfa", "cosT"], writes=["d_t1"])
                P.op("dve", lambda e, sl=sl: e.tensor_tensor(out=t2[:], in0=psr[:], in1=sinT[:, sl], op=ALU.mult), reads=["ps_a0", "sinT"], writes=["d_t2"])
                if dl == 1:
                    P.op("dve", lambda e, sl=sl: e.tensor_tensor(out=dst16[:, sl], in0=t1[:], in1=t2[:], op=ALU.add), reads=["d_t1", "d_t2"], writes=[dkey])
                else:
                    u0 = t0 // dl
                    nu = 512 // dl
                    P.op("dve", lambda e, u0=u0, nu=nu: e.tensor_tensor(
                        out=dview[:, u0:u0 + nu, :], in0=t1[:].rearrange("p (u r) -> p u r", r=dl),
                        in1=t2[:].rearrange("p (u r) -> p u r", r=dl), op=ALU.add), reads=["d_t1", "d_t2"], writes=[dkey])

        def vprep(src, dl):
            nb = SEQ // dl // 128
            P.dma("sp", tmp[:], src, writes=["tmpa"])
            for q0 in range(0, 32, 4):
                pi = c.rot("vps", 2)

                def tr(e, q0=q0, pi=pi):
                    ins = None
                    for s in range(4):
                        q = q0 + s
                        r, bp = q // nb, q % nb
                        ta = bp * 128 * dl + r
                        sl = slice(ta, ta + dl * 127 + 1, dl) if dl > 1 else slice(ta, ta + 128)
                        ins = e.transpose(psU[pi][:, s * 128:(s + 1) * 128], tmp[:, sl], ident[:])
                    return ins
                P.op("pe", tr, reads=["tmpa", "ident"], writes=[f"ps_c{pi}"])
                P.op("act", lambda e, q0=q0, pi=pi: e.copy(out=v16[:, q0 * 128:(q0 + 4) * 128], in_=psU[pi][:]), reads=[f"ps_c{pi}"], writes=["v16"])

        for hl in range(NH):
            for g, (window, dl) in enumerate(DIL):
                nb = SEQ // dl // 128
                r0 = ((0 * 3 + g) * 8 + hl) * 128
                r1 = ((1 * 3 + g) * 8 + hl) * 128
                r2 = ((2 * 3 + g) * 8 + hl) * 128
                prep(yT[r0:r0 + 128, :], g, dl, gq, "gq", q16, "q16")
                prep(yT[r1:r1 + 128, :], g, dl, gk, "gk", k16, "k16")
                vprep(yT[r2:r2 + 128, :], dl)
                for qb in range(0, 32, 2):
                    r, b0 = qb // nb, qb % nb
                    ui = c.rot("psU", 2)
                    for s in range(2):
                        q = qb + s
                        bp = q % nb
                        ei = c.rot("Ef", 2)
                        lo = 0 if bp > 0 else 128
                        qs = slice(q * 128, (q + 1) * 128)

                        def mms(e, ei=ei, q=q, bp=bp, qs=qs):
                            ins = None
                            if bp > 0:
                                ins = e.matmul(pss[ei][:, 0:128], k16[:, (q - 1) * 128:q * 128], q16[:, qs], start=True, stop=True)
                            ins = e.matmul(pss[ei][:, 128:256], k16[:, qs], q16[:, qs], start=True, stop=True)
                            return ins
                        P.op("pe", mms, reads=["k16", "q16"], writes=[f"ps_b{ei}"])
                        P.op("act", lambda e, ei=ei, lo=lo: e.activation(out=Ef[ei][:, lo:256], in_=pss[ei][:, lo:256], func=AF.Exp, scale=sm_scale),
                             reads=[f"ps_b{ei}"], writes=[f"Ef{ei}"])
                        P.op("dve", lambda e, ei=ei, lo=lo: e.tensor_tensor(out=Em[ei][:, lo:256], in0=Ef[ei][:, lo:256], in1=mask[:, lo:256], op=ALU.mult),
                             reads=[f"Ef{ei}", "mask"], writes=[f"Em{ei}"])

                        def mmu(e, ei=ei, q=q, bp=bp, s=s, ui=ui):
                            o = psU[ui][:, s * 128:(s + 1) * 128]
                            if bp > 0:
                                e.matmul(o, v16[:, (q - 1) * 128:q * 128], Em[ei][:, 0:128], start=True, stop=False)
                            return e.matmul(o, v16[:, q * 128:(q + 1) * 128], Em[ei][:, 128:256], start=(bp == 0), stop=True)
                        P.op("pe", mmu, reads=[f"Em{ei}", "v16"], writes=[f"ps_c{ui}"])

                        def mmz(e, ei=ei, bp=bp, s=s, ui=ui):
                            o = psZ[ui][:, s * 128:(s + 1) * 128]
                            if bp > 0:
                                e.matmul(o, ones16[:], Em[ei][:, 0:128], start=True, stop=False)
                            return e.matmul(o, ones16[:], Em[ei][:, 128:256], start=(bp == 0), stop=True)
                        P.op("pe", mmz, reads=[f"Em{ei}", "ones16"], writes=[f"ps_d{ui}"])
                    ta = r + dl * b0 * 128
                    tsl = slice(ta, ta + dl * 255 + 1, dl) if dl > 1 else slice(ta, ta + 256)
                    if g == 0:
                        P.op("dve", lambda e, ui=ui, tsl=tsl: e.tensor_copy(out=Uacc[:, tsl], in_=psU[ui][:, 0:256]), reads=[f"ps_c{ui}"], writes=["Uacc"])
                        P.op("act", lambda e, ui=ui, tsl=tsl: e.copy(out=Zacc[:, tsl], in_=psZ[ui][:, 0:256]), reads=[f"ps_d{ui}"], writes=["ang"])
                    else:
                        P.op("dve", lambda e, ui=ui, tsl=tsl: e.tensor_tensor(out=Uacc[:, tsl], in0=Uacc[:, tsl], in1=psU[ui][:, 0:256], op=ALU.add),
                             reads=[f"ps_c{ui}", "Uacc"], writes=["Uacc"])
                        P.op("dve", lambda e, ui=ui, tsl=tsl: e.tensor_tensor(out=Zacc[:, tsl], in0=Zacc[:, tsl], in1=psZ[ui][:, 0:256], op=ALU.add),
                             reads=[f"ps_d{ui}", "ang"], writes=["ang"])
            P.op("dve", lambda e: e.reciprocal(out=Zacc[:], in_=Zacc[:]), reads=["ang"], writes=["ang"])
            P.op("dve", lambda e: e.tensor_tensor(out=Uacc[:], in0=Uacc[:], in1=Zacc[:], op=ALU.mult), reads=["ang", "Uacc"], writes=["Uacc"])
            P.dma("sp", oT[hl * 128:(hl + 1) * 128, :], Uacc[:], reads=["Uacc"], final=final)


def dil_perm(dl):
    L = SEQ // dl
    return (np.arange(L)[None, :] * dl + np.arange(dl)[:, None]).reshape(-1)
HC = 64
HNC = SEQ // HC


def hgrn_consts():
    cm = np.ones((128, SEQ), np.float32); cm[:, ::HC] = 0.0
    p = np.arange(HC)[:, None]; f = np.arange(HC)[None, :]
    tm = (p <= f).astype(np.float32)
    return cm, tm, np.eye(128, dtype=np.float32)


def build_hgrncore(NH=16, layer=2):
    nc = _new_nc()
    yT = _din(nc, "yT", [8192, SEQ])
    lbl = _din(nc, "lbl", [128, NH * 4]); ng_d = _din(nc, "ng", [HC, 128])
    cm_d = _din(nc, "cm", [128, SEQ]); tm_d = _din(nc, "tm", [HC, HC]); id_d = _din(nc, "ident", [128, 128])
    oT = _dout(nc, "oT", [NH * 128, SEQ])
    with contextlib.ExitStack() as stack:
        P = Prog(nc, stack); c = Ctx(nc, stack, P)
        emit_hgrncore(c, yT, lbl, ng_d, cm_d, tm_d, id_d, oT, NH, layer, True)
        P.emit()
    return nc


def emit_hgrncore(c, yT, lbl, ng_d, cm_d, tm_d, id_d, oT, NH, layer, final):
    P = c.P
    if True:
        cm = c.sb("cm_t", [128, SEQ], F32); tm = c.sb("tm_t", [HC, HC], F32); ident = c.sb("id_t", [128, 128], BF16)
        ident32 = c.sb("id32_t", [128, 128], F32)
        P.dma("sp", ident32[:], id_d, writes=["ident32"])
        gnat = c.sb("gnat", [128, SEQ], F32)
        ofm = c.sb("ofm", [128, 512], F32)
        psG = c.ps("ps_G", [128, 1024], F32)
        ng = c.sb("ng_t", [HC, 128], F32); lb4 = c.sb("lb4", [128, NH * 4], F32)
        P.dma("sp", cm[:], cm_d, writes=["cm"]); P.dma("sp", tm[:], tm_d, writes=["tm"])
        P.dma("pool", ident[:], id_d, writes=["ident"]); P.dma("sp", ng[:], ng_d, writes=["ng"])
        P.dma("sp", lb4[:], lbl, writes=["lb4"])
        lb = c.sb("lb", [128, NH], F32); oml = c.sb("oml", [128, NH], F32); den = c.sb("den", [128, NH], F32)
        P.op("act", lambda e: e.activation(out=lb4[:], in_=lb4[:], func=AF.Exp), reads=["lb4"], writes=["lb4"])
        l3 = lb4[:].rearrange("p (h l) -> p h l", l=4)
        P.op("dve", lambda e: e.tensor_reduce(out=den[:], in_=l3, axis=AX.X, op=ALU.add), reads=["lb4"], writes=["den"])
        P.op("dve", lambda e: e.reciprocal(out=den[:], in_=den[:]), reads=["den"], writes=["den"])
        P.op("dve", lambda e: e.tensor_copy(out=lb[:], in_=l3[:, :, 1]), reads=["lb4"], writes=["lb"])
        for l in range(2, layer + 1):
            P.op("dve", lambda e, l=l: e.tensor_tensor(out=lb[:], in0=lb[:], in1=l3[:, :, l], op=ALU.add), reads=["lb4", "lb"], writes=["lb"])
        P.op("dve", lambda e: e.tensor_tensor(out=lb[:], in0=lb[:], in1=den[:], op=ALU.mult), reads=["lb", "den"], writes=["lb"])
        P.op("dve", lambda e: e.tensor_scalar(out=oml[:], in0=lb[:], scalar1=-1.0, scalar2=1.0, op0=ALU.mult, op1=ALU.add), reads=["lb"], writes=["oml"])

        fb = c.sb("fb", [128, SEQ], F32); A = c.sb("A", [128, SEQ], F32); tmp = c.sb("htmp", [128, SEQ], F32)
        kk = c.sb("kk", [128, SEQ], F32); qf = c.sb("qf", [128, SEQ], F32)
        qd16 = c.sb("qd16", [128, SEQ], BF16); ki16 = c.sb("ki16", [128, SEQ], BF16); ke16 = c.sb("ke16", [128, SEQ], BF16)
        ketok = c.sb("ketok", [HC, HNC * 128], BF16); v16 = c.sb("v16", [HC, HNC * 128], BF16)
        dec = c.sb("dec", [128, HNC], F32)
        S32 = c.sb("S32", [128, 128], F32); S16 = c.sb("S16", [128, 128], BF16)
        att16 = [c.sb(f"att16_{i}", [HC, HC], BF16) for i in range(2)]
        gate = c.sb("gate", [HC, 8 * 128], F32); osb = c.sb("osb", [HC, 8 * 128], F32); sqb = c.sb("sqb", [HC, 8 * 128], F32)
        ss = c.sb("ss", [HC, 8], F32)
        psT = c.ps("ps_T", [128, 1024], BF16)
        psA = [c.ps(f"ps_a{i}") for i in range(2)]
        psO = [c.ps(f"ps_b{i}") for i in range(2)]
        psS = [c.ps("ps_c0"), c.ps("ps_c0")]
        A3 = A[:].rearrange("p (n c) -> p n c", c=HC)
        tmp3 = tmp[:].rearrange("p (n c) -> p n c", c=HC)
        for hl in range(NH):
            P.dma("sp", fb[:], yT[2048 + hl * 128:2048 + (hl + 1) * 128, :], writes=["fb"])
            P.dma("sp", qf[:], yT[hl * 128:(hl + 1) * 128, :], writes=["qf"])
            P.dma("sp", kk[:], yT[4096 + hl * 128:4096 + (hl + 1) * 128, :], writes=["kk"])
            P.dma("sp", gnat[:], yT[6144 + hl * 128:6144 + (hl + 1) * 128, :], writes=["gnat"])
            for n0 in range(0, HNC, 8):
                def trv(e, n0=n0):
                    ins = None
                    for i in range(8):
                        n = n0 + i
                        ins = e.transpose(psG[0:HC, i * 128:(i + 1) * 128], kk[:, n * HC:(n + 1) * HC], ident32[:])
                    return ins
                P.op("pe", trv, reads=["kk", "ident32"], writes=["ps_G"])
                P.op("act", lambda e, n0=n0: e.copy(out=v16[:, n0 * 128:(n0 + 8) * 128], in_=psG[0:HC, :]), reads=["ps_G"], writes=["v16"])
            P.op("act", lambda e: e.activation(out=fb[:], in_=fb[:], func=AF.Exp, scale=-1.0), reads=["fb"], writes=["fb"])
            P.op("dve", lambda e: e.tensor_scalar(out=fb[:], in0=fb[:], scalar1=1.0, scalar2=None, op0=ALU.add), reads=["fb"], writes=["fb"])
            P.op("dve", lambda e: e.reciprocal(out=fb[:], in_=fb[:]), reads=["fb"], writes=["fb"])
            P.op("dve", lambda e, hl=hl: e.tensor_scalar(out=fb[:], in0=fb[:], scalar1=oml[:, hl:hl + 1], scalar2=lb[:, hl:hl + 1], op0=ALU.mult, op1=ALU.add),
                 reads=["fb", "oml", "lb"], writes=["fb"])
            P.op("dve", lambda e: e.tensor_scalar(out=kk[:], in0=fb[:], scalar1=-1.0, scalar2=1.0, op0=ALU.mult, op1=ALU.add), reads=["fb"], writes=["kk"])
            P.op("act", lambda e: e.activation(out=fb[:], in_=fb[:], func=AF.Ln), reads=["fb"], writes=["fb"])
            P.op("dve", lambda e: e.tensor_tensor_scan(out=A[:], data0=cm[:], data1=fb[:], initial=0.0, op0=ALU.mult, op1=ALU.add),
                 reads=["cm", "fb"], writes=["A"])
            P.op("act", lambda e: e.activation(out=tmp[:], in_=A[:], func=AF.Exp), reads=["A"], writes=["htmp"])
            P.op("dve", lambda e: e.tensor_tensor(out=qd16[:], in0=qf[:], in1=tmp[:], op=ALU.mult), reads=["qf", "htmp"], writes=["qd16"])
            P.op("act", lambda e: e.copy(out=dec[:], in_=tmp3[:, :, HC - 1]), reads=["htmp"], writes=["dec"])
            P.op("act", lambda e: e.activation(out=tmp[:], in_=A[:], func=AF.Exp, scale=-1.0), reads=["A", "dec"], writes=["htmp"])
            P.op("dve", lambda e: e.tensor_tensor(out=ki16[:], in0=kk[:], in1=tmp[:], op=ALU.mult), reads=["kk", "htmp"], writes=["ki16"])
            P.op("dve", lambda e: e.tensor_tensor(out=tmp3, in0=A3[:, :, HC - 1:HC].broadcast_to([128, HNC, HC]), in1=A3, op=ALU.subtract),
                 reads=["A", "ki16"], writes=["htmp"])
            P.op("act", lambda e: e.activation(out=tmp[:], in_=tmp[:], func=AF.Exp), reads=["htmp"], writes=["htmp"])
            P.op("dve", lambda e: e.tensor_tensor(out=ke16[:], in0=kk[:], in1=tmp[:], op=ALU.mult), reads=["kk", "htmp"], writes=["ke16"])
            for n0 in range(0, HNC, 8):
                def tr(e, n0=n0):
                    ins = None
                    for i in range(8):
                        n = n0 + i
                        ins = e.transpose(psT[0:HC, i * 128:(i + 1) * 128], ke16[:, n * HC:(n + 1) * HC], ident[:])
                    return ins
                P.op("pe", tr, reads=["ke16", "ident"], writes=["ps_T"])
                P.op("act", lambda e, n0=n0: e.copy(out=ketok[:, n0 * 128:(n0 + 8) * 128], in_=psT[0:HC, :]), reads=["ps_T"], writes=["ketok"])
            P.op("pool", lambda e: e.memset(S32[:], 0.0), writes=["S32"])
            P.op("pool", lambda e: e.memset(S16[:], 0.0), writes=["S16"])
            for n in range(HNC):
                cs = slice(n * HC, (n + 1) * HC)
                vs = slice(n * 128, (n + 1) * 128)
                ai = c.rot("psA", 2)
                j = n % 8
                if j == 0:
                    def trg(e, n=n):
                        ins = None
                        for i in range(8):
                            ins = e.transpose(psG[0:HC, i * 128:(i + 1) * 128], gnat[:, (n + i) * HC:(n + i + 1) * HC], ident32[:])
                        return ins
                    P.op("pe", trg, reads=["gnat", "ident32"], writes=["ps_G"])
                    P.op("act", lambda e: e.activation(out=gate[:], in_=psG[0:HC, :], func=AF.Silu), reads=["ps_G"], writes=["gate"])
                P.op("pe", lambda e, ai=ai, cs=cs: e.matmul(psA[ai][0:HC, 0:HC], ki16[:, cs], qd16[:, cs], start=True, stop=True),
                     reads=["ki16", "qd16"], writes=[f"ps_a{ai}"])
                P.op("dve", lambda e, ai=ai: e.tensor_tensor(out=att16[ai][:], in0=psA[ai][0:HC, 0:HC], in1=tm[:], op=ALU.mult),
                     reads=[f"ps_a{ai}", "tm"], writes=[f"att16_{ai}"])

                def mmo(e, ai=ai, cs=cs, vs=vs):
                    e.matmul(psO[ai][0:HC, 0:128], qd16[:, cs], S16[:], start=True, stop=False)
                    return e.matmul(psO[ai][0:HC, 0:128], att16[ai][:], v16[:, vs], start=False, stop=True)
                P.op("pe", mmo, reads=["qd16", "S16", f"att16_{ai}", "v16"], writes=[f"ps_b{ai}"])
                P.op("act", lambda e, ai=ai, j=j: e.copy(out=osb[:, j * 128:(j + 1) * 128], in_=psO[ai][0:HC, 0:128]), reads=[f"ps_b{ai}"], writes=["osb"])
                P.op("pe", lambda e, ai=ai, vs=vs: e.matmul(psS[ai][:, 0:128], ketok[:, vs], v16[:, vs], start=True, stop=True),
                     reads=["ketok", "v16"], writes=["ps_c0"])
                P.op("dve", lambda e, ai=ai, n=n: e.scalar_tensor_tensor(out=S32[:], in0=S32[:], scalar=dec[:, n:n + 1], in1=psS[ai][:, 0:128], op0=ALU.mult, op1=ALU.add),
                     reads=["S32", "dec", "ps_c0"], writes=["S32"])
                P.op("act", lambda e: e.copy(out=S16[:], in_=S32[:]), reads=["S32"], writes=["S16"])
                if j == 7:
                    o3 = osb[:].rearrange("p (j e) -> p j e", e=128)
                    P.op("dve", lambda e: e.tensor_tensor(out=sqb[:], in0=osb[:], in1=osb[:], op=ALU.mult), reads=["osb"], writes=["sqb"])
                    P.op("dve", lambda e: e.tensor_reduce(out=ss[:], in_=sqb[:].rearrange("p (j e) -> p j e", e=128), axis=AX.X, op=ALU.add),
                         reads=["sqb"], writes=["ss"])
                    P.op("act", lambda e: e.activation(out=ss[:], in_=ss[:], func=AF.Ln, bias=EPS, scale=1.0 / 128), reads=["ss"], writes=["ss"])
                    P.op("act", lambda e: e.activation(out=ss[:], in_=ss[:], func=AF.Exp, scale=-0.5), reads=["ss"], writes=["ss"])
                    P.op("dve", lambda e, o3=o3: e.tensor_tensor(out=o3, in0=o3, in1=ss[:].unsqueeze(2).broadcast_to([HC, 8, 128]), op=ALU.mult),
                         reads=["osb", "ss"], writes=["osb"])
                    P.op("dve", lambda e, o3=o3: e.tensor_tensor(out=o3, in0=o3, in1=ng[:].unsqueeze(1).broadcast_to([HC, 8, 128]), op=ALU.mult),
                         reads=["osb", "ng"], writes=["osb"])
                    P.op("dve", lambda e: e.tensor_tensor(out=osb[:], in0=osb[:], in1=gate[:], op=ALU.mult), reads=["osb", "gate"], writes=["osb"])

                    def tro(e):
                        ins = None
                        for i in range(8):
                            ins = e.transpose(psG[:, i * HC:(i + 1) * HC], osb[:, i * 128:(i + 1) * 128], ident32[0:HC, 0:HC])
                        return ins
                    P.op("pe", tro, reads=["osb", "ident32"], writes=["ps_G"])
                    P.op("act", lambda e: e.copy(out=ofm[:], in_=psG[:, 0:512]), reads=["ps_G"], writes=["ofm"])
                    P.dma("sp", oT[hl * 128:(hl + 1) * 128, (n - 7) * HC:(n + 1) * HC], ofm[:], reads=["ofm"], final=final)
NOSYNC = ('dve', 'pool', 'act')
RW_ROWS = 7
RWM = {0: 0, 2: 1, 3: 2, 5: 3, 6: 4, 7: 5, 8: 6}


def rwkv_consts():
    bones = np.zeros((128, 128), np.float32)
    bones[:64, :64] = 1.0; bones[64:, 64:] = 1.0
    sel = np.zeros((32, 16 * 128), np.float32)
    for t in range(16):
        for j in range(2):
            sel[t * 2 + j, t * 128 + j * 64: t * 128 + (j + 1) * 64] = 1.0
    return bones, sel


def build_rwkvA(T=4096):
    nc = _new_nc()
    xT = _din(nc, "xT", [D, T]); gain = _din(nc, "gain", [128, KC]); mu_d = _din(nc, "mu", [128, 6 * KC])
    wrkv = _din(nc, "wrkv", [3, D, D])
    w0_d = _din(nc, "w0", [128, KC]); w1 = _din(nc, "w1", [D, 96]); w2 = _din(nc, "w2", [96, D])
    a0_d = _din(nc, "a0", [128, KC]); a1 = _din(nc, "a1", [D, 96]); a2 = _din(nc, "a2", [96, D])
    g1 = _din(nc, "g1", [D, 256]); g2 = _din(nc, "g2", [256, D])
    kk_d = _din(nc, "k_k", [128, KC]); ka_d = _din(nc, "k_a", [128, KC]); bones_d = _din(nc, "bones", [128, 128])
    yT = _dout(nc, "yT", [RW_ROWS * D, T])
    with contextlib.ExitStack() as stack:
        P = Prog(nc, stack); c = Ctx(nc, stack, P)
        emit_rwkvA(c, xT, gain, mu_d, wrkv, w0_d, w1, w2, a0_d, a1, a2, g1, g2, kk_d, ka_d, bones_d, yT, T, True)
        P.emit()
    return nc


def emit_rwkvA(c, xT, gain, mu_d, wrkv, w0_d, w1, w2, a0_d, a1, a2, g1, g2, kk_d, ka_d, bones_d, yT, T, final):
    P = c.P
    c.pwg = 2
    if True:
        gn = c.sb("gn", [128, KC], F32); mu = c.sb("mu_t", [128, 6 * KC], F32)
        w0 = c.sb("w0_t", [128, KC], F32); a0 = c.sb("a0_t", [128, KC], F32)
        k_k = c.sb("kk_t", [128, KC], F32); k_a = c.sb("ka_t", [128, KC], F32); omka = c.sb("omka", [128, KC], F32)
        bones = c.sb("bones_t", [128, 128], F32)
        for t, d, k in ((gn, gain, "gn"), (mu, mu_d, "mu"), (w0, w0_d, "w0"), (a0, a0_d, "a0"), (k_k, kk_d, "k_k"), (k_a, ka_d, "k_a"), (bones, bones_d, "bones")):
            P.dma("sp", t[:], d, writes=[k])
        P.op("dve", lambda e: e.tensor_scalar(out=omka[:], in0=k_a[:], scalar1=-1.0, scalar2=1.0, op0=ALU.mult, op1=ALU.add), reads=["k_a"], writes=["omka"])
        nw0 = c.sb("nw0", [128, KC], F32); na0 = c.sb("na0", [128, KC], F32); th = c.sb("th", [128, 512], F32)
        P.op("dve", lambda e: e.tensor_scalar(out=nw0[:], in0=w0[:], scalar1=-1.0, scalar2=None, op0=ALU.mult), reads=["w0"], writes=["nw0"])
        P.op("dve", lambda e: e.tensor_scalar(out=na0[:], in0=a0[:], scalar1=-1.0, scalar2=None, op0=ALU.mult), reads=["a0"], writes=["na0"])
        w2b = c.sb("w2b", [96, D], BF16); a2b = c.sb("a2b", [96, D], BF16); g2b = c.sb("g2b", [128, 2, D], BF16)
        P.dma("pool", w2b[:], w2, writes=["w2b"]); P.dma("pool", a2b[:], a2, writes=["a2b"])
        P.dma("pool", g2b[:], g2.rearrange("(c p) n -> p c n", p=128), writes=["g2b"])
        hfp = c.sb("hfp", [128, KC, 513], F32); diff = c.sb("diff", [128, KC, 512], F32)
        mx = [c.sb(f"mx{i}", [128, KC, 512], BF16) for i in range(2)]
        kbuf = c.sb("kbuf", [128, KC, 512], F32)
        t1 = c.sb("t1", [128, 2, 512], BF16)
        ysb = [c.sb(f"ysb{i}", [128, 512], F32) for i in range(2)]
        asb = c.sb("asb", [128, 512], F32); kkr = c.sb("kkr", [128, 512], F32); sq = c.sb("r_sq", [128, 512], F32)
        rn = c.sb("rn", [128, 512], F32)
        psb = [c.ps(f"ps_b{i}") for i in range(2)]
        psn2 = c.ps("ps_c0")
        yv = yT.rearrange("(r kc p) t -> r p kc t", p=128, kc=KC)
        xv = _fm(xT)

        def store(row, j, tg, src_ap, key):
            if row not in RWM:
                return
            P.dma("sp", yv[RWM[row]][:, j, tg:tg + 512], src_ap, reads=[key], final=final)

        for tt in range(T // 512):
            tg = tt * 512
            if tt == 0:
                P.op("pool", lambda e: e.memset(hfp[:, :, 0:1], 0.0), writes=["hfp"])
                emit_norm(c, xv[:, :, 0:512], 512, gn, hfp, 1, "hfp")
            else:
                emit_norm(c, xv[:, :, tg - 1:tg + 512], 513, gn, hfp, 0, "hfp")
            P.op("dve", lambda e: e.tensor_tensor(out=diff[:], in0=hfp[:, :, 0:512], in1=hfp[:, :, 1:513], op=ALU.subtract), reads=["hfp"], writes=["diff"])
            for i in range(6):
                mi = c.rot("mx", 2)
                m = mx[mi]
                for kc in range(KC):
                    P.op("dve", lambda e, kc=kc, i=i, m=m: e.scalar_tensor_tensor(
                        out=m[:, kc, :], in0=diff[:, kc, :], scalar=mu[:, i * KC + kc:i * KC + kc + 1], in1=hfp[:, kc, 1:513],
                        op0=ALU.mult, op1=ALU.add), reads=["diff", "hfp", "mu"], writes=[f"mx{mi}"])
                mk = [f"mx{mi}"]
                if i < 3:
                    def cb(j, t0, n, ps, psk, i=i, tg=tg):
                        if i == 1:
                            P.op("act", lambda e: e.copy(out=kbuf[:, j, :], in_=ps[:, 0:512]), reads=[psk], writes=["kbuf"])
                            store(1, j, tg, kbuf[:, j, :], "kbuf")
                        else:
                            yi = c.rot("ysb", 2)
                            P.op("act", lambda e: e.copy(out=ysb[yi][:], in_=ps[:, 0:512]), reads=[psk], writes=[f"ysb{yi}"])
                            store(i, j, tg, ysb[yi][:], f"ysb{yi}")
                    emit_proj(c, wrkv[i].rearrange("(kc p) n -> p kc n", p=128), 0, KC, m, mk, 512, cb)
                elif i == 3 or i == 4:
                    wl = w1 if i == 3 else a1

                    def cb(j, t0, n, ps, psk, i=i):
                        if i == 3:
                            P.op("act", lambda e: e.activation(out=th[0:96, :], in_=ps[0:96, 0:512], func=AF.Exp, scale=-2.0), reads=[psk], writes=["th"])
                            P.op("dve", lambda e: e.tensor_scalar(out=th[0:96, :], in0=th[0:96, :], scalar1=1.0, scalar2=None, op0=ALU.add), reads=["th"], writes=["th"])
                            P.op("dve", lambda e: e.reciprocal(out=th[0:96, :], in_=th[0:96, :]), reads=["th"], writes=["th"])
                            P.op("dve", lambda e: e.tensor_scalar(out=t1[0:96, 0, :], in0=th[0:96, :], scalar1=2.0, scalar2=-1.0, op0=ALU.mult, op1=ALU.add), reads=["th"], writes=["t1"])
                        else:
                            P.op("act", lambda e: e.copy(out=t1[0:96, 0, :], in_=ps[0:96, 0:512]), reads=[psk], writes=["t1"])
                    emit_proj(c, wl.rearrange("(kc p) n -> p kc n", p=128), 0, 1, m, mk, 512, cb, cw=96)
                    w2x, w2k, bias, bk = (w2b, "w2b", w0, "w0") if i == 3 else (a2b, "a2b", a0, "a0")
                    for j in range(KC):
                        pi = c.rot("ps_b", 2)
                        P.op("pe", lambda e, pi=pi, j=j, w2x=w2x: e.matmul(psb[pi][:], w2x[0:96, j * 128:(j + 1) * 128], t1[0:96, 0, :], start=True, stop=True),
                             reads=["t1", w2k], writes=[f"ps_b{pi}"])
                        if i == 3:
                            yi = c.rot("ysb", 2)
                            P.op("act", lambda e, pi=pi, j=j, yi=yi: e.activation(out=ysb[yi][:], in_=psb[pi][:], func=AF.Exp, bias=nw0[:, j:j + 1], scale=-1.0),
                                 reads=[f"ps_b{pi}", "nw0"], writes=[f"ysb{yi}"])
                            P.op("dve", lambda e, yi=yi: e.tensor_scalar(out=ysb[yi][:], in0=ysb[yi][:], scalar1=1.0, scalar2=None, op0=ALU.add), reads=[f"ysb{yi}"], writes=[f"ysb{yi}"])
                            P.op("dve", lambda e, yi=yi: e.reciprocal(out=ysb[yi][:], in_=ysb[yi][:]), reads=[f"ysb{yi}"], writes=[f"ysb{yi}"])
                            P.op("act", lambda e, yi=yi: e.activation(out=ysb[yi][:], in_=ysb[yi][:], func=AF.Exp, scale=-float(np.exp(-0.5))),
                                 reads=[f"ysb{yi}"], writes=[f"ysb{yi}"])
                            store(3, j, tg, ysb[yi][:], f"ysb{yi}")
                        else:
                            P.op("act", lambda e, pi=pi, j=j: e.activation(out=asb[:], in_=psb[pi][:], func=AF.Exp, bias=na0[:, j:j + 1], scale=-1.0),
                                 reads=[f"ps_b{pi}", "na0"], writes=["asb"])
                            P.op("dve", lambda e: e.tensor_scalar(out=asb[:], in0=asb[:], scalar1=1.0, scalar2=None, op0=ALU.add), reads=["asb"], writes=["asb"])
                            P.op("dve", lambda e: e.reciprocal(out=asb[:], in_=asb[:]), reads=["asb"], writes=["asb"])
                            store(4, j, tg, asb[:], "asb")
                            P.op("dve", lambda e, j=j: e.tensor_scalar(out=kkr[:], in0=kbuf[:, j, :], scalar1=k_k[:, j:j + 1], scalar2=None, op0=ALU.mult),
                                 reads=["kbuf", "k_k"], writes=["kkr"])
                            P.op("act", lambda e: e.activation(out=sq[:], in_=kkr[:], func=AF.Square), reads=["kkr"], writes=["r_sq"])
                            P.op("pe", lambda e: e.matmul(psn2[:], bones[:], sq[:], start=True, stop=True), reads=["r_sq", "bones"], writes=["ps_c0"])
                            P.op("dve", lambda e: e.tensor_scalar(out=rn[:], in0=psn2[:], scalar1=1e-24, scalar2=None, op0=ALU.max), reads=["ps_c0"], writes=["rn"])
                            P.op("act", lambda e: e.activation(out=rn[:], in_=rn[:], func=AF.Ln), reads=["rn"], writes=["rn"])
                            P.op("act", lambda e: e.activation(out=rn[:], in_=rn[:], func=AF.Exp, scale=-0.5), reads=["rn"], writes=["rn"])
                            P.op("dve", lambda e: e.tensor_tensor(out=kkr[:], in0=kkr[:], in1=rn[:], op=ALU.mult), reads=["kkr", "rn"], writes=["kkr"])
                            yi = c.rot("ysb", 2)
                            P.op("dve", lambda e, yi=yi: e.tensor_scalar(out=ysb[yi][:], in0=kkr[:], scalar1=-1.0, scalar2=None, op0=ALU.mult), reads=["kkr"], writes=[f"ysb{yi}"])
                            store(6, j, tg, ysb[yi][:], f"ysb{yi}")
                            yi = c.rot("ysb", 2)
                            P.op("dve", lambda e, yi=yi: e.tensor_tensor(out=ysb[yi][:], in0=kkr[:], in1=asb[:], op=ALU.mult), reads=["kkr", "asb"], writes=[f"ysb{yi}"])
                            store(7, j, tg, ysb[yi][:], f"ysb{yi}")
                            yi = c.rot("ysb", 2)
                            P.op("dve", lambda e, j=j: e.tensor_scalar(out=rn[:], in0=asb[:], scalar1=k_a[:, j:j + 1], scalar2=omka[:, j:j + 1], op0=ALU.mult, op1=ALU.add),
                                 reads=["asb", "k_a", "omka"], writes=["rn"])
                            P.op("dve", lambda e, yi=yi, j=j: e.tensor_tensor(out=ysb[yi][:], in0=kbuf[:, j, :], in1=rn[:], op=ALU.mult), reads=["kbuf", "rn"], writes=[f"ysb{yi}"])
                            store(8, j, tg, ysb[yi][:], f"ysb{yi}")
                else:
                    def cb(j, t0, n, ps, psk):
                        P.op("act", lambda e: e.activation(out=th[:], in_=ps[:, 0:512], func=AF.Exp, scale=-1.0), reads=[psk], writes=["th"])
                        P.op("dve", lambda e: e.tensor_scalar(out=th[:], in0=th[:], scalar1=1.0, scalar2=None, op0=ALU.add), reads=["th"], writes=["th"])
                        P.op("dve", lambda e: e.reciprocal(out=th[:], in_=th[:]), reads=["th"], writes=["th"])
                        P.op("dve", lambda e: e.tensor_copy(out=t1[:, j, :], in_=th[:]), reads=["th"], writes=["t1"])
                    emit_proj(c, g1.rearrange("(kc p) n -> p kc n", p=128), 0, 2, m, mk, 512, cb)
                    for j in range(KC):
                        pi = c.rot("ps_b", 2)

                        def mm(e, pi=pi, j=j):
                            e.matmul(psb[pi][:], g2b[:, 0, j * 128:(j + 1) * 128], t1[:, 0, :], start=True, stop=False)
                            return e.matmul(psb[pi][:], g2b[:, 1, j * 128:(j + 1) * 128], t1[:, 1, :], start=False, stop=True)
                        P.op("pe", mm, reads=["t1", "g2b"], writes=[f"ps_b{pi}"])
                        yi = c.rot("ysb", 2)
                        P.op("act", lambda e, pi=pi, yi=yi: e.copy(out=ysb[yi][:], in_=psb[pi][:]), reads=[f"ps_b{pi}"], writes=[f"ysb{yi}"])
                        store(5, j, tg, ysb[yi][:], f"ysb{yi}")


def build_rwkvB(NSTEP=SEQ, NPASS=2):
    nc = _new_nc()
    yT = _din(nc, "yT", [RW_ROWS * D, NSTEP]); sel_d = _din(nc, "sel", [32, 16 * 128]); id_d = _din(nc, "ident", [128, 128])
    ys = _dout(nc, "ys", [D, NSTEP])
    tm = nc.dram_tensor("rw_tm", [5, NSTEP, 1024], F32, kind="Internal").ap()
    with contextlib.ExitStack() as stack:
        P = Prog(nc, stack); c = Ctx(nc, stack, P)
        emit_rwkvB(c, yT, sel_d, id_d, tm, ys, NSTEP, NPASS, True)
        P.emit()
    return nc


def emit_rwkvB(c, yT, sel_d, id_d, tm, ys, NSTEP, NPASS, final):
    P = c.P
    VB = 256
    ROWS = (RWM[6], RWM[3], RWM[7], RWM[8], RWM[0])
    if True:
        sel = c.sb("sel_t", [32, 16 * 128], F32); ident = c.sb("ident_t", [128, 128], F32)
        P.dma("sp", sel[:], sel_d, writes=["sel"]); P.dma("sp", ident[:], id_d, writes=["ident"])
        opb = [c.sb(f"opb{i}", [32, 5, 512], F32) for i in range(2)]
        ob16 = [c.sb(f"ob16_{i}", [32, 6, 512], BF16) for i in range(2)]
        sel16 = c.sb("sel16_t", [32, 16 * 128], BF16)
        P.dma("pool", sel16[:], sel_d, writes=["sel16"])
        vb = [c.sb(f"vb{i}", [128, 8, VB], F32) for i in range(2)]
        yb = [c.sb(f"yb{i}", [128, 8, VB], F32) for i in range(2)]
        S = c.sb("S", [128, 512], F32)
        tmp = c.sb("s_tmp", [128, 512], F32); tmp2 = c.sb("s_tmp2", [128, 512], F32); tmp3 = c.sb("s_tmp3", [128, 512], F32)
        tmp4 = c.sb("s_tmp4", [128, 512], F32); kc_ = c.sb("s_kc", [128, 512], F32)
        sa = c.sb("s_sa", [128, 8], F32); X = c.sb("s_X", [128, 512], F32); wsb = c.sb("s_w", [128, 512], F32)
        fmb = [c.sb(f"fmb{i}", [128, 8, 128], F32) for i in range(2)]
        tmb = [c.sb(f"tmb{i}", [128, 1024], F32) for i in range(2)]
        NPS = 7
        pss = [c.ps(f"ps_r{i}") for i in range(NPS)]
        r3 = lambda t: t[:].rearrange("p (i k) -> p i k", k=64)
        for hh in range(NPASS):
            for oi, row in enumerate(ROWS):
                src = yT[row * D + hh * 1024: row * D + (hh + 1) * 1024, :].rearrange("(c p) t -> p c t", p=128)
                for tb in range(NSTEP // 128):
                    fi = c.rot("fmb", 2)
                    P.dma("sp", fmb[fi][:], src[:, :, tb * 128:(tb + 1) * 128], writes=[f"fmb{fi}"])
                    for half in range(2):
                        pi = c.rot("ps_r", NPS)

                        def tr(e, fi=fi, half=half, pi=pi):
                            ins = None
                            for q in range(4):
                                ins = e.transpose(pss[pi][:, q * 128:(q + 1) * 128], fmb[fi][:, half * 4 + q, :], ident[:])
                            return ins
                        P.op("pe", tr, reads=[f"fmb{fi}", "ident"], writes=[f"ps_r{pi}"])
                        P.op("act", lambda e, fi=fi, half=half, pi=pi: e.copy(out=tmb[fi][:, half * 512:(half + 1) * 512], in_=pss[pi][:]),
                             reads=[f"ps_r{pi}"], writes=[f"tmb{fi}"])
                    P.dma("sp", tm[oi, tb * 128:(tb + 1) * 128, :], tmb[fi][:], reads=[f"tmb{fi}"], writes=["tm_dram"])
            ov = tm.rearrange("o (k t) (j f) -> k (t j) o f", t=16, j=2)
            P.op("pool", lambda e: e.memset(S[:], 0.0), reads=["S"], writes=["S"])
            P.nosync_self = set(NOSYNC)
            vbase = RWM[2] * D + hh * 1024
            for t in range(NSTEP):
                bi = (t // 16) % 2
                if t % 16 == 0:
                    P.dma("sp", opb[bi][:], ov[t // 16], reads=["tm_dram"], writes=[f"opb{bi}"])
                vi = (t // VB) % 2
                if t % VB == 0:
                    for j in range(2):
                        P.dma("sp", vb[vi][j * 64:(j + 1) * 64, :, :],
                              yT[vbase + j * 512: vbase + (j + 1) * 512, t:t + VB].rearrange("(i v) t -> v i t", v=64), writes=[f"vb{vi}"])
                tl = t % 16
                if tl == 0:
                    P.op("act", lambda e, bi=bi: e.copy(out=ob16[bi][:, 0:5, :], in_=opb[bi][:]), reads=[f"opb{bi}"], writes=[f"ob16_{bi}"])
                    P.op("pool", lambda e, bi=bi: e.tensor_tensor(out=ob16[bi][:, 5, :], in0=opb[bi][:, 1, :], in1=ob16[bi][:, 1, :], op=ALU.subtract),
                         reads=[f"opb{bi}", f"ob16_{bi}"], writes=[f"ob16_{bi}"])
                pk = []
                for o in range(5):
                    pi = c.rot("ps_r", NPS)

                    def bm(e, pi=pi, o=o, bi=bi, tl=tl):
                        if o == 1:
                            e.matmul(pss[pi][:], sel16[:, tl * 128:(tl + 1) * 128], ob16[bi][:, 1, :], start=True, stop=False)
                            return e.matmul(pss[pi][:], sel16[:, tl * 128:(tl + 1) * 128], ob16[bi][:, 5, :], start=False, stop=True)
                        return e.matmul(pss[pi][:], sel16[:, tl * 128:(tl + 1) * 128], ob16[bi][:, o, :], start=True, stop=True)
                    P.op("pe", bm, reads=[f"ob16_{bi}", "sel16"], writes=[f"ps_r{pi}"])
                    pk.append(pi)
                pn, pw, pb, pkk, pr = pk
                tv = t % VB
                P.op("act", lambda e, pkk=pkk: e.copy(out=kc_[:], in_=pss[pkk][:]), reads=[f"ps_r{pkk}"], writes=["s_kc"])
                P.op("act", lambda e, pw=pw: e.copy(out=wsb[:], in_=pss[pw][:]), reads=[f"ps_r{pw}"], writes=["s_w"])
                P.op("pool", lambda e, vi=vi, tv=tv: e.tensor_tensor(out=r3(tmp3), in0=r3(kc_), in1=vb[vi][:, :, tv:tv + 1].broadcast_to([128, 8, 64]), op=ALU.mult),
                     reads=["s_kc", f"vb{vi}"], writes=["s_tmp3"])
                P.op("pool", lambda e: e.tensor_tensor(out=X[:], in0=S[:], in1=wsb[:], op=ALU.mult), reads=["S", "s_w"], writes=["s_X"])
                P.op("pool", lambda e: e.tensor_tensor(out=X[:], in0=X[:], in1=tmp3[:], op=ALU.add), reads=["s_X", "s_tmp3"], writes=["s_X"])
                P.op("dve", lambda e, pn=pn: e.tensor_tensor(out=tmp[:], in0=S[:], in1=pss[pn][:], op=ALU.mult), reads=["S", f"ps_r{pn}"], writes=["s_tmp"])
                P.op("dve", lambda e: e.tensor_reduce(out=sa[:], in_=r3(tmp), axis=AX.X, op=ALU.add), reads=["s_tmp"], writes=["s_sa"])
                P.op("dve", lambda e, pb=pb: e.tensor_tensor(out=r3(tmp2), in0=pss[pb][:].rearrange("p (i k) -> p i k", k=64), in1=sa[:].unsqueeze(2).broadcast_to([128, 8, 64]), op=ALU.mult),
                     reads=["s_sa", f"ps_r{pb}"], writes=["s_tmp2"])
                P.op("dve", lambda e: e.tensor_tensor(out=S[:], in0=X[:], in1=tmp2[:], op=ALU.add), reads=["s_X", "s_tmp2", "s_tmp"], writes=["S"])
                P.op("dve", lambda e, pr=pr: e.tensor_tensor(out=tmp4[:], in0=S[:], in1=pss[pr][:], op=ALU.mult), reads=["S", f"ps_r{pr}"], writes=["s_tmp4"])
                P.op("dve", lambda e, vi=vi, tv=tv: e.tensor_reduce(out=yb[vi][:, :, tv], in_=r3(tmp4), axis=AX.X, op=ALU.add), reads=["s_tmp4"], writes=[f"yb{vi}"])
                if tv == VB - 1:
                    for j in range(2):
                        P.dma("sp", ys[hh * 1024 + j * 512: hh * 1024 + (j + 1) * 512, t - VB + 1:t + 1].rearrange("(i v) t -> v i t", v=64),
                              yb[vi][j * 64:(j + 1) * 64, :, :], reads=[f"yb{vi}"], final=final)
            P.nosync_self = set(DEFAULT_NOSYNC)
            P.op("pool", lambda e: e.memset(sa[:], 0.0), reads=[f"opb0", f"opb1"], writes=["s_sa", "tm_dram"])


def build_rwkvC(T=4096):
    nc = _new_nc()
    xT = _din(nc, "xT", [D, T]); ysT = _din(nc, "ysT", [D, T]); yT = _din(nc, "yT", [RW_ROWS * D, T])
    lnw_d = _din(nc, "lnw", [128, KC]); lnb_d = _din(nc, "lnb", [128, KC]); rk_d = _din(nc, "r_k", [128, KC])
    bones_d = _din(nc, "bones", [128, 128]); w_out = _din(nc, "w_out", [D, D])
    oT = _dout(nc, "oT", [D, T])
    with contextlib.ExitStack() as stack:
        P = Prog(nc, stack); c = Ctx(nc, stack, P)
        emit_rwkvC(c, xT, ysT, yT, lnw_d, lnb_d, rk_d, bones_d, w_out, oT, T, True)
        P.emit()
    return nc


def emit_rwkvC(c, xT, ysT, yT, lnw_d, lnb_d, rk_d, bones_d, w_out, oT, T, final):
    P = c.P
    if True:
        lnw = c.sb("lnw_t", [128, KC], F32); lnb = c.sb("lnb_t", [128, KC], F32); rk = c.sb("rk_t", [128, KC], F32)
        bones = c.sb("bones_t", [128, 128], F32)
        for t, d, k in ((lnw, lnw_d, "lnw"), (lnb, lnb_d, "lnb"), (rk, rk_d, "rk"), (bones, bones_d, "bones")):
            P.dma("sp", t[:], d, writes=[k])
        TH = 1024
        zT = c.sb("zT", [128, KC, TH], BF16)
        names = ["cy", "cr", "ck", "cv", "cg"]
        tl = {nm: [c.sb(f"{nm}{i}", [128, 512], F32) for i in range(2)] for nm in names}
        yc = c.sb("c_yc", [128, 512], F32); sq = c.sb("c_sq", [128, 512], F32); rs = c.sb("c_rs", [128, 512], F32)
        rkk = c.sb("c_rkk", [128, 512], F32)
        ps1 = c.ps("ps_a0"); ps2 = c.ps("ps_a1"); ps3 = c.ps("ps_b0")
        yv = yT.rearrange("(r kc p) t -> r p kc t", p=128, kc=KC)
        ysv = _fm(ysT)
        for hf in range(T // TH):
            for j in range(KC):
                for t0 in range(0, TH, 512):
                    tg = hf * TH + t0
                    bi = c.rot("cbuf", 2)
                    srcs = {"cy": ysv[:, j, tg:tg + 512], "cr": yv[RWM[0]][:, j, tg:tg + 512], "ck": yv[RWM[8]][:, j, tg:tg + 512],
                            "cv": yv[RWM[2]][:, j, tg:tg + 512], "cg": yv[RWM[5]][:, j, tg:tg + 512]}
                    for nm in names:
                        P.dma("sp", tl[nm][bi][:], srcs[nm], writes=[f"{nm}{bi}"])
                    y, r, km, v, g = (tl[nm][bi] for nm in names)
                    ky, kr, kk_, kv, kg = (f"{nm}{bi}" for nm in names)
                    P.op("pe", lambda e, y=y: e.matmul(ps1[:], bones[:], y[:], start=True, stop=True), reads=[ky, "bones"], writes=["ps_a0"])
                    P.op("dve", lambda e, y=y: e.scalar_tensor_tensor(out=yc[:], in0=ps1[:], scalar=-1.0 / 64, in1=y[:], op0=ALU.mult, op1=ALU.add),
                         reads=["ps_a0", ky], writes=["c_yc"])
                    P.op("act", lambda e: e.activation(out=sq[:], in_=yc[:], func=AF.Square), reads=["c_yc"], writes=["c_sq"])
                    P.op("pe", lambda e: e.matmul(ps2[:], bones[:], sq[:], start=True, stop=True), reads=["c_sq", "bones"], writes=["ps_a1"])
                    P.op("act", lambda e: e.activation(out=rs[:], in_=ps2[:], func=AF.Ln, bias=64e-5, scale=1.0 / 64), reads=["ps_a1"], writes=["c_rs"])
                    P.op("act", lambda e: e.activation(out=rs[:], in_=rs[:], func=AF.Exp, scale=-0.5), reads=["c_rs"], writes=["c_rs"])
                    P.op("dve", lambda e: e.tensor_tensor(out=yc[:], in0=yc[:], in1=rs[:], op=ALU.mult), reads=["c_yc", "c_rs"], writes=["c_yc"])
                    P.op("dve", lambda e, j=j: e.tensor_scalar(out=yc[:], in0=yc[:], scalar1=lnw[:, j:j + 1], scalar2=lnb[:, j:j + 1], op0=ALU.mult, op1=ALU.add),
                         reads=["c_yc", "lnw", "lnb"], writes=["c_yc"])
                    P.op("dve", lambda e, j=j, r=r, km=km: e.scalar_tensor_tensor(out=rkk[:], in0=r[:], scalar=rk[:, j:j + 1], in1=km[:], op0=ALU.mult, op1=ALU.mult),
                         reads=[kr, kk_, "rk"], writes=["c_rkk"])
                    P.op("pe", lambda e: e.matmul(ps3[:], bones[:], rkk[:], start=True, stop=True), reads=["c_rkk", "bones"], writes=["ps_b0"])
                    P.op("dve", lambda e, v=v: e.tensor_tensor(out=rkk[:], in0=ps3[:], in1=v[:], op=ALU.mult), reads=["ps_b0", kv], writes=["c_rkk"])
                    P.op("dve", lambda e: e.tensor_tensor(out=yc[:], in0=yc[:], in1=rkk[:], op=ALU.add), reads=["c_yc", "c_rkk"], writes=["c_yc"])
                    P.op("dve", lambda e, j=j, t0=t0, g=g: e.tensor_tensor(out=zT[:, j, t0:t0 + 512], in0=yc[:], in1=g[:], op=ALU.mult), reads=["c_yc", kg], writes=["zT"])
            emit_outproj(c, w_out.rearrange("(c p) n -> p c n", p=128), KC, zT, ["zT"], _fm(xT), _fm(oT), hf * TH, TH, 1.0, final)


TF = SEQ


def build_fused():
    nc = _new_nc()
    A = {}

    def din(name, shape, dt=F32):
        A[name] = _din(nc, name, shape, dt)
        return A[name]
    xT = din("xT", [D, TF]); memT = din("memT", [D, ML]); posb = din("posb", [128, TF], I32)
    din("ffn_gain", [4, 2, 128, KC]); din("ffn_w_gate", [4, 2, D, FF]); din("ffn_w_up", [4, 2, D, FF]); din("ffn_w_down", [4, 2, FF, D])
    din("mix_gain", [4, 128, KC]); din("xg", [4, 128, KC]); din("mg", [4, 128, KC]); din("xq", [4, 128, 4]); din("xk", [4, 128, 4])
    din("xattn_wq", [4, D, D]); din("xattn_wkv", [4, D, 2 * D]); din("xattn_wo", [4, D, D])
    din("conv_w_in", [1, D, 3 * D]); din("conv_cw", [128, KC * 3]); din("conv_w_out", [1, D, D])
    din("dil_w_qkv", [1, D, 9216]); din("dil_gq", [128, 3]); din("dil_gk", [128, 3]); din("dil_w_out", [1, 1024, D])
    din("hgrn_w_in", [1, D, 4 * D]); din("hgrn_lbl", [128, 64]); din("hgrn_ng", [HC, 128]); din("hgrn_w_out", [1, D, D])
    din("rwkv_mu", [128, 6 * KC]); din("rwkv_w_rkv", [1, 3, D, D])
    for nm in ("w0", "a0", "k_k", "k_a", "lnw", "lnb", "r_k"):
        din("rwkv_" + nm, [128, KC])
    din("rwkv_w1", [1, D, 96]); din("rwkv_w2", [1, 96, D]); din("rwkv_a1", [1, D, 96]); din("rwkv_a2", [1, 96, D])
    din("rwkv_g1", [1, D, 256]); din("rwkv_g2", [1, 256, D]); din("rwkv_w_out", [1, D, D])
    din("c_invf", [128, 1]); din("c_rm", [128, 128]); din("c_mask", [128, 256]); din("c_ident", [128, 128])
    din("c_cm", [128, SEQ]); din("c_tm", [HC, HC]); din("c_bones", [128, 128]); din("c_sel", [32, 16 * 128])
    oT = _dout(nc, "oT", [D, TF])

    def scratch(name, shape):
        return nc.dram_tensor(name, list(shape), F32, kind="Internal").ap()
    xa = scratch("scr_xa", [D, TF]); xb = scratch("scr_xb", [D, TF])
    yTs = scratch("scr_y", [RW_ROWS * D, TF]); yQs = scratch("scr_q", [9216, TF]); sTs = scratch("scr_s", [D, TF]); ysT = scratch("scr_ys", [D, TF])
    rw_tm = scratch("scr_tm", [5, TF, 1024])

    with contextlib.ExitStack() as stack:
        P = Prog(nc, stack)
        state = {"k": 0}

        def stage(fn, last=False):
            k = state["k"]
            state["k"] += 1
            with contextlib.ExitStack() as st:
                c = Ctx(nc, st, P, pfx=f"s{k}_")
                fn(c, st, f"s{k}_")
                P.barrier()
                if last:
                    P.emit()
                else:
                    P.flush()

        cur = xT
        bufs = [xa, xb]
        nb = 0

        def nxt():
            nonlocal nb
            b = bufs[nb % 2]
            nb += 1
            return b

        for i in range(4):
            dst = nxt()
            stage(lambda c, st, pf, cur=cur, dst=dst, i=i: emit_ffn(nc, st, P, cur, A["ffn_gain"][i, 0], A["ffn_w_gate"][i, 0], A["ffn_w_up"][i, 0],
                                                                      A["ffn_w_down"][i, 0], dst, TF, pf, final=False))
            cur = dst
            dst = nxt()
            mg = A["mix_gain"][i]
            if i == 0:
                stage(lambda c, st, pf, cur=cur, dst=dst: emit_conv(c, cur, mg, A["conv_w_in"][0], A["conv_cw"], A["conv_w_out"][0], dst, TF, False))
            elif i == 1:
                stage(lambda c, st, pf, cur=cur: emit_normproj(c, cur, mg, A["dil_w_qkv"][0], yQs, 9216, TF))
                stage(lambda c, st, pf: emit_dilcore(c, yQs, posb, A["c_invf"], A["c_rm"], A["c_mask"], A["c_ident"], A["dil_gq"], A["dil_gk"],
                                                     sTs[0:1024, :], 8, False))
                stage(lambda c, st, pf, cur=cur, dst=dst: emit_outproj_stage(c, cur, sTs[0:1024, :], A["dil_w_out"][0], dst, 8, TF))
            elif i == 2:
                stage(lambda c, st, pf, cur=cur: emit_normproj(c, cur, mg, A["hgrn_w_in"][0], yQs[0:8192, :], 8192, TF))
                stage(lambda c, st, pf: emit_hgrncore(c, yQs[0:8192, :], A["hgrn_lbl"], A["hgrn_ng"], A["c_cm"], A["c_tm"], A["c_ident"], sTs, 16, 2, False))
                stage(lambda c, st, pf, cur=cur, dst=dst: emit_outproj_stage(c, cur, sTs, A["hgrn_w_out"][0], dst, 16, TF))
            else:
                stage(lambda c, st, pf, cur=cur: emit_rwkvA(c, cur, mg, A["rwkv_mu"], A["rwkv_w_rkv"][0], A["rwkv_w0"], A["rwkv_w1"][0], A["rwkv_w2"][0],
                                                            A["rwkv_a0"], A["rwkv_a1"][0], A["rwkv_a2"][0], A["rwkv_g1"][0], A["rwkv_g2"][0],
                                                            A["rwkv_k_k"], A["rwkv_k_a"], A["c_bones"], yTs, TF, False))
                stage(lambda c, st, pf: emit_rwkvB(c, yTs, A["c_sel"], A["c_ident"], rw_tm, ysT, TF, 2, False))
                stage(lambda c, st, pf, cur=cur, dst=dst: emit_rwkvC(c, cur, ysT, yTs, A["rwkv_lnw"], A["rwkv_lnb"], A["rwkv_r_k"], A["c_bones"],
                                                                     A["rwkv_w_out"][0], dst, TF, False))
            cur = dst
            dst = nxt()
            stage(lambda c, st, pf, cur=cur, dst=dst, i=i: emit_xattn(c, cur, memT, A["xg"][i], A["mg"][i], A["xq"][i], A["xk"][i],
                                                                       A["xattn_wq"][i], A["xattn_wkv"][i], A["xattn_wo"][i], dst, TF, False))
            cur = dst
            last = (i == 3)
            dst = oT if last else nxt()
            stage(lambda c, st, pf, cur=cur, dst=dst, i=i, last=last: emit_ffn(nc, st, P, cur, A["ffn_gain"][i, 1], A["ffn_w_gate"][i, 1], A["ffn_w_up"][i, 1],
                                                                                A["ffn_w_down"][i, 1], dst, TF, pf, final=last), last=last)
            cur = dst
    return nc


_NC_CACHE = {}


def _pc(v):
    return np.ascontiguousarray(np.asarray(v, np.float32).reshape(-1, 128).T)


def _c(a):
    return np.ascontiguousarray(a)


def kernel(**inp):
    inp = {k: np.asarray(v) for k, v in inp.items()}
    x = inp["x"]
    B, S, _ = x.shape
    if "fused" not in _NC_CACHE:
        _NC_CACHE["fused"] = build_fused()
    nc = _NC_CACHE["fused"]
    invf, rm, mask = dil_consts()
    cm, tm, ident = hgrn_consts()
    bones, sel = rwkv_consts()
    shared = {
        "ffn_gain": _c(np.stack([np.stack([_pc(inp["ffn_norm"][i, j]) for j in range(2)]) for i in range(4)])),
        "ffn_w_gate": inp["ffn_w_gate"], "ffn_w_up": inp["ffn_w_up"], "ffn_w_down": inp["ffn_w_down"],
        "mix_gain": _c(np.stack([_pc(inp["mix_norm"][i]) for i in range(4)])),
        "xg": _c(np.stack([_pc(inp["xattn_norm"][i]) for i in range(4)])),
        "mg": _c(np.stack([_pc(inp["mem_norm"][i]) for i in range(4)])),
        "xq": _c(np.stack([_pc(inp["xattn_q_gain"][i]) for i in range(4)])),
        "xk": _c(np.stack([_pc(inp["xattn_k_gain"][i]) for i in range(4)])),
        "xattn_wq": inp["xattn_wq"], "xattn_wkv": inp["xattn_wkv"], "xattn_wo": inp["xattn_wo"],
        "conv_w_in": inp["conv_w_in"], "conv_w_out": inp["conv_w_out"],
        "conv_cw": _c(inp["conv_w"][0].T.reshape(16, 128, 3).transpose(1, 0, 2).reshape(128, 48)),
        "dil_w_qkv": inp["dil_w_qkv"], "dil_w_out": inp["dil_w_out"],
        "dil_gq": _c(inp["dil_q_gain"][0].T), "dil_gk": _c(inp["dil_k_gain"][0].T),
        "hgrn_w_in": inp["hgrn_w_in"], "hgrn_w_out": inp["hgrn_w_out"],
        "hgrn_lbl": _c(inp["hgrn_lb_logits"].reshape(4, 16, 128).transpose(2, 1, 0).reshape(128, 64)),
        "hgrn_ng": _c(np.tile(inp["hgrn_norm"][0][None], (HC, 1))),
        "rwkv_mu": _c(np.concatenate([_pc(inp["rwkv_mu"][0][i]) for i in range(6)], axis=1)),
        "rwkv_w_rkv": inp["rwkv_w_rkv"],
        "rwkv_w0": _pc(inp["rwkv_w0"][0]), "rwkv_a0": _pc(inp["rwkv_a0"][0]), "rwkv_k_k": _pc(inp["rwkv_k_k"][0]),
        "rwkv_k_a": _pc(inp["rwkv_k_a"][0]), "rwkv_lnw": _pc(inp["rwkv_ln_w"][0]), "rwkv_lnb": _pc(inp["rwkv_ln_b"][0]),
        "rwkv_r_k": _pc(inp["rwkv_r_k"][0].reshape(-1)),
        "rwkv_w1": inp["rwkv_w1"], "rwkv_w2": inp["rwkv_w2"], "rwkv_a1": inp["rwkv_a1"], "rwkv_a2": inp["rwkv_a2"],
        "rwkv_g1": inp["rwkv_g1"], "rwkv_g2": inp["rwkv_g2"], "rwkv_w_out": inp["rwkv_w_out"],
        "c_invf": invf, "c_rm": rm, "c_mask": mask, "c_ident": ident, "c_cm": cm, "c_tm": tm, "c_bones": bones, "c_sel": sel,
    }
    shared = {k: _c(np.asarray(v, np.float32)) for k, v in shared.items()}
    in_maps = []
    for b in range(B):
        m = dict(shared)
        m["xT"] = _c(x[b].T)
        m["memT"] = _c(inp["mem"][b].T)
        m["posb"] = _c(np.tile(inp["positions"][b][None].astype(np.int32), (128, 1)))
        in_maps.append(m)
    res = run_bass_kernel_spmd(nc, in_maps, core_ids=list(range(B)))
    out = np.empty((B, S, D), np.float32)
    for b in range(B):
        out[b] = res.results[b]["oT"].T
    return out
```

```python
import contextlib
import numpy as np
import concourse.bass as bass
import concourse.mybir as mybir
from concourse.bass_utils import run_bass_kernel_spmd

F32 = mybir.dt.float32
BF16 = mybir.dt.bfloat16
I32 = mybir.dt.int32
AF = mybir.ActivationFunctionType
ALU = mybir.AluOpType
AX = mybir.AxisListType

D = 2048
KC = D // 128
FF = 5632
FC = FF // 128
NCORES = 8
EPS = 1e-6


DEFAULT_NOSYNC = ()


class Prog:
    ENGS = ("pe", "act", "dve", "pool", "sp")
    NRING = 6

    def __init__(self, nc, stack):
        self.nc = nc
        self.stack = stack
        self.streams = {e: [] for e in self.ENGS}
        self.count = {e: 0 for e in self.ENGS}
        self.sems = {}
        for e in ("pe", "act", "dve", "pool"):
            self.sems[e] = stack.enter_context(nc.semaphore("s_" + e))
        self.rings = {}
        self.dma_k = {}
        for q in ("sp", "pool", "act"):
            self.rings[q] = [stack.enter_context(nc.semaphore(f"r_{q}{i}")) for i in range(self.NRING)]
            self.dma_k[q] = 0
        self.seen = {e: {} for e in self.ENGS}
        self.nosync_self = set(DEFAULT_NOSYNC)
        self.res = {}
        self.final_events = []

    def _need(self, eng, ev, waits):
        if ev is None:
            return
        sem, val = ev
        if eng in self.nosync_self and eng in self.sems and sem is self.sems[eng]:
            return
        key = id(sem)
        if self.seen[eng].get(key, 0) >= val:
            return
        self.seen[eng][key] = val
        waits.append((sem, val))

    def _deps(self, eng, reads, writes):
        waits = []
        for r in reads:
            st = self.res.get(r)
            if st is not None:
                self._need(eng, st["w"], waits)
        for w in writes:
            st = self.res.get(w)
            if st is not None:
                self._need(eng, st["w"], waits)
                for ev in st["r"].values():
                    self._need(eng, ev, waits)
        return waits

    def _commit(self, ev, reads, writes):
        for r in reads:
            st = self.res.setdefault(r, {"w": None, "r": {}})
            st["r"][id(ev[0])] = ev
        for w in writes:
            self.res[w] = {"w": ev, "r": {}}

    def op(self, eng, fn, reads=(), writes=()):
        waits = self._deps(eng, reads, writes)
        self.count[eng] += 1
        ev = (self.sems[eng], self.count[eng])
        self.streams[eng].append((waits, fn, (self.sems[eng], 1)))
        self._commit(ev, reads, writes)
        return ev

    def dma(self, q, out, in_, reads=(), writes=(), final=False):
        waits = self._deps(q, reads, writes)
        k = self.dma_k[q]
        self.dma_k[q] += 1
        sem = self.rings[q][k % self.NRING]
        gen = k // self.NRING
        if gen > 0:
            self._need(q, (sem, 16 * gen), waits)
        ev = (sem, 16 * (gen + 1))

        def fn(e, out=out, in_=in_):
            return e.dma_start(out=out, in_=in_, allow_slow_non_contiguous=True)

        self.streams[q].append((waits, fn, (sem, 16)))
        self._commit(ev, reads, writes)
        if final:
            self.final_events.append(ev)
        return ev

    def barrier(self):
        evs = []
        for e in ("pe", "act", "dve", "pool"):
            if self.count[e] > 0:
                evs.append((self.sems[e], self.count[e]))
        for q in ("sp", "pool", "act"):
            k = self.dma_k[q]
            for i in range(self.NRING):
                n = (k - i + self.NRING - 1) // self.NRING if k > i else 0
                if n > 0:
                    evs.append((self.rings[q][i], 16 * n))
        for eng in self.ENGS:
            waits = []
            for ev in evs:
                self._need(eng, ev, waits)
            if waits:
                self.streams[eng].append((waits, None, None))
        self.res = {}

    def emit(self):
        fw = []
        for ev in self.final_events:
            self._need("sp", ev, fw)
        self.streams["sp"].append((fw, None, None))
        self.flush()

    def flush(self):
        nc = self.nc
        with nc.Block() as block:
            def run(eng_obj, name):
                for waits, fn, inc in self.streams[name]:
                    for sem, val in waits:
                        eng_obj.wait_ge(sem, val)
                    if fn is not None:
                        ins = fn(eng_obj)
                        ins.then_inc(inc[0], inc[1])

            @block.tensor
            def _(e):
                run(e, "pe")

            @block.scalar
            def _(e):
                run(e, "act")

            @block.vector
            def _(e):
                run(e, "dve")

            @block.gpsimd
            def _(e):
                run(e, "pool")

            @block.sync
            def _(e):
                run(e, "sp")
        self.streams = {e: [] for e in self.ENGS}


def _sb(nc, stack, name, shape, dt):
    return stack.enter_context(nc.sbuf_tensor("t_" + name, list(shape), dt))


def _ps(nc, stack, name, shape, dt=F32):
    return stack.enter_context(nc.psum_tensor("t_" + name, list(shape), dt))


class Ctx:
    def __init__(self, nc, stack, P, pfx=""):
        self.nc, self.stack, self.P, self.pfx = nc, stack, P, pfx
        self.tiles = {}
        self.cnt = {}
        self.nxb = 1
        self.pwg = 1

    def sb(self, name, shape, dt):
        if name not in self.tiles:
            self.tiles[name] = _sb(self.nc, self.stack, self.pfx + name, shape, dt)
        return self.tiles[name]

    def ps(self, name, shape=(128, 512), dt=F32):
        if name not in self.tiles:
            self.tiles[name] = _ps(self.nc, self.stack, self.pfx + name, shape, dt)
        return self.tiles[name]

    def rot(self, name, n):
        k = self.cnt.get(name, 0)
        self.cnt[name] = k + 1
        return k % n

    def const_ones(self):
        if "ones32" not in self.tiles:
            t = self.sb("ones32", [128, 128], F32)
            self.P.op("pool", lambda e: e.memset(t[:], 1.0), writes=["ones32"])
            tb = self.sb("ones16", [128, 128], BF16)
            self.P.op("pool", lambda e: e.memset(tb[:], 1.0), writes=["ones16"])
        return self.tiles["ones32"], self.tiles["ones16"]


def emit_norm(c, src, ntok, gn, xn, xoff, xn_key, eps=EPS, scale_d=1.0 / D, kcs=KC, sumsq_ones=None, ones_key="ones32", gn_key="gn"):
    P = c.P
    ones32, _ = c.const_ones()
    if sumsq_ones is None:
        sumsq_ones = ones32
    TN = 256
    xins = [c.sb(f"n_xin{i}", [128, KC, TN], F32) for i in range(c.nxb)]
    sqs = [c.sb(f"n_sq{i}", [128, TN], F32) for i in range(2)]
    rstd = c.sb("n_rstd", [128, TN], F32)
    psn = c.ps("ps_n", [128, 512])
    for a in range(0, ntok, TN):
        n = min(TN, ntok - a)
        xb = c.rot("n_xin", c.nxb)
        xin = xins[xb]
        xkey = f"n_xin{xb}"
        P.dma("sp", xin[:, 0:kcs, 0:n], src[:, :, a:a + n], writes=[xkey])
        for kc in range(kcs):
            si = c.rot("n_sq", 2)
            s = sqs[si]
            P.op("act", lambda e, s=s, kc=kc, n=n, xin=xin: e.activation(out=s[:, 0:n], in_=xin[:, kc, 0:n], func=AF.Square),
                 reads=[xkey], writes=[f"n_sq{si}"])
            P.op("pe", lambda e, s=s, kc=kc, n=n: e.matmul(psn[:, 0:n], sumsq_ones[:], s[:, 0:n], start=(kc == 0), stop=(kc == kcs - 1)),
                 reads=[f"n_sq{si}", ones_key], writes=["ps_n"])
        P.op("act", lambda e, n=n: e.activation(out=rstd[:, 0:n], in_=psn[:, 0:n], func=AF.Ln, bias=eps, scale=scale_d),
             reads=["ps_n"], writes=["n_rstd"])
        P.op("act", lambda e, n=n: e.activation(out=rstd[:, 0:n], in_=rstd[:, 0:n], func=AF.Exp, scale=-0.5), reads=["n_rstd"], writes=["n_rstd"])
        for kc in range(kcs):
            P.op("dve", lambda e, kc=kc, a=a, n=n, xin=xin: e.scalar_tensor_tensor(
                out=xn[:, kc, xoff + a:xoff + a + n], in0=xin[:, kc, 0:n], scalar=gn[:, kc:kc + 1],
                in1=rstd[:, 0:n], op0=ALU.mult, op1=ALU.mult),
                reads=[xkey, "n_rstd", gn_key], writes=[xn_key])


def emit_proj(c, wv, col0, nchunks, xn, xn_keys, ntok, cb, kcs=KC, cw=128, toff=0):
    P = c.P
    G = c.pwg if (cw == 128 and nchunks % c.pwg == 0) else 1
    wb = [c.sb(f"p_w{i}", [128, KC, 128 * c.pwg], BF16) for i in range(2)]
    pss = [c.ps(f"ps_a{i}") for i in range(2)]
    for j0 in range(0, nchunks, G):
        b = c.rot("p_w", 2)
        P.dma("pool", wb[b][:, 0:kcs, 0:cw * G], wv[:, :, col0 + j0 * cw: col0 + (j0 + G) * cw], writes=[f"p_w{b}"])
        for jj in range(G):
            j = j0 + jj
            for t0 in range(0, ntok, 512):
                n = min(512, ntok - t0)
                pi = c.rot("ps_a", 2)
                ps = pss[pi]

                def mm(e, b=b, ps=ps, t0=t0, n=n, jj=jj):
                    ins = None
                    for kc in range(kcs):
                        ins = e.matmul(ps[0:cw, 0:n], wb[b][:, kc, jj * cw:(jj + 1) * cw], xn[:, kc, toff + t0:toff + t0 + n],
                                       start=(kc == 0), stop=(kc == kcs - 1))
                    return ins
                P.op("pe", mm, reads=[f"p_w{b}"] + list(xn_keys), writes=[f"ps_a{pi}"])
                cb(j, t0, n, ps, f"ps_a{pi}")


def emit_outproj(c, wv, cc, src, src_keys, xTv, oTv, t0g, ntok, scale, final, soff=0):
    P = c.P
    wb = [c.sb(f"o_w{cc}_{i}", [128, cc, 128], BF16) for i in range(2)]
    pss = [c.ps(f"ps_c{i}") for i in range(2)]
    xres = [c.sb(f"o_xres{i}", [128, 512], F32) for i in range(2)]
    osb = [c.sb(f"o_osb{i}", [128, 512], F32) for i in range(2)]
    for nn in range(KC):
        b = c.rot("o_w", 2)
        P.dma("pool", wb[b][:, 0:cc, :], wv[:, :, nn * 128:(nn + 1) * 128], writes=[f"o_w{cc}_{b}"])
        for t0 in range(0, ntok, 512):
            n = min(512, ntok - t0)
            pb = c.rot("ps_c", 2)
            P.dma("sp", xres[pb][:, 0:n], xTv[:, nn, t0g + t0:t0g + t0 + n], writes=[f"o_xres{pb}"])

            def mm(e, b=b, pb=pb, t0=t0, n=n):
                ins = None
                for f in range(cc):
                    ins = e.matmul(pss[pb][:, 0:n], wb[b][:, f, :], src[:, f, soff + t0:soff + t0 + n],
                                   start=(f == 0), stop=(f == cc - 1))
                return ins
            P.op("pe", mm, reads=[f"o_w{cc}_{b}"] + list(src_keys), writes=[f"ps_c{pb}"])
            P.op("dve", lambda e, pb=pb, n=n: e.scalar_tensor_tensor(
                out=osb[pb][:, 0:n], in0=pss[pb][:, 0:n], scalar=float(scale), in1=xres[pb][:, 0:n],
                op0=ALU.mult, op1=ALU.add),
                reads=[f"ps_c{pb}", f"o_xres{pb}"], writes=[f"o_osb{pb}"])
            P.dma("sp", oTv[:, nn, t0g + t0:t0g + t0 + n], osb[pb][:, 0:n], reads=[f"o_osb{pb}"], final=final)


def _fm(ap):
    return ap.rearrange("(kc p) t -> p kc t", p=128)


def _new_nc():
    return bass.Bass("TRN2", target_bir_lowering=False)


def _din(nc, name, shape, dt=F32):
    return nc.dram_tensor(name, list(shape), dt, kind="ExternalInput").ap()


def _dout(nc, name, shape, dt=F32):
    return nc.dram_tensor(name, list(shape), dt, kind="ExternalOutput").ap()


def emit_normproj(c, xT, gain, w, yT, N, T, final=False):
    P = c.P
    c.nxb, c.pwg = 2, 2
    gn = c.sb("gn", [128, KC], F32)
    P.dma("sp", gn[:], gain, writes=["gn"])
    TB = min(T, 2048)
    xn = c.sb("xn", [128, KC, TB], BF16)
    ysb = [c.sb(f"ysb{i}", [128, 512], F32) for i in range(2)]
    yv = _fm(yT)
    for tb in range(0, T, TB):
        emit_norm(c, _fm(xT)[:, :, tb:tb + TB], TB, gn, xn, 0, "xn")

        def cb(j, t0, n, ps, psk, tb=tb):
            i = c.rot("ysb", 2)
            P.op("act", lambda e: e.copy(out=ysb[i][:, 0:n], in_=ps[:, 0:n]), reads=[psk], writes=[f"ysb{i}"])
            P.dma("sp", yv[:, j, tb + t0:tb + t0 + n], ysb[i][:, 0:n], reads=[f"ysb{i}"], final=final)
        emit_proj(c, _fm(w), 0, N // 128, xn, ["xn"], TB, cb)


def build_normproj(N, T=2048):
    nc = _new_nc()
    xT = _din(nc, "xT", [D, T]); gain = _din(nc, "gain", [128, KC]); w = _din(nc, "w", [D, N])
    yT = _dout(nc, "yT", [N, T])
    with contextlib.ExitStack() as stack:
        P = Prog(nc, stack); c = Ctx(nc, stack, P)
        emit_normproj(c, xT, gain, w, yT, N, T, final=True)
        P.emit()
    return nc


def emit_outproj_stage(c, xT, sT, w, oT, CC, T, final=False):
    P = c.P
    TB = min(T, 2048)
    src = c.sb("src", [128, CC, TB], BF16)
    sv = sT.rearrange("(c p) t -> p c t", p=128)
    for tb in range(0, T, TB):
        for cc in range(CC):
            P.dma("pool", src[:, cc, :], sv[:, cc, tb:tb + TB], writes=["src"])
        emit_outproj(c, w.rearrange("(c p) n -> p c n", p=128), CC, src, ["src"], _fm(xT), _fm(oT), tb, TB, 1.0, final)


def build_outproj(CC, T=2048):
    nc = _new_nc()
    xT = _din(nc, "xT", [D, T]); sT = _din(nc, "sT", [CC * 128, T]); w = _din(nc, "w", [CC * 128, D])
    oT = _dout(nc, "oT", [D, T])
    with contextlib.ExitStack() as stack:
        P = Prog(nc, stack); c = Ctx(nc, stack, P)
        emit_outproj_stage(c, xT, sT, w, oT, CC, T, final=True)
        P.emit()
    return nc
def build_ffn(T=2048):
    nc = bass.Bass("TRN2", target_bir_lowering=False)
    xT = nc.dram_tensor("xT", [D, T], F32, kind="ExternalInput").ap()
    gain = nc.dram_tensor("gain", [128, KC], F32, kind="ExternalInput").ap()
    wg = nc.dram_tensor("wg", [D, FF], F32, kind="ExternalInput").ap()
    wu = nc.dram_tensor("wu", [D, FF], F32, kind="ExternalInput").ap()
    wd = nc.dram_tensor("wd", [FF, D], F32, kind="ExternalInput").ap()
    oT = nc.dram_tensor("oT", [D, T], F32, kind="ExternalOutput").ap()
    with contextlib.ExitStack() as stack:
        P = Prog(nc, stack)
        emit_ffn(nc, stack, P, xT, gain, wg, wu, wd, oT, T, "f")
        P.emit()
    return nc


def emit_ffn(nc, stack, P, xT, gain, wg, wu, wd, oT, T, pfx, final=True):
    TH = 1024
    NH = T // TH
    TN = 256
    TT = 512
    FG = 2
    xTv = xT.rearrange("(kc p) t -> p kc t", p=128)
    oTv = oT.rearrange("(kc p) t -> p kc t", p=128)
    wgv = wg.rearrange("(kc p) f -> p kc f", p=128)
    wuv = wu.rearrange("(kc p) f -> p kc f", p=128)
    wdv = wd.rearrange("(fc p) n -> p fc n", p=128)

    act = _sb(nc, stack, pfx + "act", [128, FC, TH], BF16)
    xn = _sb(nc, stack, pfx + "xn", [128, KC, TH], BF16)
    wgb = [_sb(nc, stack, pfx + f"wg{i}", [128, KC, FG * 128], BF16) for i in range(2)]
    wub = [_sb(nc, stack, pfx + f"wu{i}", [128, KC, FG * 128], BF16) for i in range(2)]
    wdb = [_sb(nc, stack, pfx + f"wd{i}", [128, FC, 128], BF16) for i in range(2)]
    xin = _sb(nc, stack, pfx + "xin", [128, KC, TN], F32)
    sq = [_sb(nc, stack, pfx + f"sq{i}", [128, TN], F32) for i in range(2)]
    rstd = _sb(nc, stack, pfx + "rstd", [128, TN], F32)
    ones = _sb(nc, stack, pfx + "ones", [128, 128], F32)
    gn = _sb(nc, stack, pfx + "gn", [128, KC], F32)
    sil = [_sb(nc, stack, pfx + f"sil{i}", [128, TT], F32) for i in range(2)]
    xres = [_sb(nc, stack, pfx + f"xres{i}", [128, TT], F32) for i in range(2)]
    osb = [_sb(nc, stack, pfx + f"osb{i}", [128, TT], F32) for i in range(2)]
    ps_n = _ps(nc, stack, pfx + "psn", [128, TN])
    ps_g = [_ps(nc, stack, pfx + f"psg{i}", [128, TT]) for i in range(2)]
    ps_u = [_ps(nc, stack, pfx + f"psu{i}", [128, TT]) for i in range(2)]
    ps_o = [_ps(nc, stack, pfx + f"pso{i}", [128, TT]) for i in range(2)]

    K = lambda *a: (pfx,) + a
    P.op("pool", lambda e: e.memset(ones[:], 1.0), writes=[K("ones")])
    P.dma("sp", gn[:], gain, writes=[K("gn")])

    cnt = {"gi": 0, "di": 0, "ei": 0, "si": 0}
    NTT = TH // TN
    xn_keys = [K("xn", nt) for nt in range(NTT)]
    act_keys = [K("act", f) for f in range(FC)]

    def norm_tile(h, nt):
        ta = h * TH + nt * TN
        P.dma("sp", xin[:], xTv[:, :, ta:ta + TN], writes=[K("xin")])
        for kc in range(KC):
            si = cnt["si"]
            s = sq[si % 2]
            P.op("act", lambda e, s=s, kc=kc: e.activation(out=s[:], in_=xin[:, kc, :], func=AF.Square),
                 reads=[K("xin")], writes=[K("sq", si % 2)])
            P.op("pe", lambda e, s=s, kc=kc: e.matmul(ps_n[:], ones[:], s[:], start=(kc == 0), stop=(kc == KC - 1)),
                 reads=[K("sq", si % 2), K("ones")], writes=[K("psn")])
            cnt["si"] += 1
        P.op("act", lambda e: e.activation(out=rstd[:], in_=ps_n[:], func=AF.Ln, bias=EPS, scale=1.0 / D),
             reads=[K("psn")], writes=[K("rstd")])
        P.op("act", lambda e: e.activation(out=rstd[:], in_=rstd[:], func=AF.Exp, scale=-0.5), reads=[K("rstd")], writes=[K("rstd")])
        for kc in range(KC):
            P.op("dve", lambda e, kc=kc, nt=nt: e.scalar_tensor_tensor(
                out=xn[:, kc, nt * TN:(nt + 1) * TN], in0=xin[:, kc, :], scalar=gn[:, kc:kc + 1],
                in1=rstd[:], op0=ALU.mult, op1=ALU.mult),
                reads=[K("xin"), K("rstd"), K("gn")], writes=[K("xn", nt)])

    def gate_up(h):
        for fg in range(FC // FG):
            b = cnt["gi"] % 2
            f0 = fg * FG * 128
            P.dma("pool", wgb[b][:], wgv[:, :, f0:f0 + FG * 128], writes=[K("wg", b)])
            P.dma("pool", wub[b][:], wuv[:, :, f0:f0 + FG * 128], writes=[K("wu", b)])
            for fc in range(FG):
                f = fg * FG + fc
                for tt in range(TH // TT):
                    pb = cnt["ei"] % 2

                    def mm(e, wt, pt, fc=fc, tt=tt):
                        ins = None
                        for kc in range(KC):
                            ins = e.matmul(pt[:], wt[:, kc, fc * 128:(fc + 1) * 128],
                                           xn[:, kc, tt * TT:(tt + 1) * TT],
                                           start=(kc == 0), stop=(kc == KC - 1))
                        return ins
                    P.op("pe", lambda e, b=b, pb=pb, mm=mm: mm(e, wgb[b], ps_g[pb]),
                         reads=[K("wg", b)] + xn_keys, writes=[K("psg", pb)])
                    P.op("pe", lambda e, b=b, pb=pb, mm=mm: mm(e, wub[b], ps_u[pb]),
                         reads=[K("wu", b)] + xn_keys, writes=[K("psu", pb)])
                    P.op("act", lambda e, pb=pb: e.activation(out=sil[pb][:], in_=ps_g[pb][:], func=AF.Silu),
                         reads=[K("psg", pb)], writes=[K("sil", pb)])
                    P.op("dve", lambda e, pb=pb, f=f, tt=tt: e.tensor_tensor(
                        out=act[:, f, tt * TT:(tt + 1) * TT], in0=sil[pb][:], in1=ps_u[pb][:], op=ALU.mult),
                        reads=[K("sil", pb), K("psu", pb)], writes=[K("act", f)])
                    cnt["ei"] += 1
            cnt["gi"] += 1

    def down_chunk(h, n):
        t0 = h * TH
        b = cnt["di"] % 2
        P.dma("pool", wdb[b][:], wdv[:, :, n * 128:(n + 1) * 128], writes=[K("wd", b)])
        for tt in range(TH // TT):
            pb = cnt["ei"] % 2
            ta = t0 + tt * TT
            P.dma("sp", xres[pb][:], xTv[:, n, ta:ta + TT], writes=[K("xres", pb)])

            def mmd(e, b=b, pb=pb, tt=tt):
                ins = None
                for f in range(FC):
                    ins = e.matmul(ps_o[pb][:], wdb[b][:, f, :], act[:, f, tt * TT:(tt + 1) * TT],
                                   start=(f == 0), stop=(f == FC - 1))
                return ins
            P.op("pe", mmd, reads=[K("wd", b)] + act_keys, writes=[K("pso", pb)])
            P.op("dve", lambda e, pb=pb: e.scalar_tensor_tensor(
                out=osb[pb][:], in0=ps_o[pb][:], scalar=0.5, in1=xres[pb][:], op0=ALU.mult, op1=ALU.add),
                reads=[K("pso", pb), K("xres", pb)], writes=[K("osb", pb)])
            P.dma("sp", oTv[:, n, ta:ta + TT], osb[pb][:], reads=[K("osb", pb)], final=final)
            cnt["ei"] += 1
        cnt["di"] += 1

    for nt in range(NTT):
        norm_tile(0, nt)
    for h in range(NH):
        gate_up(h)
        per = KC // NTT
        for n in range(KC):
            down_chunk(h, n)
            if h + 1 < NH and n % per == per - 1:
                norm_tile(h + 1, n // per)
XH, XD, ML = 4, 512, 256


def build_xattn(T=2048):
    nc = _new_nc()
    xT = _din(nc, "xT", [D, T]); memT = _din(nc, "memT", [D, ML])
    gx = _din(nc, "gx", [128, KC]); gm = _din(nc, "gm", [128, KC])
    gq = _din(nc, "gq", [128, 4]); gk = _din(nc, "gk", [128, 4])
    wq = _din(nc, "wq", [D, D]); wkv = _din(nc, "wkv", [D, 2 * D]); wo = _din(nc, "wo", [D, D])
    oT = _dout(nc, "oT", [D, T])
    with contextlib.ExitStack() as stack:
        P = Prog(nc, stack); c = Ctx(nc, stack, P)
        emit_xattn(c, xT, memT, gx, gm, gq, gk, wq, wkv, wo, oT, T, True)
        P.emit()
    return nc


def emit_xattn(c, xT, memT, gx, gm, gq, gk, wq, wkv, wo, oT, T, final):
    P = c.P
    c.pwg = 2
    if True:
        ones32, ones16 = c.const_ones()
        gxt = c.sb("gx", [128, KC], F32); gmt = c.sb("gm", [128, KC], F32)
        gqt = c.sb("gq", [128, 4], F32); gkt = c.sb("gk", [128, 4], F32)
        P.dma("sp", gxt[:], gx, writes=["gx"]); P.dma("sp", gmt[:], gm, writes=["gm"])
        P.dma("sp", gqt[:], gq, writes=["gq"]); P.dma("sp", gkt[:], gk, writes=["gk"])
        memn = c.sb("memn", [128, KC, ML], BF16)
        emit_norm(c, _fm(memT), ML, gmt, memn, 0, "memn", gn_key="gm")
        kT = c.sb("kT", [128, KC, ML], BF16)
        kraw = c.sb("kraw", [128, 4, 512], F32)
        sq = [c.sb(f"x_sq{i}", [128, 512], F32) for i in range(2)]
        rs = c.sb("x_rs", [128, 512], F32)
        psn = c.ps("ps_n")
        scale_h = 1.0 / XD

        def headnorm(raw, rawkey, n, gt, gkey, dst_fn, dstkey):
            for dc in range(4):
                si = c.rot("x_sq", 2)
                P.op("act", lambda e, si=si, dc=dc: e.activation(out=sq[si][:, 0:n], in_=raw[:, dc, 0:n], func=AF.Square),
                     reads=[rawkey], writes=[f"x_sq{si}"])
                P.op("pe", lambda e, si=si, dc=dc: e.matmul(psn[:, 0:n], ones32[:], sq[si][:, 0:n], start=(dc == 0), stop=(dc == 3)),
                     reads=[f"x_sq{si}", "ones32"], writes=["ps_n"])
            P.op("act", lambda e: e.activation(out=rs[:, 0:n], in_=psn[:, 0:n], func=AF.Ln, bias=EPS, scale=scale_h),
                 reads=["ps_n"], writes=["x_rs"])
            P.op("act", lambda e: e.activation(out=rs[:, 0:n], in_=rs[:, 0:n], func=AF.Exp, scale=-0.5), reads=["x_rs"], writes=["x_rs"])
            for dc in range(4):
                P.op("dve", lambda e, dc=dc: e.scalar_tensor_tensor(
                    out=dst_fn(dc), in0=raw[:, dc, 0:n], scalar=gt[:, dc:dc + 1], in1=rs[:, 0:n],
                    op0=ALU.mult, op1=ALU.mult), reads=[rawkey, "x_rs", gkey], writes=[dstkey])

        def cb_k(j, t0, n, ps, psk):
            dc = j % 4
            P.op("act", lambda e: e.copy(out=kraw[:, dc, 0:n], in_=ps[:, 0:n]), reads=[psk], writes=["kraw"])
            if dc == 3:
                h = j // 4
                headnorm(kraw, "kraw", ML, gkt, "gk", lambda dc2: kT[:, h * 4 + dc2, :], "kT")
        emit_proj(c, _fm(wkv), 0, KC, memn, ["memn"], ML, cb_k)
        v_sb = c.sb("v_sb", [128, 2, D], BF16)
        wvb = [c.sb(f"x_wv{i}", [128, KC, 256], BF16) for i in range(2)]
        wkvv = _fm(wkv)
        psb = [c.ps(f"ps_b{i}") for i in range(2)]
        for ct in range(8):
            b = c.rot("x_wv", 2)
            P.dma("pool", wvb[b][:], wkvv[:, :, D + ct * 256: D + (ct + 1) * 256], writes=[f"x_wv{b}"])
            for mc in range(2):
                pi = c.rot("ps_b", 2)

                def mm(e, b=b, pi=pi, mc=mc):
                    ins = None
                    for kc in range(KC):
                        ins = e.matmul(psb[pi][:, 0:256], memn[:, kc, mc * 128:(mc + 1) * 128], wvb[b][:, kc, :],
                                       start=(kc == 0), stop=(kc == KC - 1))
                    return ins
                P.op("pe", mm, reads=[f"x_wv{b}", "memn"], writes=[f"ps_b{pi}"])
                P.op("act", lambda e, pi=pi, mc=mc, ct=ct: e.copy(out=v_sb[:, mc, ct * 256:(ct + 1) * 256], in_=psb[pi][:, 0:256]),
                     reads=[f"ps_b{pi}"], writes=["v_sb"])
        TH = 1024
        xn = c.sb("xn", [128, KC, TH], BF16)
        oall = c.sb("oall", [128, KC, TH], BF16)
        qraw = c.sb("qraw", [128, 4, 1024], F32)
        qns = [c.sb(f"qn{i}", [128, 4, 512], BF16) for i in range(2)]
        Es = [c.sb(f"E{i}", [128, 2, 512], BF16) for i in range(2)]
        rdens = [c.sb(f"rden{i}", [128, 512], F32) for i in range(2)]
        psc = [c.ps(f"ps_c{i}") for i in range(2)]
        sm_scale = float(XD) ** -0.5
        for hf in range(T // TH):
            tg = hf * TH
            emit_norm(c, _fm(xT)[:, :, tg:tg + TH], TH, gxt, xn, 0, "xn", gn_key="gx")

            def attn(h, t0, n):
                bq = (t0 // 512) % 2
                qn, E, rden = qns[bq], Es[bq], rdens[bq]
                kq, kE, kr = f"qn{bq}", f"E{bq}", f"rden{bq}"
                for mc in range(2):
                    pi = c.rot("ps_b", 2)

                    def mm(e, pi=pi, mc=mc):
                        ins = None
                        for dc in range(4):
                            ins = e.matmul(psb[pi][:, 0:n], kT[:, h * 4 + dc, mc * 128:(mc + 1) * 128], qn[:, dc, 0:n],
                                           start=(dc == 0), stop=(dc == 3))
                        return ins
                    P.op("pe", mm, reads=["kT", kq], writes=[f"ps_b{pi}"])
                    P.op("act", lambda e, pi=pi, mc=mc: e.activation(out=E[:, mc, 0:n], in_=psb[pi][:, 0:n], func=AF.Exp, scale=sm_scale),
                         reads=[f"ps_b{pi}"], writes=[kE])

                def mmz(e):
                    ins = None
                    for mc in range(2):
                        ins = e.matmul(psn[:, 0:n], ones16[:], E[:, mc, 0:n], start=(mc == 0), stop=(mc == 1))
                    return ins
                P.op("pe", mmz, reads=[kE, "ones16"], writes=["ps_n"])
                P.op("dve", lambda e: e.reciprocal(out=rden[:, 0:n], in_=psn[:, 0:n]), reads=["ps_n"], writes=[kr])
                for dc in range(4):
                    pi = c.rot("ps_c", 2)

                    def mmo(e, pi=pi, dc=dc):
                        ins = None
                        for mc in range(2):
                            ins = e.matmul(psc[pi][:, 0:n], v_sb[:, mc, h * XD + dc * 128: h * XD + (dc + 1) * 128], E[:, mc, 0:n],
                                           start=(mc == 0), stop=(mc == 1))
                        return ins
                    P.op("pe", mmo, reads=[kE, "v_sb"], writes=[f"ps_c{pi}"])
                    P.op("dve", lambda e, pi=pi, dc=dc: e.tensor_tensor(
                        out=oall[:, h * 4 + dc, t0:t0 + n], in0=psc[pi][:, 0:n], in1=rden[:, 0:n], op=ALU.mult),
                        reads=[f"ps_c{pi}", kr], writes=["oall"])

            def cb_q(h, dc, t0, n, ps, psk):
                P.op("act", lambda e: e.copy(out=qraw[:, dc, t0:t0 + n], in_=ps[:, 0:n]), reads=[psk], writes=[f"qraw{t0}"])
            for h in range(XH):
                def cb2(j, t0, n, ps, psk, h=h):
                    cb_q(h, j, t0, n, ps, psk)
                emit_proj(c, _fm(wq), h * XD, 4, xn, ["xn"], TH, cb2)
                for t0 in range(0, TH, 512):
                    bq = (t0 // 512) % 2
                    headnorm(qraw[:, :, t0:t0 + 512], f"qraw{t0}", 512, gqt, "gq", lambda dc2, bq=bq: qns[bq][:, dc2, 0:512], f"qn{bq}")
                for t0 in range(0, TH, 512):
                    attn(h, t0, 512)
            emit_outproj(c, wo.rearrange("(c p) n -> p c n", p=128), KC, oall, ["oall"], _fm(xT), _fm(oT), tg, TH, 1.0, final)
def build_conv(T=4096):
    nc = _new_nc()
    xT = _din(nc, "xT", [D, T]); gain = _din(nc, "gain", [128, KC])
    w_in = _din(nc, "w_in", [D, 3 * D]); cwd = _din(nc, "cw", [128, KC * 3]); w_out = _din(nc, "w_out", [D, D])
    oT = _dout(nc, "oT", [D, T])
    with contextlib.ExitStack() as stack:
        P = Prog(nc, stack); c = Ctx(nc, stack, P)
        emit_conv(c, xT, gain, w_in, cwd, w_out, oT, T, True)
        P.emit()
    return nc


def emit_conv(c, xT, gain, w_in, cwd, w_out, oT, T, final):
    P = c.P
    c.nxb, c.pwg = 2, 1
    if True:
        gn = c.sb("gn", [128, KC], F32); cw = c.sb("cwt", [128, KC * 3], F32)
        P.dma("sp", gn[:], gain, writes=["gn"]); P.dma("sp", cw[:], cwd, writes=["cwt"])
        TH = 1024
        NE = TH + 2
        xn = c.sb("xn", [128, KC, NE], BF16)
        gT = c.sb("gT", [128, KC, TH], BF16)
        cgs = c.sb("cgs", [128, NE], F32); zb = c.sb("zb", [128, NE], F32)
        bb = c.sb("bb", [128, NE], F32); yb = c.sb("yb", [128, TH], F32)
        xv = _fm(xT)
        for hf in range(T // TH):
            tg = hf * TH
            if hf == 0:
                P.op("pool", lambda e: e.memset(xn[:, :, 0:2], 0.0), writes=["xn"])
                emit_norm(c, xv[:, :, 0:TH], TH, gn, xn, 2, "xn")
            else:
                emit_norm(c, xv[:, :, tg - 2:tg + TH], NE, gn, xn, 0, "xn")
            for j in range(KC):
                def cb_cg(_, t0, n, ps, psk):
                    P.op("act", lambda e: e.copy(out=cgs[:, t0:t0 + n], in_=ps[:, 0:n]), reads=[psk], writes=["cgs"])

                def cb_u(_, t0, n, ps, psk):
                    P.op("dve", lambda e: e.tensor_tensor(out=zb[:, t0:t0 + n], in0=cgs[:, t0:t0 + n], in1=ps[:, 0:n], op=ALU.mult),
                         reads=[psk, "cgs"], writes=["zb"])

                def cb_b(_, t0, n, ps, psk):
                    P.op("act", lambda e: e.copy(out=bb[:, t0:t0 + n], in_=ps[:, 0:n]), reads=[psk], writes=["bb"])
                emit_proj(c, _fm(w_in), D + j * 128, 1, xn, ["xn"], NE, cb_cg)
                emit_proj(c, _fm(w_in), 2 * D + j * 128, 1, xn, ["xn"], NE, cb_u)
                emit_proj(c, _fm(w_in), j * 128, 1, xn, ["xn"], NE, cb_b)
                P.op("dve", lambda e, j=j: e.tensor_scalar(out=yb[:], in0=zb[:, 2:2 + TH], scalar1=cw[:, j * 3 + 2:j * 3 + 3], scalar2=None, op0=ALU.mult),
                     reads=["zb", "cwt"], writes=["yb"])
                P.op("dve", lambda e, j=j: e.scalar_tensor_tensor(out=yb[:], in0=zb[:, 1:1 + TH], scalar=cw[:, j * 3 + 1:j * 3 + 2], in1=yb[:], op0=ALU.mult, op1=ALU.add),
                     reads=["zb", "cwt", "yb"], writes=["yb"])
                P.op("dve", lambda e, j=j: e.scalar_tensor_tensor(out=yb[:], in0=zb[:, 0:TH], scalar=cw[:, j * 3:j * 3 + 1], in1=yb[:], op0=ALU.mult, op1=ALU.add),
                     reads=["zb", "cwt", "yb"], writes=["yb"])
                P.op("dve", lambda e, j=j: e.tensor_tensor(out=gT[:, j, :], in0=yb[:], in1=bb[:, 2:2 + TH], op=ALU.mult),
                     reads=["yb", "bb"], writes=["gT"])
            emit_outproj(c, w_out.rearrange("(c p) n -> p c n", p=128), KC, gT, ["gT"], xv, _fm(oT), tg, TH, 1.0, final)
DIL = ((128, 1), (512, 4), (2048, 16))
SEQ = 4096
PI = 3.14159265358979
MAGIC = 12582912.0


def dil_consts():
    invf = np.zeros((128, 1), np.float32)
    fr = (500000.0 ** (-np.arange(0, 32, 2, dtype=np.float32) / 32)).astype(np.float32)
    invf[0:16, 0] = fr; invf[16:32, 0] = fr
    rm = np.zeros((128, 128), np.float32)
    for m in range(16):
        rm[m + 16, m] = -1.0
        rm[m, m + 16] = 1.0
    p = np.arange(128)[:, None]; f = np.arange(128)[None, :]
    mask = np.concatenate([(p >= f), (p <= f)], axis=1).astype(np.float32)
    return invf, rm, mask


def build_dilcore(NH=8):
    nc = _new_nc()
    yT = _din(nc, "yT", [9216, SEQ]); posb = _din(nc, "posb", [128, SEQ], I32)
    invf_d = _din(nc, "invf", [128, 1]); rm_d = _din(nc, "rm", [128, 128]); mask_d = _din(nc, "mask", [128, 256])
    id_d = _din(nc, "ident", [128, 128])
    gq_d = _din(nc, "gq", [128, 3]); gk_d = _din(nc, "gk", [128, 3])
    oT = _dout(nc, "oT", [NH * 128, SEQ])
    with contextlib.ExitStack() as stack:
        P = Prog(nc, stack); c = Ctx(nc, stack, P)
        emit_dilcore(c, yT, posb, invf_d, rm_d, mask_d, id_d, gq_d, gk_d, oT, NH, True)
        P.emit()
    return nc


def emit_dilcore(c, yT, posb, invf_d, rm_d, mask_d, id_d, gq_d, gk_d, oT, NH, final):
    P = c.P
    G = 3
    if True:
        ones32, ones16 = c.const_ones()
        invf = c.sb("invf_t", [128, 1], F32); rm = c.sb("rm_t", [128, 128], F32); mask = c.sb("mask_t", [128, 256], F32)
        ident = c.sb("ident_t", [128, 128], F32)
        gq = c.sb("gq_t", [128, G], F32); gk = c.sb("gk_t", [128, G], F32)
        for t, dd, k in ((invf, invf_d, "invf"), (rm, rm_d, "rm"), (mask, mask_d, "mask"), (gq, gq_d, "gq"), (gk, gk_d, "gk"), (ident, id_d, "ident")):
            P.dma("sp", t[:], dd, writes=[k])
        posi = c.sb("posi", [128, SEQ], I32)
        ang = c.sb("ang", [128, SEQ], F32); tmp = c.sb("tmpa", [128, SEQ], F32); kf = c.sb("kfa", [128, SEQ], F32)
        cosT = c.sb("cosT", [128, SEQ], F32); sinT = c.sb("sinT", [128, SEQ], F32)
        qf = kf
        q16 = c.sb("q16", [128, SEQ], BF16); k16 = c.sb("k16", [128, SEQ], BF16)
        v16 = c.sb("v16", [128, 32 * 128], BF16)
        Uacc = c.sb("Uacc", [128, SEQ], F32); Zacc = ang
        sq = c.sb("d_sq", [128, 512], F32); rs = c.sb("d_rs", [128, 512], F32); t1 = c.sb("d_t1", [128, 512], F32)
        t2 = c.sb("d_t2", [128, 512], F32)
        Ef = [c.sb(f"Ef{i}", [128, 256], F32) for i in range(2)]
        Em = [c.sb(f"Em{i}", [128, 256], BF16) for i in range(2)]
        psn = c.ps("ps_n"); psr = c.ps("ps_a0")
        pss = [c.ps(f"ps_b{i}") for i in range(2)]
        psU = [c.ps(f"ps_c{i}") for i in range(2)]
        psZ = [c.ps(f"ps_d{i}") for i in range(2)]
        C1 = 6.28125
        C2 = 2.0 * PI - C1
        sm_scale = 128.0 ** -0.5

        def table(dst, dkey, shift):
            P.op("dve", lambda e: e.tensor_scalar(out=tmp[:], in0=ang[:], scalar1=float(shift), scalar2=None, op0=ALU.add),
                 reads=["ang"], writes=["tmpa"])
            P.op("dve", lambda e: e.tensor_scalar(out=kf[:], in0=tmp[:], scalar1=1.0 / (2 * PI), scalar2=MAGIC, op0=ALU.mult, op1=ALU.add),
                 reads=["tmpa"], writes=["kfa"])
            P.op("dve", lambda e: e.tensor_scalar(out=kf[:], in0=kf[:], scalar1=-MAGIC, scalar2=None, op0=ALU.add),
                 reads=["kfa"], writes=["kfa"])
            P.op("dve", lambda e: e.scalar_tensor_tensor(out=tmp[:], in0=kf[:], scalar=-C1, in1=tmp[:], op0=ALU.mult, op1=ALU.add),
                 reads=["kfa", "tmpa"], writes=["tmpa"])
            P.op("dve", lambda e: e.scalar_tensor_tensor(out=tmp[:], in0=kf[:], scalar=-C2, in1=tmp[:], op0=ALU.mult, op1=ALU.add),
                 reads=["kfa", "tmpa"], writes=["tmpa"])
            P.op("dve", lambda e: e.tensor_scalar(out=tmp[:], in0=tmp[:], scalar1=3.1415925, scalar2=-3.1415925, op0=ALU.min, op1=ALU.max),
                 reads=["tmpa"], writes=["tmpa"])
            P.op("act", lambda e: e.activation(out=dst[:], in_=tmp[:], func=AF.Sin), reads=["tmpa"], writes=[dkey])

        P.dma("sp", posi[:], posb, writes=["posi"])
        P.op("dve", lambda e: e.tensor_copy(out=ang[:], in_=posi[:]), reads=["posi"], writes=["ang"])
        P.op("dve", lambda e: e.tensor_scalar(out=ang[:], in0=ang[:], scalar1=invf[:, 0:1], scalar2=None, op0=ALU.mult),
             reads=["ang", "invf"], writes=["ang"])
        table(sinT, "sinT", 0.0)
        table(cosT, "cosT", PI / 2)

        def prep(src, g, dl, gt, gkey, dst16, dkey):
            P.dma("sp", qf[:], src, reads=["kfa"], writes=["kfa"])
            dview = dst16[:].rearrange("p (r u) -> p u r", r=dl) if dl > 1 else None
            for t0 in range(0, SEQ, 512):
                sl = slice(t0, t0 + 512)
                P.op("act", lambda e, sl=sl: e.activation(out=sq[:], in_=qf[:, sl], func=AF.Square), reads=["kfa"], writes=["d_sq"])
                P.op("pe", lambda e: e.matmul(psn[:], ones32[:], sq[:], start=True, stop=True), reads=["d_sq", "ones32"], writes=["ps_n"])
                P.op("act", lambda e: e.activation(out=rs[:], in_=psn[:], func=AF.Ln, bias=EPS, scale=1.0 / 128), reads=["ps_n"], writes=["d_rs"])
                P.op("act", lambda e: e.activation(out=rs[:], in_=rs[:], func=AF.Exp, scale=-0.5), reads=["d_rs"], writes=["d_rs"])
                P.op("dve", lambda e, sl=sl: e.scalar_tensor_tensor(out=qf[:, sl], in0=qf[:, sl], scalar=gt[:, g:g + 1], in1=rs[:], op0=ALU.mult, op1=ALU.mult),
                     reads=["kfa", "d_rs", gkey], writes=["kfa"])
                P.op("pe", lambda e, sl=sl: e.matmul(psr[:], rm[:], qf[:, sl], start=True, stop=True), reads=["kfa", "rm"], writes=["ps_a0"])
                P.op("dve", lambda e, sl=sl: e.tensor_tensor(out=t1[:], in0=qf[:, sl], in1=cosT[:, sl], op=ALU.mult), reads=["kfa", "cosT"], writes=["d_t1"])
                P.op("dve", lambda e, sl=sl: e.tensor_tensor(out=t2[:], in0=psr[:], in1=sinT[:, sl], op=ALU.mult), reads=["ps_a0", "sinT"], writes=["d_t2"])
                if dl == 1:
                    P.op("dve", lambda e, sl=sl: e.tensor_tensor(out=dst16[:, sl], in0=t1[:], in1=t2[:], op=ALU.add), reads=["d_t1", "d_t2"], writes=[dkey])
                else:
                    u0 = t0 // dl
                    nu = 512 // dl
                    P.op("dve", lambda e, u0=u0, nu=nu: e.tensor_tensor(
                        out=dview[:, u0:u0 + nu, :], in0=t1[:].rearrange("p (u r) -> p u r", r=dl),
                        in1=t2[:].rearrange("p (u r) -> p u r", r=dl), op=ALU.add), reads=["d_t1", "d_t2"], writes=[dkey])

        def vprep(src, dl):
            nb = SEQ // dl // 128
            P.dma("sp", tmp[:], src, writes=["tmpa"])
            for q0 in range(0, 32, 4):
                pi = c.rot("vps", 2)

                def tr(e, q0=q0, pi=pi):
                    ins = None
                    for s in range(4):
                        q = q0 + s
                        r, bp = q // nb, q % nb
                        ta = bp * 128 * dl + r
                        sl = slice(ta, ta + dl * 127 + 1, dl) if dl > 1 else slice(ta, ta + 128)
                        ins = e.transpose(psU[pi][:, s * 128:(s + 1) * 128], tmp[:, sl], ident[:])
                    return ins
                P.op("pe", tr, reads=["tmpa", "ident"], writes=[f"ps_c{pi}"])
                P.op("act", lambda e, q0=q0, pi=pi: e.copy(out=v16[:, q0 * 128:(q0 + 4) * 128], in_=psU[pi][:]), reads=[f"ps_c{pi}"], writes=["v16"])

        for hl in range(NH):
            for g, (window, dl) in enumerate(DIL):
                nb = SEQ // dl // 128
                r0 = ((0 * 3 + g) * 8 + hl) * 128
                r1 = ((1 * 3 + g) * 8 + hl) * 128
                r2 = ((2 * 3 + g) * 8 + hl) * 128
                prep(yT[r0:r0 + 128, :], g, dl, gq, "gq", q16, "q16")
                prep(yT[r1:r1 + 128, :], g, dl, gk, "gk", k16, "k16")
                vprep(yT[r2:r2 + 128, :], dl)
                for qb in range(0, 32, 2):
                    r, b0 = qb // nb, qb % nb
                    ui = c.rot("psU", 2)
                    for s in range(2):
                        q = qb + s
                        bp = q % nb
                        ei = c.rot("Ef", 2)
                        lo = 0 if bp > 0 else 128
                        qs = slice(q * 128, (q + 1) * 128)

                        def mms(e, ei=ei, q=q, bp=bp, qs=qs):
                            ins = None
                            if bp > 0:
                                ins = e.matmul(pss[ei][:, 0:128], k16[:, (q - 1) * 128:q * 128], q16[:, qs], start=True, stop=True)
                            ins = e.matmul(pss[ei][:, 128:256], k16[:, qs], q16[:, qs], start=True, stop=True)
                            return ins
                        P.op("pe", mms, reads=["k16", "q16"], writes=[f"ps_b{ei}"])
                        P.op("act", lambda e, ei=ei, lo=lo: e.activation(out=Ef[ei][:, lo:256], in_=pss[ei][:, lo:256], func=AF.Exp, scale=sm_scale),
                             reads=[f"ps_b{ei}"], writes=[f"Ef{ei}"])
                        P.op("dve", lambda e, ei=ei, lo=lo: e.tensor_tensor(out=Em[ei][:, lo:256], in0=Ef[ei][:, lo:256], in1=mask[:, lo:256], op=ALU.mult),
                             reads=[f"Ef{ei}", "mask"], writes=[f"Em{ei}"])

                        def mmu(e, ei=ei, q=q, bp=bp, s=s, ui=ui):
                            o = psU[ui][:, s * 128:(s + 1) * 128]
                            if bp > 0:
                                e.matmul(o, v16[:, (q - 1) * 128:q * 128], Em[ei][:, 0:128], start=True, stop=False)
                            return e.matmul(o, v16[:, q * 128:(q + 1) * 128], Em[ei][:, 128:256], start=(bp == 0), stop=True)
                        P.op("pe", mmu, reads=[f"Em{ei}", "v16"], writes=[f"ps_c{ui}"])

                        def mmz(e, ei=ei, bp=bp, s=s, ui=ui):
                            o = psZ[ui][:, s * 128:(s + 1) * 128]
                            if bp > 0:
                                e.matmul(o, ones16[:], Em[ei][:, 0:128], start=True, stop=False)
                            return e.matmul(o, ones16[:], Em[ei][:, 128:256], start=(bp == 0), stop=True)
                        P.op("pe", mmz, reads=[f"Em{ei}", "ones16"], writes=[f"ps_d{ui}"])
                    ta = r + dl * b0 * 128
                    tsl = slice(ta, ta + dl * 255 + 1, dl) if dl > 1 else slice(ta, ta + 256)
                    if g == 0:
                        P.op("dve", lambda e, ui=ui, tsl=tsl: e.tensor_copy(out=Uacc[:, tsl], in_=psU[ui][:, 0:256]), reads=[f"ps_c{ui}"], writes=["Uacc"])
                        P.op("act", lambda e, ui=ui, tsl=tsl: e.copy(out=Zacc[:, tsl], in_=psZ[ui][:, 0:256]), reads=[f"ps_d{ui}"], writes=["ang"])
                    else:
                        P.op("dve", lambda e, ui=ui, tsl=tsl: e.tensor_tensor(out=Uacc[:, tsl], in0=Uacc[:, tsl], in1=psU[ui][:, 0:256], op=ALU.add),
                             reads=[f"ps_c{ui}", "Uacc"], writes=["Uacc"])
                        P.op("dve", lambda e, ui=ui, tsl=tsl: e.tensor_tensor(out=Zacc[:, tsl], in0=Zacc[:, tsl], in1=psZ[ui][:, 0:256], op=ALU.add),
                             reads=[f"ps_d{ui}", "ang"], writes=["ang"])
            P.op("dve", lambda e: e.reciprocal(out=Zacc[:], in_=Zacc[:]), reads=["ang"], writes=["ang"])
            P.op("dve", lambda e: e.tensor_tensor(out=Uacc[:], in0=Uacc[:], in1=Zacc[:], op=ALU.mult), reads=["ang", "Uacc"], writes=["Uacc"])
            P.dma("sp", oT[hl * 128:(hl + 1) * 128, :], Uacc[:], reads=["Uacc"], final=final)


def dil_perm(dl):
    L = SEQ // dl
    return (np.arange(L)[None, :] * dl + np.arange(dl)[:, None]).reshape(-1)
HC = 64
HNC = SEQ // HC


def hgrn_consts():
    cm = np.ones((128, SEQ), np.float32); cm[:, ::HC] = 0.0
    p = np.arange(HC)[:, None]; f = np.arange(HC)[None, :]
    tm = (p <= f).astype(np.float32)
    return cm, tm, np.eye(128, dtype=np.float32)


def build_hgrncore(NH=16, layer=2):
    nc = _new_nc()
    yT = _din(nc, "yT", [8192, SEQ])
    lbl = _din(nc, "lbl", [128, NH * 4]); ng_d = _din(nc, "ng", [HC, 128])
    cm_d = _din(nc, "cm", [128, SEQ]); tm_d = _din(nc, "tm", [HC, HC]); id_d = _din(nc, "ident", [128, 128])
    oT = _dout(nc, "oT", [NH * 128, SEQ])
    with contextlib.ExitStack() as stack:
        P = Prog(nc, stack); c = Ctx(nc, stack, P)
        emit_hgrncore(c, yT, lbl, ng_d, cm_d, tm_d, id_d, oT, NH, layer, True)
        P.emit()
    return nc


def emit_hgrncore(c, yT, lbl, ng_d, cm_d, tm_d, id_d, oT, NH, layer, final):
    P = c.P
    if True:
        cm = c.sb("cm_t", [128, SEQ], F32); tm = c.sb("tm_t", [HC, HC], F32); ident = c.sb("id_t", [128, 128], BF16)
        ident32 = c.sb("id32_t", [128, 128], F32)
        P.dma("sp", ident32[:], id_d, writes=["ident32"])
        gnat = c.sb("gnat", [128, SEQ], F32)
        ofm = c.sb("ofm", [128, 512], F32)
        psG = c.ps("ps_G", [128, 1024], F32)
        ng = c.sb("ng_t", [HC, 128], F32); lb4 = c.sb("lb4", [128, NH * 4], F32)
        P.dma("sp", cm[:], cm_d, writes=["cm"]); P.dma("sp", tm[:], tm_d, writes=["tm"])
        P.dma("pool", ident[:], id_d, writes=["ident"]); P.dma("sp", ng[:], ng_d, writes=["ng"])
        P.dma("sp", lb4[:], lbl, writes=["lb4"])
        lb = c.sb("lb", [128, NH], F32); oml = c.sb("oml", [128, NH], F32); den = c.sb("den", [128, NH], F32)
        P.op("act", lambda e: e.activation(out=lb4[:], in_=lb4[:], func=AF.Exp), reads=["lb4"], writes=["lb4"])
        l3 = lb4[:].rearrange("p (h l) -> p h l", l=4)
        P.op("dve", lambda e: e.tensor_reduce(out=den[:], in_=l3, axis=AX.X, op=ALU.add), reads=["lb4"], writes=["den"])
        P.op("dve", lambda e: e.reciprocal(out=den[:], in_=den[:]), reads=["den"], writes=["den"])
        P.op("dve", lambda e: e.tensor_copy(out=lb[:], in_=l3[:, :, 1]), reads=["lb4"], writes=["lb"])
        for l in range(2, layer + 1):
            P.op("dve", lambda e, l=l: e.tensor_tensor(out=lb[:], in0=lb[:], in1=l3[:, :, l], op=ALU.add), reads=["lb4", "lb"], writes=["lb"])
        P.op("dve", lambda e: e.tensor_tensor(out=lb[:], in0=lb[:], in1=den[:], op=ALU.mult), reads=["lb", "den"], writes=["lb"])
        P.op("dve", lambda e: e.tensor_scalar(out=oml[:], in0=lb[:], scalar1=-1.0, scalar2=1.0, op0=ALU.mult, op1=ALU.add), reads=["lb"], writes=["oml"])

        fb = c.sb("fb", [128, SEQ], F32); A = c.sb("A", [128, SEQ], F32); tmp = c.sb("htmp", [128, SEQ], F32)
        kk = c.sb("kk", [128, SEQ], F32); qf = c.sb("qf", [128, SEQ], F32)
        qd16 = c.sb("qd16", [128, SEQ], BF16); ki16 = c.sb("ki16", [128, SEQ], BF16); ke16 = c.sb("ke16", [128, SEQ], BF16)
        ketok = c.sb("ketok", [HC, HNC * 128], BF16); v16 = c.sb("v16", [HC, HNC * 128], BF16)
        dec = c.sb("dec", [128, HNC], F32)
        S32 = c.sb("S32", [128, 128], F32); S16 = c.sb("S16", [128, 128], BF16)
        att16 = [c.sb(f"att16_{i}", [HC, HC], BF16) for i in range(2)]
        gate = c.sb("gate", [HC, 8 * 128], F32); osb = c.sb("osb", [HC, 8 * 128], F32); sqb = c.sb("sqb", [HC, 8 * 128], F32)
        ss = c.sb("ss", [HC, 8], F32)
        psT = c.ps("ps_T", [128, 1024], BF16)
        psA = [c.ps(f"ps_a{i}") for i in range(2)]
        psO = [c.ps(f"ps_b{i}") for i in range(2)]
        psS = [c.ps("ps_c0"), c.ps("ps_c0")]
        A3 = A[:].rearrange("p (n c) -> p n c", c=HC)
        tmp3 = tmp[:].rearrange("p (n c) -> p n c", c=HC)
        for hl in range(NH):
            P.dma("sp", fb[:], yT[2048 + hl * 128:2048 + (hl + 1) * 128, :], writes=["fb"])
            P.dma("sp", qf[:], yT[hl * 128:(hl + 1) * 128, :], writes=["qf"])
            P.dma("sp", kk[:], yT[4096 + hl * 128:4096 + (hl + 1) * 128, :], writes=["kk"])
            P.dma("sp", gnat[:], yT[6144 + hl * 128:6144 + (hl + 1) * 128, :], writes=["gnat"])
            for n0 in range(0, HNC, 8):
                def trv(e, n0=n0):
                    ins = None
                    for i in range(8):
                        n = n0 + i
                        ins = e.transpose(psG[0:HC, i * 128:(i + 1) * 128], kk[:, n * HC:(n + 1) * HC], ident32[:])
                    return ins
                P.op("pe", trv, reads=["kk", "ident32"], writes=["ps_G"])
                P.op("act", lambda e, n0=n0: e.copy(out=v16[:, n0 * 128:(n0 + 8) * 128], in_=psG[0:HC, :]), reads=["ps_G"], writes=["v16"])
            P.op("act", lambda e: e.activation(out=fb[:], in_=fb[:], func=AF.Exp, scale=-1.0), reads=["fb"], writes=["fb"])
            P.op("dve", lambda e: e.tensor_scalar(out=fb[:], in0=fb[:], scalar1=1.0, scalar2=None, op0=ALU.add), reads=["fb"], writes=["fb"])
            P.op("dve", lambda e: e.reciprocal(out=fb[:], in_=fb[:]), reads=["fb"], writes=["fb"])
            P.op("dve", lambda e, hl=hl: e.tensor_scalar(out=fb[:], in0=fb[:], scalar1=oml[:, hl:hl + 1], scalar2=lb[:, hl:hl + 1], op0=ALU.mult, op1=ALU.add),
                 reads=["fb", "oml", "lb"], writes=["fb"])
            P.op("dve", lambda e: e.tensor_scalar(out=kk[:], in0=fb[:], scalar1=-1.0, scalar2=1.0, op0=ALU.mult, op1=ALU.add), reads=["fb"], writes=["kk"])
            P.op("act", lambda e: e.activation(out=fb[:], in_=fb[:], func=AF.Ln), reads=["fb"], writes=["fb"])
            P.op("dve", lambda e: e.tensor_tensor_scan(out=A[:], data0=cm[:], data1=fb[:], initial=0.0, op0=ALU.mult, op1=ALU.add),
                 reads=["cm", "fb"], writes=["A"])
            P.op("act", lambda e: e.activation(out=tmp[:], in_=A[:], func=AF.Exp), reads=["A"], writes=["htmp"])
            P.op("dve", lambda e: e.tensor_tensor(out=qd16[:], in0=qf[:], in1=tmp[:], op=ALU.mult), reads=["qf", "htmp"], writes=["qd16"])
            P.op("act", lambda e: e.copy(out=dec[:], in_=tmp3[:, :, HC - 1]), reads=["htmp"], writes=["dec"])
            P.op("act", lambda e: e.activation(out=tmp[:], in_=A[:], func=AF.Exp, scale=-1.0), reads=["A", "dec"], writes=["htmp"])
            P.op("dve", lambda e: e.tensor_tensor(out=ki16[:], in0=kk[:], in1=tmp[:], op=ALU.mult), reads=["kk", "htmp"], writes=["ki16"])
            P.op("dve", lambda e: e.tensor_tensor(out=tmp3, in0=A3[:, :, HC - 1:HC].broadcast_to([128, HNC, HC]), in1=A3, op=ALU.subtract),
                 reads=["A", "ki16"], writes=["htmp"])
            P.op("act", lambda e: e.activation(out=tmp[:], in_=tmp[:], func=AF.Exp), reads=["htmp"], writes=["htmp"])
            P.op("dve", lambda e: e.tensor_tensor(out=ke16[:], in0=kk[:], in1=tmp[:], op=ALU.mult), reads=["kk", "htmp"], writes=["ke16"])
            for n0 in range(0, HNC, 8):
                def tr(e, n0=n0):
                    ins = None
                    for i in range(8):
                        n = n0 + i
                        ins = e.transpose(psT[0:HC, i * 128:(i + 1) * 128], ke16[:, n * HC:(n + 1) * HC], ident[:])
                    return ins
                P.op("pe", tr, reads=["ke16", "ident"], writes=["ps_T"])
                P.op("act", lambda e, n0=n0: e.copy(out=ketok[:, n0 * 128:(n0 + 8) * 128], in_=psT[0:HC, :]), reads=["ps_T"], writes=["ketok"])
            P.op("pool", lambda e: e.memset(S32[:], 0.0), writes=["S32"])
            P.op("pool", lambda e: e.memset(S16[:], 0.0), writes=["S16"])
            for n in range(HNC):
                cs = slice(n * HC, (n + 1) * HC)
                vs = slice(n * 128, (n + 1) * 128)
                ai = c.rot("psA", 2)
                j = n % 8
                if j == 0:
                    def trg(e, n=n):
                        ins = None
                        for i in range(8):
                            ins = e.transpose(psG[0:HC, i * 128:(i + 1) * 128], gnat[:, (n + i) * HC:(n + i + 1) * HC], ident32[:])
                        return ins
                    P.op("pe", trg, reads=["gnat", "ident32"], writes=["ps_G"])
                    P.op("act", lambda e: e.activation(out=gate[:], in_=psG[0:HC, :], func=AF.Silu), reads=["ps_G"], writes=["gate"])
                P.op("pe", lambda e, ai=ai, cs=cs: e.matmul(psA[ai][0:HC, 0:HC], ki16[:, cs], qd16[:, cs], start=True, stop=True),
                     reads=["ki16", "qd16"], writes=[f"ps_a{ai}"])
                P.op("dve", lambda e, ai=ai: e.tensor_tensor(out=att16[ai][:], in0=psA[ai][0:HC, 0:HC], in1=tm[:], op=ALU.mult),
                     reads=[f"ps_a{ai}", "tm"], writes=[f"att16_{ai}"])

                def mmo(e, ai=ai, cs=cs, vs=vs):
                    e.matmul(psO[ai][0:HC, 0:128], qd16[:, cs], S16[:], start=True, stop=False)
                    return e.matmul(psO[ai][0:HC, 0:128], att16[ai][:], v16[:, vs], start=False, stop=True)
                P.op("pe", mmo, reads=["qd16", "S16", f"att16_{ai}", "v16"], writes=[f"ps_b{ai}"])
                P.op("act", lambda e, ai=ai, j=j: e.copy(out=osb[:, j * 128:(j + 1) * 128], in_=psO[ai][0:HC, 0:128]), reads=[f"ps_b{ai}"], writes=["osb"])
                P.op("pe", lambda e, ai=ai, vs=vs: e.matmul(psS[ai][:, 0:128], ketok[:, vs], v16[:, vs], start=True, stop=True),
                     reads=["ketok", "v16"], writes=["ps_c0"])
                P.op("dve", lambda e, ai=ai, n=n: e.scalar_tensor_tensor(out=S32[:], in0=S32[:], scalar=dec[:, n:n + 1], in1=psS[ai][:, 0:128], op0=ALU.mult, op1=ALU.add),
                     reads=["S32", "dec", "ps_c0"], writes=["S32"])
                P.op("act", lambda e: e.copy(out=S16[:], in_=S32[:]), reads=["S32"], writes=["S16"])
                if j == 7:
                    o3 = osb[:].rearrange("p (j e) -> p j e", e=128)
                    P.op("dve", lambda e: e.tensor_tensor(out=sqb[:], in0=osb[:], in1=osb[:], op=ALU.mult), reads=["osb"], writes=["sqb"])
                    P.op("dve", lambda e: e.tensor_reduce(out=ss[:], in_=sqb[:].rearrange("p (j e) -> p j e", e=128), axis=AX.X, op=ALU.add),
                         reads=["sqb"], writes=["ss"])
                    P.op("act", lambda e: e.activation(out=ss[:], in_=ss[:], func=AF.Ln, bias=EPS, scale=1.0 / 128), reads=["ss"], writes=["ss"])
                    P.op("act", lambda e: e.activation(out=ss[:], in_=ss[:], func=AF.Exp, scale=-0.5), reads=["ss"], writes=["ss"])
                    P.op("dve", lambda e, o3=o3: e.tensor_tensor(out=o3, in0=o3, in1=ss[:].unsqueeze(2).broadcast_to([HC, 8, 128]), op=ALU.mult),
                         reads=["osb", "ss"], writes=["osb"])
                    P.op("dve", lambda e, o3=o3: e.tensor_tensor(out=o3, in0=o3, in1=ng[:].unsqueeze(1).broadcast_to([HC, 8, 128]), op=ALU.mult),
                         reads=["osb", "ng"], writes=["osb"])
                    P.op("dve", lambda e: e.tensor_tensor(out=osb[:], in0=osb[:], in1=gate[:], op=ALU.mult), reads=["osb", "gate"], writes=["osb"])

                    def tro(e):
                        ins = None
                        for i in range(8):
                            ins = e.transpose(psG[:, i * HC:(i + 1) * HC], osb[:, i * 128:(i + 1) * 128], ident32[0:HC, 0:HC])
                        return ins
                    P.op("pe", tro, reads=["osb", "ident32"], writes=["ps_G"])
                    P.op("act", lambda e: e.copy(out=ofm[:], in_=psG[:, 0:512]), reads=["ps_G"], writes=["ofm"])
                    P.dma("sp", oT[hl * 128:(hl + 1) * 128, (n - 7) * HC:(n + 1) * HC], ofm[:], reads=["ofm"], final=final)
NOSYNC = ('dve', 'pool', 'act')
RW_ROWS = 7
RWM = {0: 0, 2: 1, 3: 2, 5: 3, 6: 4, 7: 5, 8: 6}


def rwkv_consts():
    bones = np.zeros((128, 128), np.float32)
    bones[:64, :64] = 1.0; bones[64:, 64:] = 1.0
    sel = np.zeros((32, 16 * 128), np.float32)
    for t in range(16):
        for j in range(2):
            sel[t * 2 + j, t * 128 + j * 64: t * 128 + (j + 1) * 64] = 1.0
    return bones, sel


def build_rwkvA(T=4096):
    nc = _new_nc()
    xT = _din(nc, "xT", [D, T]); gain = _din(nc, "gain", [128, KC]); mu_d = _din(nc, "mu", [128, 6 * KC])
    wrkv = _din(nc, "wrkv", [3, D, D])
    w0_d = _din(nc, "w0", [128, KC]); w1 = _din(nc, "w1", [D, 96]); w2 = _din(nc, "w2", [96, D])
    a0_d = _din(nc, "a0", [128, KC]); a1 = _din(nc, "a1", [D, 96]); a2 = _din(nc, "a2", [96, D])
    g1 = _din(nc, "g1", [D, 256]); g2 = _din(nc, "g2", [256, D])
    kk_d = _din(nc, "k_k", [128, KC]); ka_d = _din(nc, "k_a", [128, KC]); bones_d = _din(nc, "bones", [128, 128])
    yT = _dout(nc, "yT", [RW_ROWS * D, T])
    with contextlib.ExitStack() as stack:
        P = Prog(nc, stack); c = Ctx(nc, stack, P)
        emit_rwkvA(c, xT, gain, mu_d, wrkv, w0_d, w1, w2, a0_d, a1, a2, g1, g2, kk_d, ka_d, bones_d, yT, T, True)
        P.emit()
    return nc


def emit_rwkvA(c, xT, gain, mu_d, wrkv, w0_d, w1, w2, a0_d, a1, a2, g1, g2, kk_d, ka_d, bones_d, yT, T, final):
    P = c.P
    c.pwg = 2
    if True:
        gn = c.sb("gn", [128, KC], F32); mu = c.sb("mu_t", [128, 6 * KC], F32)
        w0 = c.sb("w0_t", [128, KC], F32); a0 = c.sb("a0_t", [128, KC], F32)
        k_k = c.sb("kk_t", [128, KC], F32); k_a = c.sb("ka_t", [128, KC], F32); omka = c.sb("omka", [128, KC], F32)
        bones = c.sb("bones_t", [128, 128], F32)
        for t, d, k in ((gn, gain, "gn"), (mu, mu_d, "mu"), (w0, w0_d, "w0"), (a0, a0_d, "a0"), (k_k, kk_d, "k_k"), (k_a, ka_d, "k_a"), (bones, bones_d, "bones")):
            P.dma("sp", t[:], d, writes=[k])
        P.op("dve", lambda e: e.tensor_scalar(out=omka[:], in0=k_a[:], scalar1=-1.0, scalar2=1.0, op0=ALU.mult, op1=ALU.add), reads=["k_a"], writes=["omka"])
        nw0 = c.sb("nw0", [128, KC], F32); na0 = c.sb("na0", [128, KC], F32); th = c.sb("th", [128, 512], F32)
        P.op("dve", lambda e: e.tensor_scalar(out=nw0[:], in0=w0[:], scalar1=-1.0, scalar2=None, op0=ALU.mult), reads=["w0"], writes=["nw0"])
        P.op("dve", lambda e: e.tensor_scalar(out=na0[:], in0=a0[:], scalar1=-1.0, scalar2=None, op0=ALU.mult), reads=["a0"], writes=["na0"])
        w2b = c.sb("w2b", [96, D], BF16); a2b = c.sb("a2b", [96, D], BF16); g2b = c.sb("g2b", [128, 2, D], BF16)
        P.dma("pool", w2b[:], w2, writes=["w2b"]); P.dma("pool", a2b[:], a2, writes=["a2b"])
        P.dma("pool", g2b[:], g2.rearrange("(c p) n -> p c n", p=128), writes=["g2b"])
        hfp = c.sb("hfp", [128, KC, 513], F32); diff = c.sb("diff", [128, KC, 512], F32)
        mx = [c.sb(f"mx{i}", [128, KC, 512], BF16) for i in range(2)]
        kbuf = c.sb("kbuf", [128, KC, 512], F32)
        t1 = c.sb("t1", [128, 2, 512], BF16)
        ysb = [c.sb(f"ysb{i}", [128, 512], F32) for i in range(2)]
        asb = c.sb("asb", [128, 512], F32); kkr = c.sb("kkr", [128, 512], F32); sq = c.sb("r_sq", [128, 512], F32)
        rn = c.sb("rn", [128, 512], F32)
        psb = [c.ps(f"ps_b{i}") for i in range(2)]
        psn2 = c.ps("ps_c0")
        yv = yT.rearrange("(r kc p) t -> r p kc t", p=128, kc=KC)
        xv = _fm(xT)

        def store(row, j, tg, src_ap, key):
            if row not in RWM:
                return
            P.dma("sp", yv[RWM[row]][:, j, tg:tg + 512], src_ap, reads=[key], final=final)

        for tt in range(T // 512):
            tg = tt * 512
            if tt == 0:
                P.op("pool", lambda e: e.memset(hfp[:, :, 0:1], 0.0), writes=["hfp"])
                emit_norm(c, xv[:, :, 0:512], 512, gn, hfp, 1, "hfp")
            else:
                emit_norm(c, xv[:, :, tg - 1:tg + 512], 513, gn, hfp, 0, "hfp")
            P.op("dve", lambda e: e.tensor_tensor(out=diff[:], in0=hfp[:, :, 0:512], in1=hfp[:, :, 1:513], op=ALU.subtract), reads=["hfp"], writes=["diff"])
            for i in range(6):
                mi = c.rot("mx", 2)
                m = mx[mi]
                for kc in range(KC):
                    P.op("dve", lambda e, kc=kc, i=i, m=m: e.scalar_tensor_tensor(
                        out=m[:, kc, :], in0=diff[:, kc, :], scalar=mu[:, i * KC + kc:i * KC + kc + 1], in1=hfp[:, kc, 1:513],
                        op0=ALU.mult, op1=ALU.add), reads=["diff", "hfp", "mu"], writes=[f"mx{mi}"])
                mk = [f"mx{mi}"]
                if i < 3:
                    def cb(j, t0, n, ps, psk, i=i, tg=tg):
                        if i == 1:
                            P.op("act", lambda e: e.copy(out=kbuf[:, j, :], in_=ps[:, 0:512]), reads=[psk], writes=["kbuf"])
                            store(1, j, tg, kbuf[:, j, :], "kbuf")
                        else:
                            yi = c.rot("ysb", 2)
                            P.op("act", lambda e: e.copy(out=ysb[yi][:], in_=ps[:, 0:512]), reads=[psk], writes=[f"ysb{yi}"])
                            store(i, j, tg, ysb[yi][:], f"ysb{yi}")
                    emit_proj(c, wrkv[i].rearrange("(kc p) n -> p kc n", p=128), 0, KC, m, mk, 512, cb)
                elif i == 3 or i == 4:
                    wl = w1 if i == 3 else a1

                    def cb(j, t0, n, ps, psk, i=i):
                        if i == 3:
                            P.op("act", lambda e: e.activation(out=th[0:96, :], in_=ps[0:96, 0:512], func=AF.Exp, scale=-2.0), reads=[psk], writes=["th"])
                            P.op("dve", lambda e: e.tensor_scalar(out=th[0:96, :], in0=th[0:96, :], scalar1=1.0, scalar2=None, op0=ALU.add), reads=["th"], writes=["th"])
                            P.op("dve", lambda e: e.reciprocal(out=th[0:96, :], in_=th[0:96, :]), reads=["th"], writes=["th"])
                            P.op("dve", lambda e: e.tensor_scalar(out=t1[0:96, 0, :], in0=th[0:96, :], scalar1=2.0, scalar2=-1.0, op0=ALU.mult, op1=ALU.add), reads=["th"], writes=["t1"])
                        else:
                            P.op("act", lambda e: e.copy(out=t1[0:96, 0, :], in_=ps[0:96, 0:512]), reads=[psk], writes=["t1"])
                    emit_proj(c, wl.rearrange("(kc p) n -> p kc n", p=128), 0, 1, m, mk, 512, cb, cw=96)
                    w2x, w2k, bias, bk = (w2b, "w2b", w0, "w0") if i == 3 else (a2b, "a2b", a0, "a0")
                    for j in range(KC):
                        pi = c.rot("ps_b", 2)
                        P.op("pe", lambda e, pi=pi, j=j, w2x=w2x: e.matmul(psb[pi][:], w2x[0:96, j * 128:(j + 1) * 128], t1[0:96, 0, :], start=True, stop=True),
                             reads=["t1", w2k], writes=[f"ps_b{pi}"])
                        if i == 3:
                            yi = c.rot("ysb", 2)
                            P.op("act", lambda e, pi=pi, j=j, yi=yi: e.activation(out=ysb[yi][:], in_=psb[pi][:], func=AF.Exp, bias=nw0[:, j:j + 1], scale=-1.0),
                                 reads=[f"ps_b{pi}", "nw0"], writes=[f"ysb{yi}"])
                            P.op("dve", lambda e, yi=yi: e.tensor_scalar(out=ysb[yi][:], in0=ysb[yi][:], scalar1=1.0, scalar2=None, op0=ALU.add), reads=[f"ysb{yi}"], writes=[f"ysb{yi}"])
                            P.op("dve", lambda e, yi=yi: e.reciprocal(out=ysb[yi][:], in_=ysb[yi][:]), reads=[f"ysb{yi}"], writes=[f"ysb{yi}"])
                            P.op("act", lambda e, yi=yi: e.activation(out=ysb[yi][:], in_=ysb[yi][:], func=AF.Exp, scale=-float(np.exp(-0.5))),
                                 reads=[f"ysb{yi}"], writes=[f"ysb{yi}"])
                            store(3, j, tg, ysb[yi][:], f"ysb{yi}")
                        else:
                            P.op("act", lambda e, pi=pi, j=j: e.activation(out=asb[:], in_=psb[pi][:], func=AF.Exp, bias=na0[:, j:j + 1], scale=-1.0),
                                 reads=[f"ps_b{pi}", "na0"], writes=["asb"])
                            P.op("dve", lambda e: e.tensor_scalar(out=asb[:], in0=asb[:], scalar1=1.0, scalar2=None, op0=ALU.add), reads=["asb"], writes=["asb"])
                            P.op("dve", lambda e: e.reciprocal(out=asb[:], in_=asb[:]), reads=["asb"], writes=["asb"])
                            store(4, j, tg, asb[:], "asb")
                            P.op("dve", lambda e, j=j: e.tensor_scalar(out=kkr[:], in0=kbuf[:, j, :], scalar1=k_k[:, j:j + 1], scalar2=None, op0=ALU.mult),
                                 reads=["kbuf", "k_k"], writes=["kkr"])
                            P.op("act", lambda e: e.activation(out=sq[:], in_=kkr[:], func=AF.Square), reads=["kkr"], writes=["r_sq"])
                            P.op("pe", lambda e: e.matmul(psn2[:], bones[:], sq[:], start=True, stop=True), reads=["r_sq", "bones"], writes=["ps_c0"])
                            P.op("dve", lambda e: e.tensor_scalar(out=rn[:], in0=psn2[:], scalar1=1e-24, scalar2=None, op0=ALU.max), reads=["ps_c0"], writes=["rn"])
                            P.op("act", lambda e: e.activation(out=rn[:], in_=rn[:], func=AF.Ln), reads=["rn"], writes=["rn"])
                            P.op("act", lambda e: e.activation(out=rn[:], in_=rn[:], func=AF.Exp, scale=-0.5), reads=["rn"], writes=["rn"])
                            P.op("dve", lambda e: e.tensor_tensor(out=kkr[:], in0=kkr[:], in1=rn[:], op=ALU.mult), reads=["kkr", "rn"], writes=["kkr"])
                            yi = c.rot("ysb", 2)
                            P.op("dve", lambda e, yi=yi: e.tensor_scalar(out=ysb[yi][:], in0=kkr[:], scalar1=-1.0, scalar2=None, op0=ALU.mult), reads=["kkr"], writes=[f"ysb{yi}"])
                            store(6, j, tg, ysb[yi][:], f"ysb{yi}")
                            yi = c.rot("ysb", 2)
                            P.op("dve", lambda e, yi=yi: e.tensor_tensor(out=ysb[yi][:], in0=kkr[:], in1=asb[:], op=ALU.mult), reads=["kkr", "asb"], writes=[f"ysb{yi}"])
                            store(7, j, tg, ysb[yi][:], f"ysb{yi}")
                            yi = c.rot("ysb", 2)
                            P.op("dve", lambda e, j=j: e.tensor_scalar(out=rn[:], in0=asb[:], scalar1=k_a[:, j:j + 1], scalar2=omka[:, j:j + 1], op0=ALU.mult, op1=ALU.add),
                                 reads=["asb", "k_a", "omka"], writes=["rn"])
                            P.op("dve", lambda e, yi=yi, j=j: e.tensor_tensor(out=ysb[yi][:], in0=kbuf[:, j, :], in1=rn[:], op=ALU.mult), reads=["kbuf", "rn"], writes=[f"ysb{yi}"])
                            store(8, j, tg, ysb[yi][:], f"ysb{yi}")
                else:
                    def cb(j, t0, n, ps, psk):
                        P.op("act", lambda e: e.activation(out=th[:], in_=ps[:, 0:512], func=AF.Exp, scale=-1.0), reads=[psk], writes=["th"])
                        P.op("dve", lambda e: e.tensor_scalar(out=th[:], in0=th[:], scalar1=1.0, scalar2=None, op0=ALU.add), reads=["th"], writes=["th"])
                        P.op("dve", lambda e: e.reciprocal(out=th[:], in_=th[:]), reads=["th"], writes=["th"])
                        P.op("dve", lambda e: e.tensor_copy(out=t1[:, j, :], in_=th[:]), reads=["th"], writes=["t1"])
                    emit_proj(c, g1.rearrange("(kc p) n -> p kc n", p=128), 0, 2, m, mk, 512, cb)
                    for j in range(KC):
                        pi = c.rot("ps_b", 2)

                        def mm(e, pi=pi, j=j):
                            e.matmul(psb[pi][:], g2b[:, 0, j * 128:(j + 1) * 128], t1[:, 0, :], start=True, stop=False)
                            return e.matmul(psb[pi][:], g2b[:, 1, j * 128:(j + 1) * 128], t1[:, 1, :], start=False, stop=True)
                        P.op("pe", mm, reads=["t1", "g2b"], writes=[f"ps_b{pi}"])
                        yi = c.rot("ysb", 2)
                        P.op("act", lambda e, pi=pi, yi=yi: e.copy(out=ysb[yi][:], in_=psb[pi][:]), reads=[f"ps_b{pi}"], writes=[f"ysb{yi}"])
                        store(5, j, tg, ysb[yi][:], f"ysb{yi}")


class _Deferred:
    def __init__(self, P, lst):
        self.P, self.lst = P, lst

    def dma(self, *a, **k):
        self.lst.append(lambda: self.P.dma(*a, **k))

    def op(self, *a, **k):
        self.lst.append(lambda: self.P.op(*a, **k))


def rwkv_bmask():
    m = np.zeros((16, 512), np.float32)
    for i in range(8):
        m[2 * i:2 * i + 2, i * 64:(i + 1) * 64] = 1.0
    return m


def build_rwkvB(NSTEP=SEQ, NPASS=2):
    nc = _new_nc()
    yT = _din(nc, "yT", [RW_ROWS * D, NSTEP]); bm_d = _din(nc, "bmask", [16, 512]); id_d = _din(nc, "ident", [128, 128])
    ys = _dout(nc, "ys", [D, NSTEP])
    tm = nc.dram_tensor("rw_tm", [3 * NPASS, NSTEP, 1024], BF16, kind="Internal").ap()
    with contextlib.ExitStack() as stack:
        P = Prog(nc, stack); c = Ctx(nc, stack, P)
        emit_rwkvB(c, yT, bm_d, id_d, tm, ys, NSTEP, NPASS, True)
        P.emit()
    return nc


def emit_rwkvB(c, yT, bm_d, id_d, tm, ys, NSTEP, NPASS, final):
    P = c.P
    VB = 128
    SB = 16
    YB = 32
    ident = c.sb("ident_t", [128, 128], F32); bmask = c.sb("bmask_t", [16, 512], F32)
    P.dma("sp", ident[:], id_d, writes=["ident"]); P.dma("sp", bmask[:], bm_d, writes=["bmask"])
    fmb = [c.sb(f"fmb{i}", [128, 8, 128], F32) for i in range(2)]
    tmb = [c.sb(f"tmb{i}", [128, 1024], BF16) for i in range(2)]
    ps_t = [c.ps(f"ps_t{i}") for i in range(2)]
    TM_ROWS = (RWM[7], RWM[8], RWM[2])
    for hh in range(NPASS):
        for oi, row in enumerate(TM_ROWS):
            src = yT[row * D + hh * 1024: row * D + (hh + 1) * 1024, :].rearrange("(c p) t -> p c t", p=128)
            for tb in range(NSTEP // 128):
                fi = c.rot("fmb", 2)
                P.dma("sp", fmb[fi][:], src[:, :, tb * 128:(tb + 1) * 128], writes=[f"fmb{fi}"])
                for half in range(2):
                    pi = c.rot("ps_t", 2)

                    def tr(e, fi=fi, half=half, pi=pi):
                        ins = None
                        for q in range(4):
                            ins = e.transpose(ps_t[pi][:, q * 128:(q + 1) * 128], fmb[fi][:, half * 4 + q, :], ident[:])
                        return ins
                    P.op("pe", tr, reads=[f"fmb{fi}", "ident"], writes=[f"ps_t{pi}"])
                    P.op("act", lambda e, fi=fi, half=half, pi=pi: e.copy(out=tmb[fi][:, half * 512:(half + 1) * 512], in_=ps_t[pi][:]),
                         reads=[f"ps_t{pi}"], writes=[f"tmb{fi}"])
                P.dma("sp", tm[hh * 3 + oi, tb * 128:(tb + 1) * 128, :], tmb[fi][:], reads=[f"tmb{fi}"], writes=["tm_dram"])
    T_ = {}
    for hh in range(NPASS):
        s = f"_{hh}"
        A = dict(
            stmp=c.sb("stmp" + s, [128, 512], F32), ST32=c.sb("ST32" + s, [128, 512], F32), ST16=c.sb("ST16" + s, [128, 512], BF16), sam=c.sb("sam" + s, [16, 512], BF16),
            xnk=c.sb("xnk" + s, [128, 8, VB], F32), xr=c.sb("xr" + s, [128, 8, VB], F32),
            xw=[c.sb(f"xw{i}" + s, [128, 8, VB], F32) for i in range(2)],
            NKl=[c.sb(f"NKl{i}" + s, [128, VB * 16], BF16) for i in range(2)],
            R2l=[c.sb(f"R2l{i}" + s, [128, VB * 16], BF16) for i in range(2)],
            Bl=[c.sb(f"Bl{i}" + s, [16, SB, 128], BF16) for i in range(2)],
            Kl=[c.sb(f"Kl{i}" + s, [16, SB, 128], BF16) for i in range(2)],
            Vr=[c.sb(f"Vr{i}" + s, [16, SB, 512], BF16) for i in range(2)],
            ysb=[c.sb(f"ysb{i}" + s, [64, 16, YB], F32) for i in range(2)],
            ps_sa=c.ps("ps_sa" + s), ps_u=c.ps("ps_u" + s), ps_y=c.ps("ps_y" + s))
        T_[hh] = A
        for nm in ("ST32", "ST16", "stmp"):
            P.op("pool", lambda e, t=A[nm]: e.memset(t[:], 0.0), writes=[nm + s])
        for nm in ("NKl", "R2l", "Bl", "Kl", "Vr"):
            for i in range(2):
                P.op("pool", lambda e, t=A[nm][i]: e.memset(t[:], 0.0), writes=[f"{nm}{i}{s}"])
    P.nosync_self = set(NOSYNC)
    chains = [[] for _ in range(NPASS)]
    for t in range(NSTEP):
        for hh in range(NPASS):
            stages = [[] for _ in range(7)]
            pre = []
            A = T_[hh]
            s = f"_{hh}"
            ST32, ST16, sam = A["ST32"], A["ST16"], A["sam"]
            base = hh * 1024
            fb = (t // VB) % 2
            tv = t % VB
            DEF = _Deferred(P, pre)
            if tv == 0:
                lds = [("xnk", RWM[6], A["xnk"], "xnk" + s, t), ("xr", RWM[0], A["xr"], "xr" + s, t)]
                if t == 0:
                    lds.append(("xw", RWM[3], A["xw"][0], "xw0" + s, 0))
                if t + VB < NSTEP:
                    lds.append(("xw", RWM[3], A["xw"][(fb + 1) % 2], f"xw{(fb + 1) % 2}" + s, t + VB))
                for nm, row, tile_, key, tq in lds:
                    for j in range(2):
                        DEF.dma("sp", tile_[j * 64:(j + 1) * 64, :, :],
                              yT[row * D + base + j * 512: row * D + base + (j + 1) * 512, tq:tq + VB].rearrange("(i k) t -> k i t", k=64),
                              writes=[key])
                for srcT, skey, dstL, dkey in ((A["xnk"], "xnk" + s, A["NKl"][fb], f"NKl{fb}" + s), (A["xr"], "xr" + s, A["R2l"][fb], f"R2l{fb}" + s)):
                    dv = dstL[:].rearrange("p (t i j) -> p t i j", i=8, j=2)
                    for j in range(2):
                        DEF.op("pool", lambda e, dv=dv, srcT=srcT, j=j: e.tensor_copy(
                            out=dv[j * 64:(j + 1) * 64, :, :, j], in_=srcT[j * 64:(j + 1) * 64, :, :].rearrange("p i t -> p t i")),
                            reads=[skey], writes=[dkey])
            bi = (t // SB) % 2
            tl = t % SB
            if tl == 0:
                for oi, (nm, width) in enumerate((("Bl", 128), ("Kl", 128))):
                    for j in range(2):
                        DEF.dma("sp", A[nm][bi][j:16:2, :, j * 64:(j + 1) * 64],
                              tm[hh * 3 + oi, t:t + SB, j * 512:(j + 1) * 512].rearrange("t (i k) -> i t k", k=64),
                              reads=["tm_dram"], writes=[f"{nm}{bi}{s}"])
                vsrc = tm[hh * 3 + 2, t:t + SB, :].rearrange("t (j i v) -> i j t v", j=2, i=8)
                for i in range(8):
                    DEF.dma("sp", A["Vr"][bi][2 * i:2 * i + 2, :, i * 64:(i + 1) * 64], vsrc[i], reads=["tm_dram"], writes=[f"Vr{bi}{s}"])
            NKl, R2l, xw = A["NKl"][fb], A["R2l"][fb], A["xw"][fb]
            Bl, Kl, Vr = A["Bl"][bi], A["Kl"][bi], A["Vr"][bi]
            ps_sa, ps_u, ps_y = A["ps_sa"], A["ps_u"], A["ps_y"]
            stages[0].append(lambda ps_sa=ps_sa, NKl=NKl, ST16=ST16, tv=tv, fb=fb, s=s: P.op(
                "pe", lambda e: e.matmul(ps_sa[0:16, :], NKl[:, tv * 16:(tv + 1) * 16], ST16[:], start=True, stop=True),
                reads=[f"NKl{fb}{s}", "ST16" + s], writes=["ps_sa" + s]))
            stages[1].append(lambda sam=sam, ps_sa=ps_sa, s=s: P.op(
                "dve", lambda e: e.tensor_tensor(out=sam[:], in0=ps_sa[0:16, :], in1=bmask[:], op=ALU.mult),
                reads=["ps_sa" + s, "bmask"], writes=["sam" + s]))

            def mmu(e, ps_u=ps_u, Bl=Bl, Kl=Kl, Vr=Vr, sam=sam, tl=tl):
                e.matmul(ps_u[:], Bl[:, tl, :], sam[:], start=True, stop=False)
                return e.matmul(ps_u[:], Kl[:, tl, :], Vr[:, tl, :], start=False, stop=True)
            stages[2].append(lambda mmu=mmu, bi=bi, s=s: P.op("pe", mmu, reads=[f"Bl{bi}{s}", f"Kl{bi}{s}", f"Vr{bi}{s}", "sam" + s], writes=["ps_u" + s]))

            stmp = A["stmp"]

            tn = t + 1
            xwn = A["xw"][(tn // VB) % 2] if tn < NSTEP else None
            tvn = tn % VB

            def upd(e, ST32=ST32, ST16=ST16, stmp=stmp, ps_u=ps_u, xwn=xwn, tvn=tvn):
                e.tensor_tensor(out=ST16[:], in0=stmp[:], in1=ps_u[:], op=ALU.add)
                ins = e.tensor_tensor(out=ST32[:], in0=stmp[:], in1=ps_u[:], op=ALU.add)
                if xwn is not None:
                    ins = e.tensor_tensor(out=stmp[:].rearrange("p (i v) -> p i v", v=64), in0=ST32[:].rearrange("p (i v) -> p i v", v=64),
                                          in1=xwn[:, :, tvn:tvn + 1].broadcast_to([128, 8, 64]), op=ALU.mult)
                return ins
            stages[3].append(lambda upd=upd, s=s, fbn=(tn // VB) % 2: P.op("dve", upd, reads=["ST32" + s, f"xw{fbn}{s}", "ps_u" + s, "ST16" + s, "stmp" + s], writes=["ST32" + s, "ST16" + s, "stmp" + s]))
            ty = t % YB

            def mmy(e, ps_y=ps_y, ST16=ST16, R2l=R2l, tv=tv, ty=ty):
                ins = None
                for i in range(8):
                    ins = e.matmul(ps_y[0:64, ty * 16 + i * 2: ty * 16 + i * 2 + 2], ST16[:, i * 64:(i + 1) * 64],
                                   R2l[:, tv * 16 + i * 2: tv * 16 + i * 2 + 2], start=True, stop=True)
                return ins
            stages[5].append(lambda mmy=mmy, fb=fb, s=s: P.op("pe", mmy, reads=["ST16" + s, f"R2l{fb}{s}"], writes=["ps_y" + s]))
            if ty == YB - 1:
                def yout(A=A, ps_y=ps_y, s=s, t=t, base=base):
                    yi = (t // YB) % 2
                    ysb = A["ysb"][yi]
                    P.op("act", lambda e: e.copy(out=ysb[:].rearrange("p c t -> p t c"), in_=ps_y[0:64, :].rearrange("p (t c) -> p t c", c=16)),
                         reads=["ps_y" + s], writes=[f"ysb{yi}{s}"])
                    for j in range(2):
                        P.dma("sp", ys[base + j * 512: base + (j + 1) * 512, t - YB + 1:t + 1].rearrange("(i v) t -> v i t", v=64),
                              ysb[:, j:16:2, :], reads=[f"ysb{yi}{s}"], final=final)
                stages[6].append(yout)
            chains[hh].append(pre + [fn for st in stages for fn in st])
    flat = []
    for hh in range(NPASS):
        ops = []
        for step_ops in chains[hh]:
            ops.extend(step_ops)
        flat.append(ops)
    LAG = 3 * hh if False else 3
    n0 = max(len(f) for f in flat)
    for k in range(n0 + LAG * (NPASS - 1)):
        for hh in range(NPASS):
            kk = k - LAG * hh
            if 0 <= kk < len(flat[hh]):
                flat[hh][kk]()
    P.nosync_self = set(DEFAULT_NOSYNC)


def build_rwkvC(T=4096):
    nc = _new_nc()
    xT = _din(nc, "xT", [D, T]); ysT = _din(nc, "ysT", [D, T]); yT = _din(nc, "yT", [RW_ROWS * D, T])
    lnw_d = _din(nc, "lnw", [128, KC]); lnb_d = _din(nc, "lnb", [128, KC]); rk_d = _din(nc, "r_k", [128, KC])
    bones_d = _din(nc, "bones", [128, 128]); w_out = _din(nc, "w_out", [D, D])
    oT = _dout(nc, "oT", [D, T])
    with contextlib.ExitStack() as stack:
        P = Prog(nc, stack); c = Ctx(nc, stack, P)
        emit_rwkvC(c, xT, ysT, yT, lnw_d, lnb_d, rk_d, bones_d, w_out, oT, T, True)
        P.emit()
    return nc


def emit_rwkvC(c, xT, ysT, yT, lnw_d, lnb_d, rk_d, bones_d, w_out, oT, T, final):
    P = c.P
    if True:
        lnw = c.sb("lnw_t", [128, KC], F32); lnb = c.sb("lnb_t", [128, KC], F32); rk = c.sb("rk_t", [128, KC], F32)
        bones = c.sb("bones_t", [128, 128], F32)
        for t, d, k in ((lnw, lnw_d, "lnw"), (lnb, lnb_d, "lnb"), (rk, rk_d, "rk"), (bones, bones_d, "bones")):
            P.dma("sp", t[:], d, writes=[k])
        TH = 1024
        zT = c.sb("zT", [128, KC, TH], BF16)
        names = ["cy", "cr", "ck", "cv", "cg"]
        tl = {nm: [c.sb(f"{nm}{i}", [128, 512], F32) for i in range(2)] for nm in names}
        yc = c.sb("c_yc", [128, 512], F32); sq = c.sb("c_sq", [128, 512], F32); rs = c.sb("c_rs", [128, 512], F32)
        rkk = c.sb("c_rkk", [128, 512], F32)
        ps1 = c.ps("ps_a0"); ps2 = c.ps("ps_a1"); ps3 = c.ps("ps_b0")
        yv = yT.rearrange("(r kc p) t -> r p kc t", p=128, kc=KC)
        ysv = _fm(ysT)
        for hf in range(T // TH):
            for j in range(KC):
                for t0 in range(0, TH, 512):
                    tg = hf * TH + t0
                    bi = c.rot("cbuf", 2)
                    srcs = {"cy": ysv[:, j, tg:tg + 512], "cr": yv[RWM[0]][:, j, tg:tg + 512], "ck": yv[RWM[8]][:, j, tg:tg + 512],
                            "cv": yv[RWM[2]][:, j, tg:tg + 512], "cg": yv[RWM[5]][:, j, tg:tg + 512]}
                    for nm in names:
                        P.dma("sp", tl[nm][bi][:], srcs[nm], writes=[f"{nm}{bi}"])
                    y, r, km, v, g = (tl[nm][bi] for nm in names)
                    ky, kr, kk_, kv, kg = (f"{nm}{bi}" for nm in names)
                    P.op("pe", lambda e, y=y: e.matmul(ps1[:], bones[:], y[:], start=True, stop=True), reads=[ky, "bones"], writes=["ps_a0"])
                    P.op("dve", lambda e, y=y: e.scalar_tensor_tensor(out=yc[:], in0=ps1[:], scalar=-1.0 / 64, in1=y[:], op0=ALU.mult, op1=ALU.add),
                         reads=["ps_a0", ky], writes=["c_yc"])
                    P.op("act", lambda e: e.activation(out=sq[:], in_=yc[:], func=AF.Square), reads=["c_yc"], writes=["c_sq"])
                    P.op("pe", lambda e: e.matmul(ps2[:], bones[:], sq[:], start=True, stop=True), reads=["c_sq", "bones"], writes=["ps_a1"])
                    P.op("act", lambda e: e.activation(out=rs[:], in_=ps2[:], func=AF.Ln, bias=64e-5, scale=1.0 / 64), reads=["ps_a1"], writes=["c_rs"])
                    P.op("act", lambda e: e.activation(out=rs[:], in_=rs[:], func=AF.Exp, scale=-0.5), reads=["c_rs"], writes=["c_rs"])
                    P.op("dve", lambda e: e.tensor_tensor(out=yc[:], in0=yc[:], in1=rs[:], op=ALU.mult), reads=["c_yc", "c_rs"], writes=["c_yc"])
                    P.op("dve", lambda e, j=j: e.tensor_scalar(out=yc[:], in0=yc[:], scalar1=lnw[:, j:j + 1], scalar2=lnb[:, j:j + 1], op0=ALU.mult, op1=ALU.add),
                         reads=["c_yc", "lnw", "lnb"], writes=["c_yc"])
                    P.op("dve", lambda e, j=j, r=r, km=km: e.scalar_tensor_tensor(out=rkk[:], in0=r[:], scalar=rk[:, j:j + 1], in1=km[:], op0=ALU.mult, op1=ALU.mult),
                         reads=[kr, kk_, "rk"], writes=["c_rkk"])
                    P.op("pe", lambda e: e.matmul(ps3[:], bones[:], rkk[:], start=True, stop=True), reads=["c_rkk", "bones"], writes=["ps_b0"])
                    P.op("dve", lambda e, v=v: e.tensor_tensor(out=rkk[:], in0=ps3[:], in1=v[:], op=ALU.mult), reads=["ps_b0", kv], writes=["c_rkk"])
                    P.op("dve", lambda e: e.tensor_tensor(out=yc[:], in0=yc[:], in1=rkk[:], op=ALU.add), reads=["c_yc", "c_rkk"], writes=["c_yc"])
                    P.op("dve", lambda e, j=j, t0=t0, g=g: e.tensor_tensor(out=zT[:, j, t0:t0 + 512], in0=yc[:], in1=g[:], op=ALU.mult), reads=["c_yc", kg], writes=["zT"])
            emit_outproj(c, w_out.rearrange("(c p) n -> p c n", p=128), KC, zT, ["zT"], _fm(xT), _fm(oT), hf * TH, TH, 1.0, final)


TF = SEQ


def build_fused():
    nc = _new_nc()
    A = {}

    def din(name, shape, dt=F32):
        A[name] = _din(nc, name, shape, dt)
        return A[name]
    xT = din("xT", [D, TF]); memT = din("memT", [D, ML]); posb = din("posb", [128, TF], I32)
    din("ffn_gain", [4, 2, 128, KC]); din("ffn_w_gate", [4, 2, D, FF]); din("ffn_w_up", [4, 2, D, FF]); din("ffn_w_down", [4, 2, FF, D])
    din("mix_gain", [4, 128, KC]); din("xg", [4, 128, KC]); din("mg", [4, 128, KC]); din("xq", [4, 128, 4]); din("xk", [4, 128, 4])
    din("xattn_wq", [4, D, D]); din("xattn_wkv", [4, D, 2 * D]); din("xattn_wo", [4, D, D])
    din("conv_w_in", [1, D, 3 * D]); din("conv_cw", [128, KC * 3]); din("conv_w_out", [1, D, D])
    din("dil_w_qkv", [1, D, 9216]); din("dil_gq", [128, 3]); din("dil_gk", [128, 3]); din("dil_w_out", [1, 1024, D])
    din("hgrn_w_in", [1, D, 4 * D]); din("hgrn_lbl", [128, 64]); din("hgrn_ng", [HC, 128]); din("hgrn_w_out", [1, D, D])
    din("rwkv_mu", [128, 6 * KC]); din("rwkv_w_rkv", [1, 3, D, D])
    for nm in ("w0", "a0", "k_k", "k_a", "lnw", "lnb", "r_k"):
        din("rwkv_" + nm, [128, KC])
    din("rwkv_w1", [1, D, 96]); din("rwkv_w2", [1, 96, D]); din("rwkv_a1", [1, D, 96]); din("rwkv_a2", [1, 96, D])
    din("rwkv_g1", [1, D, 256]); din("rwkv_g2", [1, 256, D]); din("rwkv_w_out", [1, D, D])
    din("c_invf", [128, 1]); din("c_rm", [128, 128]); din("c_mask", [128, 256]); din("c_ident", [128, 128])
    din("c_cm", [128, SEQ]); din("c_tm", [HC, HC]); din("c_bones", [128, 128]); din("c_sel", [16, 512])
    oT = _dout(nc, "oT", [D, TF])

    def scratch(name, shape):
        return nc.dram_tensor(name, list(shape), F32, kind="Internal").ap()
    xa = scratch("scr_xa", [D, TF]); xb = scratch("scr_xb", [D, TF])
    yTs = scratch("scr_y", [RW_ROWS * D, TF]); yQs = scratch("scr_q", [9216, TF]); sTs = scratch("scr_s", [D, TF]); ysT = scratch("scr_ys", [D, TF])
    rw_tm = nc.dram_tensor("scr_tm", [6, TF, 1024], BF16, kind="Internal").ap()

    with contextlib.ExitStack() as stack:
        P = Prog(nc, stack)
        state = {"k": 0}

        def stage(fn, last=False):
            k = state["k"]
            state["k"] += 1
            with contextlib.ExitStack() as st:
                c = Ctx(nc, st, P, pfx=f"s{k}_")
                fn(c, st, f"s{k}_")
                P.barrier()
                if last:
                    P.emit()
                else:
                    P.flush()

        cur = xT
        bufs = [xa, xb]
        nb = 0

        def nxt():
            nonlocal nb
            b = bufs[nb % 2]
            nb += 1
            return b

        for i in range(4):
            dst = nxt()
            stage(lambda c, st, pf, cur=cur, dst=dst, i=i: emit_ffn(nc, st, P, cur, A["ffn_gain"][i, 0], A["ffn_w_gate"][i, 0], A["ffn_w_up"][i, 0],
                                                                      A["ffn_w_down"][i, 0], dst, TF, pf, final=False))
            cur = dst
            dst = nxt()
            mg = A["mix_gain"][i]
            if i == 0:
                stage(lambda c, st, pf, cur=cur, dst=dst: emit_conv(c, cur, mg, A["conv_w_in"][0], A["conv_cw"], A["conv_w_out"][0], dst, TF, False))
            elif i == 1:
                stage(lambda c, st, pf, cur=cur: emit_normproj(c, cur, mg, A["dil_w_qkv"][0], yQs, 9216, TF))
                stage(lambda c, st, pf: emit_dilcore(c, yQs, posb, A["c_invf"], A["c_rm"], A["c_mask"], A["c_ident"], A["dil_gq"], A["dil_gk"],
                                                     sTs[0:1024, :], 8, False))
                stage(lambda c, st, pf, cur=cur, dst=dst: emit_outproj_stage(c, cur, sTs[0:1024, :], A["dil_w_out"][0], dst, 8, TF))
            elif i == 2:
                stage(lambda c, st, pf, cur=cur: emit_normproj(c, cur, mg, A["hgrn_w_in"][0], yQs[0:8192, :], 8192, TF))
                stage(lambda c, st, pf: emit_hgrncore(c, yQs[0:8192, :], A["hgrn_lbl"], A["hgrn_ng"], A["c_cm"], A["c_tm"], A["c_ident"], sTs, 16, 2, False))
                stage(lambda c, st, pf, cur=cur, dst=dst: emit_outproj_stage(c, cur, sTs, A["hgrn_w_out"][0], dst, 16, TF))
            else:
                stage(lambda c, st, pf, cur=cur: emit_rwkvA(c, cur, mg, A["rwkv_mu"], A["rwkv_w_rkv"][0], A["rwkv_w0"], A["rwkv_w1"][0], A["rwkv_w2"][0],
                                                            A["rwkv_a0"], A["rwkv_a1"][0], A["rwkv_a2"][0], A["rwkv_g1"][0], A["rwkv_g2"][0],
                                                            A["rwkv_k_k"], A["rwkv_k_a"], A["c_bones"], yTs, TF, False))
                stage(lambda c, st, pf: emit_rwkvB(c, yTs, A["c_sel"], A["c_ident"], rw_tm, ysT, TF, 2, False))
                stage(lambda c, st, pf, cur=cur, dst=dst: emit_rwkvC(c, cur, ysT, yTs, A["rwkv_lnw"], A["rwkv_lnb"], A["rwkv_r_k"], A["c_bones"],
                                                                     A["rwkv_w_out"][0], dst, TF, False))
            cur = dst
            dst = nxt()
            stage(lambda c, st, pf, cur=cur, dst=dst, i=i: emit_xattn(c, cur, memT, A["xg"][i], A["mg"][i], A["xq"][i], A["xk"][i],
                                                                       A["xattn_wq"][i], A["xattn_wkv"][i], A["xattn_wo"][i], dst, TF, False))
            cur = dst
            last = (i == 3)
            dst = oT if last else nxt()
            stage(lambda c, st, pf, cur=cur, dst=dst, i=i, last=last: emit_ffn(nc, st, P, cur, A["ffn_gain"][i, 1], A["ffn_w_gate"][i, 1], A["ffn_w_up"][i, 1],
                                                                                A["ffn_w_down"][i, 1], dst, TF, pf, final=last), last=last)
            cur = dst
    return nc


_NC_CACHE = {}


def _pc(v):
    return np.ascontiguousarray(np.asarray(v, np.float32).reshape(-1, 128).T)


def _c(a):
    return np.ascontiguousarray(a)


def kernel(**inp):
    inp = {k: np.asarray(v) for k, v in inp.items()}
    x = inp["x"]
    B, S, _ = x.shape
    if "fused" not in _NC_CACHE:
        _NC_CACHE["fused"] = build_fused()
    nc = _NC_CACHE["fused"]
    invf, rm, mask = dil_consts()
    cm, tm, ident = hgrn_consts()
    bones, sel = rwkv_consts()
    shared = {
        "ffn_gain": _c(np.stack([np.stack([_pc(inp["ffn_norm"][i, j]) for j in range(2)]) for i in range(4)])),
        "ffn_w_gate": inp["ffn_w_gate"], "ffn_w_up": inp["ffn_w_up"], "ffn_w_down": inp["ffn_w_down"],
        "mix_gain": _c(np.stack([_pc(inp["mix_norm"][i]) for i in range(4)])),
        "xg": _c(np.stack([_pc(inp["xattn_norm"][i]) for i in range(4)])),
        "mg": _c(np.stack([_pc(inp["mem_norm"][i]) for i in range(4)])),
        "xq": _c(np.stack([_pc(inp["xattn_q_gain"][i]) for i in range(4)])),
        "xk": _c(np.stack([_pc(inp["xattn_k_gain"][i]) for i in range(4)])),
        "xattn_wq": inp["xattn_wq"], "xattn_wkv": inp["xattn_wkv"], "xattn_wo": inp["xattn_wo"],
        "conv_w_in": inp["conv_w_in"], "conv_w_out": inp["conv_w_out"],
        "conv_cw": _c(inp["conv_w"][0].T.reshape(16, 128, 3).transpose(1, 0, 2).reshape(128, 48)),
        "dil_w_qkv": inp["dil_w_qkv"], "dil_w_out": inp["dil_w_out"],
        "dil_gq": _c(inp["dil_q_gain"][0].T), "dil_gk": _c(inp["dil_k_gain"][0].T),
        "hgrn_w_in": inp["hgrn_w_in"], "hgrn_w_out": inp["hgrn_w_out"],
        "hgrn_lbl": _c(inp["hgrn_lb_logits"].reshape(4, 16, 128).transpose(2, 1, 0).reshape(128, 64)),
        "hgrn_ng": _c(np.tile(inp["hgrn_norm"][0][None], (HC, 1))),
        "rwkv_mu": _c(np.concatenate([_pc(inp["rwkv_mu"][0][i]) for i in range(6)], axis=1)),
        "rwkv_w_rkv": inp["rwkv_w_rkv"],
        "rwkv_w0": _pc(inp["rwkv_w0"][0]), "rwkv_a0": _pc(inp["rwkv_a0"][0]), "rwkv_k_k": _pc(inp["rwkv_k_k"][0]),
        "rwkv_k_a": _pc(inp["rwkv_k_a"][0]), "rwkv_lnw": _pc(inp["rwkv_ln_w"][0]), "rwkv_lnb": _pc(inp["rwkv_ln_b"][0]),
        "rwkv_r_k": _pc(inp["rwkv_r_k"][0].reshape(-1)),
        "rwkv_w1": inp["rwkv_w1"], "rwkv_w2": inp["rwkv_w2"], "rwkv_a1": inp["rwkv_a1"], "rwkv_a2": inp["rwkv_a2"],
        "rwkv_g1": inp["rwkv_g1"], "rwkv_g2": inp["rwkv_g2"], "rwkv_w_out": inp["rwkv_w_out"],
        "c_invf": invf, "c_rm": rm, "c_mask": mask, "c_ident": ident, "c_cm": cm, "c_tm": tm, "c_bones": bones, "c_sel": rwkv_bmask(),
    }
    shared = {k: _c(np.asarray(v, np.float32)) for k, v in shared.items()}
    in_maps = []
    for b in range(B):
        m = dict(shared)
        m["xT"] = _c(x[b].T)
        m["memT"] = _c(inp["mem"][b].T)
        m["posb"] = _c(np.tile(inp["positions"][b][None].astype(np.int32), (128, 1)))
        in_maps.append(m)
    res = run_bass_kernel_spmd(nc, in_maps, core_ids=list(range(B)))
    out = np.empty((B, S, D), np.float32)
    for b in range(B):
        out[b] = res.results[b]["oT"].T
    return out
```

```python
import contextlib
import numpy as np
import concourse.bass as bass
import concourse.mybir as mybir
from concourse.bass_utils import run_bass_kernel_spmd

F32 = mybir.dt.float32
BF16 = mybir.dt.bfloat16
I32 = mybir.dt.int32
AF = mybir.ActivationFunctionType
ALU = mybir.AluOpType
AX = mybir.AxisListType

D = 2048
KC = D // 128
FF = 5632
FC = FF // 128
NCORES = 8
EPS = 1e-6


DEFAULT_NOSYNC = ()


class Prog:
    ENGS = ("pe", "act", "dve", "pool", "sp")
    NRING = 12

    def __init__(self, nc, stack):
        self.nc = nc
        self.stack = stack
        self.streams = {e: [] for e in self.ENGS}
        self.count = {e: 0 for e in self.ENGS}
        self.sems = {}
        for e in ("pe", "act", "dve", "pool"):
            self.sems[e] = stack.enter_context(nc.semaphore("s_" + e))
        self.rings = {}
        self.dma_k = {}
        for q in ("sp", "pool", "act"):
            self.rings[q] = [stack.enter_context(nc.semaphore(f"r_{q}{i}")) for i in range(self.NRING)]
            self.dma_k[q] = 0
        self.seen = {e: {} for e in self.ENGS}
        self.nosync_self = set(DEFAULT_NOSYNC)
        self.res = {}
        self.final_events = []

    def _need(self, eng, ev, waits):
        if ev is None:
            return
        sem, val = ev
        if eng in self.nosync_self and eng in self.sems and sem is self.sems[eng]:
            return
        key = id(sem)
        if self.seen[eng].get(key, 0) >= val:
            return
        self.seen[eng][key] = val
        waits.append((sem, val))

    def _deps(self, eng, reads, writes):
        waits = []
        for r in reads:
            st = self.res.get(r)
            if st is not None:
                self._need(eng, st["w"], waits)
        for w in writes:
            st = self.res.get(w)
            if st is not None:
                self._need(eng, st["w"], waits)
                for ev in st["r"].values():
                    self._need(eng, ev, waits)
        return waits

    def _commit(self, ev, reads, writes):
        for r in reads:
            st = self.res.setdefault(r, {"w": None, "r": {}})
            st["r"][id(ev[0])] = ev
        for w in writes:
            self.res[w] = {"w": ev, "r": {}}

    def op(self, eng, fn, reads=(), writes=()):
        waits = self._deps(eng, reads, writes)
        self.count[eng] += 1
        ev = (self.sems[eng], self.count[eng])
        self.streams[eng].append((waits, fn, (self.sems[eng], 1)))
        self._commit(ev, reads, writes)
        return ev

    def dma(self, q, out, in_, reads=(), writes=(), final=False):
        waits = self._deps(q, reads, writes)
        k = self.dma_k[q]
        self.dma_k[q] += 1
        sem = self.rings[q][k % self.NRING]
        gen = k // self.NRING
        if gen > 0:
            self._need(q, (sem, 16 * gen), waits)
        ev = (sem, 16 * (gen + 1))

        def fn(e, out=out, in_=in_):
            return e.dma_start(out=out, in_=in_, allow_slow_non_contiguous=True)

        self.streams[q].append((waits, fn, (sem, 16)))
        self._commit(ev, reads, writes)
        if final:
            self.final_events.append(ev)
        return ev

    def barrier(self):
        evs = []
        for e in ("pe", "act", "dve", "pool"):
            if self.count[e] > 0:
                evs.append((self.sems[e], self.count[e]))
        for q in ("sp", "pool", "act"):
            k = self.dma_k[q]
            for i in range(self.NRING):
                n = (k - i + self.NRING - 1) // self.NRING if k > i else 0
                if n > 0:
                    evs.append((self.rings[q][i], 16 * n))
        for eng in self.ENGS:
            waits = []
            for ev in evs:
                self._need(eng, ev, waits)
            if waits:
                self.streams[eng].append((waits, None, None))
        self.res = {}

    def emit(self):
        fw = []
        for ev in self.final_events:
            self._need("sp", ev, fw)
        self.streams["sp"].append((fw, None, None))
        self.flush()

    def flush(self):
        nc = self.nc
        with nc.Block() as block:
            def run(eng_obj, name):
                for waits, fn, inc in self.streams[name]:
                    for sem, val in waits:
                        eng_obj.wait_ge(sem, val)
                    if fn is not None:
                        ins = fn(eng_obj)
                        ins.then_inc(inc[0], inc[1])

            @block.tensor
            def _(e):
                run(e, "pe")

            @block.scalar
            def _(e):
                run(e, "act")

            @block.vector
            def _(e):
                run(e, "dve")

            @block.gpsimd
            def _(e):
                run(e, "pool")

            @block.sync
            def _(e):
                run(e, "sp")
        self.streams = {e: [] for e in self.ENGS}


def _sb(nc, stack, name, shape, dt):
    return stack.enter_context(nc.sbuf_tensor("t_" + name, list(shape), dt))


def _ps(nc, stack, name, shape, dt=F32):
    return stack.enter_context(nc.psum_tensor("t_" + name, list(shape), dt))


class Ctx:
    def __init__(self, nc, stack, P, pfx=""):
        self.nc, self.stack, self.P, self.pfx = nc, stack, P, pfx
        self.tiles = {}
        self.cnt = {}
        self.nxb = 1
        self.pwg = 1

    def sb(self, name, shape, dt):
        if name not in self.tiles:
            self.tiles[name] = _sb(self.nc, self.stack, self.pfx + name, shape, dt)
        return self.tiles[name]

    def ps(self, name, shape=(128, 512), dt=F32):
        if name not in self.tiles:
            self.tiles[name] = _ps(self.nc, self.stack, self.pfx + name, shape, dt)
        return self.tiles[name]

    def rot(self, name, n):
        k = self.cnt.get(name, 0)
        self.cnt[name] = k + 1
        return k % n

    def const_ones(self):
        if "ones32" not in self.tiles:
            t = self.sb("ones32", [128, 128], F32)
            self.P.op("pool", lambda e: e.memset(t[:], 1.0), writes=["ones32"])
            tb = self.sb("ones16", [128, 128], BF16)
            self.P.op("pool", lambda e: e.memset(tb[:], 1.0), writes=["ones16"])
        return self.tiles["ones32"], self.tiles["ones16"]


def emit_norm(c, src, ntok, gn, xn, xoff, xn_key, eps=EPS, scale_d=1.0 / D, kcs=KC, sumsq_ones=None, ones_key="ones32", gn_key="gn"):
    P = c.P
    ones32, _ = c.const_ones()
    if sumsq_ones is None:
        sumsq_ones = ones32
    TN = 256
    xins = [c.sb(f"n_xin{i}", [128, KC, TN], F32) for i in range(c.nxb)]
    sqs = [c.sb(f"n_sq{i}", [128, TN], F32) for i in range(2)]
    rstd = c.sb("n_rstd", [128, TN], F32)
    psn = c.ps("ps_n", [128, 512])
    for a in range(0, ntok, TN):
        n = min(TN, ntok - a)
        xb = c.rot("n_xin", c.nxb)
        xin = xins[xb]
        xkey = f"n_xin{xb}"
        P.dma("sp", xin[:, 0:kcs, 0:n], src[:, :, a:a + n], writes=[xkey])
        for kc in range(kcs):
            si = c.rot("n_sq", 2)
            s = sqs[si]
            P.op("act", lambda e, s=s, kc=kc, n=n, xin=xin: e.activation(out=s[:, 0:n], in_=xin[:, kc, 0:n], func=AF.Square),
                 reads=[xkey], writes=[f"n_sq{si}"])
            P.op("pe", lambda e, s=s, kc=kc, n=n: e.matmul(psn[:, 0:n], sumsq_ones[:], s[:, 0:n], start=(kc == 0), stop=(kc == kcs - 1)),
                 reads=[f"n_sq{si}", ones_key], writes=["ps_n"])
        P.op("act", lambda e, n=n: e.activation(out=rstd[:, 0:n], in_=psn[:, 0:n], func=AF.Ln, bias=eps, scale=scale_d),
             reads=["ps_n"], writes=["n_rstd"])
        P.op("act", lambda e, n=n: e.activation(out=rstd[:, 0:n], in_=rstd[:, 0:n], func=AF.Exp, scale=-0.5), reads=["n_rstd"], writes=["n_rstd"])
        for kc in range(kcs):
            P.op("dve", lambda e, kc=kc, a=a, n=n, xin=xin: e.scalar_tensor_tensor(
                out=xn[:, kc, xoff + a:xoff + a + n], in0=xin[:, kc, 0:n], scalar=gn[:, kc:kc + 1],
                in1=rstd[:, 0:n], op0=ALU.mult, op1=ALU.mult),
                reads=[xkey, "n_rstd", gn_key], writes=[xn_key])


def emit_proj(c, wv, col0, nchunks, xn, xn_keys, ntok, cb, kcs=KC, cw=128, toff=0):
    P = c.P
    G = c.pwg if (cw == 128 and nchunks % c.pwg == 0) else 1
    wb = [c.sb(f"p_w{i}", [128, KC, 128 * c.pwg], BF16) for i in range(2)]
    pss = [c.ps(f"ps_a{i}") for i in range(2)]
    for j0 in range(0, nchunks, G):
        b = c.rot("p_w", 2)
        P.dma("pool", wb[b][:, 0:kcs, 0:cw * G], wv[:, :, col0 + j0 * cw: col0 + (j0 + G) * cw], writes=[f"p_w{b}"])
        for jj in range(G):
            j = j0 + jj
            for t0 in range(0, ntok, 512):
                n = min(512, ntok - t0)
                pi = c.rot("ps_a", 2)
                ps = pss[pi]

                def mm(e, b=b, ps=ps, t0=t0, n=n, jj=jj):
                    ins = None
                    for kc in range(kcs):
                        ins = e.matmul(ps[0:cw, 0:n], wb[b][:, kc, jj * cw:(jj + 1) * cw], xn[:, kc, toff + t0:toff + t0 + n],
                                       start=(kc == 0), stop=(kc == kcs - 1))
                    return ins
                P.op("pe", mm, reads=[f"p_w{b}"] + list(xn_keys), writes=[f"ps_a{pi}"])
                cb(j, t0, n, ps, f"ps_a{pi}")


def emit_outproj(c, wv, cc, src, src_keys, xTv, oTv, t0g, ntok, scale, final, soff=0):
    P = c.P
    wb = [c.sb(f"o_w{cc}_{i}", [128, cc, 128], BF16) for i in range(2)]
    pss = [c.ps(f"ps_c{i}") for i in range(2)]
    xres = [c.sb(f"o_xres{i}", [128, 512], F32) for i in range(2)]
    osb = [c.sb(f"o_osb{i}", [128, 512], F32) for i in range(2)]
    for nn in range(KC):
        b = c.rot("o_w", 2)
        P.dma("pool", wb[b][:, 0:cc, :], wv[:, :, nn * 128:(nn + 1) * 128], writes=[f"o_w{cc}_{b}"])
        for t0 in range(0, ntok, 512):
            n = min(512, ntok - t0)
            pb = c.rot("ps_c", 2)
            P.dma("sp", xres[pb][:, 0:n], xTv[:, nn, t0g + t0:t0g + t0 + n], writes=[f"o_xres{pb}"])

            def mm(e, b=b, pb=pb, t0=t0, n=n):
                ins = None
                for f in range(cc):
                    ins = e.matmul(pss[pb][:, 0:n], wb[b][:, f, :], src[:, f, soff + t0:soff + t0 + n],
                                   start=(f == 0), stop=(f == cc - 1))
                return ins
            P.op("pe", mm, reads=[f"o_w{cc}_{b}"] + list(src_keys), writes=[f"ps_c{pb}"])
            P.op("dve", lambda e, pb=pb, n=n: e.scalar_tensor_tensor(
                out=osb[pb][:, 0:n], in0=pss[pb][:, 0:n], scalar=float(scale), in1=xres[pb][:, 0:n],
                op0=ALU.mult, op1=ALU.add),
                reads=[f"ps_c{pb}", f"o_xres{pb}"], writes=[f"o_osb{pb}"])
            P.dma("sp", oTv[:, nn, t0g + t0:t0g + t0 + n], osb[pb][:, 0:n], reads=[f"o_osb{pb}"], final=final)


def _fm(ap):
    return ap.rearrange("(kc p) t -> p kc t", p=128)


def _new_nc():
    return bass.Bass("TRN2", target_bir_lowering=False)


def _din(nc, name, shape, dt=F32):
    return nc.dram_tensor(name, list(shape), dt, kind="ExternalInput").ap()


def _dout(nc, name, shape, dt=F32):
    return nc.dram_tensor(name, list(shape), dt, kind="ExternalOutput").ap()


def emit_normproj(c, xT, gain, w, yT, N, T, final=False):
    P = c.P
    c.nxb, c.pwg = 2, 2
    gn = c.sb("gn", [128, KC], F32)
    P.dma("sp", gn[:], gain, writes=["gn"])
    TB = min(T, 2048)
    xn = c.sb("xn", [128, KC, TB], BF16)
    ysb = [c.sb(f"ysb{i}", [128, 512], F32) for i in range(2)]
    yv = _fm(yT)
    for tb in range(0, T, TB):
        emit_norm(c, _fm(xT)[:, :, tb:tb + TB], TB, gn, xn, 0, "xn")

        def cb(j, t0, n, ps, psk, tb=tb):
            i = c.rot("ysb", 2)
            P.op("act", lambda e: e.copy(out=ysb[i][:, 0:n], in_=ps[:, 0:n]), reads=[psk], writes=[f"ysb{i}"])
            P.dma("sp", yv[:, j, tb + t0:tb + t0 + n], ysb[i][:, 0:n], reads=[f"ysb{i}"], final=final)
        emit_proj(c, _fm(w), 0, N // 128, xn, ["xn"], TB, cb)


def build_normproj(N, T=2048):
    nc = _new_nc()
    xT = _din(nc, "xT", [D, T]); gain = _din(nc, "gain", [128, KC]); w = _din(nc, "w", [D, N])
    yT = _dout(nc, "yT", [N, T])
    with contextlib.ExitStack() as stack:
        P = Prog(nc, stack); c = Ctx(nc, stack, P)
        emit_normproj(c, xT, gain, w, yT, N, T, final=True)
        P.emit()
    return nc


def emit_outproj_stage(c, xT, sT, w, oT, CC, T, final=False):
    P = c.P
    TB = min(T, 2048)
    src = c.sb("src", [128, CC, TB], BF16)
    sv = sT.rearrange("(c p) t -> p c t", p=128)
    for tb in range(0, T, TB):
        for cc in range(CC):
            P.dma("pool", src[:, cc, :], sv[:, cc, tb:tb + TB], writes=["src"])
        emit_outproj(c, w.rearrange("(c p) n -> p c n", p=128), CC, src, ["src"], _fm(xT), _fm(oT), tb, TB, 1.0, final)


def build_outproj(CC, T=2048):
    nc = _new_nc()
    xT = _din(nc, "xT", [D, T]); sT = _din(nc, "sT", [CC * 128, T]); w = _din(nc, "w", [CC * 128, D])
    oT = _dout(nc, "oT", [D, T])
    with contextlib.ExitStack() as stack:
        P = Prog(nc, stack); c = Ctx(nc, stack, P)
        emit_outproj_stage(c, xT, sT, w, oT, CC, T, final=True)
        P.emit()
    return nc
def build_ffn(T=2048):
    nc = bass.Bass("TRN2", target_bir_lowering=False)
    xT = nc.dram_tensor("xT", [D, T], F32, kind="ExternalInput").ap()
    gain = nc.dram_tensor("gain", [128, KC], F32, kind="ExternalInput").ap()
    wg = nc.dram_tensor("wg", [D, FF], F32, kind="ExternalInput").ap()
    wu = nc.dram_tensor("wu", [D, FF], F32, kind="ExternalInput").ap()
    wd = nc.dram_tensor("wd", [FF, D], F32, kind="ExternalInput").ap()
    oT = nc.dram_tensor("oT", [D, T], F32, kind="ExternalOutput").ap()
    with contextlib.ExitStack() as stack:
        P = Prog(nc, stack)
        emit_ffn(nc, stack, P, xT, gain, wg, wu, wd, oT, T, "f")
        P.emit()
    return nc


def emit_ffn(nc, stack, P, xT, gain, wg, wu, wd, oT, T, pfx, final=True):
    TH = 1024
    NH = T // TH
    TN = 256
    TT = 512
    FG = 2
    xTv = xT.rearrange("(kc p) t -> p kc t", p=128)
    oTv = oT.rearrange("(kc p) t -> p kc t", p=128)
    wgv = wg.rearrange("(kc p) f -> p kc f", p=128)
    wuv = wu.rearrange("(kc p) f -> p kc f", p=128)
    wdv = wd.rearrange("(fc p) n -> p fc n", p=128)

    act = _sb(nc, stack, pfx + "act", [128, FC, TH], BF16)
    xn = _sb(nc, stack, pfx + "xn", [128, KC, TH], BF16)
    wgb = [_sb(nc, stack, pfx + f"wg{i}", [128, KC, FG * 128], BF16) for i in range(2)]
    wub = [_sb(nc, stack, pfx + f"wu{i}", [128, KC, FG * 128], BF16) for i in range(2)]
    wdb = [_sb(nc, stack, pfx + f"wd{i}", [128, FC, 128], BF16) for i in range(2)]
    xin = _sb(nc, stack, pfx + "xin", [128, KC, TN], F32)
    sq = [_sb(nc, stack, pfx + f"sq{i}", [128, TN], F32) for i in range(2)]
    rstd = _sb(nc, stack, pfx + "rstd", [128, TN], F32)
    ones = _sb(nc, stack, pfx + "ones", [128, 128], F32)
    gn = _sb(nc, stack, pfx + "gn", [128, KC], F32)
    sil = [_sb(nc, stack, pfx + f"sil{i}", [128, TT], F32) for i in range(2)]
    xres = [_sb(nc, stack, pfx + f"xres{i}", [128, TT], F32) for i in range(2)]
    osb = [_sb(nc, stack, pfx + f"osb{i}", [128, TT], F32) for i in range(2)]
    ps_n = _ps(nc, stack, pfx + "psn", [128, TN])
    ps_g = [_ps(nc, stack, pfx + f"psg{i}", [128, TT]) for i in range(2)]
    ps_u = [_ps(nc, stack, pfx + f"psu{i}", [128, TT]) for i in range(2)]
    ps_o = [_ps(nc, stack, pfx + f"pso{i}", [128, TT]) for i in range(2)]

    K = lambda *a: (pfx,) + a
    P.op("pool", lambda e: e.memset(ones[:], 1.0), writes=[K("ones")])
    P.dma("sp", gn[:], gain, writes=[K("gn")])

    cnt = {"gi": 0, "di": 0, "ei": 0, "si": 0}
    NTT = TH // TN
    xn_keys = [K("xn", nt) for nt in range(NTT)]
    act_keys = [K("act", f) for f in range(FC)]

    def norm_tile(h, nt):
        ta = h * TH + nt * TN
        P.dma("sp", xin[:], xTv[:, :, ta:ta + TN], writes=[K("xin")])
        for kc in range(KC):
            si = cnt["si"]
            s = sq[si % 2]
            P.op("act", lambda e, s=s, kc=kc: e.activation(out=s[:], in_=xin[:, kc, :], func=AF.Square),
                 reads=[K("xin")], writes=[K("sq", si % 2)])
            P.op("pe", lambda e, s=s, kc=kc: e.matmul(ps_n[:], ones[:], s[:], start=(kc == 0), stop=(kc == KC - 1)),
                 reads=[K("sq", si % 2), K("ones")], writes=[K("psn")])
            cnt["si"] += 1
        P.op("act", lambda e: e.activation(out=rstd[:], in_=ps_n[:], func=AF.Ln, bias=EPS, scale=1.0 / D),
             reads=[K("psn")], writes=[K("rstd")])
        P.op("act", lambda e: e.activation(out=rstd[:], in_=rstd[:], func=AF.Exp, scale=-0.5), reads=[K("rstd")], writes=[K("rstd")])
        for kc in range(KC):
            P.op("dve", lambda e, kc=kc, nt=nt: e.scalar_tensor_tensor(
                out=xn[:, kc, nt * TN:(nt + 1) * TN], in0=xin[:, kc, :], scalar=gn[:, kc:kc + 1],
                in1=rstd[:], op0=ALU.mult, op1=ALU.mult),
                reads=[K("xin"), K("rstd"), K("gn")], writes=[K("xn", nt)])

    def gate_up(h):
        for fg in range(FC // FG):
            b = cnt["gi"] % 2
            f0 = fg * FG * 128
            P.dma("pool", wgb[b][:], wgv[:, :, f0:f0 + FG * 128], writes=[K("wg", b)])
            P.dma("pool", wub[b][:], wuv[:, :, f0:f0 + FG * 128], writes=[K("wu", b)])
            for fc in range(FG):
                f = fg * FG + fc
                for tt in range(TH // TT):
                    pb = cnt["ei"] % 2

                    def mm(e, wt, pt, fc=fc, tt=tt):
                        ins = None
                        for kc in range(KC):
                            ins = e.matmul(pt[:], wt[:, kc, fc * 128:(fc + 1) * 128],
                                           xn[:, kc, tt * TT:(tt + 1) * TT],
                                           start=(kc == 0), stop=(kc == KC - 1))
                        return ins
                    P.op("pe", lambda e, b=b, pb=pb, mm=mm: mm(e, wgb[b], ps_g[pb]),
                         reads=[K("wg", b)] + xn_keys, writes=[K("psg", pb)])
                    P.op("pe", lambda e, b=b, pb=pb, mm=mm: mm(e, wub[b], ps_u[pb]),
                         reads=[K("wu", b)] + xn_keys, writes=[K("psu", pb)])
                    P.op("act", lambda e, pb=pb: e.activation(out=sil[pb][:], in_=ps_g[pb][:], func=AF.Silu),
                         reads=[K("psg", pb)], writes=[K("sil", pb)])
                    P.op("dve", lambda e, pb=pb, f=f, tt=tt: e.tensor_tensor(
                        out=act[:, f, tt * TT:(tt + 1) * TT], in0=sil[pb][:], in1=ps_u[pb][:], op=ALU.mult),
                        reads=[K("sil", pb), K("psu", pb)], writes=[K("act", f)])
                    cnt["ei"] += 1
            cnt["gi"] += 1

    def down_chunk(h, n):
        t0 = h * TH
        b = cnt["di"] % 2
        P.dma("pool", wdb[b][:], wdv[:, :, n * 128:(n + 1) * 128], writes=[K("wd", b)])
        for tt in range(TH // TT):
            pb = cnt["ei"] % 2
            ta = t0 + tt * TT
            P.dma("sp", xres[pb][:], xTv[:, n, ta:ta + TT], writes=[K("xres", pb)])

            def mmd(e, b=b, pb=pb, tt=tt):
                ins = None
                for f in range(FC):
                    ins = e.matmul(ps_o[pb][:], wdb[b][:, f, :], act[:, f, tt * TT:(tt + 1) * TT],
                                   start=(f == 0), stop=(f == FC - 1))
                return ins
            P.op("pe", mmd, reads=[K("wd", b)] + act_keys, writes=[K("pso", pb)])
            P.op("dve", lambda e, pb=pb: e.scalar_tensor_tensor(
                out=osb[pb][:], in0=ps_o[pb][:], scalar=0.5, in1=xres[pb][:], op0=ALU.mult, op1=ALU.add),
                reads=[K("pso", pb), K("xres", pb)], writes=[K("osb", pb)])
            P.dma("sp", oTv[:, n, ta:ta + TT], osb[pb][:], reads=[K("osb", pb)], final=final)
            cnt["ei"] += 1
        cnt["di"] += 1

    for nt in range(NTT):
        norm_tile(0, nt)
    for h in range(NH):
        gate_up(h)
        per = KC // NTT
        for n in range(KC):
            down_chunk(h, n)
            if h + 1 < NH and n % per == per - 1:
                norm_tile(h + 1, n // per)
XH, XD, ML = 4, 512, 256


def build_xattn(T=2048):
    nc = _new_nc()
    xT = _din(nc, "xT", [D, T]); memT = _din(nc, "memT", [D, ML])
    gx = _din(nc, "gx", [128, KC]); gm = _din(nc, "gm", [128, KC])
    gq = _din(nc, "gq", [128, 4]); gk = _din(nc, "gk", [128, 4])
    wq = _din(nc, "wq", [D, D]); wkv = _din(nc, "wkv", [D, 2 * D]); wo = _din(nc, "wo", [D, D])
    oT = _dout(nc, "oT", [D, T])
    with contextlib.ExitStack() as stack:
        P = Prog(nc, stack); c = Ctx(nc, stack, P)
        emit_xattn(c, xT, memT, gx, gm, gq, gk, wq, wkv, wo, oT, T, True)
        P.emit()
    return nc


def emit_xattn(c, xT, memT, gx, gm, gq, gk, wq, wkv, wo, oT, T, final):
    P = c.P
    c.pwg = 2
    if True:
        ones32, ones16 = c.const_ones()
        gxt = c.sb("gx", [128, KC], F32); gmt = c.sb("gm", [128, KC], F32)
        gqt = c.sb("gq", [128, 4], F32); gkt = c.sb("gk", [128, 4], F32)
        P.dma("sp", gxt[:], gx, writes=["gx"]); P.dma("sp", gmt[:], gm, writes=["gm"])
        P.dma("sp", gqt[:], gq, writes=["gq"]); P.dma("sp", gkt[:], gk, writes=["gk"])
        memn = c.sb("memn", [128, KC, ML], BF16)
        emit_norm(c, _fm(memT), ML, gmt, memn, 0, "memn", gn_key="gm")
        kT = c.sb("kT", [128, KC, ML], BF16)
        kraw = c.sb("kraw", [128, 4, 512], F32)
        sq = [c.sb(f"x_sq{i}", [128, 512], F32) for i in range(2)]
        rs = c.sb("x_rs", [128, 512], F32)
        psn = c.ps("ps_n")
        scale_h = 1.0 / XD

        def headnorm(raw, rawkey, n, gt, gkey, dst_fn, dstkey):
            for dc in range(4):
                si = c.rot("x_sq", 2)
                P.op("act", lambda e, si=si, dc=dc: e.activation(out=sq[si][:, 0:n], in_=raw[:, dc, 0:n], func=AF.Square),
                     reads=[rawkey], writes=[f"x_sq{si}"])
                P.op("pe", lambda e, si=si, dc=dc: e.matmul(psn[:, 0:n], ones32[:], sq[si][:, 0:n], start=(dc == 0), stop=(dc == 3)),
                     reads=[f"x_sq{si}", "ones32"], writes=["ps_n"])
            P.op("act", lambda e: e.activation(out=rs[:, 0:n], in_=psn[:, 0:n], func=AF.Ln, bias=EPS, scale=scale_h),
                 reads=["ps_n"], writes=["x_rs"])
            P.op("act", lambda e: e.activation(out=rs[:, 0:n], in_=rs[:, 0:n], func=AF.Exp, scale=-0.5), reads=["x_rs"], writes=["x_rs"])
            for dc in range(4):
                P.op("dve", lambda e, dc=dc: e.scalar_tensor_tensor(
                    out=dst_fn(dc), in0=raw[:, dc, 0:n], scalar=gt[:, dc:dc + 1], in1=rs[:, 0:n],
                    op0=ALU.mult, op1=ALU.mult), reads=[rawkey, "x_rs", gkey], writes=[dstkey])

        def cb_k(j, t0, n, ps, psk):
            dc = j % 4
            P.op("act", lambda e: e.copy(out=kraw[:, dc, 0:n], in_=ps[:, 0:n]), reads=[psk], writes=["kraw"])
            if dc == 3:
                h = j // 4
                headnorm(kraw, "kraw", ML, gkt, "gk", lambda dc2: kT[:, h * 4 + dc2, :], "kT")
        emit_proj(c, _fm(wkv), 0, KC, memn, ["memn"], ML, cb_k)
        v_sb = c.sb("v_sb", [128, 2, D], BF16)
        wvb = [c.sb(f"x_wv{i}", [128, KC, 256], BF16) for i in range(2)]
        wkvv = _fm(wkv)
        psb = [c.ps(f"ps_b{i}") for i in range(2)]
        for ct in range(8):
            b = c.rot("x_wv", 2)
            P.dma("pool", wvb[b][:], wkvv[:, :, D + ct * 256: D + (ct + 1) * 256], writes=[f"x_wv{b}"])
            for mc in range(2):
                pi = c.rot("ps_b", 2)

                def mm(e, b=b, pi=pi, mc=mc):
                    ins = None
                    for kc in range(KC):
                        ins = e.matmul(psb[pi][:, 0:256], memn[:, kc, mc * 128:(mc + 1) * 128], wvb[b][:, kc, :],
                                       start=(kc == 0), stop=(kc == KC - 1))
                    return ins
                P.op("pe", mm, reads=[f"x_wv{b}", "memn"], writes=[f"ps_b{pi}"])
                P.op("act", lambda e, pi=pi, mc=mc, ct=ct: e.copy(out=v_sb[:, mc, ct * 256:(ct + 1) * 256], in_=psb[pi][:, 0:256]),
                     reads=[f"ps_b{pi}"], writes=["v_sb"])
        TH = 1024
        xn = c.sb("xn", [128, KC, TH], BF16)
        oall = c.sb("oall", [128, KC, TH], BF16)
        qraw = c.sb("qraw", [128, 4, 1024], F32)
        qns = [c.sb(f"qn{i}", [128, 4, 512], BF16) for i in range(2)]
        Es = [c.sb(f"E{i}", [128, 2, 512], BF16) for i in range(2)]
        rdens = [c.sb(f"rden{i}", [128, 512], F32) for i in range(2)]
        psc = [c.ps(f"ps_c{i}") for i in range(2)]
        sm_scale = float(XD) ** -0.5
        for hf in range(T // TH):
            tg = hf * TH
            emit_norm(c, _fm(xT)[:, :, tg:tg + TH], TH, gxt, xn, 0, "xn", gn_key="gx")

            def attn(h, t0, n):
                bq = (t0 // 512) % 2
                qn, E, rden = qns[bq], Es[bq], rdens[bq]
                kq, kE, kr = f"qn{bq}", f"E{bq}", f"rden{bq}"
                for mc in range(2):
                    pi = c.rot("ps_b", 2)

                    def mm(e, pi=pi, mc=mc):
                        ins = None
                        for dc in range(4):
                            ins = e.matmul(psb[pi][:, 0:n], kT[:, h * 4 + dc, mc * 128:(mc + 1) * 128], qn[:, dc, 0:n],
                                           start=(dc == 0), stop=(dc == 3))
                        return ins
                    P.op("pe", mm, reads=["kT", kq], writes=[f"ps_b{pi}"])
                    P.op("act", lambda e, pi=pi, mc=mc: e.activation(out=E[:, mc, 0:n], in_=psb[pi][:, 0:n], func=AF.Exp, scale=sm_scale),
                         reads=[f"ps_b{pi}"], writes=[kE])

                def mmz(e):
                    ins = None
                    for mc in range(2):
                        ins = e.matmul(psn[:, 0:n], ones16[:], E[:, mc, 0:n], start=(mc == 0), stop=(mc == 1))
                    return ins
                P.op("pe", mmz, reads=[kE, "ones16"], writes=["ps_n"])
                P.op("dve", lambda e: e.reciprocal(out=rden[:, 0:n], in_=psn[:, 0:n]), reads=["ps_n"], writes=[kr])
                for dc in range(4):
                    pi = c.rot("ps_c", 2)

                    def mmo(e, pi=pi, dc=dc):
                        ins = None
                        for mc in range(2):
                            ins = e.matmul(psc[pi][:, 0:n], v_sb[:, mc, h * XD + dc * 128: h * XD + (dc + 1) * 128], E[:, mc, 0:n],
                                           start=(mc == 0), stop=(mc == 1))
                        return ins
                    P.op("pe", mmo, reads=[kE, "v_sb"], writes=[f"ps_c{pi}"])
                    P.op("dve", lambda e, pi=pi, dc=dc: e.tensor_tensor(
                        out=oall[:, h * 4 + dc, t0:t0 + n], in0=psc[pi][:, 0:n], in1=rden[:, 0:n], op=ALU.mult),
                        reads=[f"ps_c{pi}", kr], writes=["oall"])

            def cb_q(h, dc, t0, n, ps, psk):
                P.op("act", lambda e: e.copy(out=qraw[:, dc, t0:t0 + n], in_=ps[:, 0:n]), reads=[psk], writes=[f"qraw{t0}"])
            for h in range(XH):
                def cb2(j, t0, n, ps, psk, h=h):
                    cb_q(h, j, t0, n, ps, psk)
                emit_proj(c, _fm(wq), h * XD, 4, xn, ["xn"], TH, cb2)
                for t0 in range(0, TH, 512):
                    bq = (t0 // 512) % 2
                    headnorm(qraw[:, :, t0:t0 + 512], f"qraw{t0}", 512, gqt, "gq", lambda dc2, bq=bq: qns[bq][:, dc2, 0:512], f"qn{bq}")
                for t0 in range(0, TH, 512):
                    attn(h, t0, 512)
            emit_outproj(c, wo.rearrange("(c p) n -> p c n", p=128), KC, oall, ["oall"], _fm(xT), _fm(oT), tg, TH, 1.0, final)
def build_conv(T=4096):
    nc = _new_nc()
    xT = _din(nc, "xT", [D, T]); gain = _din(nc, "gain", [128, KC])
    w_in = _din(nc, "w_in", [D, 3 * D]); cwd = _din(nc, "cw", [128, KC * 3]); w_out = _din(nc, "w_out", [D, D])
    oT = _dout(nc, "oT", [D, T])
    with contextlib.ExitStack() as stack:
        P = Prog(nc, stack); c = Ctx(nc, stack, P)
        emit_conv(c, xT, gain, w_in, cwd, w_out, oT, T, True)
        P.emit()
    return nc


def emit_conv(c, xT, gain, w_in, cwd, w_out, oT, T, final):
    P = c.P
    c.nxb, c.pwg = 2, 1
    if True:
        gn = c.sb("gn", [128, KC], F32); cw = c.sb("cwt", [128, KC * 3], F32)
        P.dma("sp", gn[:], gain, writes=["gn"]); P.dma("sp", cw[:], cwd, writes=["cwt"])
        TH = 1024
        NE = TH + 2
        xn = c.sb("xn", [128, KC, NE], BF16)
        gT = c.sb("gT", [128, KC, TH], BF16)
        cgs = c.sb("cgs", [128, NE], F32); zb = c.sb("zb", [128, NE], F32)
        bb = c.sb("bb", [128, NE], F32); yb = c.sb("yb", [128, TH], F32)
        xv = _fm(xT)
        for hf in range(T // TH):
            tg = hf * TH
            if hf == 0:
                P.op("pool", lambda e: e.memset(xn[:, :, 0:2], 0.0), writes=["xn"])
                emit_norm(c, xv[:, :, 0:TH], TH, gn, xn, 2, "xn")
            else:
                emit_norm(c, xv[:, :, tg - 2:tg + TH], NE, gn, xn, 0, "xn")
            for j in range(KC):
                def cb_cg(_, t0, n, ps, psk):
                    P.op("act", lambda e: e.copy(out=cgs[:, t0:t0 + n], in_=ps[:, 0:n]), reads=[psk], writes=["cgs"])

                def cb_u(_, t0, n, ps, psk):
                    P.op("dve", lambda e: e.tensor_tensor(out=zb[:, t0:t0 + n], in0=cgs[:, t0:t0 + n], in1=ps[:, 0:n], op=ALU.mult),
                         reads=[psk, "cgs"], writes=["zb"])

                def cb_b(_, t0, n, ps, psk):
                    P.op("act", lambda e: e.copy(out=bb[:, t0:t0 + n], in_=ps[:, 0:n]), reads=[psk], writes=["bb"])
                emit_proj(c, _fm(w_in), D + j * 128, 1, xn, ["xn"], NE, cb_cg)
                emit_proj(c, _fm(w_in), 2 * D + j * 128, 1, xn, ["xn"], NE, cb_u)
                emit_proj(c, _fm(w_in), j * 128, 1, xn, ["xn"], NE, cb_b)
                P.op("dve", lambda e, j=j: e.tensor_scalar(out=yb[:], in0=zb[:, 2:2 + TH], scalar1=cw[:, j * 3 + 2:j * 3 + 3], scalar2=None, op0=ALU.mult),
                     reads=["zb", "cwt"], writes=["yb"])
                P.op("dve", lambda e, j=j: e.scalar_tensor_tensor(out=yb[:], in0=zb[:, 1:1 + TH], scalar=cw[:, j * 3 + 1:j * 3 + 2], in1=yb[:], op0=ALU.mult, op1=ALU.add),
                     reads=["zb", "cwt", "yb"], writes=["yb"])
                P.op("dve", lambda e, j=j: e.scalar_tensor_tensor(out=yb[:], in0=zb[:, 0:TH], scalar=cw[:, j * 3:j * 3 + 1], in1=yb[:], op0=ALU.mult, op1=ALU.add),
                     reads=["zb", "cwt", "yb"], writes=["yb"])
                P.op("dve", lambda e, j=j: e.tensor_tensor(out=gT[:, j, :], in0=yb[:], in1=bb[:, 2:2 + TH], op=ALU.mult),
                     reads=["yb", "bb"], writes=["gT"])
            emit_outproj(c, w_out.rearrange("(c p) n -> p c n", p=128), KC, gT, ["gT"], xv, _fm(oT), tg, TH, 1.0, final)
DIL = ((128, 1), (512, 4), (2048, 16))
SEQ = 4096
PI = 3.14159265358979
MAGIC = 12582912.0


def dil_consts():
    invf = np.zeros((128, 1), np.float32)
    fr = (500000.0 ** (-np.arange(0, 32, 2, dtype=np.float32) / 32)).astype(np.float32)
    invf[0:16, 0] = fr; invf[16:32, 0] = fr
    rm = np.zeros((128, 128), np.float32)
    for m in range(16):
        rm[m + 16, m] = -1.0
        rm[m, m + 16] = 1.0
    p = np.arange(128)[:, None]; f = np.arange(128)[None, :]
    mask = np.concatenate([(p >= f), (p <= f)], axis=1).astype(np.float32)
    return invf, rm, mask


def build_dilcore(NH=8):
    nc = _new_nc()
    yT = _din(nc, "yT", [9216, SEQ]); posb = _din(nc, "posb", [128, SEQ], I32)
    invf_d = _din(nc, "invf", [128, 1]); rm_d = _din(nc, "rm", [128, 128]); mask_d = _din(nc, "mask", [128, 256])
    id_d = _din(nc, "ident", [128, 128])
    gq_d = _din(nc, "gq", [128, 3]); gk_d = _din(nc, "gk", [128, 3])
    oT = _dout(nc, "oT", [NH * 128, SEQ])
    with contextlib.ExitStack() as stack:
        P = Prog(nc, stack); c = Ctx(nc, stack, P)
        emit_dilcore(c, yT, posb, invf_d, rm_d, mask_d, id_d, gq_d, gk_d, oT, NH, True)
        P.emit()
    return nc


def emit_dilcore(c, yT, posb, invf_d, rm_d, mask_d, id_d, gq_d, gk_d, oT, NH, final):
    P = c.P
    G = 3
    if True:
        ones32, ones16 = c.const_ones()
        invf = c.sb("invf_t", [128, 1], F32); rm = c.sb("rm_t", [128, 128], F32); mask = c.sb("mask_t", [128, 256], F32)
        ident = c.sb("ident_t", [128, 128], F32)
        gq = c.sb("gq_t", [128, G], F32); gk = c.sb("gk_t", [128, G], F32)
        for t, dd, k in ((invf, invf_d, "invf"), (rm, rm_d, "rm"), (mask, mask_d, "mask"), (gq, gq_d, "gq"), (gk, gk_d, "gk"), (ident, id_d, "ident")):
            P.dma("sp", t[:], dd, writes=[k])
        posi = c.sb("posi", [128, SEQ], I32)
        ang = c.sb("ang", [128, SEQ], F32); tmp = c.sb("tmpa", [128, SEQ], F32); kf = c.sb("kfa", [128, SEQ], F32)
        cosT = c.sb("cosT", [128, SEQ], F32); sinT = c.sb("sinT", [128, SEQ], F32)
        qf = kf
        q16 = c.sb("q16", [128, SEQ], BF16); k16 = c.sb("k16", [128, SEQ], BF16)
        v16 = c.sb("v16", [128, 32 * 128], BF16)
        Uacc = c.sb("Uacc", [128, SEQ], F32); Zacc = ang
        sq = c.sb("d_sq", [128, 512], F32); rs = c.sb("d_rs", [128, 512], F32); t1 = c.sb("d_t1", [128, 512], F32)
        t2 = c.sb("d_t2", [128, 512], F32)
        Ef = [c.sb(f"Ef{i}", [128, 256], F32) for i in range(2)]
        Em = [c.sb(f"Em{i}", [128, 256], BF16) for i in range(2)]
        psn = c.ps("ps_n"); psr = c.ps("ps_a0")
        pss = [c.ps(f"ps_b{i}") for i in range(2)]
        psU = [c.ps(f"ps_c{i}") for i in range(2)]
        psZ = [c.ps(f"ps_d{i}") for i in range(2)]
        C1 = 6.28125
        C2 = 2.0 * PI - C1
        sm_scale = 128.0 ** -0.5

        def table(dst, dkey, shift):
            P.op("dve", lambda e: e.tensor_scalar(out=tmp[:], in0=ang[:], scalar1=float(shift), scalar2=None, op0=ALU.add),
                 reads=["ang"], writes=["tmpa"])
            P.op("dve", lambda e: e.tensor_scalar(out=kf[:], in0=tmp[:], scalar1=1.0 / (2 * PI), scalar2=MAGIC, op0=ALU.mult, op1=ALU.add),
                 reads=["tmpa"], writes=["kfa"])
            P.op("dve", lambda e: e.tensor_scalar(out=kf[:], in0=kf[:], scalar1=-MAGIC, scalar2=None, op0=ALU.add),
                 reads=["kfa"], writes=["kfa"])
            P.op("dve", lambda e: e.scalar_tensor_tensor(out=tmp[:], in0=kf[:], scalar=-C1, in1=tmp[:], op0=ALU.mult, op1=ALU.add),
                 reads=["kfa", "tmpa"], writes=["tmpa"])
            P.op("dve", lambda e: e.scalar_tensor_tensor(out=tmp[:], in0=kf[:], scalar=-C2, in1=tmp[:], op0=ALU.mult, op1=ALU.add),
                 reads=["kfa", "tmpa"], writes=["tmpa"])
            P.op("dve", lambda e: e.tensor_scalar(out=tmp[:], in0=tmp[:], scalar1=3.1415925, scalar2=-3.1415925, op0=ALU.min, op1=ALU.max),
                 reads=["tmpa"], writes=["tmpa"])
            P.op("act", lambda e: e.activation(out=dst[:], in_=tmp[:], func=AF.Sin), reads=["tmpa"], writes=[dkey])

        P.dma("sp", posi[:], posb, writes=["posi"])
        P.op("dve", lambda e: e.tensor_copy(out=ang[:], in_=posi[:]), reads=["posi"], writes=["ang"])
        P.op("dve", lambda e: e.tensor_scalar(out=ang[:], in0=ang[:], scalar1=invf[:, 0:1], scalar2=None, op0=ALU.mult),
             reads=["ang", "invf"], writes=["ang"])
        table(sinT, "sinT", 0.0)
        table(cosT, "cosT", PI / 2)

        def prep(src, g, dl, gt, gkey, dst16, dkey):
            P.dma("sp", qf[:], src, reads=["kfa"], writes=["kfa"])
            dview = dst16[:].rearrange("p (r u) -> p u r", r=dl) if dl > 1 else None
            for t0 in range(0, SEQ, 512):
                sl = slice(t0, t0 + 512)
                P.op("act", lambda e, sl=sl: e.activation(out=sq[:], in_=qf[:, sl], func=AF.Square), reads=["kfa"], writes=["d_sq"])
                P.op("pe", lambda e: e.matmul(psn[:], ones32[:], sq[:], start=True, stop=True), reads=["d_sq", "ones32"], writes=["ps_n"])
                P.op("act", lambda e: e.activation(out=rs[:], in_=psn[:], func=AF.Ln, bias=EPS, scale=1.0 / 128), reads=["ps_n"], writes=["d_rs"])
                P.op("act", lambda e: e.activation(out=rs[:], in_=rs[:], func=AF.Exp, scale=-0.5), reads=["d_rs"], writes=["d_rs"])
                P.op("dve", lambda e, sl=sl: e.scalar_tensor_tensor(out=qf[:, sl], in0=qf[:, sl], scalar=gt[:, g:g + 1], in1=rs[:], op0=ALU.mult, op1=ALU.mult),
                     reads=["kfa", "d_rs", gkey], writes=["kfa"])
                P.op("pe", lambda e, sl=sl: e.matmul(psr[:], rm[:], qf[:, sl], start=True, stop=True), reads=["kfa", "rm"], writes=["ps_a0"])
                P.op("dve", lambda e, sl=sl: e.tensor_tensor(out=t1[:], in0=qf[:, sl], in1=cosT[:, sl], op=ALU.mult), reads=["kfa", "cosT"], writes=["d_t1"])
                P.op("dve", lambda e, sl=sl: e.tensor_tensor(out=t2[:], in0=psr[:], in1=sinT[:, sl], op=ALU.mult), reads=["ps_a0", "sinT"], writes=["d_t2"])
                if dl == 1:
                    P.op("dve", lambda e, sl=sl: e.tensor_tensor(out=dst16[:, sl], in0=t1[:], in1=t2[:], op=ALU.add), reads=["d_t1", "d_t2"], writes=[dkey])
                else:
                    u0 = t0 // dl
                    nu = 512 // dl
                    P.op("dve", lambda e, u0=u0, nu=nu: e.tensor_tensor(
                        out=dview[:, u0:u0 + nu, :], in0=t1[:].rearrange("p (u r) -> p u r", r=dl),
                        in1=t2[:].rearrange("p (u r) -> p u r", r=dl), op=ALU.add), reads=["d_t1", "d_t2"], writes=[dkey])

        def vprep(src, dl):
            nb = SEQ // dl // 128
            P.dma("sp", tmp[:], src, writes=["tmpa"])
            for q0 in range(0, 32, 4):
                pi = c.rot("vps", 2)

                def tr(e, q0=q0, pi=pi):
                    ins = None
                    for s in range(4):
                        q = q0 + s
                        r, bp = q // nb, q % nb
                        ta = bp * 128 * dl + r
                        sl = slice(ta, ta + dl * 127 + 1, dl) if dl > 1 else slice(ta, ta + 128)
                        ins = e.transpose(psU[pi][:, s * 128:(s + 1) * 128], tmp[:, sl], ident[:])
                    return ins
                P.op("pe", tr, reads=["tmpa", "ident"], writes=[f"ps_c{pi}"])
                P.op("act", lambda e, q0=q0, pi=pi: e.copy(out=v16[:, q0 * 128:(q0 + 4) * 128], in_=psU[pi][:]), reads=[f"ps_c{pi}"], writes=["v16"])

        for hl in range(NH):
            for g, (window, dl) in enumerate(DIL):
                nb = SEQ // dl // 128
                r0 = ((0 * 3 + g) * 8 + hl) * 128
                r1 = ((1 * 3 + g) * 8 + hl) * 128
                r2 = ((2 * 3 + g) * 8 + hl) * 128
                prep(yT[r0:r0 + 128, :], g, dl, gq, "gq", q16, "q16")
                prep(yT[r1:r1 + 128, :], g, dl, gk, "gk", k16, "k16")
                vprep(yT[r2:r2 + 128, :], dl)
                for qb in range(0, 32, 2):
                    r, b0 = qb // nb, qb % nb
                    ui = c.rot("psU", 2)
                    for s in range(2):
                        q = qb + s
                        bp = q % nb
                        ei = c.rot("Ef", 2)
                        lo = 0 if bp > 0 else 128
                        qs = slice(q * 128, (q + 1) * 128)

                        def mms(e, ei=ei, q=q, bp=bp, qs=qs):
                            ins = None
                            if bp > 0:
                                ins = e.matmul(pss[ei][:, 0:128], k16[:, (q - 1) * 128:q * 128], q16[:, qs], start=True, stop=True)
                            ins = e.matmul(pss[ei][:, 128:256], k16[:, qs], q16[:, qs], start=True, stop=True)
                            return ins
                        P.op("pe", mms, reads=["k16", "q16"], writes=[f"ps_b{ei}"])
                        P.op("act", lambda e, ei=ei, lo=lo: e.activation(out=Ef[ei][:, lo:256], in_=pss[ei][:, lo:256], func=AF.Exp, scale=sm_scale),
                             reads=[f"ps_b{ei}"], writes=[f"Ef{ei}"])
                        P.op("dve", lambda e, ei=ei, lo=lo: e.tensor_tensor(out=Em[ei][:, lo:256], in0=Ef[ei][:, lo:256], in1=mask[:, lo:256], op=ALU.mult),
                             reads=[f"Ef{ei}", "mask"], writes=[f"Em{ei}"])

                        def mmu(e, ei=ei, q=q, bp=bp, s=s, ui=ui):
                            o = psU[ui][:, s * 128:(s + 1) * 128]
                            if bp > 0:
                                e.matmul(o, v16[:, (q - 1) * 128:q * 128], Em[ei][:, 0:128], start=True, stop=False)
                            return e.matmul(o, v16[:, q * 128:(q + 1) * 128], Em[ei][:, 128:256], start=(bp == 0), stop=True)
                        P.op("pe", mmu, reads=[f"Em{ei}", "v16"], writes=[f"ps_c{ui}"])

                        def mmz(e, ei=ei, bp=bp, s=s, ui=ui):
                            o = psZ[ui][:, s * 128:(s + 1) * 128]
                            if bp > 0:
                                e.matmul(o, ones16[:], Em[ei][:, 0:128], start=True, stop=False)
                            return e.matmul(o, ones16[:], Em[ei][:, 128:256], start=(bp == 0), stop=True)
                        P.op("pe", mmz, reads=[f"Em{ei}", "ones16"], writes=[f"ps_d{ui}"])
                    ta = r + dl * b0 * 128
                    tsl = slice(ta, ta + dl * 255 + 1, dl) if dl > 1 else slice(ta, ta + 256)
                    if g == 0:
                        P.op("dve", lambda e, ui=ui, tsl=tsl: e.tensor_copy(out=Uacc[:, tsl], in_=psU[ui][:, 0:256]), reads=[f"ps_c{ui}"], writes=["Uacc"])
                        P.op("act", lambda e, ui=ui, tsl=tsl: e.copy(out=Zacc[:, tsl], in_=psZ[ui][:, 0:256]), reads=[f"ps_d{ui}"], writes=["ang"])
                    else:
                        P.op("dve", lambda e, ui=ui, tsl=tsl: e.tensor_tensor(out=Uacc[:, tsl], in0=Uacc[:, tsl], in1=psU[ui][:, 0:256], op=ALU.add),
                             reads=[f"ps_c{ui}", "Uacc"], writes=["Uacc"])
                        P.op("dve", lambda e, ui=ui, tsl=tsl: e.tensor_tensor(out=Zacc[:, tsl], in0=Zacc[:, tsl], in1=psZ[ui][:, 0:256], op=ALU.add),
                             reads=[f"ps_d{ui}", "ang"], writes=["ang"])
            P.op("dve", lambda e: e.reciprocal(out=Zacc[:], in_=Zacc[:]), reads=["ang"], writes=["ang"])
            P.op("dve", lambda e: e.tensor_tensor(out=Uacc[:], in0=Uacc[:], in1=Zacc[:], op=ALU.mult), reads=["ang", "Uacc"], writes=["Uacc"])
            P.dma("sp", oT[hl * 128:(hl + 1) * 128, :], Uacc[:], reads=["Uacc"], final=final)


def dil_perm(dl):
    L = SEQ // dl
    return (np.arange(L)[None, :] * dl + np.arange(dl)[:, None]).reshape(-1)
HC = 64
HNC = SEQ // HC


def hgrn_consts():
    cm = np.ones((128, SEQ), np.float32); cm[:, ::HC] = 0.0
    p = np.arange(HC)[:, None]; f = np.arange(HC)[None, :]
    tm = (p <= f).astype(np.float32)
    return cm, tm, np.eye(128, dtype=np.float32)


def build_hgrncore(NH=16, layer=2):
    nc = _new_nc()
    yT = _din(nc, "yT", [8192, SEQ])
    lbl = _din(nc, "lbl", [128, NH * 4]); ng_d = _din(nc, "ng", [HC, 128])
    cm_d = _din(nc, "cm", [128, SEQ]); tm_d = _din(nc, "tm", [HC, HC]); id_d = _din(nc, "ident", [128, 128])
    oT = _dout(nc, "oT", [NH * 128, SEQ])
    with contextlib.ExitStack() as stack:
        P = Prog(nc, stack); c = Ctx(nc, stack, P)
        emit_hgrncore(c, yT, lbl, ng_d, cm_d, tm_d, id_d, oT, NH, layer, True)
        P.emit()
    return nc


def emit_hgrncore(c, yT, lbl, ng_d, cm_d, tm_d, id_d, oT, NH, layer, final):
    P = c.P
    if True:
        cm = c.sb("cm_t", [128, SEQ], F32); tm = c.sb("tm_t", [HC, HC], F32); ident = c.sb("id_t", [128, 128], BF16)
        ident32 = c.sb("id32_t", [128, 128], F32)
        P.dma("sp", ident32[:], id_d, writes=["ident32"])
        gnat = c.sb("gnat", [128, SEQ], F32)
        ofm = c.sb("ofm", [128, 512], F32)
        psG = c.ps("ps_G", [128, 1024], F32)
        ng = c.sb("ng_t", [HC, 128], F32); lb4 = c.sb("lb4", [128, NH * 4], F32)
        P.dma("sp", cm[:], cm_d, writes=["cm"]); P.dma("sp", tm[:], tm_d, writes=["tm"])
        P.dma("pool", ident[:], id_d, writes=["ident"]); P.dma("sp", ng[:], ng_d, writes=["ng"])
        P.dma("sp", lb4[:], lbl, writes=["lb4"])
        lb = c.sb("lb", [128, NH], F32); oml = c.sb("oml", [128, NH], F32); den = c.sb("den", [128, NH], F32)
        P.op("act", lambda e: e.activation(out=lb4[:], in_=lb4[:], func=AF.Exp), reads=["lb4"], writes=["lb4"])
        l3 = lb4[:].rearrange("p (h l) -> p h l", l=4)
        P.op("dve", lambda e: e.tensor_reduce(out=den[:], in_=l3, axis=AX.X, op=ALU.add), reads=["lb4"], writes=["den"])
        P.op("dve", lambda e: e.reciprocal(out=den[:], in_=den[:]), reads=["den"], writes=["den"])
        P.op("dve", lambda e: e.tensor_copy(out=lb[:], in_=l3[:, :, 1]), reads=["lb4"], writes=["lb"])
        for l in range(2, layer + 1):
            P.op("dve", lambda e, l=l: e.tensor_tensor(out=lb[:], in0=lb[:], in1=l3[:, :, l], op=ALU.add), reads=["lb4", "lb"], writes=["lb"])
        P.op("dve", lambda e: e.tensor_tensor(out=lb[:], in0=lb[:], in1=den[:], op=ALU.mult), reads=["lb", "den"], writes=["lb"])
        P.op("dve", lambda e: e.tensor_scalar(out=oml[:], in0=lb[:], scalar1=-1.0, scalar2=1.0, op0=ALU.mult, op1=ALU.add), reads=["lb"], writes=["oml"])

        fb = c.sb("fb", [128, SEQ], F32); A = c.sb("A", [128, SEQ], F32); tmp = c.sb("htmp", [128, SEQ], F32)
        kk = c.sb("kk", [128, SEQ], F32); qf = c.sb("qf", [128, SEQ], F32)
        qd16 = c.sb("qd16", [128, SEQ], BF16); ki16 = c.sb("ki16", [128, SEQ], BF16); ke16 = c.sb("ke16", [128, SEQ], BF16)
        ketok = c.sb("ketok", [HC, HNC * 128], BF16); v16 = c.sb("v16", [HC, HNC * 128], BF16)
        dec = c.sb("dec", [128, HNC], F32)
        S32 = c.sb("S32", [128, 128], F32); S16 = c.sb("S16", [128, 128], BF16)
        att16 = [c.sb(f"att16_{i}", [HC, HC], BF16) for i in range(2)]
        gate = c.sb("gate", [HC, 8 * 128], F32); osb = c.sb("osb", [HC, 8 * 128], F32); sqb = c.sb("sqb", [HC, 8 * 128], F32)
        ss = c.sb("ss", [HC, 8], F32)
        psT = c.ps("ps_T", [128, 1024], BF16)
        psA = [c.ps(f"ps_a{i}") for i in range(2)]
        psO = [c.ps(f"ps_b{i}") for i in range(2)]
        psS = [c.ps("ps_c0"), c.ps("ps_c0")]
        A3 = A[:].rearrange("p (n c) -> p n c", c=HC)
        tmp3 = tmp[:].rearrange("p (n c) -> p n c", c=HC)
        for hl in range(NH):
            P.dma("sp", fb[:], yT[2048 + hl * 128:2048 + (hl + 1) * 128, :], writes=["fb"])
            P.dma("sp", qf[:], yT[hl * 128:(hl + 1) * 128, :], writes=["qf"])
            P.dma("sp", kk[:], yT[4096 + hl * 128:4096 + (hl + 1) * 128, :], writes=["kk"])
            P.dma("sp", gnat[:], yT[6144 + hl * 128:6144 + (hl + 1) * 128, :], writes=["gnat"])
            for n0 in range(0, HNC, 8):
                def trv(e, n0=n0):
                    ins = None
                    for i in range(8):
                        n = n0 + i
                        ins = e.transpose(psG[0:HC, i * 128:(i + 1) * 128], kk[:, n * HC:(n + 1) * HC], ident32[:])
                    return ins
                P.op("pe", trv, reads=["kk", "ident32"], writes=["ps_G"])
                P.op("act", lambda e, n0=n0: e.copy(out=v16[:, n0 * 128:(n0 + 8) * 128], in_=psG[0:HC, :]), reads=["ps_G"], writes=["v16"])
            P.op("act", lambda e: e.activation(out=fb[:], in_=fb[:], func=AF.Exp, scale=-1.0), reads=["fb"], writes=["fb"])
            P.op("dve", lambda e: e.tensor_scalar(out=fb[:], in0=fb[:], scalar1=1.0, scalar2=None, op0=ALU.add), reads=["fb"], writes=["fb"])
            P.op("dve", lambda e: e.reciprocal(out=fb[:], in_=fb[:]), reads=["fb"], writes=["fb"])
            P.op("dve", lambda e, hl=hl: e.tensor_scalar(out=fb[:], in0=fb[:], scalar1=oml[:, hl:hl + 1], scalar2=lb[:, hl:hl + 1], op0=ALU.mult, op1=ALU.add),
                 reads=["fb", "oml", "lb"], writes=["fb"])
            P.op("dve", lambda e: e.tensor_scalar(out=kk[:], in0=fb[:], scalar1=-1.0, scalar2=1.0, op0=ALU.mult, op1=ALU.add), reads=["fb"], writes=["kk"])
            P.op("act", lambda e: e.activation(out=fb[:], in_=fb[:], func=AF.Ln), reads=["fb"], writes=["fb"])
            P.op("dve", lambda e: e.tensor_tensor_scan(out=A[:], data0=cm[:], data1=fb[:], initial=0.0, op0=ALU.mult, op1=ALU.add),
                 reads=["cm", "fb"], writes=["A"])
            P.op("act", lambda e: e.activation(out=tmp[:], in_=A[:], func=AF.Exp), reads=["A"], writes=["htmp"])
            P.op("dve", lambda e: e.tensor_tensor(out=qd16[:], in0=qf[:], in1=tmp[:], op=ALU.mult), reads=["qf", "htmp"], writes=["qd16"])
            P.op("act", lambda e: e.copy(out=dec[:], in_=tmp3[:, :, HC - 1]), reads=["htmp"], writes=["dec"])
            P.op("act", lambda e: e.activation(out=tmp[:], in_=A[:], func=AF.Exp, scale=-1.0), reads=["A", "dec"], writes=["htmp"])
            P.op("dve", lambda e: e.tensor_tensor(out=ki16[:], in0=kk[:], in1=tmp[:], op=ALU.mult), reads=["kk", "htmp"], writes=["ki16"])
            P.op("dve", lambda e: e.tensor_tensor(out=tmp3, in0=A3[:, :, HC - 1:HC].broadcast_to([128, HNC, HC]), in1=A3, op=ALU.subtract),
                 reads=["A", "ki16"], writes=["htmp"])
            P.op("act", lambda e: e.activation(out=tmp[:], in_=tmp[:], func=AF.Exp), reads=["htmp"], writes=["htmp"])
            P.op("dve", lambda e: e.tensor_tensor(out=ke16[:], in0=kk[:], in1=tmp[:], op=ALU.mult), reads=["kk", "htmp"], writes=["ke16"])
            for n0 in range(0, HNC, 8):
                def tr(e, n0=n0):
                    ins = None
                    for i in range(8):
                        n = n0 + i
                        ins = e.transpose(psT[0:HC, i * 128:(i + 1) * 128], ke16[:, n * HC:(n + 1) * HC], ident[:])
                    return ins
                P.op("pe", tr, reads=["ke16", "ident"], writes=["ps_T"])
                P.op("act", lambda e, n0=n0: e.copy(out=ketok[:, n0 * 128:(n0 + 8) * 128], in_=psT[0:HC, :]), reads=["ps_T"], writes=["ketok"])
            P.op("pool", lambda e: e.memset(S32[:], 0.0), writes=["S32"])
            P.op("pool", lambda e: e.memset(S16[:], 0.0), writes=["S16"])
            for n in range(HNC):
                cs = slice(n * HC, (n + 1) * HC)
                vs = slice(n * 128, (n + 1) * 128)
                ai = c.rot("psA", 2)
                j = n % 8
                if j == 0:
                    def trg(e, n=n):
                        ins = None
                        for i in range(8):
                            ins = e.transpose(psG[0:HC, i * 128:(i + 1) * 128], gnat[:, (n + i) * HC:(n + i + 1) * HC], ident32[:])
                        return ins
                    P.op("pe", trg, reads=["gnat", "ident32"], writes=["ps_G"])
                    P.op("act", lambda e: e.activation(out=gate[:], in_=psG[0:HC, :], func=AF.Silu), reads=["ps_G"], writes=["gate"])
                P.op("pe", lambda e, ai=ai, cs=cs: e.matmul(psA[ai][0:HC, 0:HC], ki16[:, cs], qd16[:, cs], start=True, stop=True),
                     reads=["ki16", "qd16"], writes=[f"ps_a{ai}"])
                P.op("dve", lambda e, ai=ai: e.tensor_tensor(out=att16[ai][:], in0=psA[ai][0:HC, 0:HC], in1=tm[:], op=ALU.mult),
                     reads=[f"ps_a{ai}", "tm"], writes=[f"att16_{ai}"])

                def mmo(e, ai=ai, cs=cs, vs=vs):
                    e.matmul(psO[ai][0:HC, 0:128], qd16[:, cs], S16[:], start=True, stop=False)
                    return e.matmul(psO[ai][0:HC, 0:128], att16[ai][:], v16[:, vs], start=False, stop=True)
                P.op("pe", mmo, reads=["qd16", "S16", f"att16_{ai}", "v16"], writes=[f"ps_b{ai}"])
                P.op("act", lambda e, ai=ai, j=j: e.copy(out=osb[:, j * 128:(j + 1) * 128], in_=psO[ai][0:HC, 0:128]), reads=[f"ps_b{ai}"], writes=["osb"])
                P.op("pe", lambda e, ai=ai, vs=vs: e.matmul(psS[ai][:, 0:128], ketok[:, vs], v16[:, vs], start=True, stop=True),
                     reads=["ketok", "v16"], writes=["ps_c0"])
                P.op("dve", lambda e, ai=ai, n=n: e.scalar_tensor_tensor(out=S32[:], in0=S32[:], scalar=dec[:, n:n + 1], in1=psS[ai][:, 0:128], op0=ALU.mult, op1=ALU.add),
                     reads=["S32", "dec", "ps_c0"], writes=["S32"])
                P.op("act", lambda e: e.copy(out=S16[:], in_=S32[:]), reads=["S32"], writes=["S16"])
                if j == 7:
                    o3 = osb[:].rearrange("p (j e) -> p j e", e=128)
                    P.op("dve", lambda e: e.tensor_tensor(out=sqb[:], in0=osb[:], in1=osb[:], op=ALU.mult), reads=["osb"], writes=["sqb"])
                    P.op("dve", lambda e: e.tensor_reduce(out=ss[:], in_=sqb[:].rearrange("p (j e) -> p j e", e=128), axis=AX.X, op=ALU.add),
                         reads=["sqb"], writes=["ss"])
                    P.op("act", lambda e: e.activation(out=ss[:], in_=ss[:], func=AF.Ln, bias=EPS, scale=1.0 / 128), reads=["ss"], writes=["ss"])
                    P.op("act", lambda e: e.activation(out=ss[:], in_=ss[:], func=AF.Exp, scale=-0.5), reads=["ss"], writes=["ss"])
                    P.op("dve", lambda e, o3=o3: e.tensor_tensor(out=o3, in0=o3, in1=ss[:].unsqueeze(2).broadcast_to([HC, 8, 128]), op=ALU.mult),
                         reads=["osb", "ss"], writes=["osb"])
                    P.op("dve", lambda e, o3=o3: e.tensor_tensor(out=o3, in0=o3, in1=ng[:].unsqueeze(1).broadcast_to([HC, 8, 128]), op=ALU.mult),
                         reads=["osb", "ng"], writes=["osb"])
                    P.op("dve", lambda e: e.tensor_tensor(out=osb[:], in0=osb[:], in1=gate[:], op=ALU.mult), reads=["osb", "gate"], writes=["osb"])

                    def tro(e):
                        ins = None
                        for i in range(8):
                            ins = e.transpose(psG[:, i * HC:(i + 1) * HC], osb[:, i * 128:(i + 1) * 128], ident32[0:HC, 0:HC])
                        return ins
                    P.op("pe", tro, reads=["osb", "ident32"], writes=["ps_G"])
                    P.op("act", lambda e: e.copy(out=ofm[:], in_=psG[:, 0:512]), reads=["ps_G"], writes=["ofm"])
                    P.dma("sp", oT[hl * 128:(hl + 1) * 128, (n - 7) * HC:(n + 1) * HC], ofm[:], reads=["ofm"], final=final)
NOSYNC = ('dve', 'pool', 'act')
RW_ROWS = 7
RWM = {0: 0, 2: 1, 3: 2, 5: 3, 6: 4, 7: 5, 8: 6}


def rwkv_consts():
    bones = np.zeros((128, 128), np.float32)
    bones[:64, :64] = 1.0; bones[64:, 64:] = 1.0
    sel = np.zeros((32, 16 * 128), np.float32)
    for t in range(16):
        for j in range(2):
            sel[t * 2 + j, t * 128 + j * 64: t * 128 + (j + 1) * 64] = 1.0
    return bones, sel


def build_rwkvA(T=4096):
    nc = _new_nc()
    xT = _din(nc, "xT", [D, T]); gain = _din(nc, "gain", [128, KC]); mu_d = _din(nc, "mu", [128, 6 * KC])
    wrkv = _din(nc, "wrkv", [3, D, D])
    w0_d = _din(nc, "w0", [128, KC]); w1 = _din(nc, "w1", [D, 96]); w2 = _din(nc, "w2", [96, D])
    a0_d = _din(nc, "a0", [128, KC]); a1 = _din(nc, "a1", [D, 96]); a2 = _din(nc, "a2", [96, D])
    g1 = _din(nc, "g1", [D, 256]); g2 = _din(nc, "g2", [256, D])
    kk_d = _din(nc, "k_k", [128, KC]); ka_d = _din(nc, "k_a", [128, KC]); bones_d = _din(nc, "bones", [128, 128])
    yT = _dout(nc, "yT", [RW_ROWS * D, T])
    with contextlib.ExitStack() as stack:
        P = Prog(nc, stack); c = Ctx(nc, stack, P)
        emit_rwkvA(c, xT, gain, mu_d, wrkv, w0_d, w1, w2, a0_d, a1, a2, g1, g2, kk_d, ka_d, bones_d, yT, T, True)
        P.emit()
    return nc


def emit_rwkvA(c, xT, gain, mu_d, wrkv, w0_d, w1, w2, a0_d, a1, a2, g1, g2, kk_d, ka_d, bones_d, yT, T, final):
    P = c.P
    c.pwg = 2
    if True:
        gn = c.sb("gn", [128, KC], F32); mu = c.sb("mu_t", [128, 6 * KC], F32)
        w0 = c.sb("w0_t", [128, KC], F32); a0 = c.sb("a0_t", [128, KC], F32)
        k_k = c.sb("kk_t", [128, KC], F32); k_a = c.sb("ka_t", [128, KC], F32); omka = c.sb("omka", [128, KC], F32)
        bones = c.sb("bones_t", [128, 128], F32)
        for t, d, k in ((gn, gain, "gn"), (mu, mu_d, "mu"), (w0, w0_d, "w0"), (a0, a0_d, "a0"), (k_k, kk_d, "k_k"), (k_a, ka_d, "k_a"), (bones, bones_d, "bones")):
            P.dma("sp", t[:], d, writes=[k])
        P.op("dve", lambda e: e.tensor_scalar(out=omka[:], in0=k_a[:], scalar1=-1.0, scalar2=1.0, op0=ALU.mult, op1=ALU.add), reads=["k_a"], writes=["omka"])
        nw0 = c.sb("nw0", [128, KC], F32); na0 = c.sb("na0", [128, KC], F32); th = c.sb("th", [128, 512], F32)
        P.op("dve", lambda e: e.tensor_scalar(out=nw0[:], in0=w0[:], scalar1=-1.0, scalar2=None, op0=ALU.mult), reads=["w0"], writes=["nw0"])
        P.op("dve", lambda e: e.tensor_scalar(out=na0[:], in0=a0[:], scalar1=-1.0, scalar2=None, op0=ALU.mult), reads=["a0"], writes=["na0"])
        w2b = c.sb("w2b", [96, D], BF16); a2b = c.sb("a2b", [96, D], BF16); g2b = c.sb("g2b", [128, 2, D], BF16)
        P.dma("pool", w2b[:], w2, writes=["w2b"]); P.dma("pool", a2b[:], a2, writes=["a2b"])
        P.dma("pool", g2b[:], g2.rearrange("(c p) n -> p c n", p=128), writes=["g2b"])
        hfp = c.sb("hfp", [128, KC, 513], F32); diff = c.sb("diff", [128, KC, 512], F32)
        mx = [c.sb(f"mx{i}", [128, KC, 512], BF16) for i in range(2)]
        kbuf = c.sb("kbuf", [128, KC, 512], F32)
        t1 = c.sb("t1", [128, 2, 512], BF16)
        ysb = [c.sb(f"ysb{i}", [128, 512], F32) for i in range(2)]
        asb = c.sb("asb", [128, 512], F32); kkr = c.sb("kkr", [128, 512], F32); sq = c.sb("r_sq", [128, 512], F32)
        rn = c.sb("rn", [128, 512], F32)
        psb = [c.ps(f"ps_b{i}") for i in range(2)]
        psn2 = c.ps("ps_c0")
        yv = yT.rearrange("(r kc p) t -> r p kc t", p=128, kc=KC)
        xv = _fm(xT)

        def store(row, j, tg, src_ap, key):
            if row not in RWM:
                return
            P.dma("sp", yv[RWM[row]][:, j, tg:tg + 512], src_ap, reads=[key], final=final)

        for tt in range(T // 512):
            tg = tt * 512
            if tt == 0:
                P.op("pool", lambda e: e.memset(hfp[:, :, 0:1], 0.0), writes=["hfp"])
                emit_norm(c, xv[:, :, 0:512], 512, gn, hfp, 1, "hfp")
            else:
                emit_norm(c, xv[:, :, tg - 1:tg + 512], 513, gn, hfp, 0, "hfp")
            P.op("dve", lambda e: e.tensor_tensor(out=diff[:], in0=hfp[:, :, 0:512], in1=hfp[:, :, 1:513], op=ALU.subtract), reads=["hfp"], writes=["diff"])
            for i in range(6):
                mi = c.rot("mx", 2)
                m = mx[mi]
                for kc in range(KC):
                    P.op("dve", lambda e, kc=kc, i=i, m=m: e.scalar_tensor_tensor(
                        out=m[:, kc, :], in0=diff[:, kc, :], scalar=mu[:, i * KC + kc:i * KC + kc + 1], in1=hfp[:, kc, 1:513],
                        op0=ALU.mult, op1=ALU.add), reads=["diff", "hfp", "mu"], writes=[f"mx{mi}"])
                mk = [f"mx{mi}"]
                if i < 3:
                    def cb(j, t0, n, ps, psk, i=i, tg=tg):
                        if i == 1:
                            P.op("act", lambda e: e.copy(out=kbuf[:, j, :], in_=ps[:, 0:512]), reads=[psk], writes=["kbuf"])
                            store(1, j, tg, kbuf[:, j, :], "kbuf")
                        else:
                            yi = c.rot("ysb", 2)
                            P.op("act", lambda e: e.copy(out=ysb[yi][:], in_=ps[:, 0:512]), reads=[psk], writes=[f"ysb{yi}"])
                            store(i, j, tg, ysb[yi][:], f"ysb{yi}")
                    emit_proj(c, wrkv[i].rearrange("(kc p) n -> p kc n", p=128), 0, KC, m, mk, 512, cb)
                elif i == 3 or i == 4:
                    wl = w1 if i == 3 else a1

                    def cb(j, t0, n, ps, psk, i=i):
                        if i == 3:
                            P.op("act", lambda e: e.activation(out=th[0:96, :], in_=ps[0:96, 0:512], func=AF.Exp, scale=-2.0), reads=[psk], writes=["th"])
                            P.op("dve", lambda e: e.tensor_scalar(out=th[0:96, :], in0=th[0:96, :], scalar1=1.0, scalar2=None, op0=ALU.add), reads=["th"], writes=["th"])
                            P.op("dve", lambda e: e.reciprocal(out=th[0:96, :], in_=th[0:96, :]), reads=["th"], writes=["th"])
                            P.op("dve", lambda e: e.tensor_scalar(out=t1[0:96, 0, :], in0=th[0:96, :], scalar1=2.0, scalar2=-1.0, op0=ALU.mult, op1=ALU.add), reads=["th"], writes=["t1"])
                        else:
                            P.op("act", lambda e: e.copy(out=t1[0:96, 0, :], in_=ps[0:96, 0:512]), reads=[psk], writes=["t1"])
                    emit_proj(c, wl.rearrange("(kc p) n -> p kc n", p=128), 0, 1, m, mk, 512, cb, cw=96)
                    w2x, w2k, bias, bk = (w2b, "w2b", w0, "w0") if i == 3 else (a2b, "a2b", a0, "a0")
                    for j in range(KC):
                        pi = c.rot("ps_b", 2)
                        P.op("pe", lambda e, pi=pi, j=j, w2x=w2x: e.matmul(psb[pi][:], w2x[0:96, j * 128:(j + 1) * 128], t1[0:96, 0, :], start=True, stop=True),
                             reads=["t1", w2k], writes=[f"ps_b{pi}"])
                        if i == 3:
                            yi = c.rot("ysb", 2)
                            P.op("act", lambda e, pi=pi, j=j, yi=yi: e.activation(out=ysb[yi][:], in_=psb[pi][:], func=AF.Exp, bias=nw0[:, j:j + 1], scale=-1.0),
                                 reads=[f"ps_b{pi}", "nw0"], writes=[f"ysb{yi}"])
                            P.op("dve", lambda e, yi=yi: e.tensor_scalar(out=ysb[yi][:], in0=ysb[yi][:], scalar1=1.0, scalar2=None, op0=ALU.add), reads=[f"ysb{yi}"], writes=[f"ysb{yi}"])
                            P.op("dve", lambda e, yi=yi: e.reciprocal(out=ysb[yi][:], in_=ysb[yi][:]), reads=[f"ysb{yi}"], writes=[f"ysb{yi}"])
                            P.op("act", lambda e, yi=yi: e.activation(out=ysb[yi][:], in_=ysb[yi][:], func=AF.Exp, scale=-float(np.exp(-0.5))),
                                 reads=[f"ysb{yi}"], writes=[f"ysb{yi}"])
                            store(3, j, tg, ysb[yi][:], f"ysb{yi}")
                        else:
                            P.op("act", lambda e, pi=pi, j=j: e.activation(out=asb[:], in_=psb[pi][:], func=AF.Exp, bias=na0[:, j:j + 1], scale=-1.0),
                                 reads=[f"ps_b{pi}", "na0"], writes=["asb"])
                            P.op("dve", lambda e: e.tensor_scalar(out=asb[:], in0=asb[:], scalar1=1.0, scalar2=None, op0=ALU.add), reads=["asb"], writes=["asb"])
                            P.op("dve", lambda e: e.reciprocal(out=asb[:], in_=asb[:]), reads=["asb"], writes=["asb"])
                            store(4, j, tg, asb[:], "asb")
                            P.op("dve", lambda e, j=j: e.tensor_scalar(out=kkr[:], in0=kbuf[:, j, :], scalar1=k_k[:, j:j + 1], scalar2=None, op0=ALU.mult),
                                 reads=["kbuf", "k_k"], writes=["kkr"])
                            P.op("act", lambda e: e.activation(out=sq[:], in_=kkr[:], func=AF.Square), reads=["kkr"], writes=["r_sq"])
                            P.op("pe", lambda e: e.matmul(psn2[:], bones[:], sq[:], start=True, stop=True), reads=["r_sq", "bones"], writes=["ps_c0"])
                            P.op("dve", lambda e: e.tensor_scalar(out=rn[:], in0=psn2[:], scalar1=1e-24, scalar2=None, op0=ALU.max), reads=["ps_c0"], writes=["rn"])
                            P.op("act", lambda e: e.activation(out=rn[:], in_=rn[:], func=AF.Ln), reads=["rn"], writes=["rn"])
                            P.op("act", lambda e: e.activation(out=rn[:], in_=rn[:], func=AF.Exp, scale=-0.5), reads=["rn"], writes=["rn"])
                            P.op("dve", lambda e: e.tensor_tensor(out=kkr[:], in0=kkr[:], in1=rn[:], op=ALU.mult), reads=["kkr", "rn"], writes=["kkr"])
                            yi = c.rot("ysb", 2)
                            P.op("dve", lambda e, yi=yi: e.tensor_scalar(out=ysb[yi][:], in0=kkr[:], scalar1=-1.0, scalar2=None, op0=ALU.mult), reads=["kkr"], writes=[f"ysb{yi}"])
                            store(6, j, tg, ysb[yi][:], f"ysb{yi}")
                            yi = c.rot("ysb", 2)
                            P.op("dve", lambda e, yi=yi: e.tensor_tensor(out=ysb[yi][:], in0=kkr[:], in1=asb[:], op=ALU.mult), reads=["kkr", "asb"], writes=[f"ysb{yi}"])
                            store(7, j, tg, ysb[yi][:], f"ysb{yi}")
                            yi = c.rot("ysb", 2)
                            P.op("dve", lambda e, j=j: e.tensor_scalar(out=rn[:], in0=asb[:], scalar1=k_a[:, j:j + 1], scalar2=omka[:, j:j + 1], op0=ALU.mult, op1=ALU.add),
                                 reads=["asb", "k_a", "omka"], writes=["rn"])
                            P.op("dve", lambda e, yi=yi, j=j: e.tensor_tensor(out=ysb[yi][:], in0=kbuf[:, j, :], in1=rn[:], op=ALU.mult), reads=["kbuf", "rn"], writes=[f"ysb{yi}"])
                            store(8, j, tg, ysb[yi][:], f"ysb{yi}")
                else:
                    def cb(j, t0, n, ps, psk):
                        P.op("act", lambda e: e.activation(out=th[:], in_=ps[:, 0:512], func=AF.Exp, scale=-1.0), reads=[psk], writes=["th"])
                        P.op("dve", lambda e: e.tensor_scalar(out=th[:], in0=th[:], scalar1=1.0, scalar2=None, op0=ALU.add), reads=["th"], writes=["th"])
                        P.op("dve", lambda e: e.reciprocal(out=th[:], in_=th[:]), reads=["th"], writes=["th"])
                        P.op("dve", lambda e: e.tensor_copy(out=t1[:, j, :], in_=th[:]), reads=["th"], writes=["t1"])
                    emit_proj(c, g1.rearrange("(kc p) n -> p kc n", p=128), 0, 2, m, mk, 512, cb)
                    for j in range(KC):
                        pi = c.rot("ps_b", 2)

                        def mm(e, pi=pi, j=j):
                            e.matmul(psb[pi][:], g2b[:, 0, j * 128:(j + 1) * 128], t1[:, 0, :], start=True, stop=False)
                            return e.matmul(psb[pi][:], g2b[:, 1, j * 128:(j + 1) * 128], t1[:, 1, :], start=False, stop=True)
                        P.op("pe", mm, reads=["t1", "g2b"], writes=[f"ps_b{pi}"])
                        yi = c.rot("ysb", 2)
                        P.op("act", lambda e, pi=pi, yi=yi: e.copy(out=ysb[yi][:], in_=psb[pi][:]), reads=[f"ps_b{pi}"], writes=[f"ysb{yi}"])
                        store(5, j, tg, ysb[yi][:], f"ysb{yi}")


class _Deferred:
    def __init__(self, P, lst):
        self.P, self.lst = P, lst

    def dma(self, *a, **k):
        self.lst.append(lambda: self.P.dma(*a, **k))

    def op(self, *a, **k):
        self.lst.append(lambda: self.P.op(*a, **k))


def rwkv_bmask():
    m = np.zeros((16, 512), np.float32)
    for i in range(8):
        m[2 * i:2 * i + 2, i * 64:(i + 1) * 64] = 1.0
    return m


def build_rwkvB(NSTEP=SEQ, NPASS=2):
    nc = _new_nc()
    yT = _din(nc, "yT", [RW_ROWS * D, NSTEP]); bm_d = _din(nc, "bmask", [16, 512]); id_d = _din(nc, "ident", [128, 128])
    ys = _dout(nc, "ys", [D, NSTEP])
    tm = nc.dram_tensor("rw_tm", [3 * NPASS, NSTEP, 1024], BF16, kind="Internal").ap()
    with contextlib.ExitStack() as stack:
        P = Prog(nc, stack); c = Ctx(nc, stack, P)
        emit_rwkvB(c, yT, bm_d, id_d, tm, ys, NSTEP, NPASS, True)
        P.emit()
    return nc


def emit_rwkvB(c, yT, bm_d, id_d, tm, ys, NSTEP, NPASS, final):
    P = c.P
    VB = 128
    SB = 16
    YB = 32
    ident = c.sb("ident_t", [128, 128], F32); bmask = c.sb("bmask_t", [16, 512], F32)
    P.dma("sp", ident[:], id_d, writes=["ident"]); P.dma("sp", bmask[:], bm_d, writes=["bmask"])
    fmb = [c.sb(f"fmb{i}", [128, 8, 128], F32) for i in range(2)]
    tmb = [c.sb(f"tmb{i}", [128, 1024], BF16) for i in range(2)]
    ps_t = [c.ps(f"ps_t{i}") for i in range(2)]
    TM_ROWS = (RWM[7], RWM[8], RWM[2])
    for hh in range(NPASS):
        for oi, row in enumerate(TM_ROWS):
            src = yT[row * D + hh * 1024: row * D + (hh + 1) * 1024, :].rearrange("(c p) t -> p c t", p=128)
            for tb in range(NSTEP // 128):
                fi = c.rot("fmb", 2)
                P.dma("sp", fmb[fi][:], src[:, :, tb * 128:(tb + 1) * 128], writes=[f"fmb{fi}"])
                for half in range(2):
                    pi = c.rot("ps_t", 2)

                    def tr(e, fi=fi, half=half, pi=pi):
                        ins = None
                        for q in range(4):
                            ins = e.transpose(ps_t[pi][:, q * 128:(q + 1) * 128], fmb[fi][:, half * 4 + q, :], ident[:])
                        return ins
                    P.op("pe", tr, reads=[f"fmb{fi}", "ident"], writes=[f"ps_t{pi}"])
                    P.op("act", lambda e, fi=fi, half=half, pi=pi: e.copy(out=tmb[fi][:, half * 512:(half + 1) * 512], in_=ps_t[pi][:]),
                         reads=[f"ps_t{pi}"], writes=[f"tmb{fi}"])
                P.dma("sp", tm[hh * 3 + oi, tb * 128:(tb + 1) * 128, :], tmb[fi][:], reads=[f"tmb{fi}"], writes=["tm_dram"])
    T_ = {}
    for hh in range(NPASS):
        s = f"_{hh}"
        A = dict(
            stmp=c.sb("stmp" + s, [128, 512], F32), ST32=c.sb("ST32" + s, [128, 512], F32), ST16=c.sb("ST16" + s, [128, 512], BF16), sam=c.sb("sam" + s, [16, 512], BF16),
            xnk=c.sb("xnk" + s, [128, 8, VB], F32), xr=c.sb("xr" + s, [128, 8, VB], F32),
            xw=[c.sb(f"xw{i}" + s, [128, 8, VB], F32) for i in range(2)],
            NKl=[c.sb(f"NKl{i}" + s, [128, VB * 16], BF16) for i in range(2)],
            R2l=[c.sb(f"R2l{i}" + s, [128, VB * 16], BF16) for i in range(2)],
            Bl=[c.sb(f"Bl{i}" + s, [16, SB, 128], BF16) for i in range(2)],
            Kl=[c.sb(f"Kl{i}" + s, [16, SB, 128], BF16) for i in range(2)],
            Vr=[c.sb(f"Vr{i}" + s, [16, SB, 512], BF16) for i in range(2)],
            ysb=[c.sb(f"ysb{i}" + s, [64, 16, YB], F32) for i in range(2)],
            ps_sa=c.ps("ps_sa" + s), ps_u=c.ps("ps_u" + s), ps_y=c.ps("ps_y" + s))
        T_[hh] = A
        for nm in ("ST32", "ST16", "stmp"):
            P.op("pool", lambda e, t=A[nm]: e.memset(t[:], 0.0), writes=[nm + s])
        for nm in ("NKl", "R2l", "Bl", "Kl", "Vr"):
            for i in range(2):
                P.op("pool", lambda e, t=A[nm][i]: e.memset(t[:], 0.0), writes=[f"{nm}{i}{s}"])
    P.nosync_self = set(NOSYNC)
    chains = [[] for _ in range(NPASS)]
    for t in range(NSTEP):
        for hh in range(NPASS):
            stages = [[] for _ in range(7)]
            pre = []
            A = T_[hh]
            s = f"_{hh}"
            ST32, ST16, sam = A["ST32"], A["ST16"], A["sam"]
            base = hh * 1024
            fb = (t // VB) % 2
            tv = t % VB
            DEF = _Deferred(P, pre)
            def load_fm(tq, fbq, with_w0):
                lds = [("xnk", RWM[6], A["xnk"], "xnk" + s, tq), ("xr", RWM[0], A["xr"], "xr" + s, tq)]
                if with_w0:
                    lds.append(("xw", RWM[3], A["xw"][0], "xw0" + s, 0))
                for nm, row, tile_, key, tqq in lds:
                    for j in range(2):
                        DEF.dma("sp", tile_[j * 64:(j + 1) * 64, :, :],
                              yT[row * D + base + j * 512: row * D + base + (j + 1) * 512, tqq:tqq + VB].rearrange("(i k) t -> k i t", k=64),
                              writes=[key])
                for srcT, skey, dstL, dkey in ((A["xnk"], "xnk" + s, A["NKl"][fbq], f"NKl{fbq}" + s), (A["xr"], "xr" + s, A["R2l"][fbq], f"R2l{fbq}" + s)):
                    dv = dstL[:].rearrange("p (t i j) -> p t i j", i=8, j=2)
                    for j in range(2):
                        DEF.op("pool", lambda e, dv=dv, srcT=srcT, j=j: e.tensor_copy(
                            out=dv[j * 64:(j + 1) * 64, :, :, j], in_=srcT[j * 64:(j + 1) * 64, :, :].rearrange("p i t -> p t i")),
                            reads=[skey], writes=[dkey])

            def load_tm(tq, biq):
                for oi, nm in enumerate(("Bl", "Kl")):
                    for j in range(2):
                        DEF.dma("sp", A[nm][biq][j:16:2, :, j * 64:(j + 1) * 64],
                              tm[hh * 3 + oi, tq:tq + SB, j * 512:(j + 1) * 512].rearrange("t (i k) -> i t k", k=64),
                              reads=["tm_dram"], writes=[f"{nm}{biq}{s}"])
                vsrc = tm[hh * 3 + 2, tq:tq + SB, :].rearrange("t (j i v) -> i j t v", j=2, i=8)
                for i in range(8):
                    DEF.dma("sp", A["Vr"][biq][2 * i:2 * i + 2, :, i * 64:(i + 1) * 64], vsrc[i], reads=["tm_dram"], writes=[f"Vr{biq}{s}"])

            bi = (t // SB) % 2
            tl = t % SB
            if t == 0:
                load_fm(0, 0, True)
                load_tm(0, 0)
            if tv == 1 and t - 1 + VB < NSTEP:
                tq = t - 1 + VB
                for j in range(2):
                    DEF.dma("sp", A["xw"][(fb + 1) % 2][j * 64:(j + 1) * 64, :, :],
                          yT[RWM[3] * D + base + j * 512: RWM[3] * D + base + (j + 1) * 512, tq:tq + VB].rearrange("(i k) t -> k i t", k=64),
                          writes=[f"xw{(fb + 1) % 2}" + s])
                load_fm(tq, (fb + 1) % 2, False)
            if tl == 1 and t - 1 + SB < NSTEP:
                load_tm(t - 1 + SB, (bi + 1) % 2)
            NKl, R2l, xw = A["NKl"][fb], A["R2l"][fb], A["xw"][fb]
            Bl, Kl, Vr = A["Bl"][bi], A["Kl"][bi], A["Vr"][bi]
            ps_sa, ps_u, ps_y = A["ps_sa"], A["ps_u"], A["ps_y"]
            stages[0].append(lambda ps_sa=ps_sa, NKl=NKl, ST16=ST16, tv=tv, fb=fb, s=s: P.op(
                "pe", lambda e: e.matmul(ps_sa[0:16, :], NKl[:, tv * 16:(tv + 1) * 16], ST16[:], start=True, stop=True),
                reads=[f"NKl{fb}{s}", "ST16" + s], writes=["ps_sa" + s]))
            stages[1].append(lambda sam=sam, ps_sa=ps_sa, s=s: P.op(
                "dve", lambda e: e.tensor_tensor(out=sam[:], in0=ps_sa[0:16, :], in1=bmask[:], op=ALU.mult),
                reads=["ps_sa" + s, "bmask"], writes=["sam" + s]))

            def mmu(e, ps_u=ps_u, Bl=Bl, Kl=Kl, Vr=Vr, sam=sam, tl=tl):
                e.matmul(ps_u[:], Bl[:, tl, :], sam[:], start=True, stop=False)
                return e.matmul(ps_u[:], Kl[:, tl, :], Vr[:, tl, :], start=False, stop=True)
            stages[2].append(lambda mmu=mmu, bi=bi, s=s: P.op("pe", mmu, reads=[f"Bl{bi}{s}", f"Kl{bi}{s}", f"Vr{bi}{s}", "sam" + s], writes=["ps_u" + s]))

            stmp = A["stmp"]

            tn = t + 1
            xwn = A["xw"][(tn // VB) % 2] if tn < NSTEP else None
            tvn = tn % VB

            def upd(e, ST32=ST32, ST16=ST16, stmp=stmp, ps_u=ps_u, xwn=xwn, tvn=tvn):
                e.tensor_tensor(out=ST16[:], in0=stmp[:], in1=ps_u[:], op=ALU.add)
                ins = e.tensor_tensor(out=ST32[:], in0=stmp[:], in1=ps_u[:], op=ALU.add)
                if xwn is not None:
                    ins = e.tensor_tensor(out=stmp[:].rearrange("p (i v) -> p i v", v=64), in0=ST32[:].rearrange("p (i v) -> p i v", v=64),
                                          in1=xwn[:, :, tvn:tvn + 1].broadcast_to([128, 8, 64]), op=ALU.mult)
                return ins
            stages[3].append(lambda upd=upd, s=s, fbn=(tn // VB) % 2: P.op("dve", upd, reads=["ST32" + s, f"xw{fbn}{s}", "ps_u" + s, "ST16" + s, "stmp" + s], writes=["ST32" + s, "ST16" + s, "stmp" + s]))
            ty = t % YB

            def mmy(e, ps_y=ps_y, ST16=ST16, R2l=R2l, tv=tv, ty=ty):
                ins = None
                for i in range(8):
                    ins = e.matmul(ps_y[0:64, ty * 16 + i * 2: ty * 16 + i * 2 + 2], ST16[:, i * 64:(i + 1) * 64],
                                   R2l[:, tv * 16 + i * 2: tv * 16 + i * 2 + 2], start=True, stop=True)
                return ins
            stages[5].append(lambda mmy=mmy, fb=fb, s=s: P.op("pe", mmy, reads=["ST16" + s, f"R2l{fb}{s}"], writes=["ps_y" + s]))
            if ty == YB - 1:
                def yout(A=A, ps_y=ps_y, s=s, t=t, base=base):
                    yi = (t // YB) % 2
                    ysb = A["ysb"][yi]
                    P.op("act", lambda e: e.copy(out=ysb[:].rearrange("p c t -> p t c"), in_=ps_y[0:64, :].rearrange("p (t c) -> p t c", c=16)),
                         reads=["ps_y" + s], writes=[f"ysb{yi}{s}"])
                    for j in range(2):
                        P.dma("sp", ys[base + j * 512: base + (j + 1) * 512, t - YB + 1:t + 1].rearrange("(i v) t -> v i t", v=64),
                              ysb[:, j:16:2, :], reads=[f"ysb{yi}{s}"], final=final)
                stages[6].append(yout)
            chains[hh].append(pre + [fn for st in stages for fn in st])
    flat = []
    for hh in range(NPASS):
        ops = []
        for step_ops in chains[hh]:
            ops.extend(step_ops)
        flat.append(ops)
    LAG = 3 * hh if False else 3
    n0 = max(len(f) for f in flat)
    for k in range(n0 + LAG * (NPASS - 1)):
        for hh in range(NPASS):
            kk = k - LAG * hh
            if 0 <= kk < len(flat[hh]):
                flat[hh][kk]()
    P.nosync_self = set(DEFAULT_NOSYNC)


def build_rwkvC(T=4096):
    nc = _new_nc()
    xT = _din(nc, "xT", [D, T]); ysT = _din(nc, "ysT", [D, T]); yT = _din(nc, "yT", [RW_ROWS * D, T])
    lnw_d = _din(nc, "lnw", [128, KC]); lnb_d = _din(nc, "lnb", [128, KC]); rk_d = _din(nc, "r_k", [128, KC])
    bones_d = _din(nc, "bones", [128, 128]); w_out = _din(nc, "w_out", [D, D])
    oT = _dout(nc, "oT", [D, T])
    with contextlib.ExitStack() as stack:
        P = Prog(nc, stack); c = Ctx(nc, stack, P)
        emit_rwkvC(c, xT, ysT, yT, lnw_d, lnb_d, rk_d, bones_d, w_out, oT, T, True)
        P.emit()
    return nc


def emit_rwkvC(c, xT, ysT, yT, lnw_d, lnb_d, rk_d, bones_d, w_out, oT, T, final):
    P = c.P
    if True:
        lnw = c.sb("lnw_t", [128, KC], F32); lnb = c.sb("lnb_t", [128, KC], F32); rk = c.sb("rk_t", [128, KC], F32)
        bones = c.sb("bones_t", [128, 128], F32)
        for t, d, k in ((lnw, lnw_d, "lnw"), (lnb, lnb_d, "lnb"), (rk, rk_d, "rk"), (bones, bones_d, "bones")):
            P.dma("sp", t[:], d, writes=[k])
        TH = 1024
        zT = c.sb("zT", [128, KC, TH], BF16)
        names = ["cy", "cr", "ck", "cv", "cg"]
        tl = {nm: [c.sb(f"{nm}{i}", [128, 512], F32) for i in range(2)] for nm in names}
        yc = c.sb("c_yc", [128, 512], F32); sq = c.sb("c_sq", [128, 512], F32); rs = c.sb("c_rs", [128, 512], F32)
        rkk = c.sb("c_rkk", [128, 512], F32)
        ps1 = c.ps("ps_a0"); ps2 = c.ps("ps_a1"); ps3 = c.ps("ps_b0")
        yv = yT.rearrange("(r kc p) t -> r p kc t", p=128, kc=KC)
        ysv = _fm(ysT)
        for hf in range(T // TH):
            for j in range(KC):
                for t0 in range(0, TH, 512):
                    tg = hf * TH + t0
                    bi = c.rot("cbuf", 2)
                    srcs = {"cy": ysv[:, j, tg:tg + 512], "cr": yv[RWM[0]][:, j, tg:tg + 512], "ck": yv[RWM[8]][:, j, tg:tg + 512],
                            "cv": yv[RWM[2]][:, j, tg:tg + 512], "cg": yv[RWM[5]][:, j, tg:tg + 512]}
                    for nm in names:
                        P.dma("sp", tl[nm][bi][:], srcs[nm], writes=[f"{nm}{bi}"])
                    y, r, km, v, g = (tl[nm][bi] for nm in names)
                    ky, kr, kk_, kv, kg = (f"{nm}{bi}" for nm in names)
                    P.op("pe", lambda e, y=y: e.matmul(ps1[:], bones[:], y[:], start=True, stop=True), reads=[ky, "bones"], writes=["ps_a0"])
                    P.op("dve", lambda e, y=y: e.scalar_tensor_tensor(out=yc[:], in0=ps1[:], scalar=-1.0 / 64, in1=y[:], op0=ALU.mult, op1=ALU.add),
                         reads=["ps_a0", ky], writes=["c_yc"])
                    P.op("act", lambda e: e.activation(out=sq[:], in_=yc[:], func=AF.Square), reads=["c_yc"], writes=["c_sq"])
                    P.op("pe", lambda e: e.matmul(ps2[:], bones[:], sq[:], start=True, stop=True), reads=["c_sq", "bones"], writes=["ps_a1"])
                    P.op("act", lambda e: e.activation(out=rs[:], in_=ps2[:], func=AF.Ln, bias=64e-5, scale=1.0 / 64), reads=["ps_a1"], writes=["c_rs"])
                    P.op("act", lambda e: e.activation(out=rs[:], in_=rs[:], func=AF.Exp, scale=-0.5), reads=["c_rs"], writes=["c_rs"])
                    P.op("dve", lambda e: e.tensor_tensor(out=yc[:], in0=yc[:], in1=rs[:], op=ALU.mult), reads=["c_yc", "c_rs"], writes=["c_yc"])
                    P.op("dve", lambda e, j=j: e.tensor_scalar(out=yc[:], in0=yc[:], scalar1=lnw[:, j:j + 1], scalar2=lnb[:, j:j + 1], op0=ALU.mult, op1=ALU.add),
                         reads=["c_yc", "lnw", "lnb"], writes=["c_yc"])
                    P.op("dve", lambda e, j=j, r=r, km=km: e.scalar_tensor_tensor(out=rkk[:], in0=r[:], scalar=rk[:, j:j + 1], in1=km[:], op0=ALU.mult, op1=ALU.mult),
                         reads=[kr, kk_, "rk"], writes=["c_rkk"])
                    P.op("pe", lambda e: e.matmul(ps3[:], bones[:], rkk[:], start=True, stop=True), reads=["c_rkk", "bones"], writes=["ps_b0"])
                    P.op("dve", lambda e, v=v: e.tensor_tensor(out=rkk[:], in0=ps3[:], in1=v[:], op=ALU.mult), reads=["ps_b0", kv], writes=["c_rkk"])
                    P.op("dve", lambda e: e.tensor_tensor(out=yc[:], in0=yc[:], in1=rkk[:], op=ALU.add), reads=["c_yc", "c_rkk"], writes=["c_yc"])
                    P.op("dve", lambda e, j=j, t0=t0, g=g: e.tensor_tensor(out=zT[:, j, t0:t0 + 512], in0=yc[:], in1=g[:], op=ALU.mult), reads=["c_yc", kg], writes=["zT"])
            emit_outproj(c, w_out.rearrange("(c p) n -> p c n", p=128), KC, zT, ["zT"], _fm(xT), _fm(oT), hf * TH, TH, 1.0, final)


TF = SEQ


def build_fused():
    nc = _new_nc()
    A = {}

    def din(name, shape, dt=F32):
        A[name] = _din(nc, name, shape, dt)
        return A[name]
    xT = din("xT", [D, TF]); memT = din("memT", [D, ML]); posb = din("posb", [128, TF], I32)
    din("ffn_gain", [4, 2, 128, KC]); din("ffn_w_gate", [4, 2, D, FF]); din("ffn_w_up", [4, 2, D, FF]); din("ffn_w_down", [4, 2, FF, D])
    din("mix_gain", [4, 128, KC]); din("xg", [4, 128, KC]); din("mg", [4, 128, KC]); din("xq", [4, 128, 4]); din("xk", [4, 128, 4])
    din("xattn_wq", [4, D, D]); din("xattn_wkv", [4, D, 2 * D]); din("xattn_wo", [4, D, D])
    din("conv_w_in", [1, D, 3 * D]); din("conv_cw", [128, KC * 3]); din("conv_w_out", [1, D, D])
    din("dil_w_qkv", [1, D, 9216]); din("dil_gq", [128, 3]); din("dil_gk", [128, 3]); din("dil_w_out", [1, 1024, D])
    din("hgrn_w_in", [1, D, 4 * D]); din("hgrn_lbl", [128, 64]); din("hgrn_ng", [HC, 128]); din("hgrn_w_out", [1, D, D])
    din("rwkv_mu", [128, 6 * KC]); din("rwkv_w_rkv", [1, 3, D, D])
    for nm in ("w0", "a0", "k_k", "k_a", "lnw", "lnb", "r_k"):
        din("rwkv_" + nm, [128, KC])
    din("rwkv_w1", [1, D, 96]); din("rwkv_w2", [1, 96, D]); din("rwkv_a1", [1, D, 96]); din("rwkv_a2", [1, 96, D])
    din("rwkv_g1", [1, D, 256]); din("rwkv_g2", [1, 256, D]); din("rwkv_w_out", [1, D, D])
    din("c_invf", [128, 1]); din("c_rm", [128, 128]); din("c_mask", [128, 256]); din("c_ident", [128, 128])
    din("c_cm", [128, SEQ]); din("c_tm", [HC, HC]); din("c_bones", [128, 128]); din("c_sel", [16, 512])
    oT = _dout(nc, "oT", [D, TF])

    def scratch(name, shape):
        return nc.dram_tensor(name, list(shape), F32, kind="Internal").ap()
    xa = scratch("scr_xa", [D, TF]); xb = scratch("scr_xb", [D, TF])
    yTs = scratch("scr_y", [RW_ROWS * D, TF]); yQs = scratch("scr_q", [9216, TF]); sTs = scratch("scr_s", [D, TF]); ysT = scratch("scr_ys", [D, TF])
    rw_tm = nc.dram_tensor("scr_tm", [6, TF, 1024], BF16, kind="Internal").ap()

    with contextlib.ExitStack() as stack:
        P = Prog(nc, stack)
        state = {"k": 0}

        def stage(fn, last=False):
            k = state["k"]
            state["k"] += 1
            with contextlib.ExitStack() as st:
                c = Ctx(nc, st, P, pfx=f"s{k}_")
                fn(c, st, f"s{k}_")
                P.barrier()
                if last:
                    P.emit()
                else:
                    P.flush()

        cur = xT
        bufs = [xa, xb]
        nb = 0

        def nxt():
            nonlocal nb
            b = bufs[nb % 2]
            nb += 1
            return b

        for i in range(4):
            dst = nxt()
            stage(lambda c, st, pf, cur=cur, dst=dst, i=i: emit_ffn(nc, st, P, cur, A["ffn_gain"][i, 0], A["ffn_w_gate"][i, 0], A["ffn_w_up"][i, 0],
                                                                      A["ffn_w_down"][i, 0], dst, TF, pf, final=False))
            cur = dst
            dst = nxt()
            mg = A["mix_gain"][i]
            if i == 0:
                stage(lambda c, st, pf, cur=cur, dst=dst: emit_conv(c, cur, mg, A["conv_w_in"][0], A["conv_cw"], A["conv_w_out"][0], dst, TF, False))
            elif i == 1:
                stage(lambda c, st, pf, cur=cur: emit_normproj(c, cur, mg, A["dil_w_qkv"][0], yQs, 9216, TF))
                stage(lambda c, st, pf: emit_dilcore(c, yQs, posb, A["c_invf"], A["c_rm"], A["c_mask"], A["c_ident"], A["dil_gq"], A["dil_gk"],
                                                     sTs[0:1024, :], 8, False))
                stage(lambda c, st, pf, cur=cur, dst=dst: emit_outproj_stage(c, cur, sTs[0:1024, :], A["dil_w_out"][0], dst, 8, TF))
            elif i == 2:
                stage(lambda c, st, pf, cur=cur: emit_normproj(c, cur, mg, A["hgrn_w_in"][0], yQs[0:8192, :], 8192, TF))
                stage(lambda c, st, pf: emit_hgrncore(c, yQs[0:8192, :], A["hgrn_lbl"], A["hgrn_ng"], A["c_cm"], A["c_tm"], A["c_ident"], sTs, 16, 2, False))
                stage(lambda c, st, pf, cur=cur, dst=dst: emit_outproj_stage(c, cur, sTs, A["hgrn_w_out"][0], dst, 16, TF))
            else:
                stage(lambda c, st, pf, cur=cur: emit_rwkvA(c, cur, mg, A["rwkv_mu"], A["rwkv_w_rkv"][0], A["rwkv_w0"], A["rwkv_w1"][0], A["rwkv_w2"][0],
                                                            A["rwkv_a0"], A["rwkv_a1"][0], A["rwkv_a2"][0], A["rwkv_g1"][0], A["rwkv_g2"][0],
                                                            A["rwkv_k_k"], A["rwkv_k_a"], A["c_bones"], yTs, TF, False))
                stage(lambda c, st, pf: emit_rwkvB(c, yTs, A["c_sel"], A["c_ident"], rw_tm, ysT, TF, 2, False))
                stage(lambda c, st, pf, cur=cur, dst=dst: emit_rwkvC(c, cur, ysT, yTs, A["rwkv_lnw"], A["rwkv_lnb"], A["rwkv_r_k"], A["c_bones"],
                                                                     A["rwkv_w_out"][0], dst, TF, False))
            cur = dst
            dst = nxt()
            stage(lambda c, st, pf, cur=cur, dst=dst, i=i: emit_xattn(c, cur, memT, A["xg"][i], A["mg"][i], A["xq"][i], A["xk"][i],
                                                                       A["xattn_wq"][i], A["xattn_wkv"][i], A["xattn_wo"][i], dst, TF, False))
            cur = dst
            last = (i == 3)
            dst = oT if last else nxt()
            stage(lambda c, st, pf, cur=cur, dst=dst, i=i, last=last: emit_ffn(nc, st, P, cur, A["ffn_gain"][i, 1], A["ffn_w_gate"][i, 1], A["ffn_w_up"][i, 1],
                                                                                A["ffn_w_down"][i, 1], dst, TF, pf, final=last), last=last)
            cur = dst
    return nc


_NC_CACHE = {}


def _pc(v):
    return np.ascontiguousarray(np.asarray(v, np.float32).reshape(-1, 128).T)


def _c(a):
    return np.ascontiguousarray(a)


def kernel(**inp):
    inp = {k: np.asarray(v) for k, v in inp.items()}
    x = inp["x"]
    B, S, _ = x.shape
    if "fused" not in _NC_CACHE:
        _NC_CACHE["fused"] = build_fused()
    nc = _NC_CACHE["fused"]
    invf, rm, mask = dil_consts()
    cm, tm, ident = hgrn_consts()
    bones, sel = rwkv_consts()
    shared = {
        "ffn_gain": _c(np.stack([np.stack([_pc(inp["ffn_norm"][i, j]) for j in range(2)]) for i in range(4)])),
        "ffn_w_gate": inp["ffn_w_gate"], "ffn_w_up": inp["ffn_w_up"], "ffn_w_down": inp["ffn_w_down"],
        "mix_gain": _c(np.stack([_pc(inp["mix_norm"][i]) for i in range(4)])),
        "xg": _c(np.stack([_pc(inp["xattn_norm"][i]) for i in range(4)])),
        "mg": _c(np.stack([_pc(inp["mem_norm"][i]) for i in range(4)])),
        "xq": _c(np.stack([_pc(inp["xattn_q_gain"][i]) for i in range(4)])),
        "xk": _c(np.stack([_pc(inp["xattn_k_gain"][i]) for i in range(4)])),
        "xattn_wq": inp["xattn_wq"], "xattn_wkv": inp["xattn_wkv"], "xattn_wo": inp["xattn_wo"],
        "conv_w_in": inp["conv_w_in"], "conv_w_out": inp["conv_w_out"],
        "conv_cw": _c(inp["conv_w"][0].T.reshape(16, 128, 3).transpose(1, 0, 2).reshape(128, 48)),
        "dil_w_qkv": inp["dil_w_qkv"], "dil_w_out": inp["dil_w_out"],
        "dil_gq": _c(inp["dil_q_gain"][0].T), "dil_gk": _c(inp["dil_k_gain"][0].T),
        "hgrn_w_in": inp["hgrn_w_in"], "hgrn_w_out": inp["hgrn_w_out"],
        "hgrn_lbl": _c(inp["hgrn_lb_logits"].reshape(4, 16, 128).transpose(2, 1, 0).reshape(128, 64)),
        "hgrn_ng": _c(np.tile(inp["hgrn_norm"][0][None], (HC, 1))),
        "rwkv_mu": _c(np.concatenate([_pc(inp["rwkv_mu"][0][i]) for i in range(6)], axis=1)),
        "rwkv_w_rkv": inp["rwkv_w_rkv"],
        "rwkv_w0": _pc(inp["rwkv_w0"][0]), "rwkv_a0": _pc(inp["rwkv_a0"][0]), "rwkv_k_k": _pc(inp["rwkv_k_k"][0]),
        "rwkv_k_a": _pc(inp["rwkv_k_a"][0]), "rwkv_lnw": _pc(inp["rwkv_ln_w"][0]), "rwkv_lnb": _pc(inp["rwkv_ln_b"][0]),
        "rwkv_r_k": _pc(inp["rwkv_r_k"][0].reshape(-1)),
        "rwkv_w1": inp["rwkv_w1"], "rwkv_w2": inp["rwkv_w2"], "rwkv_a1": inp["rwkv_a1"], "rwkv_a2": inp["rwkv_a2"],
        "rwkv_g1": inp["rwkv_g1"], "rwkv_g2": inp["rwkv_g2"], "rwkv_w_out": inp["rwkv_w_out"],
        "c_invf": invf, "c_rm": rm, "c_mask": mask, "c_ident": ident, "c_cm": cm, "c_tm": tm, "c_bones": bones, "c_sel": rwkv_bmask(),
    }
    shared = {k: _c(np.asarray(v, np.float32)) for k, v in shared.items()}
    in_maps = []
    for b in range(B):
        m = dict(shared)
        m["xT"] = _c(x[b].T)
        m["memT"] = _c(inp["mem"][b].T)
        m["posb"] = _c(np.tile(inp["positions"][b][None].astype(np.int32), (128, 1)))
        in_maps.append(m)
    res = run_bass_kernel_spmd(nc, in_maps, core_ids=list(range(B)))
    out = np.empty((B, S, D), np.float32)
    for b in range(B):
        out[b] = res.results[b]["oT"].T
    return out
```
